# Optimizing a Trainium2 kernel written in Bass

```python
import math
import jax
import jax.numpy as jnp
from jax import lax
import numpy as np

D_MODEL = 2048
BATCH = 2
SEQ = 4096
DEPTH = 4

GRID_W = 64
CTX_LEN = 256

HEAD_W = 128
HY_C = 512
HY_ORDER = 2
HY_SHORT = 3
HY_BANDS = 16
HY_EMB = 1 + 2 * HY_BANDS
HY_FFN = 64
HY_FAST_DECAY = 0.3
HY_SLOW_DECAY = 1.5
HY_DECAY_TARGET = 1e-2

LRU_C = 512
LRU_BLOCKS = 4
LRU_BW = LRU_C // LRU_BLOCKS
LRU_CONV = 4
RG_C = 8.0

MLA_H = 8
Q_RANK = 512
KV_RANK = 256
NOPE_D = 128
ROPE_D = 64
V_D = 128
ROPE_BASE = 10000.0
Q_BLOCK = 128
MLA_SCALE = (NOPE_D + ROPE_D) ** -0.5

D_MIX = HY_C + LRU_C + MLA_H * V_D
N_MIX_HEADS = D_MIX // HEAD_W
D_FF = 5632
FFN_CONV = 3
N_MOD = 6
EPS = 1e-6

OFF_HY = 0
OFF_LRU_G = OFF_HY + (HY_ORDER + 1) * HY_C
OFF_MLA_Q = OFF_LRU_G + LRU_C
OFF_LRU_X = OFF_MLA_Q + Q_RANK
OFF_MLA_KV = OFF_LRU_X + LRU_C
OFF_MLA_KR = OFF_MLA_KV + KV_RANK
D_IN = OFF_MLA_KR + ROPE_D

kernel_name = 'hymba_style_hyena_rglru_mla_dit'

F32 = jnp.float32


def rmsnorm(x, g):
    xf = x.astype(F32)
    y = xf * lax.rsqrt(jnp.mean(xf * xf, axis=-1, keepdims=True) + EPS)
    return (y * g.astype(F32)).astype(x.dtype)


def modulate(h, shift, scale):
    return h * (1 + scale) + shift


def dwconv(x, w, left):
    k = w.shape[0]
    length = x.shape[1]
    xp = jnp.pad(x, ((0, 0), (left, k - 1 - left), (0, 0)))
    w = w.astype(x.dtype)
    out = xp[:, 0:length] * w[0]
    for j in range(1, k):
        out = out + xp[:, j:j + length] * w[j]
    return out


def hyena_filter_spectrum(length, w1, b1, w2, b2, w3):
    t = jnp.arange(length, dtype=F32) / length
    ang = 2.0 * math.pi * t[:, None] * jnp.arange(1, HY_BANDS + 1, dtype=F32)
    feats = jnp.concatenate([t[:, None], jnp.sin(ang), jnp.cos(ang)], axis=-1)
    hid = jnp.sin(feats @ w1.astype(F32) + b1.astype(F32))
    hid = jnp.sin(hid @ w2.astype(F32) + b2.astype(F32))
    h = (hid @ w3.astype(F32)).reshape(length, HY_ORDER, 2, HY_C)
    deltas = jnp.abs(jnp.linspace(math.log(HY_DECAY_TARGET) / HY_SLOW_DECAY,
                                  math.log(HY_DECAY_TARGET) / HY_FAST_DECAY, HY_C, dtype=F32))
    h = h * jnp.exp(-t[:, None] * deltas)[:, None, None, :]
    fwd, bwd = h[:, :, 0], h[:, :, 1]
    k = jnp.concatenate([fwd, jnp.zeros((1, HY_ORDER, HY_C), F32), bwd[:0:-1]], axis=0)
    k = k / jnp.sum(jnp.abs(k), axis=0, keepdims=True)
    return jnp.fft.rfft(k, axis=0)


def long_conv(z, spec):
    length = z.shape[1]
    zf = jnp.fft.rfft(z, n=2 * length, axis=1)
    return jnp.fft.irfft(zf * spec, n=2 * length, axis=1)[:, :length]


def hyena_mix(u, conv_w, w1, b1, w2, b2, w3, bias):
    length = u.shape[1]
    parts = jnp.split(dwconv(u, conv_w, HY_SHORT // 2).astype(F32), HY_ORDER + 1, axis=-1)
    spec = hyena_filter_spectrum(length, w1, b1, w2, b2, w3)
    z = parts[0]
    for o in range(HY_ORDER):
        z = parts[o + 1] * (long_conv(z, spec[:, o]) + z * bias[o].astype(F32))
    return z.astype(u.dtype)


def _lru_combine(left, right):
    a1, b1 = left
    a2, b2 = right
    return a1 * a2, a2 * b1 + b2


def rglru_scan(xr, conv_w, wa, ba, wx, bx, lam, h0, reverse):
    bsz, length, _ = xr.shape
    xc = dwconv(xr, conv_w, 0 if reverse else LRU_CONV - 1)
    xb = xc.reshape(bsz, length, LRU_BLOCKS, LRU_BW)
    r = jax.nn.sigmoid(jnp.einsum('blnd,nde->blne', xb, wa).reshape(bsz, length, LRU_C).astype(F32) + ba.astype(F32))
    i = jax.nn.sigmoid(jnp.einsum('blnd,nde->blne', xb, wx).reshape(bsz, length, LRU_C).astype(F32) + bx.astype(F32))
    log_a = -RG_C * r * jax.nn.softplus(-lam.astype(F32))
    a = jnp.exp(log_a)
    b = jnp.sqrt(-jnp.expm1(2.0 * log_a)) * (i * xc.astype(F32))
    if h0 is not None:
        first = length - 1 if reverse else 0
        b = b.at[:, first].add(a[:, first] * h0)
    _, h = lax.associative_scan(_lru_combine, (a, b), reverse=reverse, axis=1)
    return h


def lru_output(gate_pre, h):
    return (jax.nn.gelu(gate_pre.astype(F32), approximate=True) * h).astype(gate_pre.dtype)


def axial_rope_tables(length):
    rows = length // GRID_W
    row = jnp.repeat(jnp.arange(rows, dtype=F32), GRID_W)
    col = jnp.tile(jnp.arange(GRID_W, dtype=F32), rows)
    n_freq = ROPE_D // 4
    inv = ROPE_BASE ** (-jnp.arange(n_freq, dtype=F32) / n_freq)
    ang = jnp.concatenate([row[:, None] * inv, col[:, None] * inv], axis=-1)
    return jnp.cos(ang), jnp.sin(ang)


def apply_rope(x, cos, sin):
    half = x.shape[-1] // 2
    x1 = x[..., :half].astype(F32)
    x2 = x[..., half:].astype(F32)
    return jnp.concatenate([x1 * cos - x2 * sin, x2 * cos + x1 * sin], axis=-1).astype(x.dtype)


def mla_query(uq, gq, wuq, rope):
    bsz, length, _ = uq.shape
    q = (rmsnorm(uq, gq) @ wuq).reshape(bsz, length, MLA_H, NOPE_D + ROPE_D)
    if rope is not None:
        cos, sin = rope
        q = jnp.concatenate([q[..., :NOPE_D], apply_rope(q[..., NOPE_D:], cos[:, None], sin[:, None])], axis=-1)
    return q


def mla_keys_values(ukv, ukr, gkv, wukv, rope):
    bsz, length, _ = ukv.shape
    kv = (rmsnorm(ukv, gkv) @ wukv).reshape(bsz, length, MLA_H, NOPE_D + V_D)
    k_nope, v = kv[..., :NOPE_D], kv[..., NOPE_D:]
    k_rope = ukr if rope is None else apply_rope(ukr, rope[0], rope[1])
    k = jnp.concatenate([k_nope, jnp.broadcast_to(k_rope[:, :, None], (bsz, length, MLA_H, ROPE_D))], axis=-1)
    return k, v


def attend(q, k, v):
    s = jnp.einsum('bqhd,bkhd->bhqk', q, k, preferred_element_type=F32) * MLA_SCALE
    p = jax.nn.softmax(s, axis=-1).astype(v.dtype)
    return jnp.einsum('bhqk,bkhd->bqhd', p, v)


def latent_attention(q, k_lat, v_lat, k_ctx, v_ctx):
    bsz, length = q.shape[:2]
    k = jnp.concatenate([k_ctx, k_lat], axis=1)
    v = jnp.concatenate([v_ctx, v_lat], axis=1)
    nb = length // Q_BLOCK
    qb = q.reshape(bsz, nb, Q_BLOCK, MLA_H, NOPE_D + ROPE_D).swapaxes(0, 1)
    o = lax.map(lambda qi: attend(qi, k, v), qb)
    return o.swapaxes(0, 1).reshape(bsz, length, MLA_H * V_D)


def mix_out(y_hy, y_lru, y_mla, head_g, w_out):
    y = jnp.concatenate([y_hy, y_lru, y_mla], axis=-1)
    bsz, length, _ = y.shape
    y = rmsnorm(y.reshape(bsz, length, N_MIX_HEADS, HEAD_W), head_g.reshape(N_MIX_HEADS, HEAD_W))
    return y.reshape(bsz, length, D_MIX) @ w_out


def conv_ffn(h, w_up, w_conv, w_down):
    up = dwconv(h @ w_up, w_conv, FFN_CONV // 2)
    gate, val = jnp.split(up, 2, axis=-1)
    return (jax.nn.gelu(gate, approximate=True) * val) @ w_down


def setup_inputs(seed: int = 0) -> dict:
    key = jax.random.key(seed)
    ks = iter(jax.random.split(key, 40))

    def nrm(shape, scale):
        return scale * jax.random.normal(next(ks), shape, F32)

    def gain(shape):
        return 1.0 + 0.05 * jax.random.normal(next(ks), shape, F32)

    a0 = jax.random.uniform(next(ks), (DEPTH, 2, LRU_C), F32, 0.9, 0.999) ** (1.0 / RG_C)
    lru_lam = jnp.log(a0) - jnp.log1p(-a0)
    return {
        'x': nrm((BATCH, SEQ, D_MODEL), 1.0),
        'c': nrm((BATCH, D_MODEL), 1.0),
        'ctx': nrm((BATCH, CTX_LEN, D_MODEL), 1.0),
        'c_ctx': nrm((D_MODEL,), 1.0),
        'ada_w': nrm((DEPTH, D_MODEL, N_MOD * D_MODEL), 0.5 * D_MODEL ** -0.5),
        'ada_b': nrm((DEPTH, N_MOD * D_MODEL), 0.02),
        'norm_g': gain((DEPTH, 4, D_MODEL)),
        'w_in': nrm((DEPTH, D_MODEL, D_IN), D_MODEL ** -0.5),
        'hy_conv': nrm((DEPTH, HY_SHORT, (HY_ORDER + 1) * HY_C), HY_SHORT ** -0.5),
        'hy_w1': nrm((DEPTH, HY_EMB, HY_FFN), HY_EMB ** -0.5),
        'hy_b1': nrm((DEPTH, HY_FFN), 0.1),
        'hy_w2': nrm((DEPTH, HY_FFN, HY_FFN), HY_FFN ** -0.5),
        'hy_b2': nrm((DEPTH, HY_FFN), 0.1),
        'hy_w3': nrm((DEPTH, HY_FFN, HY_ORDER * 2 * HY_C), HY_FFN ** -0.5),
        'hy_bias': nrm((DEPTH, HY_ORDER, HY_C), 0.5),
        'lru_conv': nrm((DEPTH, 2, LRU_CONV, LRU_C), LRU_CONV ** -0.5),
        'lru_wa': nrm((DEPTH, 2, LRU_BLOCKS, LRU_BW, LRU_BW), LRU_BW ** -0.5),
        'lru_ba': nrm((DEPTH, 2, LRU_C), 0.02),
        'lru_wx': nrm((DEPTH, 2, LRU_BLOCKS, LRU_BW, LRU_BW), LRU_BW ** -0.5),
        'lru_bx': nrm((DEPTH, 2, LRU_C), 0.02),
        'lru_lam': lru_lam,
        'mla_gq': gain((DEPTH, Q_RANK)),
        'mla_gkv': gain((DEPTH, KV_RANK)),
        'mla_wuq': nrm((DEPTH, Q_RANK, MLA_H * (NOPE_D + ROPE_D)), Q_RANK ** -0.5),
        'mla_wukv': nrm((DEPTH, KV_RANK, MLA_H * (NOPE_D + V_D)), KV_RANK ** -0.5),
        'head_g': gain((DEPTH, D_MIX)),
        'w_out': nrm((DEPTH, D_MIX, D_MODEL), D_MIX ** -0.5),
        'ffn_up': nrm((DEPTH, D_MODEL, 2 * D_FF), D_MODEL ** -0.5),
        'ffn_conv': nrm((DEPTH, FFN_CONV, 2 * D_FF), FFN_CONV ** -0.5),
        'ffn_down': nrm((DEPTH, D_FF, D_MODEL), D_FF ** -0.5),
    }


def reference(x, c, ctx, c_ctx, ada_w, ada_b, norm_g, w_in, hy_conv, hy_w1, hy_b1, hy_w2, hy_b2,
              hy_w3, hy_bias, lru_conv, lru_wa, lru_ba, lru_wx, lru_bx, lru_lam, mla_gq, mla_gkv,
              mla_wuq, mla_wukv, head_g, w_out, ffn_up, ffn_conv, ffn_down):
    bsz, seq, _ = x.shape
    ctx_len = ctx.shape[1]
    rope = axial_rope_tables(seq)
    xc = ctx
    for l in range(DEPTH):
        need_ctx = l < DEPTH - 1
        ng = norm_g[l]
        mod = jax.nn.silu(c) @ ada_w[l] + ada_b[l]
        sh1, sc1, g1, sh2, sc2, g2 = jnp.split(mod[:, None, :], N_MOD, axis=-1)
        mod_c = jax.nn.silu(c_ctx) @ ada_w[l] + ada_b[l]
        csh1, csc1, cg1, csh2, csc2, cg2 = jnp.split(mod_c, N_MOD, axis=-1)
        hy_p = (hy_conv[l], hy_w1[l], hy_b1[l], hy_w2[l], hy_b2[l], hy_w3[l], hy_bias[l])
        lru_f = (lru_conv[l, 0], lru_wa[l, 0], lru_ba[l, 0], lru_wx[l, 0], lru_bx[l, 0], lru_lam[l, 0])
        lru_b = (lru_conv[l, 1], lru_wa[l, 1], lru_ba[l, 1], lru_wx[l, 1], lru_bx[l, 1], lru_lam[l, 1])

        h = modulate(rmsnorm(x, ng[0]), sh1, sc1)
        hc = modulate(rmsnorm(xc, ng[0]), csh1, csc1)
        u = h @ w_in[l]
        uc_s = hc @ w_in[l, :, OFF_LRU_X:]

        xr_c = uc_s[..., :LRU_C]
        hc_f = rglru_scan(xr_c, *lru_f, None, False)
        hc_b = rglru_scan(xr_c, *lru_b, None, True)
        k_c, v_c = mla_keys_values(uc_s[..., OFF_MLA_KV - OFF_LRU_X:OFF_MLA_KR - OFF_LRU_X],
                                   uc_s[..., OFF_MLA_KR - OFF_LRU_X:], mla_gkv[l], mla_wukv[l], None)

        y_hy = hyena_mix(u[..., OFF_HY:OFF_LRU_G], *hy_p)
        xr = u[..., OFF_LRU_X:OFF_MLA_KV]
        h_f = rglru_scan(xr, *lru_f, hc_f[:, -1], False)
        h_b = rglru_scan(xr, *lru_b, hc_b[:, 0], True)
        y_lru = lru_output(u[..., OFF_LRU_G:OFF_MLA_Q], h_f + h_b)
        q = mla_query(u[..., OFF_MLA_Q:OFF_LRU_X], mla_gq[l], mla_wuq[l], rope)
        k, v = mla_keys_values(u[..., OFF_MLA_KV:OFF_MLA_KR], u[..., OFF_MLA_KR:], mla_gkv[l], mla_wukv[l], rope)
        y_mla = latent_attention(q, k, v, k_c, v_c)
        x = x + g1 * rmsnorm(mix_out(y_hy, y_lru, y_mla, head_g[l], w_out[l]), ng[1])

        if need_ctx:
            uc_o = hc @ w_in[l, :, :OFF_LRU_X]
            yc_hy = hyena_mix(uc_o[..., OFF_HY:OFF_LRU_G], *hy_p)
            yc_lru = lru_output(uc_o[..., OFF_LRU_G:OFF_MLA_Q], hc_f + hc_b)
            q_c = mla_query(uc_o[..., OFF_MLA_Q:], mla_gq[l], mla_wuq[l], None)
            yc_mla = attend(q_c, k_c, v_c).reshape(bsz, ctx_len, MLA_H * V_D)
            xc = xc + cg1 * rmsnorm(mix_out(yc_hy, yc_lru, yc_mla, head_g[l], w_out[l]), ng[1])

        h2 = modulate(rmsnorm(x, ng[2]), sh2, sc2)
        x = x + g2 * rmsnorm(conv_ffn(h2, ffn_up[l], ffn_conv[l], ffn_down[l]), ng[3])
        if need_ctx:
            hc2 = modulate(rmsnorm(xc, ng[2]), csh2, csc2)
            xc = xc + cg2 * rmsnorm(conv_ffn(hc2, ffn_up[l], ffn_conv[l], ffn_down[l]), ng[3])
    return x
```

```python
import math
import numpy as np
import ml_dtypes
import concourse.bass as bass
import concourse.mybir as mybir
from concourse.bass_utils import run_bass_kernel_spmd

F32 = mybir.dt.float32
BF16 = mybir.dt.bfloat16
AF = mybir.ActivationFunctionType
ALU = mybir.AluOpType

D = 2048
SEQ = 4096
CTX = 256
NT = SEQ + CTX
DEPTH = 4
DIN = 3392
DINX = 3456
OFF_HY, OFF_G, OFF_Q, OFF_X, OFF_KV, OFF_KR, OFF_KRR = 0, 1536, 2048, 2560, 3072, 3328, 3392
DFF = 5632
EPS = 1e-6
MLA_SCALE = 192 ** -0.5
MAGIC = 12582912.0

SP_ADAB, SP_NG, SP_HYC, SP_HYB, SP_LC, SP_BA, SP_BX, SP_LAM = 0, 96, 160, 196, 204, 236, 244, 252
SP_GQ, SP_GKV, SP_HG, SP_FC, SP_B1, SP_B2, NSP = 260, 264, 266, 282, 546, 547, 548


class Res:
    __slots__ = ("name", "w", "r")

    def __init__(self, name=""):
        self.name = name
        self.w = None
        self.r = {}


class EngQ:
    def __init__(self, name, eng, inorder=False):
        self.name = name
        self.eng = eng
        self.inorder = inorder
        self.sem = None
        self.step = 1
        self.count = 0
        self.waited = {}
        self.thunks = []


class DmaSlot:
    def __init__(self, sem, name):
        self.sem = sem
        self.name = name
        self.count = 0
        self.step = 16
        self.inorder = False


class Ring:
    def __init__(self, items):
        self.items = items
        self.i = 0

    def next(self):
        it = self.items[self.i % len(self.items)]
        self.i += 1
        return it


class Prog:
    def __init__(self, nc, n_dma_slots=20, same_engine_sync=True):
        self.nc = nc
        self.same_engine_sync = same_engine_sync
        self.ctx = []
        self.pe = EngQ("pe", nc.tensor, inorder=True)
        self.dve = EngQ("dve", nc.vector)
        self.act = EngQ("act", nc.scalar)
        self.pool = EngQ("pool", nc.gpsimd)
        self.sp = EngQ("sp", nc.sync)
        self.engs = [self.pe, self.dve, self.act, self.pool, self.sp]
        for e in self.engs:
            e.sem = self._sem("s_" + e.name)
        self.slots = {}
        for qn in ("sp", "act", "pool"):
            self.slots[qn] = [DmaSlot(self._sem(f"d_{qn}{i}"), f"d_{qn}{i}") for i in range(n_dma_slots)]
        self.slot_rr = {"sp": 0, "act": 0, "pool": 0}
        self.n_inst = 0
        self.uid = 0

    def _sem(self, name):
        cm = self.nc.semaphore(name)
        s = cm.__enter__()
        self.ctx.append(cm)
        return s

    def push(self):
        return len(self.ctx)

    def pop(self, mark):
        self.barrier()
        while len(self.ctx) > mark:
            self.ctx.pop().__exit__(None, None, None)

    def sbuf(self, name, shape, dtype):
        self.uid += 1
        cm = self.nc.sbuf_tensor(f"{name}_{self.uid}", list(shape), dtype)
        t = cm.__enter__()
        self.ctx.append(cm)
        return t

    def psum(self, name, shape, dtype=F32):
        cm = self.nc.psum_tensor(name, list(shape), dtype)
        t = cm.__enter__()
        self.ctx.append(cm)
        return t

    def ring(self, name, n, shape, dtype):
        return Ring([(self.sbuf(f"{name}{i}", shape, dtype), Res(f"{name}{i}")) for i in range(n)])

    def _deps(self, reads, writes):
        deps = []
        for r in reads:
            if r.w is not None:
                deps.append(r.w)
        for w in writes:
            if w.w is not None:
                deps.append(w.w)
            deps.extend(w.r.values())
        return deps

    def _emit_waits(self, q, deps):
        need = {}
        for (src, c) in deps:
            if src is q and (q.inorder or not self.same_engine_sync):
                continue
            if q.waited.get(id(src), 0) >= c:
                continue
            if need.get(id(src), (None, 0))[1] < c:
                need[id(src)] = (src, c)
        for src, c in need.values():
            q.waited[id(src)] = c
            sem, val = src.sem, c * src.step
            q.thunks.append(lambda e, sem=sem, val=val: e.wait_ge(sem, val))

    def _mark(self, src, c, reads, writes):
        for r in reads:
            old = r.r.get(id(src))
            if old is None or old[1] < c:
                r.r[id(src)] = (src, c)
        for w in writes:
            w.w = (src, c)
            w.r = {}

    def op(self, q, fn, reads=(), writes=()):
        self._emit_waits(q, self._deps(reads, writes))
        q.count += 1
        sem = q.sem
        q.thunks.append(lambda e, fn=fn, sem=sem: fn(e).then_inc(sem, 1))
        self._mark(q, q.count, reads, writes)
        self.n_inst += 1

    def dma(self, qname, fn, reads=(), writes=()):
        q = {"sp": self.sp, "act": self.act, "pool": self.pool}[qname]
        slots = self.slots[qname]
        i = self.slot_rr[qname]
        self.slot_rr[qname] = (i + 1) % len(slots)
        slot = slots[i]
        deps = self._deps(reads, writes)
        if slot.count > 0:
            deps.append((slot, slot.count))
        self._emit_waits(q, deps)
        slot.count += 1
        sem = slot.sem
        q.thunks.append(lambda e, fn=fn, sem=sem: fn(e).then_inc(sem, 16))
        self._mark(slot, slot.count, reads, writes)
        self.n_inst += 1

    def barrier(self):
        srcs = list(self.engs)
        for qn in self.slots:
            srcs.extend(self.slots[qn])
        deps = [(s, s.count) for s in srcs if s.count > 0]
        for q in self.engs:
            need = [(s, c) for (s, c) in deps]
            keep_sync = self.same_engine_sync
            self.same_engine_sync = True
            self._emit_waits(q, need)
            self.same_engine_sync = keep_sync

    def mm(self, out, lhsT, rhs, start, stop, reads, writes):
        self.op(self.pe, lambda e: e.matmul(out, lhsT=lhsT, rhs=rhs, start=start, stop=stop), reads, writes)

    def tr(self, out, in_, ident, reads, writes):
        self.op(self.pe, lambda e: e.transpose(out, in_, ident), reads, writes)

    def tt(self, q, out, in0, in1, op, reads, writes):
        self.op(q, lambda e: e.tensor_tensor(out=out, in0=in0, in1=in1, op=op), reads, writes)

    def ts(self, q, out, in0, s1, s2, op0, op1, reads, writes):
        if op1 is None:
            self.op(q, lambda e: e.tensor_scalar(out=out, in0=in0, scalar1=s1, scalar2=None, op0=op0), reads, writes)
        else:
            self.op(q, lambda e: e.tensor_scalar(out=out, in0=in0, scalar1=s1, scalar2=s2, op0=op0, op1=op1), reads, writes)

    def stt(self, out, in0, scalar, in1, op0, op1, reads, writes):
        self.op(self.dve, lambda e: e.scalar_tensor_tensor(out=out, in0=in0, scalar=scalar, in1=in1, op0=op0, op1=op1), reads, writes)

    def actv(self, out, in_, func, reads, writes, bias=None, scale=None):
        kw = {}
        if bias is not None:
            kw["bias"] = bias
        if scale is not None:
            kw["scale"] = scale
        self.op(self.act, lambda e: e.activation(out=out, in_=in_, func=func, **kw), reads, writes)

    def copy(self, q, out, in_, reads, writes):
        if q is self.act:
            self.op(q, lambda e: e.activation(out=out, in_=in_, func=AF.Copy), reads, writes)
        else:
            self.op(q, lambda e: e.tensor_copy(out=out, in_=in_), reads, writes)

    def recip(self, out, in_, reads, writes):
        self.op(self.dve, lambda e: e.reciprocal(out=out, in_=in_), reads, writes)

    def memset(self, q, ap, val, writes):
        self.op(q, lambda e: e.memset(ap, val), (), writes)

    def ld(self, qname, out, in_, reads=(), writes=()):
        self.dma(qname, lambda e: e.dma_start(out=out, in_=in_), reads, writes)

    def finish(self, final_res):
        deps = [r.w for r in final_res if r.w is not None]
        self._emit_waits(self.sp, deps)
        self.barrier()
        nc = self.nc
        with nc.Block() as block:
            @block.sync
            def _(e):
                for t in self.sp.thunks:
                    t(e)

            @block.tensor
            def _(e):
                for t in self.pe.thunks:
                    t(e)

            @block.vector
            def _(e):
                for t in self.dve.thunks:
                    t(e)

            @block.scalar
            def _(e):
                for t in self.act.thunks:
                    t(e)

            @block.gpsimd
            def _(e):
                for t in self.pool.thunks:
                    t(e)
        while self.ctx:
            self.ctx.pop().__exit__(None, None, None)


_CONST_CACHE = {}


def _bf(a):
    return np.ascontiguousarray(a.astype(np.float32)).astype(ml_dtypes.bfloat16)


def _dft_tables(L):
    N = 2 * L
    nfc = (L + 1 + 127) // 128
    FP = nfc * 128
    a = np.arange(FP, dtype=np.int64)
    t = np.arange(L, dtype=np.int64)
    ang = 2.0 * np.pi * ((t[:, None] * a[None, :]) % N).astype(np.float64) / N
    valid = (a <= L).astype(np.float64)
    C = np.cos(ang) * valid
    S = np.sin(ang) * valid
    ntc = L // 128
    Fc = C.reshape(ntc, 128, nfc, 128).transpose(2, 1, 0, 3)
    Fs = S.reshape(ntc, 128, nfc, 128).transpose(2, 1, 0, 3)
    w = np.where((a == 0) | (a == L), 1.0, 2.0) * valid / N
    IcM = C.T * w[:, None]
    IsM = S.T * w[:, None]
    TW = min(512, L)
    ntt = L // TW
    Ic = IcM.reshape(nfc, 128, ntt, TW).transpose(2, 1, 0, 3)
    Is = IsM.reshape(nfc, 128, ntt, TW).transpose(2, 1, 0, 3)
    return _bf(Fc), _bf(Fs), _bf(Ic), _bf(Is)


def _hy_consts(L):
    t = (np.arange(L, dtype=np.float32) / np.float32(L)).astype(np.float32)
    ang = (2.0 * math.pi * t[:, None] * np.arange(1, 17, dtype=np.float32)).astype(np.float32)
    feats = np.concatenate([t[:, None], np.sin(ang), np.cos(ang)], axis=-1).astype(np.float32)
    deltas = np.abs(np.linspace(math.log(1e-2) / 1.5, math.log(1e-2) / 0.3, 512, dtype=np.float32))
    decay = np.exp(-t[:, None] * deltas[None, :]).astype(np.float32)
    return np.ascontiguousarray(feats.T), decay


def _rope_tables():
    row = np.repeat(np.arange(SEQ // 64, dtype=np.float32), 64)
    col = np.tile(np.arange(64, dtype=np.float32), SEQ // 64)
    inv = (10000.0 ** (-np.arange(16, dtype=np.float32) / 16)).astype(np.float32)
    ang = np.concatenate([row[:, None] * inv, col[:, None] * inv], axis=-1).astype(np.float32)
    cs = np.zeros((64, 2, SEQ), np.float32)
    cs[:32, 0] = np.cos(ang).T
    cs[32:, 0] = np.cos(ang).T
    cs[:32, 1] = np.sin(ang).T
    cs[32:, 1] = np.sin(ang).T
    return cs


def _consts():
    if _CONST_CACHE:
        return _CONST_CACHE
    c = {}
    c["ident_f"] = np.eye(128, dtype=np.float32)
    c["rope_cs"] = _rope_tables()
    for L, tag in ((SEQ, "L"), (CTX, "C")):
        Fc, Fs, Ic, Is = _dft_tables(L)
        c["Fc" + tag], c["Fs" + tag], c["Ic" + tag], c["Is" + tag] = Fc, Fs, Ic, Is
        f, d = _hy_consts(L)
        c["feats" + tag], c["decay" + tag] = f, d
    _CONST_CACHE.update(c)
    return _CONST_CACHE


def _pack_small(inp):
    sp = np.zeros((128, DEPTH, NSP), np.float32)

    def put(col, arr):
        n = arr.shape[1] // 128
        sp[:, :, col:col + n] = arr.reshape(DEPTH, n, 128).transpose(2, 0, 1)

    put(SP_ADAB, inp["ada_b"])
    put(SP_NG, inp["norm_g"].reshape(DEPTH, 4 * D))
    put(SP_HYC, inp["hy_conv"].reshape(DEPTH, 3 * 1536))
    put(SP_HYB, inp["hy_bias"].reshape(DEPTH, 2 * 512))
    put(SP_LC, inp["lru_conv"].reshape(DEPTH, 2 * 4 * 512))
    put(SP_BA, inp["lru_ba"].reshape(DEPTH, 1024))
    put(SP_BX, inp["lru_bx"].reshape(DEPTH, 1024))
    put(SP_LAM, inp["lru_lam"].reshape(DEPTH, 1024))
    put(SP_GQ, inp["mla_gq"])
    put(SP_GKV, inp["mla_gkv"])
    put(SP_HG, inp["head_g"])
    put(SP_FC, inp["ffn_conv"].reshape(DEPTH, 3 * 2 * DFF))
    sp[:64, :, SP_B1] = inp["hy_b1"].T
    sp[:64, :, SP_B2] = inp["hy_b2"].T
    return sp


class Builder:
    def __init__(self, layers, dbg=(), phases=None, same_engine_sync=True):
        self.layers = list(layers)
        self.lidx = {l: i for i, l in enumerate(self.layers)}
        NL = len(self.layers)
        self.dbg = set(dbg)
        self.phases = phases
        nc = self.nc = bass.Bass("TRN2", target_bir_lowering=False)
        self.I = {}
        din = self.din
        self.x_in = din("x", [SEQ, D])
        self.ctx_in = din("ctx", [CTX, D])
        self.cvec = din("cvec", [128, 32])
        self.smallp_in = din("smallp", [128, DEPTH * NSP])
        self.ada_w = din("ada_w", [NL, D, 6 * D])
        self.w_in = din("w_in", [NL, D, DIN])
        self.hy_w1 = din("hy_w1", [NL, 33, 64])
        self.hy_w2 = din("hy_w2", [NL, 64, 64])
        self.hy_w3 = din("hy_w3", [NL, 64, 2048])
        self.lru_wa = din("lru_wa", [NL, 2, 4, 128, 128])
        self.lru_wx = din("lru_wx", [NL, 2, 4, 128, 128])
        self.mla_wuq = din("mla_wuq", [NL, 512, 1536])
        self.mla_wukv = din("mla_wukv", [NL, 256, 2048])
        self.w_out = din("w_out", [NL, D, D])
        self.ffn_up = din("ffn_up", [NL, D, 2 * DFF])
        self.ffn_down = din("ffn_down", [NL, DFF, D])
        self.ident_in = din("ident_f", [128, 128])
        self.rope_cs = din("rope_cs", [64, 2, SEQ])
        self.tab = {}
        for L, tag in ((SEQ, "L"), (CTX, "C")):
            nfc = (L + 1 + 127) // 128
            ntc = L // 128
            TW = min(512, L)
            self.tab[tag] = dict(
                L=L, nfc=nfc, ntc=ntc, TW=TW, ntt=L // TW,
                Fc=din("Fc" + tag, [nfc, 128, ntc, 128], BF16), Fs=din("Fs" + tag, [nfc, 128, ntc, 128], BF16),
                Ic=din("Ic" + tag, [L // TW, 128, nfc, TW], BF16), Is=din("Is" + tag, [L // TW, 128, nfc, TW], BF16),
                feats=din("feats" + tag, [33, L]), decay=din("decay" + tag, [L, 512]),
                S=self.dscr("S" + tag, [nfc * 128, 2048], F32),
            )
        self.out = nc.dram_tensor("out", [SEQ, D], F32, kind="ExternalOutput").ap()
        ds = self.dscr
        self.XT = ds("XT", [D, NT], F32)
        self.UT = ds("UT", [DINX, NT], F32)
        self.ZT = ds("ZT", [512, NT], F32)
        self.PT = ds("PT", [1024, NT], F32)
        self.YC = ds("YC", [D, NT], BF16)
        self.H2 = ds("H2", [D, NT], BF16)
        self.WIN = ds("WIN", [NL, D, DINX], BF16)
        self.WUQ = ds("WUQ", [NL, 512, 2048], BF16)
        self.WUKV = ds("WUKV", [NL, 256, 2048], BF16)
        self.WOUT = ds("WOUT", [NL, D, D], BF16)
        self.WUP = ds("WUP", [NL, D, 2 * DFF], BF16)
        self.WDN = ds("WDN", [NL, DFF, D], BF16)
        self.LWA = ds("LWA", [NL, 2, 4, 128, 128], BF16)
        self.LWX = ds("LWX", [NL, 2, 4, 128, 128], BF16)
        self.P = Prog(nc, same_engine_sync=same_engine_sync)
        self.r_out = Res("out")

    def din(self, name, shape, dt=F32):
        self.I[name] = (list(shape), dt)
        return self.nc.dram_tensor(name, list(shape), dt, kind="ExternalInput").ap()

    def dscr(self, name, shape, dt):
        kind = "ExternalOutput" if name in self.dbg else "Internal"
        return self.nc.dram_tensor(name, list(shape), dt, kind=kind).ap()

    def on(self, ph):
        return self.phases is None or ph in self.phases

    def build(self):
        P = self.P
        self.setup()
        for l in self.layers:
            last = (l == DEPTH - 1)
            if self.on("M"):
                self.phase_mod(l)
            if self.on("A"):
                self.phase_inproj(l)
            if self.on("H"):
                if not last:
                    self.phase_hyena(l, "C")
                self.phase_hyena(l, "L")
            if self.on("R"):
                self.phase_lru(l, not last)
            if self.on("T"):
                self.phase_attn(l, not last)
            if self.on("O"):
                self.phase_outproj(l, not last)
            if self.on("F"):
                self.phase_ffn(l, not last)
        if self.on("Z"):
            self.final()
        P.finish([self.r_out])
        return self.nc

    def setup(self):
        P = self.P
        self.pb = [(P.psum(f"pb{i}", [128, 512]), Res(f"pb{i}")) for i in range(8)]
        self.ident = P.sbuf("ident", [128, 128], F32)
        self.r_const = Res("const")
        P.ld("sp", self.ident[:], self.ident_in[:, :], writes=[self.r_const])
        self.ones_b = P.sbuf("ones_b", [128, 128], BF16)
        self.ones_f = P.sbuf("ones_f", [128, 128], F32)
        P.memset(P.dve, self.ones_b[:], 1.0, [self.r_const])
        P.memset(P.dve, self.ones_f[:], 1.0, [self.r_const])
        self.smallp = P.sbuf("smallp", [128, DEPTH * NSP], F32)
        P.ld("sp", self.smallp[:], self.smallp_in[:, :], writes=[self.r_const])
        self.scv = P.sbuf("scv", [128, 32], F32)
        cvt = P.sbuf("cvt", [128, 32], F32)
        P.ld("sp", cvt[:], self.cvec[:, :], writes=[self.r_const])
        P.actv(self.scv[:], cvt[:], AF.Silu, [self.r_const], [self.r_const])
        self.eps_t = P.sbuf("eps_t", [128, 1], F32)
        P.memset(P.dve, self.eps_t[:], EPS, [self.r_const])
        self.mods = P.sbuf("mods", [128, 6 * 32], F32)
        self.r_mods = Res("mods")
        self.lru_h0 = P.sbuf("lru_h0", [128, 8], F32)
        self.r_h0 = Res("h0")
        for l in self.layers:
            for (src, dst, rows, cols, dcols) in (
                (self.w_in[self.lidx[l]], self.WIN[self.lidx[l]], D, DIN, DINX),
                (self.w_out[self.lidx[l]], self.WOUT[self.lidx[l]], D, D, D),
                (self.ffn_up[self.lidx[l]], self.WUP[self.lidx[l]], D, 2 * DFF, 2 * DFF),
                (self.ffn_down[self.lidx[l]], self.WDN[self.lidx[l]], DFF, D, D),
                (self.mla_wukv[self.lidx[l]], self.WUKV[self.lidx[l]], 256, 2048, 2048),
            ):
                cw = 1408 if cols % 1408 == 0 else (1696 if cols % 1696 == 0 else 1024)
                assert cols % cw == 0 and cw <= 2048
                for r0 in range(0, rows, 1024):
                    r1 = min(rows, r0 + 1024)
                    for c0 in range(0, cols, cw):
                        P.ld("pool", dst[r0:r1, c0:c0 + cw], src[r0:r1, c0:c0 + cw])
            for h in range(8):
                P.ld("pool", self.WUQ[self.lidx[l]][:, h * 256:h * 256 + 192], self.mla_wuq[self.lidx[l]][:, h * 192:(h + 1) * 192])
            for d in range(2):
                P.ld("pool", self.LWA[self.lidx[l], d].rearrange("n a b -> (n a) b"), self.lru_wa[self.lidx[l], d].rearrange("n a b -> (n a) b"))
                P.ld("pool", self.LWX[self.lidx[l], d].rearrange("n a b -> (n a) b"), self.lru_wx[self.lidx[l], d].rearrange("n a b -> (n a) b"))
            mk = P.push()
            wk = P.sbuf("wk", [128, 16, 64], F32)
            wr = P.sbuf("wr", [128, 16, 64], BF16)
            r_wk, r_wr = Res(), Res()
            P.ld("sp", wk[:], self.w_in[self.lidx[l]].rearrange("(kc p) n -> p kc n", p=128)[:, :, OFF_KR:OFF_KR + 64], writes=[r_wk])
            P.ts(P.dve, wr[:, :, 0:32], wk[:, :, 32:64], -1.0, None, ALU.mult, None, [r_wk], [r_wr])
            P.copy(P.dve, wr[:, :, 32:64], wk[:, :, 0:32], [r_wk], [r_wr])
            P.ld("sp", self.WIN[self.lidx[l]].rearrange("(kc p) n -> p kc n", p=128)[:, :, OFF_KRR:OFF_KRR + 64], wr[:], reads=[r_wr])
            qk = P.sbuf("qk", [128, 4, 8, 64], F32)
            qr = P.sbuf("qr", [128, 4, 8, 64], BF16)
            r_qk, r_qr = Res(), Res()
            srcv = self.mla_wuq[self.lidx[l]].rearrange("(kc p) (h n) -> p kc h n", p=128, n=192)
            dstv = self.WUQ[self.lidx[l]].rearrange("(kc p) (h n) -> p kc h n", p=128, n=256)
            for kc in range(4):
                P.ld("sp", qk[:, kc, :, :], srcv[:, kc, :, 128:192], writes=[r_qk])
            P.ts(P.dve, qr[:, :, :, 0:32], qk[:, :, :, 32:64], -1.0, None, ALU.mult, None, [r_qk], [r_qr])
            P.copy(P.dve, qr[:, :, :, 32:64], qk[:, :, :, 0:32], [r_qk], [r_qr])
            for kc in range(4):
                P.ld("sp", dstv[:, kc, :, 192:256], qr[:, kc, :, :], reads=[r_qr])
            P.pop(mk)
        mk = P.push()
        xin = P.ring("xin", 2, [128, D], F32)
        xst = P.ring("xst", 2, [128, 16, 128], F32)
        XTv = self.XT.rearrange("(c p) t -> p c t", p=128)
        k = 0
        for ti in range(NT // 128):
            src = self.ctx_in[ti * 128:(ti + 1) * 128, :] if ti < 2 else self.x_in[(ti - 2) * 128:(ti - 1) * 128, :]
            xt, r_xt = xin.next()
            P.ld("sp" if ti % 2 == 0 else "act", xt[:], src, writes=[r_xt])
            st, r_st = xst.next()
            for g in range(4):
                pt, r_pt = self.pb[k % 8]
                k += 1
                for j in range(4):
                    c = g * 4 + j
                    P.tr(pt[:, j * 128:(j + 1) * 128], xt[:, c * 128:(c + 1) * 128], self.ident[:], [r_xt, self.r_const], [r_pt])
                P.copy(P.act if g % 2 == 0 else P.dve, st[:, g * 4:(g + 1) * 4, :], pt[:].rearrange("p (j t) -> p j t", j=4), [r_pt], [r_st])
            P.ld("sp" if ti % 2 == 1 else "act", XTv[:, :, ti * 128:(ti + 1) * 128], st[:], reads=[r_st])
        P.pop(mk)

    def final(self):
        P = self.P
        mk = P.push()
        xin = P.ring("fin", 2, [128, 16, 128], F32)
        xst = P.ring("fst", 2, [128, D], F32)
        XTv = self.XT.rearrange("(c p) t -> p c t", p=128)
        k = 0
        for ti in range(SEQ // 128):
            xt, r_xt = xin.next()
            P.ld("sp" if ti % 2 == 0 else "act", xt[:], XTv[:, :, CTX + ti * 128:CTX + (ti + 1) * 128], writes=[r_xt])
            st, r_st = xst.next()
            for g in range(4):
                pt, r_pt = self.pb[k % 8]
                k += 1
                for j in range(4):
                    c = g * 4 + j
                    P.tr(pt[:, j * 128:(j + 1) * 128], xt[:, c, :], self.ident[:], [r_xt, self.r_const], [r_pt])
                P.copy(P.act if g % 2 == 0 else P.dve, st[:, g * 512:(g + 1) * 512], pt[:], [r_pt], [r_st])
            P.ld("sp" if ti % 2 == 1 else "act", self.out[ti * 128:(ti + 1) * 128, :], st[:], reads=[r_st], writes=[self.r_out])
        P.pop(mk)

    def sp_col(self, l, col, n=1, parts=128):
        base = l * NSP + col
        return self.smallp[0:parts, base:base + n]

    def rms_rstd(self, chunks, W, n_feat, sq_ring, pbi, st_ring, reads):
        P = self.P
        pt, r_pt = self.pb[pbi]
        for i, ap in enumerate(chunks):
            sq, r_sq = sq_ring.next()
            P.actv(sq[:, :W], ap, AF.Square, reads, [r_sq])
            P.mm(pt[:, :W], self.ones_b[:], sq[:, :W], i == 0, i == len(chunks) - 1, [r_sq, self.r_const], [r_pt])
        st, r_st = st_ring.next()
        P.actv(st[:, :W], pt[:, :W], AF.Sqrt, [r_pt], [r_st], bias=self.eps_t[:], scale=1.0 / n_feat)
        P.recip(st[:, :W], st[:, :W], [r_st], [r_st])
        return st, r_st

    def phase_mod(self, l):
        P = self.P
        mk = P.push()
        wr = P.ring("adaw", 2, [128, 16, 512], F32)
        modT = P.sbuf("modT", [128, 96, 2], F32)
        r_modT = Res()
        pm, r_pm = self.pb[0]
        src = self.ada_w[self.lidx[l]].rearrange("(kc p) n -> p kc n", p=128)
        scv = self.scv[:].rearrange("p (kc j) -> p kc j", j=2)
        for g in range(24):
            wt, r_wt = wr.next()
            P.ld("sp" if g % 2 == 0 else "act", wt[:], src[:, :, g * 512:(g + 1) * 512], writes=[r_wt])
            for m in range(4):
                col = (g * 4 + m) * 2
                for kc in range(16):
                    P.mm(pm[:, col:col + 2], wt[:, kc, m * 128:(m + 1) * 128], scv[:, kc, :], kc == 0, kc == 15, [r_wt, self.r_const], [r_pm])
        pmv = pm[:, 0:192].rearrange("p (m j) -> p m j", j=2)
        for j in range(2):
            P.tt(P.dve, modT[:, :, j], pmv[:, :, j], self.sp_col(l, SP_ADAB, 96), ALU.add, [r_pm, self.r_const], [r_modT])
        mods = self.mods[:].rearrange("p (k c j) -> p k c j", k=6, j=2)
        for j in range(2):
            for half, (sh, sc, g, nga, ngb) in enumerate(((0, 1, 2, 0, 1), (3, 4, 5, 2, 3))):
                ng_a = self.sp_col(l, SP_NG + nga * 16, 16)
                ng_b = self.sp_col(l, SP_NG + ngb * 16, 16)
                A, B, G = mods[:, half * 3 + 0, :, j], mods[:, half * 3 + 1, :, j], mods[:, half * 3 + 2, :, j]
                P.stt(A, modT[:, sc * 16:(sc + 1) * 16, j], 1.0, ng_a, ALU.add, ALU.mult, [r_modT, self.r_const], [self.r_mods])
                P.copy(P.dve, B, modT[:, sh * 16:(sh + 1) * 16, j], [r_modT], [self.r_mods])
                P.tt(P.dve, G, modT[:, g * 16:(g + 1) * 16, j], ng_b, ALU.mult, [r_modT, self.r_const], [self.r_mods])
        P.pop(mk)

    def mod(self, k, c, j):
        i = (k * 16 + c) * 2 + j
        return self.mods[:, i:i + 1]

    def tok_tiles(self, need_ctx=True):
        tiles = [(0, CTX, 1)] if need_ctx else []
        tiles += [(CTX + i * 512, 512, 0) for i in range(SEQ // 512)]
        return tiles

    def phase_inproj(self, l):
        P = self.P
        mk = P.push()
        xs_r = P.ring("xs", 2, [128, 16, 512], F32)
        hb_r = P.ring("hb", 2, [128, 16, 512], BF16)
        sq_r = P.ring("sq", 3, [128, 512], BF16)
        tmp_r = P.ring("tmp", 3, [128, 512], F32)
        st_r = P.ring("st", 2, [128, 512], F32)
        w_r = P.ring("wg", 2, [128, 16, 512], BF16)
        ev_r = P.ring("ev", 4, [128, 512], F32)
        XTv = self.XT.rearrange("(c p) t -> p c t", p=128)
        Wv = self.WIN[self.lidx[l]].rearrange("(kc p) n -> p kc n", p=128)
        kb = 0
        for (c0, W, j) in self.tok_tiles():
            xs, r_xs = xs_r.next()
            P.ld("sp", xs[:, :, :W], XTv[:, :, c0:c0 + W], writes=[r_xs])
            st, r_st = self.rms_rstd([xs[:, c, :W] for c in range(16)], W, D, sq_r, 7, st_r, [r_xs])
            hb, r_hb = hb_r.next()
            for c in range(16):
                tmp, r_tmp = tmp_r.next()
                P.tt(P.dve, tmp[:, :W], xs[:, c, :W], st[:, :W], ALU.mult, [r_xs, r_st], [r_tmp])
                P.actv(hb[:, c, :W], tmp[:, :W], AF.Identity, [r_tmp, self.r_mods], [r_hb], bias=self.mod(1, c, j), scale=self.mod(0, c, j))
            for g in range(7):
                ncol = 512 if g < 6 else DINX - 6 * 512
                wt, r_wt = w_r.next()
                P.ld("act", wt[:, :, :ncol], Wv[:, :, g * 512:g * 512 + ncol], writes=[r_wt])
                for m in range(ncol // 128):
                    pt, r_pt = self.pb[kb % 6]
                    kb += 1
                    for kc in range(16):
                        P.mm(pt[:, :W], wt[:, kc, m * 128:(m + 1) * 128], hb[:, kc, :W], kc == 0, kc == 15, [r_wt, r_hb], [r_pt])
                    ev, r_ev = ev_r.next()
                    P.copy(P.act if kb % 2 == 0 else P.dve, ev[:, :W], pt[:, :W], [r_pt], [r_ev])
                    row = g * 512 + m * 128
                    P.ld("sp", self.UT[row:row + 128, c0:c0 + W], ev[:, :W], reads=[r_ev])
        P.pop(mk)


    def sin_layer(self, ps_ap, bias_ap, out_ap, W, rings, reads, r_ps, r_out):
        P = self.P
        v, r_v = rings[0].next()
        t1, r_t1 = rings[1].next()
        P.ts(P.dve, v[:, :W], ps_ap, bias_ap, None, ALU.add, None, [r_ps] + reads, [r_v])
        P.ts(P.dve, t1[:, :W], v[:, :W], 1.0 / (2 * math.pi), MAGIC, ALU.mult, ALU.add, [r_v], [r_t1])
        P.ts(P.dve, t1[:, :W], t1[:, :W], -MAGIC, None, ALU.add, None, [r_t1], [r_t1])
        P.stt(v[:, :W], t1[:, :W], -2 * math.pi, v[:, :W], ALU.mult, ALU.add, [r_t1, r_v], [r_v])
        P.ts(P.dve, v[:, :W], v[:, :W], -3.1415925, 3.1415925, ALU.max, ALU.min, [r_v], [r_v])
        P.actv(out_ap, v[:, :W], AF.Sin, [r_v], [r_out])

    def hy_filter(self, l, T):
        P = self.P
        li = self.lidx[l]
        L, nfc, ntc, TW, ntt = T["L"], T["nfc"], T["ntc"], T["TW"], T["ntt"]
        mk = P.push()
        w1 = P.sbuf("hw1", [33, 64], F32)
        w2 = P.sbuf("hw2", [64, 64], F32)
        w3 = P.sbuf("hw3", [64, 2048], F32)
        r_w = Res()
        P.ld("sp", w1[:], self.hy_w1[li], writes=[r_w])
        P.ld("sp", w2[:], self.hy_w2[li], writes=[r_w])
        P.ld("sp", w3[:], self.hy_w3[li], writes=[r_w])
        hid2 = P.sbuf("hid2", [64, L], F32)
        r_hid2 = Res()
        f_r = P.ring("ft", 2, [33, 512], F32)
        h1_r = P.ring("h1", 2, [64, 512], F32)
        v_r = P.ring("sv", 2, [64, 512], F32)
        t_r = P.ring("st1", 2, [64, 512], F32)
        b1 = self.sp_col(l, SP_B1, 1, 64)
        b2 = self.sp_col(l, SP_B2, 1, 64)
        for tt in range(ntt):
            ft, r_ft = f_r.next()
            P.ld("sp", ft[:, :TW], T["feats"][:, tt * TW:(tt + 1) * TW], writes=[r_ft])
            p0, r_p0 = self.pb[4]
            P.mm(p0[0:64, :TW], w1[:, :], ft[:, :TW], True, True, [r_w, r_ft], [r_p0])
            h1, r_h1 = h1_r.next()
            self.sin_layer(p0[0:64, :TW], b1, h1[:, :TW], TW, (v_r, t_r), [self.r_const], r_p0, r_h1)
            p1, r_p1 = self.pb[5]
            P.mm(p1[0:64, :TW], w2[:, :], h1[:, :TW], True, True, [r_w, r_h1], [r_p1])
            self.sin_layer(p1[0:64, :TW], b2, hid2[:, tt * TW:(tt + 1) * TW], TW, (v_r, t_r), [self.r_const], r_p1, r_hid2)
        Sv = T["S"].rearrange("(fc p) (o rs c) -> fc p o rs c", p=128, o=2, rs=2)
        AW = ntc * 128
        for o_ in range(2):
            mk2 = P.push()
            ksum = P.sbuf("ksum", [128, ntc, 512], BF16)
            kdiff = P.sbuf("kdiff", [128, ntc, 512], BF16)
            r_k = [Res() for _ in range(ntc)]
            dec_r = P.ring("dec", 2, [128, 512], F32)
            hf_r = P.ring("hf", 2, [128, 2, 512], F32)
            ab_r = P.ring("ab", 2, [128, 2, 512], F32)
            pn, r_pn = self.pb[6]
            for lc in range(ntc):
                dec, r_dec = dec_r.next()
                P.ld("sp", dec[:], T["decay"][lc * 128:(lc + 1) * 128, :], writes=[r_dec])
                hf, r_hf = hf_r.next()
                for d in range(2):
                    g = o_ * 2 + d
                    ph, r_ph = self.pb[d + 2 * (lc % 2)]
                    P.mm(ph[:, :], hid2[:, lc * 128:(lc + 1) * 128], w3[:, g * 512:(g + 1) * 512], True, True, [r_hid2, r_w], [r_ph])
                    P.tt(P.dve, hf[:, d, :], ph[:, :], dec[:], ALU.mult, [r_ph, r_dec], [r_hf])
                if lc == 0:
                    P.memset(P.dve, hf[0:1, 1, :], 0.0, [r_hf])
                ab, r_ab = ab_r.next()
                P.actv(ab[:], hf[:], AF.Abs, [r_hf], [r_ab])
                P.mm(pn[:, :], self.ones_f[:], ab[:, 0, :], lc == 0, False, [r_ab, self.r_const], [r_pn])
                P.mm(pn[:, :], self.ones_f[:], ab[:, 1, :], False, lc == ntc - 1, [r_ab, self.r_const], [r_pn])
                P.tt(P.pool, ksum[:, lc, :], hf[:, 0, :], hf[:, 1, :], ALU.add, [r_hf], [r_k[lc]])
                P.tt(P.pool, kdiff[:, lc, :], hf[:, 0, :], hf[:, 1, :], ALU.subtract, [r_hf], [r_k[lc]])
            rn = P.sbuf("rn", [128, 512], F32)
            r_rn = Res()
            P.recip(rn[:], pn[:, :], [r_pn], [r_rn])
            tbs = [(P.sbuf(f"ftb{i}", [128, 2 * AW], BF16), Res(), Res()) for i in range(2)]
            sst_r = P.ring("sst", 2, [128, 2, 512], F32)
            for fc in range(nfc):
                tb, r_tc, r_ts = tbs[fc % 2]
                P.ld("sp", tb[:, 0:AW], T["Fc"][fc].rearrange("p tc f -> p (tc f)"), writes=[r_tc])
                P.ld("act", tb[:, AW:2 * AW], T["Fs"][fc].rearrange("p tc f -> p (tc f)"), writes=[r_ts])
                Fc_t = tb[:, 0:AW].rearrange("p (tc f) -> p tc f", f=128)
                Fs_t = tb[:, AW:2 * AW].rearrange("p (tc f) -> p tc f", f=128)
                pr, r_pr = self.pb[0 + 2 * (fc % 2)]
                ps_, r_ps = self.pb[1 + 2 * (fc % 2)]
                for tc in range(ntc):
                    P.mm(pr[:, :], Fc_t[:, tc, :], ksum[:, tc, :], tc == 0, tc == ntc - 1, [r_tc, r_k[tc]], [r_pr])
                for tc in range(ntc):
                    P.mm(ps_[:, :], Fs_t[:, tc, :], kdiff[:, tc, :], tc == 0, tc == ntc - 1, [r_ts, r_k[tc]], [r_ps])
                sst, r_sst = sst_r.next()
                P.tt(P.dve, sst[:, 0, :], pr[:, :], rn[:], ALU.mult, [r_pr, r_rn], [r_sst])
                P.tt(P.dve, sst[:, 1, :], ps_[:, :], rn[:], ALU.mult, [r_ps, r_rn], [r_sst])
                P.ld("sp", Sv[fc, :, o_, :, :], sst[:], reads=[r_sst])
            P.pop(mk2)
        P.pop(mk)

    def head_norm_store(self, y_ap, W, head, l, rows0, c0, sq_r, st_r, ob_r, pbi, r_y):
        P = self.P
        st, r_st = self.rms_rstd([y_ap], W, 128, sq_r, pbi, st_r, [r_y])
        ob, r_ob = ob_r.next()
        P.stt(ob[:, :W], y_ap, self.sp_col(l, SP_HG + head), st[:, :W], ALU.mult, ALU.mult, [r_y, r_st, self.r_const], [r_ob])
        P.ld("sp", self.YC[rows0:rows0 + 128, c0:c0 + W], ob[:, :W], reads=[r_ob])

    def phase_hyena(self, l, tag):
        P = self.P
        T = self.tab[tag]
        L, nfc, ntc, TW, ntt = T["L"], T["nfc"], T["ntc"], T["TW"], T["ntt"]
        off = 0 if tag == "C" else CTX
        self.hy_filter(l, T)
        mk = P.push()
        u_r = P.ring("hu", 2, [128, L + 2], F32)
        o_r = P.ring("ho", 2, [128, L], F32)
        for cc in range(12):
            u, r_u = u_r.next()
            P.memset(P.pool, u[:, 0:1], 0.0, [r_u])
            P.memset(P.pool, u[:, L + 1:L + 2], 0.0, [r_u])
            P.ld("sp", u[:, 1:L + 1], self.UT[cc * 128:(cc + 1) * 128, off:off + L], writes=[r_u])
            o, r_o = o_r.next()
            P.ts(P.dve, o[:, :], u[:, 0:L], self.sp_col(l, SP_HYC + 0 * 12 + cc), None, ALU.mult, None, [r_u, self.r_const], [r_o])
            P.stt(o[:, :], u[:, 1:L + 1], self.sp_col(l, SP_HYC + 1 * 12 + cc), o[:, :], ALU.mult, ALU.add, [r_u, r_o, self.r_const], [r_o])
            P.stt(o[:, :], u[:, 2:L + 2], self.sp_col(l, SP_HYC + 2 * 12 + cc), o[:, :], ALU.mult, ALU.add, [r_u, r_o, self.r_const], [r_o])
            dst = self.ZT[cc * 128:(cc + 1) * 128, off:off + L] if cc < 4 else self.PT[(cc - 4) * 128:(cc - 3) * 128, off:off + L]
            P.ld("act", dst, o[:, :], reads=[r_o])
        P.pop(mk)
        FG = 11 if nfc % 11 == 0 else nfc
        nfg = nfc // FG
        AW = max(ntc * 128, FG * TW)
        ZTv = self.ZT.rearrange("(c p) t -> p c t", p=128)
        Sv = T["S"].rearrange("(fc p) (o rs c) -> fc p o rs c", p=128, o=2, rs=2)
        for o_ in range(2):
            mk = P.push()
            zT = P.sbuf("zT", [128, ntc, 512], BF16)
            r_zT = [Res() for _ in range(ntc)]
            Yr = P.sbuf("Yr", [128, nfc, 512], BF16)
            Ys = P.sbuf("Ys", [128, nfc, 512], BF16)
            r_Y = [Res() for _ in range(nfc)]
            tbs = [(P.sbuf(f"tb{i}", [128, 2 * AW], BF16), Res(), Res()) for i in range(2)]
            zl_r = P.ring("zl", 1, [128, 4, TW], F32)
            for tt in range(ntt):
                zl, r_zl = zl_r.next()
                P.ld("sp", zl[:, :, :], ZTv[:, :, off + tt * TW:off + (tt + 1) * TW], writes=[r_zl])
                for s in range(TW // 128):
                    tc = tt * (TW // 128) + s
                    pt, r_pt = self.pb[tc % 4]
                    for cc in range(4):
                        P.tr(pt[:, cc * 128:(cc + 1) * 128], zl[:, cc, s * 128:(s + 1) * 128], self.ident[:], [r_zl, self.r_const], [r_pt])
                    P.copy(P.act if tc % 2 == 0 else P.dve, zT[:, tc, :], pt[:, :], [r_pt], [r_zT[tc]])
            S_r = P.ring("Sld", 2, [128, 2, 512], F32)
            t_rs = [P.ring(f"ty{i}", 1, [128, 512], F32) for i in range(4)]
            FW = ntc * 128
            for fc in range(nfc):
                tb, r_tc, r_ts = tbs[fc % 2]
                P.ld("sp", tb[:, 0:FW], T["Fc"][fc].rearrange("p tc f -> p (tc f)"), writes=[r_tc])
                P.ld("act", tb[:, AW:AW + FW], T["Fs"][fc].rearrange("p tc f -> p (tc f)"), writes=[r_ts])
                Fc_t = tb[:, 0:FW].rearrange("p (tc f) -> p tc f", f=128)
                Fs_t = tb[:, AW:AW + FW].rearrange("p (tc f) -> p tc f", f=128)
                St, r_S = S_r.next()
                P.ld("sp", St[:], Sv[fc, :, o_, :, :], writes=[r_S])
                pr, r_pr = self.pb[4 + 2 * (fc % 2)]
                ps_, r_ps = self.pb[5 + 2 * (fc % 2)]
                for tc in range(ntc):
                    P.mm(pr[:, :], Fc_t[:, tc, :], zT[:, tc, :], tc == 0, tc == ntc - 1, [r_tc, r_zT[tc]], [r_pr])
                for tc in range(ntc):
                    P.mm(ps_[:, :], Fs_t[:, tc, :], zT[:, tc, :], tc == 0, tc == ntc - 1, [r_ts, r_zT[tc]], [r_ps])
                (t1, r1), (t2, r2), (t3, r3), (t4, r4) = [r.next() for r in t_rs]
                P.tt(P.dve, t1[:], pr[:, :], St[:, 0, :], ALU.mult, [r_pr, r_S], [r1])
                P.tt(P.dve, t2[:], ps_[:, :], St[:, 1, :], ALU.mult, [r_ps, r_S], [r2])
                P.tt(P.dve, t3[:], pr[:, :], St[:, 1, :], ALU.mult, [r_pr, r_S], [r3])
                P.tt(P.dve, t4[:], ps_[:, :], St[:, 0, :], ALU.mult, [r_ps, r_S], [r4])
                P.tt(P.pool, Yr[:, fc, :], t1[:], t2[:], ALU.subtract, [r1, r2], [r_Y[fc]])
                P.tt(P.pool, Ys[:, fc, :], t3[:], t4[:], ALU.add, [r3, r4], [r_Y[fc]])
            zt_r = P.ring("zt", 2, [128, TW], F32)
            pp_r = P.ring("pp", 2, [128, TW], F32)
            tm_r = P.ring("tm", 2, [128, TW], F32)
            sq_r = P.ring("hsq", 2, [128, 512], BF16)
            st_r = P.ring("hst", 2, [128, 512], F32)
            ob_r = P.ring("hob", 2, [128, 512], BF16)
            k = 0
            for tt in range(ntt):
                c0 = off + tt * TW
                accs = [self.pb[cc] for cc in range(4)]
                for fg in range(nfg):
                    tb, r_tc, r_ts = tbs[k % 2]
                    k += 1
                    P.ld("sp", tb[:, 0:FG * TW], T["Ic"][tt][:, fg * FG:(fg + 1) * FG, :].rearrange("p f t -> p (f t)"), writes=[r_tc])
                    P.ld("act", tb[:, AW:AW + FG * TW], T["Is"][tt][:, fg * FG:(fg + 1) * FG, :].rearrange("p f t -> p (f t)"), writes=[r_ts])
                    Ic_t = tb[:, 0:FG * TW].rearrange("p (f t) -> p f t", t=TW)
                    Is_t = tb[:, AW:AW + FG * TW].rearrange("p (f t) -> p f t", t=TW)
                    for f_ in range(FG):
                        fc = fg * FG + f_
                        for cc in range(4):
                            pa, r_pa = accs[cc]
                            P.mm(pa[:, :TW], Yr[:, fc, cc * 128:(cc + 1) * 128], Ic_t[:, f_, :], fc == 0, False, [r_Y[fc], r_tc], [r_pa])
                            P.mm(pa[:, :TW], Ys[:, fc, cc * 128:(cc + 1) * 128], Is_t[:, f_, :], False, fc == nfc - 1, [r_Y[fc], r_ts], [r_pa])
                for cc in range(4):
                    pa, r_pa = accs[cc]
                    zt, r_zt = zt_r.next()
                    pp, r_pp = pp_r.next()
                    tm, r_tm = tm_r.next()
                    P.ld("sp", zt[:], self.ZT[cc * 128:(cc + 1) * 128, c0:c0 + TW], writes=[r_zt])
                    P.ld("act", pp[:], self.PT[o_ * 512 + cc * 128:o_ * 512 + (cc + 1) * 128, c0:c0 + TW], writes=[r_pp])
                    P.stt(tm[:], zt[:], self.sp_col(l, SP_HYB + o_ * 4 + cc), pa[:, :TW], ALU.mult, ALU.add, [r_zt, r_pa, self.r_const], [r_tm])
                    P.tt(P.pool, zt[:], tm[:], pp[:], ALU.mult, [r_tm, r_pp], [r_zt])
                    if o_ == 0:
                        P.ld("sp", self.ZT[cc * 128:(cc + 1) * 128, c0:c0 + TW], zt[:], reads=[r_zt])
                    else:
                        self.head_norm_store(zt[:], TW, cc, l, cc * 128, c0, sq_r, st_r, ob_r, 4 + cc, r_zt)
            P.pop(mk)


    def phase_lru(self, l, need_ctx):
        P = self.P
        li = self.lidx[l]
        mk = P.push()
        c8 = P.sbuf("c8", [128, 8], F32)
        r_c8 = Res()
        P.actv(c8[:], self.sp_col(l, SP_LAM, 8), AF.Exp, [self.r_const], [r_c8], scale=-1.0)
        P.actv(c8[:], c8[:], AF.Ln, [r_c8, self.r_const], [r_c8], bias=self.ones_f[:, 0:1], scale=1.0)
        P.ts(P.dve, c8[:], c8[:], -8.0, None, ALU.mult, None, [r_c8], [r_c8])
        wab = P.ring("wab", 2, [128, 4, 128], BF16)
        seqs = [(0, CTX, True), (CTX, SEQ, False)]
        bufs = {}
        for (off, Ls, isctx) in seqs:
            t = "c" if isctx else "l"
            bufs[isctx] = dict(
                xr=P.ring("xr" + t, 2, [128, Ls + 6], F32),
                xc=(P.sbuf("xc" + t, [128, Ls], F32), Res()),
                xcb=(P.sbuf("xcb" + t, [128, Ls], BF16), Res()),
                ra=(P.sbuf("ra" + t, [128, Ls], F32), Res()),
                ib=(P.sbuf("ib" + t, [128, Ls], F32), Res()),
                ta=(P.sbuf("ta" + t, [128, Ls], F32), Res()),
                h=[(P.sbuf(f"h{d}" + t, [128, Ls], F32), Res()) for d in range(2)],
            )
        sq_r = P.ring("lsq", 2, [128, 512], BF16)
        st_r = P.ring("lst", 2, [128, 512], F32)
        ob_r = P.ring("lob", 2, [128, 512], BF16)
        for n in range(4):
            wt, r_wt = wab.next()
            for d in range(2):
                P.ld("sp", wt[:, d * 2 + 0, :], self.LWA[li, d, n], writes=[r_wt])
                P.ld("sp", wt[:, d * 2 + 1, :], self.LWX[li, d, n], writes=[r_wt])
            for (off, Ls, isctx) in seqs:
                B = bufs[isctx]
                W = min(512, Ls)
                for d in range(2):
                    xr, r_xr = B["xr"].next()
                    P.memset(P.pool, xr[:, 0:3], 0.0, [r_xr])
                    P.memset(P.pool, xr[:, Ls + 3:Ls + 6], 0.0, [r_xr])
                    P.ld("sp", xr[:, 3:Ls + 3], self.UT[OFF_X + n * 128:OFF_X + (n + 1) * 128, off:off + Ls], writes=[r_xr])
                    xc, r_xc = B["xc"]
                    left = 3 if d == 0 else 0
                    for jj in range(4):
                        s0 = 3 + jj - left
                        wj = self.sp_col(l, SP_LC + d * 16 + jj * 4 + n)
                        if jj == 0:
                            P.ts(P.dve, xc[:, :], xr[:, s0:s0 + Ls], wj, None, ALU.mult, None, [r_xr, self.r_const], [r_xc])
                        else:
                            P.stt(xc[:, :], xr[:, s0:s0 + Ls], wj, xc[:, :], ALU.mult, ALU.add, [r_xr, r_xc, self.r_const], [r_xc])
                    xcb, r_xcb = B["xcb"]
                    P.copy(P.pool, xcb[:, :], xc[:, :], [r_xc], [r_xcb])
                    ra, r_ra = B["ra"]
                    ib, r_ib = B["ib"]
                    ta, r_ta = B["ta"]
                    for ti in range(Ls // W):
                        cs = slice(ti * W, (ti + 1) * W)
                        pa, r_pa = self.pb[(2 * ti) % 4]
                        px, r_px = self.pb[(2 * ti + 1) % 4]
                        P.mm(pa[:, :W], wt[:, d * 2 + 0, :], xcb[:, cs], True, True, [r_wt, r_xcb], [r_pa])
                        P.mm(px[:, :W], wt[:, d * 2 + 1, :], xcb[:, cs], True, True, [r_wt, r_xcb], [r_px])
                        P.actv(ra[:, cs], pa[:, :W], AF.Sigmoid, [r_pa, self.r_const], [r_ra], bias=self.sp_col(l, SP_BA + d * 4 + n))
                        P.actv(ib[:, cs], px[:, :W], AF.Sigmoid, [r_px, self.r_const], [r_ib], bias=self.sp_col(l, SP_BX + d * 4 + n))
                    P.actv(ra[:, :], ra[:, :], AF.Exp, [r_ra, r_c8], [r_ra], scale=c8[:, d * 4 + n:d * 4 + n + 1])
                    P.tt(P.pool, ta[:, :], ra[:, :], ra[:, :], ALU.mult, [r_ra], [r_ta])
                    P.ts(P.pool, ta[:, :], ta[:, :], -1.0, 1.0, ALU.mult, ALU.add, [r_ta], [r_ta])
                    P.actv(ta[:, :], ta[:, :], AF.Sqrt, [r_ta], [r_ta])
                    P.tt(P.dve, ib[:, :], ib[:, :], xc[:, :], ALU.mult, [r_ib, r_xc], [r_ib])
                    P.tt(P.dve, ib[:, :], ib[:, :], ta[:, :], ALU.mult, [r_ib, r_ta], [r_ib])
                    if not isctx:
                        f0 = Ls - 1 if d == 1 else 0
                        P.stt(ib[:, f0:f0 + 1], ra[:, f0:f0 + 1], self.lru_h0[:, d * 4 + n:d * 4 + n + 1], ib[:, f0:f0 + 1],
                              ALU.mult, ALU.add, [r_ra, r_ib, self.r_h0], [r_ib])
                    h, r_h = B["h"][d]
                    if d == 0:
                        P.op(P.dve, lambda e, h=h, ra=ra, ib=ib: e.tensor_tensor_scan(out=h[:, :], data0=ra[:, :], data1=ib[:, :], initial=0.0, op0=ALU.mult, op1=ALU.add), [r_ra, r_ib], [r_h])
                    else:
                        P.op(P.dve, lambda e, h=h, ra=ra, ib=ib: e.tensor_tensor_scan(out=h[:, ::-1], data0=ra[:, ::-1], data1=ib[:, ::-1], initial=0.0, op0=ALU.mult, op1=ALU.add), [r_ra, r_ib], [r_h])
                    if isctx:
                        f1 = Ls - 1 if d == 0 else 0
                        P.copy(P.dve, self.lru_h0[:, d * 4 + n:d * 4 + n + 1], h[:, f1:f1 + 1], [r_h], [self.r_h0])
                if isctx and not need_ctx:
                    continue
                xr, r_xr = B["xr"].next()
                P.ld("sp", xr[:, 0:Ls], self.UT[OFF_G + n * 128:OFF_G + (n + 1) * 128, off:off + Ls], writes=[r_xr])
                P.actv(xr[:, 0:Ls], xr[:, 0:Ls], AF.Gelu_apprx_tanh, [r_xr], [r_xr])
                (h0_, r_h0_), (h1_, r_h1_) = B["h"]
                P.tt(P.pool, h0_[:, :], h0_[:, :], h1_[:, :], ALU.add, [r_h0_, r_h1_], [r_h0_])
                P.tt(P.dve, h0_[:, :], h0_[:, :], xr[:, 0:Ls], ALU.mult, [r_h0_, r_xr], [r_h0_])
                for ti in range(Ls // W):
                    self.head_norm_store(h0_[:, ti * W:(ti + 1) * W], W, 4 + n, l, 512 + n * 128, off + ti * W, sq_r, st_r, ob_r, 4 + ti % 4, r_h0_)
        P.pop(mk)

    def phase_attn(self, l, need_ctx):
        P = self.P
        li = self.lidx[l]
        mk = P.push()
        ckv = P.sbuf("ckv", [128, 2, NT], BF16)
        r_ckv = Res()
        krope = P.sbuf("krope", [64, NT], BF16)
        r_kr = Res()
        cq = P.sbuf("cq", [128, 4, NT], BF16)
        r_cq = Res()
        sq_r = P.ring("asq", 3, [128, 512], BF16)
        st_r = P.ring("ast", 2, [128, 512], F32)
        u2_r = P.ring("au2", 2, [128, 2, 512], F32)
        u4_r = P.ring("au4", 2, [128, 4, 512], F32)
        kr_r = P.ring("akr", 2, [64, 2, 512], F32)
        cs_r = P.ring("acs", 2, [64, 2, 512], F32)
        tk_r = P.ring("atk", 2, [64, 2, 512], F32)
        UKV = self.UT[OFF_KV:OFF_KV + 256, :].rearrange("(c p) t -> p c t", p=128)
        UQ = self.UT[OFF_Q:OFF_Q + 512, :].rearrange("(c p) t -> p c t", p=128)
        for (c0, W, j) in self.tok_tiles(True):
            u2, r_u2 = u2_r.next()
            P.ld("sp", u2[:, :, :W], UKV[:, :, c0:c0 + W], writes=[r_u2])
            st, r_st = self.rms_rstd([u2[:, c, :W] for c in range(2)], W, 256, sq_r, 7, st_r, [r_u2])
            for c in range(2):
                P.stt(ckv[:, c, c0:c0 + W], u2[:, c, :W], self.sp_col(l, SP_GKV + c), st[:, :W], ALU.mult, ALU.mult, [r_u2, r_st, self.r_const], [r_ckv])
            kr, r_krt = kr_r.next()
            P.ld("act", kr[:, 0, :W], self.UT[OFF_KR:OFF_KR + 64, c0:c0 + W], writes=[r_krt])
            if j == 0:
                P.ld("act", kr[:, 1, :W], self.UT[OFF_KRR:OFF_KRR + 64, c0:c0 + W], writes=[r_krt])
                cs, r_cs = cs_r.next()
                P.ld("act", cs[:, :, :W], self.rope_cs[:, :, c0 - CTX:c0 - CTX + W], writes=[r_cs])
                tk, r_tk = tk_r.next()
                P.tt(P.pool, tk[:, :, :W], kr[:, :, :W], cs[:, :, :W], ALU.mult, [r_krt, r_cs], [r_tk])
                P.tt(P.pool, krope[:, c0:c0 + W], tk[:, 0, :W], tk[:, 1, :W], ALU.add, [r_tk], [r_kr])
            else:
                P.copy(P.pool, krope[:, c0:c0 + W], kr[:, 0, :W], [r_krt], [r_kr])
            if j == 0 or need_ctx:
                u4, r_u4 = u4_r.next()
                P.ld("sp", u4[:, :, :W], UQ[:, :, c0:c0 + W], writes=[r_u4])
                st, r_st = self.rms_rstd([u4[:, c, :W] for c in range(4)], W, 512, sq_r, 6, st_r, [r_u4])
                for c in range(4):
                    P.stt(cq[:, c, c0:c0 + W], u4[:, c, :W], self.sp_col(l, SP_GQ + c), st[:, :W], ALU.mult, ALU.mult, [r_u4, r_st, self.r_const], [r_cq])
        wkv_r = P.ring("wkv", 2, [128, 2, 256], BF16)
        wq_r = P.ring("wq", 2, [128, 4, 256], BF16)
        kn_r = P.ring("kn", 1, [128, NT], BF16)
        v_r = P.ring("vv", 1, [128, NT // 128, 128], BF16)
        qn_r = P.ring("qn", 1, [128, NT], BF16)
        qr_r = P.ring("qr", 1, [64, NT], BF16)
        pT_r = P.ring("pT", 4, [128, 512], BF16)
        ri_r = P.ring("ri", 2, [128, 512], F32)
        oo_r = P.ring("oo", 2, [128, 512], F32)
        ob_r = P.ring("aob", 2, [128, 512], BF16)
        WKV = self.WUKV[li].rearrange("(kc p) n -> p kc n", p=128)
        WQ = self.WUQ[li].rearrange("(kc p) n -> p kc n", p=128)
        kb = 0
        for h in range(8):
            wkv, r_wkv = wkv_r.next()
            wq, r_wq = wq_r.next()
            P.ld("sp", wkv[:], WKV[:, :, h * 256:(h + 1) * 256], writes=[r_wkv])
            P.ld("sp", wq[:], WQ[:, :, h * 256:(h + 1) * 256], writes=[r_wq])
            kn, r_kn = kn_r.next()
            vv, r_vv = v_r.next()
            qn, r_qn = qn_r.next()
            qr, r_qr = qr_r.next()
            for (c0, W, j) in self.tok_tiles(True):
                pt, r_pt = self.pb[kb % 4]
                kb += 1
                for kc in range(2):
                    P.mm(pt[:, :W], wkv[:, kc, 0:128], ckv[:, kc, c0:c0 + W], kc == 0, kc == 1, [r_wkv, r_ckv], [r_pt])
                P.copy(P.act, kn[:, c0:c0 + W], pt[:, :W], [r_pt], [r_kn])
                if j == 0 or need_ctx:
                    pt, r_pt = self.pb[kb % 4]
                    kb += 1
                    for kc in range(4):
                        P.mm(pt[:, :W], wq[:, kc, 0:128], cq[:, kc, c0:c0 + W], kc == 0, kc == 3, [r_wq, r_cq], [r_pt])
                    P.copy(P.dve, qn[:, c0:c0 + W], pt[:, :W], [r_pt], [r_qn])
                    pt, r_pt = self.pb[kb % 4]
                    kb += 1
                    for kc in range(4):
                        P.mm(pt[0:64, :W], wq[:, kc, 128:192], cq[:, kc, c0:c0 + W], kc == 0, kc == 3, [r_wq, r_cq], [r_pt])
                    if j == 0:
                        pt2, r_pt2 = self.pb[kb % 4]
                        kb += 1
                        for kc in range(4):
                            P.mm(pt2[0:64, :W], wq[:, kc, 192:256], cq[:, kc, c0:c0 + W], kc == 0, kc == 3, [r_wq, r_cq], [r_pt2])
                        cs, r_cs = cs_r.next()
                        P.ld("act", cs[:, :, :W], self.rope_cs[:, :, c0 - CTX:c0 - CTX + W], writes=[r_cs])
                        tk, r_tk = tk_r.next()
                        P.tt(P.dve, tk[:, 0, :W], pt[0:64, :W], cs[:, 0, :W], ALU.mult, [r_pt, r_cs], [r_tk])
                        P.tt(P.dve, tk[:, 1, :W], pt2[0:64, :W], cs[:, 1, :W], ALU.mult, [r_pt2, r_cs], [r_tk])
                        P.tt(P.pool, qr[:, c0:c0 + W], tk[:, 0, :W], tk[:, 1, :W], ALU.add, [r_tk], [r_qr])
                    else:
                        P.copy(P.dve, qr[:, c0:c0 + W], pt[0:64, :W], [r_pt], [r_qr])
            for kg in range(0, NT // 128, 4):
                nk4 = min(4, NT // 128 - kg)
                pt, r_pt = self.pb[kb % 4]
                kb += 1
                for q4 in range(nk4):
                    kc = kg + q4
                    for k2 in range(2):
                        P.mm(pt[:, q4 * 128:(q4 + 1) * 128], ckv[:, k2, kc * 128:(kc + 1) * 128], wkv[:, k2, 128:256], k2 == 0, k2 == 1, [r_ckv, r_wkv], [r_pt])
                P.copy(P.act, vv[:, kg:kg + nk4, :], pt[:, 0:nk4 * 128].rearrange("p (a b) -> p a b", b=128), [r_pt], [r_vv])
            for qi, (c0, W, j) in enumerate(self.tok_tiles(need_ctx)):
                nk = 2 if j == 1 else NT // 128
                po, r_po = self.pb[4 + 2 * (qi % 2)]
                pl, r_pl = self.pb[5 + 2 * (qi % 2)]
                pend = None
                for kc in range(nk + 1):
                    if kc < nk:
                        pS, r_pS = self.pb[kc % 4]
                        P.mm(pS[:, :W], kn[:, kc * 128:(kc + 1) * 128], qn[:, c0:c0 + W], True, False, [r_kn, r_qn], [r_pS])
                        P.mm(pS[:, :W], krope[:, kc * 128:(kc + 1) * 128], qr[:, c0:c0 + W], False, True, [r_kr, r_qr], [r_pS])
                        pT, r_pT = pT_r.next()
                        P.actv(pT[:, :W], pS[:, :W], AF.Exp, [r_pS], [r_pT], scale=MLA_SCALE)
                    if pend is not None:
                        pkc, ppT, pr_pT = pend
                        P.mm(po[:, :W], vv[:, pkc, :], ppT[:, :W], pkc == 0, pkc == nk - 1, [r_vv, pr_pT], [r_po])
                        P.mm(pl[:, :W], self.ones_b[:], ppT[:, :W], pkc == 0, pkc == nk - 1, [self.r_const, pr_pT], [r_pl])
                    pend = (kc, pT, r_pT) if kc < nk else None
                ri, r_ri = ri_r.next()
                P.recip(ri[:, :W], pl[:, :W], [r_pl], [r_ri])
                oo, r_oo = oo_r.next()
                P.tt(P.dve, oo[:, :W], po[:, :W], ri[:, :W], ALU.mult, [r_po, r_ri], [r_oo])
                self.head_norm_store(oo[:, :W], W, 8 + h, l, 1024 + h * 128, c0, sq_r, st_r, ob_r, 4 + 2 * (qi % 2), r_oo)
        P.pop(mk)

    def phase_outproj(self, l, need_ctx):
        P = self.P
        li = self.lidx[l]
        mk = P.push()
        yc_r = P.ring("yc", 2, [128, 16, 512], BF16)
        w_r = P.ring("wo", 2, [128, 16, 512], BF16)
        mix_r = P.ring("mix", 1, [128, 16, 512], F32)
        xs_r = P.ring("oxs", 1, [128, 16, 512], F32)
        hb_r = P.ring("ohb", 1, [128, 16, 512], BF16)
        sq_r = P.ring("osq", 3, [128, 512], BF16)
        st_r = P.ring("ost", 2, [128, 512], F32)
        tmp_r = P.ring("otmp", 3, [128, 512], F32)
        YCv = self.YC.rearrange("(c p) t -> p c t", p=128)
        XTv = self.XT.rearrange("(c p) t -> p c t", p=128)
        H2v = self.H2.rearrange("(c p) t -> p c t", p=128)
        Wv = self.WOUT[li].rearrange("(kc p) n -> p kc n", p=128)
        kb = 0
        for (c0, W, j) in self.tok_tiles(need_ctx):
            yc, r_yc = yc_r.next()
            P.ld("sp", yc[:, :, :W], YCv[:, :, c0:c0 + W], writes=[r_yc])
            xs, r_xs = xs_r.next()
            P.ld("sp", xs[:, :, :W], XTv[:, :, c0:c0 + W], writes=[r_xs])
            mix, r_mix = mix_r.next()
            for g in range(4):
                wt, r_wt = w_r.next()
                P.ld("act", wt[:], Wv[:, :, g * 512:(g + 1) * 512], writes=[r_wt])
                for m in range(4):
                    pt, r_pt = self.pb[kb % 6]
                    kb += 1
                    for kc in range(16):
                        P.mm(pt[:, :W], wt[:, kc, m * 128:(m + 1) * 128], yc[:, kc, :W], kc == 0, kc == 15, [r_wt, r_yc], [r_pt])
                    P.copy(P.act if kb % 2 == 0 else P.dve, mix[:, g * 4 + m, :W], pt[:, :W], [r_pt], [r_mix])
            st, r_st = self.rms_rstd([mix[:, c, :W] for c in range(16)], W, D, sq_r, 7, st_r, [r_mix])
            for c in range(16):
                tmp, r_tmp = tmp_r.next()
                P.tt(P.pool, tmp[:, :W], mix[:, c, :W], st[:, :W], ALU.mult, [r_mix, r_st], [r_tmp])
                P.stt(xs[:, c, :W], tmp[:, :W], self.mod(2, c, j), xs[:, c, :W], ALU.mult, ALU.add, [r_tmp, r_xs, self.r_mods], [r_xs])
            P.ld("sp", XTv[:, :, c0:c0 + W], xs[:, :, :W], reads=[r_xs])
            st, r_st = self.rms_rstd([xs[:, c, :W] for c in range(16)], W, D, sq_r, 6, st_r, [r_xs])
            hb, r_hb = hb_r.next()
            for c in range(16):
                tmp, r_tmp = tmp_r.next()
                P.tt(P.dve, tmp[:, :W], xs[:, c, :W], st[:, :W], ALU.mult, [r_xs, r_st], [r_tmp])
                P.actv(hb[:, c, :W], tmp[:, :W], AF.Identity, [r_tmp, self.r_mods], [r_hb], bias=self.mod(4, c, j), scale=self.mod(3, c, j))
            P.ld("sp", H2v[:, :, c0:c0 + W], hb[:, :, :W], reads=[r_hb])
        P.pop(mk)

    def phase_ffn(self, l, need_ctx):
        P = self.P
        li = self.lidx[l]
        mk = P.push()
        OT = 456
        tiles = []
        if need_ctx:
            tiles.append((0, CTX, 0, CTX, 1))
        for lo in range(0, SEQ, OT):
            tiles.append((CTX, SEQ, lo, min(SEQ, lo + OT), 0))
        WT = OT + 2
        h2_r = P.ring("fh2", 1, [128, 16, WT], BF16)
        act = P.sbuf("fact", [128, 44, OT], BF16)
        r_act = [Res() for _ in range(44)]
        wu_r = P.ring("fwu", 2, [128, 2, 16, 256], BF16)
        wd_r = P.ring("fwd", 2, [128, 44, 128], BF16)
        f_r = P.ring("ff", 1, [128, 16, OT], F32)
        xs_r = P.ring("fxs", 1, [128, 16, OT], F32)
        gc_r = P.ring("fgc", 2, [128, OT], F32)
        vc_r = P.ring("fvc", 2, [128, OT], F32)
        sq_r = P.ring("fsq", 3, [128, 512], BF16)
        st_r = P.ring("fst_", 2, [128, 512], F32)
        tmp_r = P.ring("ftmp", 3, [128, OT], F32)
        XTv = self.XT.rearrange("(c p) t -> p c t", p=128)
        H2v = self.H2.rearrange("(c p) t -> p c t", p=128)
        WU = self.WUP[li].rearrange("(kc p) n -> p kc n", p=128)
        WD = self.WDN[li].rearrange("(kc p) n -> p kc n", p=128)
        kb = 0
        for (off, Ls, lo, hi, j) in tiles:
            Wo = hi - lo
            Wt = Wo + 2
            h2, r_h2 = h2_r.next()
            a, b = lo - 1, hi + 1
            ca, cb = 0, Wt
            if a < 0:
                P.memset(P.pool, h2[:, :, 0:1], 0.0, [r_h2])
                a, ca = 0, 1
            if b > Ls:
                P.memset(P.pool, h2[:, :, Wt - 1:Wt], 0.0, [r_h2])
                b, cb = Ls, Wt - 1
            P.ld("sp", h2[:, :, ca:cb], H2v[:, :, off + a:off + b], writes=[r_h2])
            for i in range(44):
                if i % 2 == 0:
                    wu, r_wu = wu_r.next()
                    P.ld("act", wu[:, 0, :, :], WU[:, :, i * 128:i * 128 + 256], writes=[r_wu])
                    P.ld("act", wu[:, 1, :, :], WU[:, :, DFF + i * 128:DFF + i * 128 + 256], writes=[r_wu])
                s0 = (i % 2) * 128
                pg, r_pg = self.pb[(2 * i) % 6]
                pv, r_pv = self.pb[(2 * i + 1) % 6]
                for kc in range(16):
                    P.mm(pg[:, :Wt], wu[:, 0, kc, s0:s0 + 128], h2[:, kc, :Wt], kc == 0, kc == 15, [r_wu, r_h2], [r_pg])
                for kc in range(16):
                    P.mm(pv[:, :Wt], wu[:, 1, kc, s0:s0 + 128], h2[:, kc, :Wt], kc == 0, kc == 15, [r_wu, r_h2], [r_pv])
                gc, r_gc = gc_r.next()
                vc, r_vc = vc_r.next()
                for (pp, r_pp, oc, r_oc, ci) in ((pg, r_pg, gc, r_gc, i), (pv, r_pv, vc, r_vc, 44 + i)):
                    P.ts(P.dve, oc[:, :Wo], pp[:, 0:Wo], self.sp_col(l, SP_FC + 0 * 88 + ci), None, ALU.mult, None, [r_pp, self.r_const], [r_oc])
                    P.stt(oc[:, :Wo], pp[:, 1:Wo + 1], self.sp_col(l, SP_FC + 1 * 88 + ci), oc[:, :Wo], ALU.mult, ALU.add, [r_pp, r_oc, self.r_const], [r_oc])
                    P.stt(oc[:, :Wo], pp[:, 2:Wo + 2], self.sp_col(l, SP_FC + 2 * 88 + ci), oc[:, :Wo], ALU.mult, ALU.add, [r_pp, r_oc, self.r_const], [r_oc])
                P.actv(gc[:, :Wo], gc[:, :Wo], AF.Gelu_apprx_tanh, [r_gc], [r_gc])
                P.tt(P.pool, act[:, i, :Wo], gc[:, :Wo], vc[:, :Wo], ALU.mult, [r_gc, r_vc], [r_act[i]])
            f, r_f = f_r.next()
            for m in range(16):
                wd, r_wd = wd_r.next()
                P.ld("act", wd[:], WD[:, :, m * 128:(m + 1) * 128], writes=[r_wd])
                pd, r_pd = self.pb[6 + m % 2]
                for kc in range(44):
                    P.mm(pd[:, :Wo], wd[:, kc, :], act[:, kc, :Wo], kc == 0, kc == 43, [r_wd, r_act[kc]], [r_pd])
                P.copy(P.act if m % 2 == 0 else P.dve, f[:, m, :Wo], pd[:, :Wo], [r_pd], [r_f])
            st, r_st = self.rms_rstd([f[:, c, :Wo] for c in range(16)], Wo, D, sq_r, 0, st_r, [r_f])
            xs, r_xs = xs_r.next()
            P.ld("sp", xs[:, :, :Wo], XTv[:, :, off + lo:off + hi], writes=[r_xs])
            for c in range(16):
                tmp, r_tmp = tmp_r.next()
                P.tt(P.pool, tmp[:, :Wo], f[:, c, :Wo], st[:, :Wo], ALU.mult, [r_f, r_st], [r_tmp])
                P.stt(xs[:, c, :Wo], tmp[:, :Wo], self.mod(5, c, j), xs[:, c, :Wo], ALU.mult, ALU.add, [r_tmp, r_xs, self.r_mods], [r_xs])
            P.ld("sp", XTv[:, :, off + lo:off + hi], xs[:, :, :Wo], reads=[r_xs])
        P.pop(mk)


def make_inmaps(inp, n_cores=2, layers=None):
    c = _consts()
    layers = list(range(DEPTH)) if layers is None else list(layers)
    shared = {k: np.ascontiguousarray(np.asarray(inp[k], dtype=np.float32)[layers]) for k in (
        "ada_w", "w_in", "hy_w1", "hy_w2", "hy_w3", "lru_wa", "lru_wx", "mla_wuq", "mla_wukv", "w_out", "ffn_up", "ffn_down")}
    shared["smallp"] = _pack_small({k: np.asarray(v, dtype=np.float32) for k, v in inp.items()}).reshape(128, DEPTH * NSP)
    shared.update(c)
    maps = []
    for core in range(n_cores):
        b = core // (n_cores // 2)
        m = dict(shared)
        m["x"] = np.ascontiguousarray(np.asarray(inp["x"][b], dtype=np.float32))
        m["ctx"] = np.ascontiguousarray(np.asarray(inp["ctx"][b], dtype=np.float32))
        cv = np.stack([np.asarray(inp["c"][b], dtype=np.float32), np.asarray(inp["c_ctx"], dtype=np.float32)], axis=-1)
        m["cvec"] = np.ascontiguousarray(cv.reshape(16, 128, 2).transpose(1, 0, 2).reshape(128, 32))
        maps.append(m)
    return maps


_NC_CACHE = {}


def kernel(**inputs):
    if "nc" not in _NC_CACHE:
        _NC_CACHE["nc"] = Builder(range(DEPTH)).build()
    nc = _NC_CACHE["nc"]
    maps = make_inmaps(inputs)
    res = run_bass_kernel_spmd(nc, maps, core_ids=[0, 1])
    out = np.stack([np.asarray(res.results[0]["out"]), np.asarray(res.results[1]["out"])], axis=0)
    return out.astype(np.float32)
```

```python
import math
import numpy as np
import ml_dtypes
import concourse.bass as bass
import concourse.mybir as mybir
from concourse.bass_utils import run_bass_kernel_spmd

F32 = mybir.dt.float32
BF16 = mybir.dt.bfloat16
AF = mybir.ActivationFunctionType
ALU = mybir.AluOpType

D = 2048
SEQ = 4096
CTX = 256
NT = SEQ + CTX
DEPTH = 4
NR = 4
SL = SEQ // NR
NTL = CTX + SL
GROUPS = [[0, 1, 2, 3], [4, 5, 6, 7]]
UCH = 14
DIN = 3392
DINX = 3456
OFF_HY, OFF_G, OFF_Q, OFF_X, OFF_KV, OFF_KR, OFF_KRR = 0, 1536, 2048, 2560, 3072, 3328, 3392
DFF = 5632
EPS = 1e-6
MLA_SCALE = 192 ** -0.5
MAGIC = 12582912.0

SP_ADAB, SP_NG, SP_HYC, SP_HYB, SP_LC, SP_BA, SP_BX, SP_LAM = 0, 96, 160, 196, 204, 236, 244, 252
SP_GQ, SP_GKV, SP_HG, SP_FC, SP_B1, SP_B2, NSP = 260, 264, 266, 282, 546, 547, 548


class Res:
    __slots__ = ("name", "w", "r")

    def __init__(self, name=""):
        self.name = name
        self.w = None
        self.r = {}


class EngQ:
    def __init__(self, name, eng, inorder=False):
        self.name = name
        self.eng = eng
        self.inorder = inorder
        self.sem = None
        self.step = 1
        self.count = 0
        self.waited = {}
        self.thunks = []


class DmaSlot:
    def __init__(self, sem, name):
        self.sem = sem
        self.name = name
        self.count = 0
        self.step = 16
        self.inorder = False


class Ring:
    def __init__(self, items):
        self.items = items
        self.i = 0

    def next(self):
        it = self.items[self.i % len(self.items)]
        self.i += 1
        return it


class Prog:
    def __init__(self, nc, n_dma_slots=20, same_engine_sync=True):
        self.nc = nc
        self.same_engine_sync = same_engine_sync
        self.ctx = []
        self.pe = EngQ("pe", nc.tensor, inorder=True)
        self.dve = EngQ("dve", nc.vector)
        self.act = EngQ("act", nc.scalar)
        self.pool = EngQ("pool", nc.gpsimd)
        self.sp = EngQ("sp", nc.sync)
        self.engs = [self.pe, self.dve, self.act, self.pool, self.sp]
        for e in self.engs:
            e.sem = self._sem("s_" + e.name)
        self.slots = {}
        for qn in ("sp", "act", "pool"):
            self.slots[qn] = [DmaSlot(self._sem(f"d_{qn}{i}"), f"d_{qn}{i}") for i in range(n_dma_slots)]
        self.slot_rr = {"sp": 0, "act": 0, "pool": 0}
        self.n_inst = 0
        self.uid = 0

    def _sem(self, name):
        cm = self.nc.semaphore(name)
        s = cm.__enter__()
        self.ctx.append(cm)
        return s

    def push(self):
        return len(self.ctx)

    def pop(self, mark):
        self.barrier()
        while len(self.ctx) > mark:
            self.ctx.pop().__exit__(None, None, None)

    def sbuf(self, name, shape, dtype):
        self.uid += 1
        cm = self.nc.sbuf_tensor(f"{name}_{self.uid}", list(shape), dtype)
        t = cm.__enter__()
        self.ctx.append(cm)
        return t

    def psum(self, name, shape, dtype=F32):
        cm = self.nc.psum_tensor(name, list(shape), dtype)
        t = cm.__enter__()
        self.ctx.append(cm)
        return t

    def ring(self, name, n, shape, dtype):
        return Ring([(self.sbuf(f"{name}{i}", shape, dtype), Res(f"{name}{i}")) for i in range(n)])

    def _deps(self, reads, writes):
        deps = []
        for r in reads:
            if r.w is not None:
                deps.append(r.w)
        for w in writes:
            if w.w is not None:
                deps.append(w.w)
            deps.extend(w.r.values())
        return deps

    def _emit_waits(self, q, deps):
        need = {}
        for (src, c) in deps:
            if src is q and (q.inorder or not self.same_engine_sync):
                continue
            if q.waited.get(id(src), 0) >= c:
                continue
            if need.get(id(src), (None, 0))[1] < c:
                need[id(src)] = (src, c)
        for src, c in need.values():
            q.waited[id(src)] = c
            sem, val = src.sem, c * src.step
            q.thunks.append(lambda e, sem=sem, val=val: e.wait_ge(sem, val))

    def _mark(self, src, c, reads, writes):
        for r in reads:
            old = r.r.get(id(src))
            if old is None or old[1] < c:
                r.r[id(src)] = (src, c)
        for w in writes:
            w.w = (src, c)
            w.r = {}

    def op(self, q, fn, reads=(), writes=()):
        self._emit_waits(q, self._deps(reads, writes))
        q.count += 1
        sem = q.sem
        q.thunks.append(lambda e, fn=fn, sem=sem: fn(e).then_inc(sem, 1))
        self._mark(q, q.count, reads, writes)
        self.n_inst += 1

    def dma(self, qname, fn, reads=(), writes=()):
        q = {"sp": self.sp, "act": self.act, "pool": self.pool}[qname]
        slots = self.slots[qname]
        i = self.slot_rr[qname]
        self.slot_rr[qname] = (i + 1) % len(slots)
        slot = slots[i]
        deps = self._deps(reads, writes)
        if slot.count > 0:
            deps.append((slot, slot.count))
        self._emit_waits(q, deps)
        slot.count += 1
        sem = slot.sem
        q.thunks.append(lambda e, fn=fn, sem=sem: fn(e).then_inc(sem, 16))
        self._mark(slot, slot.count, reads, writes)
        self.n_inst += 1

    def barrier(self):
        srcs = list(self.engs)
        for qn in self.slots:
            srcs.extend(self.slots[qn])
        deps = [(s, s.count) for s in srcs if s.count > 0]
        for q in self.engs:
            need = [(s, c) for (s, c) in deps]
            keep_sync = self.same_engine_sync
            self.same_engine_sync = True
            self._emit_waits(q, need)
            self.same_engine_sync = keep_sync

    def mm(self, out, lhsT, rhs, start, stop, reads, writes):
        self.op(self.pe, lambda e: e.matmul(out, lhsT=lhsT, rhs=rhs, start=start, stop=stop), reads, writes)

    def tr(self, out, in_, ident, reads, writes):
        self.op(self.pe, lambda e: e.transpose(out, in_, ident), reads, writes)

    def tt(self, q, out, in0, in1, op, reads, writes):
        self.op(q, lambda e: e.tensor_tensor(out=out, in0=in0, in1=in1, op=op), reads, writes)

    def ts(self, q, out, in0, s1, s2, op0, op1, reads, writes):
        if op1 is None:
            self.op(q, lambda e: e.tensor_scalar(out=out, in0=in0, scalar1=s1, scalar2=None, op0=op0), reads, writes)
        else:
            self.op(q, lambda e: e.tensor_scalar(out=out, in0=in0, scalar1=s1, scalar2=s2, op0=op0, op1=op1), reads, writes)

    def stt(self, out, in0, scalar, in1, op0, op1, reads, writes):
        self.op(self.dve, lambda e: e.scalar_tensor_tensor(out=out, in0=in0, scalar=scalar, in1=in1, op0=op0, op1=op1), reads, writes)

    def actv(self, out, in_, func, reads, writes, bias=None, scale=None):
        kw = {}
        if bias is not None:
            kw["bias"] = bias
        if scale is not None:
            kw["scale"] = scale
        self.op(self.act, lambda e: e.activation(out=out, in_=in_, func=func, **kw), reads, writes)

    def copy(self, q, out, in_, reads, writes):
        if q is self.act:
            self.op(q, lambda e: e.activation(out=out, in_=in_, func=AF.Copy), reads, writes)
        else:
            self.op(q, lambda e: e.tensor_copy(out=out, in_=in_), reads, writes)

    def recip(self, out, in_, reads, writes):
        self.op(self.dve, lambda e: e.reciprocal(out=out, in_=in_), reads, writes)

    def memset(self, q, ap, val, writes):
        self.op(q, lambda e: e.memset(ap, val), (), writes)

    def ld(self, qname, out, in_, reads=(), writes=(), slow=False):
        if slow:
            self.dma(qname, lambda e: e.dma_start(out=out, in_=in_, allow_slow_non_contiguous=True), reads, writes)
        else:
            self.dma(qname, lambda e: e.dma_start(out=out, in_=in_), reads, writes)

    def finish(self, final_res):
        deps = [r.w for r in final_res if r.w is not None]
        self._emit_waits(self.sp, deps)
        self.barrier()
        nc = self.nc
        with nc.Block() as block:
            @block.sync
            def _(e):
                for t in self.sp.thunks:
                    t(e)

            @block.tensor
            def _(e):
                for t in self.pe.thunks:
                    t(e)

            @block.vector
            def _(e):
                for t in self.dve.thunks:
                    t(e)

            @block.scalar
            def _(e):
                for t in self.act.thunks:
                    t(e)

            @block.gpsimd
            def _(e):
                for t in self.pool.thunks:
                    t(e)
        while self.ctx:
            self.ctx.pop().__exit__(None, None, None)


_CONST_CACHE = {}


def _bf(a):
    return np.ascontiguousarray(a.astype(np.float32)).astype(ml_dtypes.bfloat16)


def _dft_tables(L):
    N = 2 * L
    nfc = (L + 1 + 127) // 128
    FP = nfc * 128
    a = np.arange(FP, dtype=np.int64)
    t = np.arange(L, dtype=np.int64)
    ang = 2.0 * np.pi * ((t[:, None] * a[None, :]) % N).astype(np.float64) / N
    valid = (a <= L).astype(np.float64)
    C = np.cos(ang) * valid
    S = np.sin(ang) * valid
    ntc = L // 128
    Fc = C.reshape(ntc, 128, nfc, 128).transpose(2, 1, 0, 3)
    Fs = S.reshape(ntc, 128, nfc, 128).transpose(2, 1, 0, 3)
    w = np.where((a == 0) | (a == L), 1.0, 2.0) * valid / N
    IcM = C.T * w[:, None]
    IsM = S.T * w[:, None]
    TW = min(512, L)
    ntt = L // TW
    Ic = IcM.reshape(nfc, 128, ntt, TW).transpose(2, 1, 0, 3)
    Is = IsM.reshape(nfc, 128, ntt, TW).transpose(2, 1, 0, 3)
    return _bf(Fc), _bf(Fs), _bf(Ic), _bf(Is)


def _hy_consts(L):
    t = (np.arange(L, dtype=np.float32) / np.float32(L)).astype(np.float32)
    ang = (2.0 * math.pi * t[:, None] * np.arange(1, 17, dtype=np.float32)).astype(np.float32)
    feats = np.concatenate([t[:, None], np.sin(ang), np.cos(ang)], axis=-1).astype(np.float32)
    deltas = np.abs(np.linspace(math.log(1e-2) / 1.5, math.log(1e-2) / 0.3, 512, dtype=np.float32))
    decay = np.exp(-t[:, None] * deltas[None, :]).astype(np.float32)
    return np.ascontiguousarray(feats.T), decay


def _rope_tables():
    row = np.repeat(np.arange(SEQ // 64, dtype=np.float32), 64)
    col = np.tile(np.arange(64, dtype=np.float32), SEQ // 64)
    inv = (10000.0 ** (-np.arange(16, dtype=np.float32) / 16)).astype(np.float32)
    ang = np.concatenate([row[:, None] * inv, col[:, None] * inv], axis=-1).astype(np.float32)
    cs = np.zeros((64, 2, SEQ), np.float32)
    cs[:32, 0] = np.cos(ang).T
    cs[32:, 0] = np.cos(ang).T
    cs[:32, 1] = np.sin(ang).T
    cs[32:, 1] = np.sin(ang).T
    return cs


def _consts():
    if _CONST_CACHE:
        return _CONST_CACHE
    c = {}
    c["ident_f"] = np.eye(128, dtype=np.float32)
    c["rope_cs"] = _rope_tables()
    for L, tag in ((SEQ, "L"), (CTX, "C")):
        Fc, Fs, Ic, Is = _dft_tables(L)
        c["Fc" + tag], c["Fs" + tag], c["Ic" + tag], c["Is" + tag] = Fc, Fs, Ic, Is
        f, d = _hy_consts(L)
        c["feats" + tag], c["decay" + tag] = f, d
    _CONST_CACHE.update(c)
    return _CONST_CACHE


def _pack_small(inp):
    sp = np.zeros((128, DEPTH, NSP), np.float32)

    def put(col, arr):
        n = arr.shape[1] // 128
        sp[:, :, col:col + n] = arr.reshape(DEPTH, n, 128).transpose(2, 0, 1)

    put(SP_ADAB, inp["ada_b"])
    put(SP_NG, inp["norm_g"].reshape(DEPTH, 4 * D))
    put(SP_HYC, inp["hy_conv"].reshape(DEPTH, 3 * 1536))
    put(SP_HYB, inp["hy_bias"].reshape(DEPTH, 2 * 512))
    put(SP_LC, inp["lru_conv"].reshape(DEPTH, 2 * 4 * 512))
    put(SP_BA, inp["lru_ba"].reshape(DEPTH, 1024))
    put(SP_BX, inp["lru_bx"].reshape(DEPTH, 1024))
    put(SP_LAM, inp["lru_lam"].reshape(DEPTH, 1024))
    put(SP_GQ, inp["mla_gq"])
    put(SP_GKV, inp["mla_gkv"])
    put(SP_HG, inp["head_g"])
    put(SP_FC, inp["ffn_conv"].reshape(DEPTH, 3 * 2 * DFF))
    sp[:64, :, SP_B1] = inp["hy_b1"].T
    sp[:64, :, SP_B2] = inp["hy_b2"].T
    return sp


class Builder:
    def __init__(self, layers, dbg=(), phases=None, same_engine_sync=True):
        self.layers = list(layers)
        self.lidx = {l: i for i, l in enumerate(self.layers)}
        NL = len(self.layers)
        self.dbg = set(dbg)
        self.phases = phases
        nc = self.nc = bass.Bass("TRN2", target_bir_lowering=False)
        self.I = {}
        din = self.din
        self.x_in = din("x", [SL, D])
        self.ctx_in = din("ctx", [CTX, D])
        self.cvec = din("cvec", [128, 32])
        self.smallp_in = din("smallp", [128, DEPTH * NSP])
        self.ada_w = din("ada_w", [NL, D, 6 * D])
        self.w_in = din("w_in", [NL, D, DIN])
        self.hy_w1 = din("hy_w1", [NL, 33, 64])
        self.hy_w2 = din("hy_w2", [NL, 64, 64])
        self.hy_w3 = din("hy_w3", [NL, 64, 2048])
        self.lru_wa = din("lru_wa", [NL, 2, 4, 128, 128])
        self.lru_wx = din("lru_wx", [NL, 2, 4, 128, 128])
        self.mla_wuq = din("mla_wuq", [NL, 512, 1536])
        self.mla_wukv = din("mla_wukv", [NL, 256, 2048])
        self.w_out = din("w_out", [NL, D, D])
        self.ffn_up = din("ffn_up", [NL, D, 2 * DFF])
        self.ffn_down = din("ffn_down", [NL, DFF, D])
        self.ident_in = din("ident_f", [128, 128])
        self.rope_cs = din("rope_cs", [64, 2, SEQ])
        self.tab = {}
        for L, tag in ((SEQ, "L"), (CTX, "C")):
            nfc = (L + 1 + 127) // 128
            ntc = L // 128
            TW = min(512, L)
            self.tab[tag] = dict(
                L=L, nfc=nfc, ntc=ntc, TW=TW, ntt=L // TW,
                Fc=din("Fc" + tag, [nfc, 128, ntc, 128], BF16), Fs=din("Fs" + tag, [nfc, 128, ntc, 128], BF16),
                Ic=din("Ic" + tag, [L // TW, 128, nfc, TW], BF16), Is=din("Is" + tag, [L // TW, 128, nfc, TW], BF16),
                feats=din("feats" + tag, [33, L]), decay=din("decay" + tag, [L, 512]),
                S=self.dscr("S" + tag, [nfc * 128, 2048], F32),
            )
        self.out = nc.dram_tensor("out", [SL, D], F32, kind="ExternalOutput").ap()
        ds = self.dscr
        self.XT = ds("XT", [D, NTL], F32)
        self.UC = ds("UC", [DINX, CTX], F32)
        self.ULOC = ds("ULOC", [UCH * 256, SL], F32)
        self.UG = ds("UG", [UCH * NR * 256, SL], F32)
        self.HB = ds("HB", [D, 2], BF16)
        self.HG = ds("HG", [NR * D, 2], BF16)
        self.HGP = ds("HGP", [(NR + 2) * D, 2], BF16)
        self.ZT = ds("ZT", [512, NT], F32)
        self.PT = ds("PT", [1024, NT], F32)
        self.YC = ds("YC", [D, NT], BF16)
        self.H2 = ds("H2", [D, NTL], BF16)
        self.WIN = ds("WIN", [NL, D, DINX], BF16)
        self.WUQ = ds("WUQ", [NL, 512, 2048], BF16)
        self.WUKV = ds("WUKV", [NL, 256, 2048], BF16)
        self.WOUT = ds("WOUT", [NL, D, D], BF16)
        self.WUP = ds("WUP", [NL, D, 2 * DFF], BF16)
        self.WDN = ds("WDN", [NL, DFF, D], BF16)
        self.LWA = ds("LWA", [NL, 2, 4, 128, 128], BF16)
        self.LWX = ds("LWX", [NL, 2, 4, 128, 128], BF16)
        self.P = Prog(nc, same_engine_sync=same_engine_sync)
        self.r_out = Res("out")
        self._rank = {}

    def din(self, name, shape, dt=F32):
        self.I[name] = (list(shape), dt)
        return self.nc.dram_tensor(name, list(shape), dt, kind="ExternalInput").ap()

    def dscr(self, name, shape, dt):
        kind = "ExternalOutput" if name in self.dbg else "Internal"
        return self.nc.dram_tensor(name, list(shape), dt, kind=kind).ap()

    def rank(self, e):
        key = id(e)
        if key not in self._rank:
            self._rank[key] = e.snap(e.partition_id() % NR)
        return self._rank[key]

    def Uap(self, row0, nrows, c0, W):
        if c0 < CTX:
            return self.UC[row0:row0 + nrows, c0:c0 + W]
        t0 = c0 - CTX
        s_, tl = t0 // SL, t0 % SL
        assert tl + W <= SL
        k, rr = row0 // 256, row0 % 256
        assert rr + nrows <= 256
        base = (k * NR + s_) * 256 + rr
        return self.UG[base:base + nrows, tl:tl + W]

    def Ulat(self, row0):
        k, rr = row0 // 256, row0 % 256
        return self.UG.rearrange("(k s r) t -> k r s t", s=NR, r=256)[k][rr:rr + 128, :, :]

    def loc_tiles(self, need_ctx=True):
        tiles = [(0, CTX, 1)] if need_ctx else []
        tiles += [(CTX + i * 512, 512, 0) for i in range(SL // 512)]
        return tiles

    def allgather(self, pairs):
        P = self.P
        P.barrier()
        for (src, dst) in pairs:
            P.op(P.pool, lambda e, src=src, dst=dst: e.collective_compute("AllGather", ALU.bypass, replica_groups=GROUPS, ins=[src.opt()], outs=[dst.opt()]), (), ())
        P.barrier()

    def on(self, ph):
        return self.phases is None or ph in self.phases

    def build(self):
        P = self.P
        self.setup()
        for l in self.layers:
            last = (l == DEPTH - 1)
            if self.on("M"):
                self.phase_mod(l)
            if self.on("A"):
                self.phase_inproj(l)
            if self.on("H"):
                if not last:
                    self.phase_hyena(l, "C")
                self.phase_hyena(l, "L")
            if self.on("R"):
                self.phase_lru(l, not last)
            if self.on("T"):
                self.phase_attn(l, not last)
            if self.on("O"):
                self.phase_outproj(l, not last)
            if self.on("F"):
                self.phase_ffn(l, not last)
        if self.on("Z"):
            self.final()
        P.finish([self.r_out])
        return self.nc

    def setup(self):
        P = self.P
        self.pb = [(P.psum(f"pb{i}", [128, 512]), Res(f"pb{i}")) for i in range(8)]
        self.ident = P.sbuf("ident", [128, 128], F32)
        self.r_const = Res("const")
        P.ld("sp", self.ident[:], self.ident_in[:, :], writes=[self.r_const])
        self.ones_b = P.sbuf("ones_b", [128, 128], BF16)
        self.ones_f = P.sbuf("ones_f", [128, 128], F32)
        P.memset(P.dve, self.ones_b[:], 1.0, [self.r_const])
        P.memset(P.dve, self.ones_f[:], 1.0, [self.r_const])
        self.smallp = P.sbuf("smallp", [128, DEPTH * NSP], F32)
        P.ld("sp", self.smallp[:], self.smallp_in[:, :], writes=[self.r_const])
        self.scv = P.sbuf("scv", [128, 32], F32)
        cvt = P.sbuf("cvt", [128, 32], F32)
        P.ld("sp", cvt[:], self.cvec[:, :], writes=[self.r_const])
        P.actv(self.scv[:], cvt[:], AF.Silu, [self.r_const], [self.r_const])
        self.eps_t = P.sbuf("eps_t", [128, 1], F32)
        P.memset(P.dve, self.eps_t[:], EPS, [self.r_const])
        self.mods = P.sbuf("mods", [128, 6 * 32], F32)
        self.r_mods = Res("mods")
        self.lru_h0 = P.sbuf("lru_h0", [128, 8], F32)
        self.r_h0 = Res("h0")
        for l in self.layers:
            for (src, dst, rows, cols, dcols) in (
                (self.w_in[self.lidx[l]], self.WIN[self.lidx[l]], D, DIN, DINX),
                (self.w_out[self.lidx[l]], self.WOUT[self.lidx[l]], D, D, D),
                (self.ffn_up[self.lidx[l]], self.WUP[self.lidx[l]], D, 2 * DFF, 2 * DFF),
                (self.ffn_down[self.lidx[l]], self.WDN[self.lidx[l]], DFF, D, D),
                (self.mla_wukv[self.lidx[l]], self.WUKV[self.lidx[l]], 256, 2048, 2048),
            ):
                cw = 1408 if cols % 1408 == 0 else (1696 if cols % 1696 == 0 else 1024)
                assert cols % cw == 0 and cw <= 2048
                for r0 in range(0, rows, 1024):
                    r1 = min(rows, r0 + 1024)
                    for c0 in range(0, cols, cw):
                        P.ld("pool", dst[r0:r1, c0:c0 + cw], src[r0:r1, c0:c0 + cw])
            for h in range(8):
                P.ld("pool", self.WUQ[self.lidx[l]][:, h * 256:h * 256 + 192], self.mla_wuq[self.lidx[l]][:, h * 192:(h + 1) * 192])
            for d in range(2):
                P.ld("pool", self.LWA[self.lidx[l], d].rearrange("n a b -> (n a) b"), self.lru_wa[self.lidx[l], d].rearrange("n a b -> (n a) b"))
                P.ld("pool", self.LWX[self.lidx[l], d].rearrange("n a b -> (n a) b"), self.lru_wx[self.lidx[l], d].rearrange("n a b -> (n a) b"))
            mk = P.push()
            wk = P.sbuf("wk", [128, 16, 64], F32)
            wr = P.sbuf("wr", [128, 16, 64], BF16)
            r_wk, r_wr = Res(), Res()
            P.ld("sp", wk[:], self.w_in[self.lidx[l]].rearrange("(kc p) n -> p kc n", p=128)[:, :, OFF_KR:OFF_KR + 64], writes=[r_wk])
            P.ts(P.dve, wr[:, :, 0:32], wk[:, :, 32:64], -1.0, None, ALU.mult, None, [r_wk], [r_wr])
            P.copy(P.dve, wr[:, :, 32:64], wk[:, :, 0:32], [r_wk], [r_wr])
            P.ld("sp", self.WIN[self.lidx[l]].rearrange("(kc p) n -> p kc n", p=128)[:, :, OFF_KRR:OFF_KRR + 64], wr[:], reads=[r_wr])
            qk = P.sbuf("qk", [128, 4, 8, 64], F32)
            qr = P.sbuf("qr", [128, 4, 8, 64], BF16)
            r_qk, r_qr = Res(), Res()
            srcv = self.mla_wuq[self.lidx[l]].rearrange("(kc p) (h n) -> p kc h n", p=128, n=192)
            dstv = self.WUQ[self.lidx[l]].rearrange("(kc p) (h n) -> p kc h n", p=128, n=256)
            for kc in range(4):
                P.ld("sp", qk[:, kc, :, :], srcv[:, kc, :, 128:192], writes=[r_qk])
            P.ts(P.dve, qr[:, :, :, 0:32], qk[:, :, :, 32:64], -1.0, None, ALU.mult, None, [r_qk], [r_qr])
            P.copy(P.dve, qr[:, :, :, 32:64], qk[:, :, :, 0:32], [r_qk], [r_qr])
            for kc in range(4):
                P.ld("sp", dstv[:, kc, :, 192:256], qr[:, kc, :, :], reads=[r_qr])
            P.pop(mk)
        mk = P.push()
        xin = P.ring("xin", 2, [128, D], F32)
        xst = P.ring("xst", 2, [128, 16, 128], F32)
        XTv = self.XT.rearrange("(c p) t -> p c t", p=128)
        k = 0
        for ti in range(NTL // 128):
            src = self.ctx_in[ti * 128:(ti + 1) * 128, :] if ti < 2 else self.x_in[(ti - 2) * 128:(ti - 1) * 128, :]
            xt, r_xt = xin.next()
            P.ld("sp" if ti % 2 == 0 else "act", xt[:], src, writes=[r_xt])
            st, r_st = xst.next()
            for g in range(4):
                pt, r_pt = self.pb[k % 8]
                k += 1
                for j in range(4):
                    c = g * 4 + j
                    P.tr(pt[:, j * 128:(j + 1) * 128], xt[:, c * 128:(c + 1) * 128], self.ident[:], [r_xt, self.r_const], [r_pt])
                P.copy(P.act if g % 2 == 0 else P.dve, st[:, g * 4:(g + 1) * 4, :], pt[:].rearrange("p (j t) -> p j t", j=4), [r_pt], [r_st])
            P.ld("sp" if ti % 2 == 1 else "act", XTv[:, :, ti * 128:(ti + 1) * 128], st[:], reads=[r_st])
        P.pop(mk)
        mk = P.push()
        zt = P.sbuf("zpad", [128, 16, 2], BF16)
        r_z = Res()
        P.memset(P.dve, zt[:], 0.0, [r_z])
        for blk in (0, NR + 1):
            P.ld("sp", self.HGP[blk * D:(blk + 1) * D, :].rearrange("(c p) k -> p c k", p=128), zt[:], reads=[r_z])
        P.pop(mk)

    def final(self):
        P = self.P
        mk = P.push()
        xin = P.ring("fin", 2, [128, 16, 128], F32)
        xst = P.ring("fst", 2, [128, D], F32)
        XTv = self.XT.rearrange("(c p) t -> p c t", p=128)
        k = 0
        for ti in range(SL // 128):
            xt, r_xt = xin.next()
            P.ld("sp" if ti % 2 == 0 else "act", xt[:], XTv[:, :, CTX + ti * 128:CTX + (ti + 1) * 128], writes=[r_xt])
            st, r_st = xst.next()
            for g in range(4):
                pt, r_pt = self.pb[k % 8]
                k += 1
                for j in range(4):
                    c = g * 4 + j
                    P.tr(pt[:, j * 128:(j + 1) * 128], xt[:, c, :], self.ident[:], [r_xt, self.r_const], [r_pt])
                P.copy(P.act if g % 2 == 0 else P.dve, st[:, g * 512:(g + 1) * 512], pt[:], [r_pt], [r_st])
            P.ld("sp" if ti % 2 == 1 else "act", self.out[ti * 128:(ti + 1) * 128, :], st[:], reads=[r_st], writes=[self.r_out])
        P.pop(mk)

    def sp_col(self, l, col, n=1, parts=128):
        base = l * NSP + col
        return self.smallp[0:parts, base:base + n]

    def rms_rstd(self, chunks, W, n_feat, sq_ring, pbi, st_ring, reads):
        P = self.P
        pt, r_pt = self.pb[pbi]
        for i, ap in enumerate(chunks):
            sq, r_sq = sq_ring.next()
            P.actv(sq[:, :W], ap, AF.Square, reads, [r_sq])
            P.mm(pt[:, :W], self.ones_b[:], sq[:, :W], i == 0, i == len(chunks) - 1, [r_sq, self.r_const], [r_pt])
        st, r_st = st_ring.next()
        P.actv(st[:, :W], pt[:, :W], AF.Sqrt, [r_pt], [r_st], bias=self.eps_t[:], scale=1.0 / n_feat)
        P.recip(st[:, :W], st[:, :W], [r_st], [r_st])
        return st, r_st

    def phase_mod(self, l):
        P = self.P
        mk = P.push()
        wr = P.ring("adaw", 2, [128, 16, 512], F32)
        modT = P.sbuf("modT", [128, 96, 2], F32)
        r_modT = Res()
        pm, r_pm = self.pb[0]
        src = self.ada_w[self.lidx[l]].rearrange("(kc p) n -> p kc n", p=128)
        scv = self.scv[:].rearrange("p (kc j) -> p kc j", j=2)
        for g in range(24):
            wt, r_wt = wr.next()
            P.ld("sp" if g % 2 == 0 else "act", wt[:], src[:, :, g * 512:(g + 1) * 512], writes=[r_wt])
            for m in range(4):
                col = (g * 4 + m) * 2
                for kc in range(16):
                    P.mm(pm[:, col:col + 2], wt[:, kc, m * 128:(m + 1) * 128], scv[:, kc, :], kc == 0, kc == 15, [r_wt, self.r_const], [r_pm])
        pmv = pm[:, 0:192].rearrange("p (m j) -> p m j", j=2)
        for j in range(2):
            P.tt(P.dve, modT[:, :, j], pmv[:, :, j], self.sp_col(l, SP_ADAB, 96), ALU.add, [r_pm, self.r_const], [r_modT])
        mods = self.mods[:].rearrange("p (k c j) -> p k c j", k=6, j=2)
        for j in range(2):
            for half, (sh, sc, g, nga, ngb) in enumerate(((0, 1, 2, 0, 1), (3, 4, 5, 2, 3))):
                ng_a = self.sp_col(l, SP_NG + nga * 16, 16)
                ng_b = self.sp_col(l, SP_NG + ngb * 16, 16)
                A, B, G = mods[:, half * 3 + 0, :, j], mods[:, half * 3 + 1, :, j], mods[:, half * 3 + 2, :, j]
                P.stt(A, modT[:, sc * 16:(sc + 1) * 16, j], 1.0, ng_a, ALU.add, ALU.mult, [r_modT, self.r_const], [self.r_mods])
                P.copy(P.dve, B, modT[:, sh * 16:(sh + 1) * 16, j], [r_modT], [self.r_mods])
                P.tt(P.dve, G, modT[:, g * 16:(g + 1) * 16, j], ng_b, ALU.mult, [r_modT, self.r_const], [self.r_mods])
        P.pop(mk)

    def mod(self, k, c, j):
        i = (k * 16 + c) * 2 + j
        return self.mods[:, i:i + 1]

    def tok_tiles(self, need_ctx=True):
        tiles = [(0, CTX, 1)] if need_ctx else []
        tiles += [(CTX + i * 512, 512, 0) for i in range(SEQ // 512)]
        return tiles

    def phase_inproj(self, l):
        P = self.P
        mk = P.push()
        xs_r = P.ring("xs", 2, [128, 16, 512], F32)
        hb_r = P.ring("hb", 2, [128, 16, 512], BF16)
        sq_r = P.ring("sq", 3, [128, 512], BF16)
        tmp_r = P.ring("tmp", 3, [128, 512], F32)
        st_r = P.ring("st", 2, [128, 512], F32)
        w_r = P.ring("wg", 2, [128, 16, 512], BF16)
        ev_r = P.ring("ev", 4, [128, 512], F32)
        XTv = self.XT.rearrange("(c p) t -> p c t", p=128)
        Wv = self.WIN[self.lidx[l]].rearrange("(kc p) n -> p kc n", p=128)
        kb = 0
        for (c0, W, j) in self.loc_tiles():
            xs, r_xs = xs_r.next()
            P.ld("sp", xs[:, :, :W], XTv[:, :, c0:c0 + W], writes=[r_xs])
            st, r_st = self.rms_rstd([xs[:, c, :W] for c in range(16)], W, D, sq_r, 7, st_r, [r_xs])
            hb, r_hb = hb_r.next()
            for c in range(16):
                tmp, r_tmp = tmp_r.next()
                P.tt(P.dve, tmp[:, :W], xs[:, c, :W], st[:, :W], ALU.mult, [r_xs, r_st], [r_tmp])
                P.actv(hb[:, c, :W], tmp[:, :W], AF.Identity, [r_tmp, self.r_mods], [r_hb], bias=self.mod(1, c, j), scale=self.mod(0, c, j))
            for g in range(7):
                ncol = 512 if g < 6 else DINX - 6 * 512
                wt, r_wt = w_r.next()
                P.ld("act", wt[:, :, :ncol], Wv[:, :, g * 512:g * 512 + ncol], writes=[r_wt])
                for m in range(ncol // 128):
                    pt, r_pt = self.pb[kb % 6]
                    kb += 1
                    for kc in range(16):
                        P.mm(pt[:, :W], wt[:, kc, m * 128:(m + 1) * 128], hb[:, kc, :W], kc == 0, kc == 15, [r_wt, r_hb], [r_pt])
                    ev, r_ev = ev_r.next()
                    P.copy(P.act if kb % 2 == 0 else P.dve, ev[:, :W], pt[:, :W], [r_pt], [r_ev])
                    row = g * 512 + m * 128
                    dstU = self.UC[row:row + 128, c0:c0 + W] if j == 1 else self.ULOC[row:row + 128, c0 - CTX:c0 - CTX + W]
                    P.ld("sp", dstU, ev[:, :W], reads=[r_ev])
        P.pop(mk)
        self.allgather([(self.ULOC[k * 256:(k + 1) * 256, :], self.UG[k * NR * 256:(k + 1) * NR * 256, :]) for k in range(UCH)])


    def sin_layer(self, ps_ap, bias_ap, out_ap, W, rings, reads, r_ps, r_out):
        P = self.P
        v, r_v = rings[0].next()
        t1, r_t1 = rings[1].next()
        P.ts(P.dve, v[:, :W], ps_ap, bias_ap, None, ALU.add, None, [r_ps] + reads, [r_v])
        P.ts(P.dve, t1[:, :W], v[:, :W], 1.0 / (2 * math.pi), MAGIC, ALU.mult, ALU.add, [r_v], [r_t1])
        P.ts(P.dve, t1[:, :W], t1[:, :W], -MAGIC, None, ALU.add, None, [r_t1], [r_t1])
        P.stt(v[:, :W], t1[:, :W], -2 * math.pi, v[:, :W], ALU.mult, ALU.add, [r_t1, r_v], [r_v])
        P.ts(P.dve, v[:, :W], v[:, :W], -3.1415925, 3.1415925, ALU.max, ALU.min, [r_v], [r_v])
        P.actv(out_ap, v[:, :W], AF.Sin, [r_v], [r_out])

    def hy_filter(self, l, T):
        P = self.P
        li = self.lidx[l]
        L, nfc, ntc, TW, ntt = T["L"], T["nfc"], T["ntc"], T["TW"], T["ntt"]
        mk = P.push()
        w1 = P.sbuf("hw1", [33, 64], F32)
        w2 = P.sbuf("hw2", [64, 64], F32)
        w3 = P.sbuf("hw3", [64, 2048], F32)
        r_w = Res()
        P.ld("sp", w1[:], self.hy_w1[li], writes=[r_w])
        P.ld("sp", w2[:], self.hy_w2[li], writes=[r_w])
        P.ld("sp", w3[:], self.hy_w3[li], writes=[r_w])
        hid2 = P.sbuf("hid2", [64, L], F32)
        r_hid2 = Res()
        f_r = P.ring("ft", 2, [33, 512], F32)
        h1_r = P.ring("h1", 2, [64, 512], F32)
        v_r = P.ring("sv", 2, [64, 512], F32)
        t_r = P.ring("st1", 2, [64, 512], F32)
        b1 = self.sp_col(l, SP_B1, 1, 64)
        b2 = self.sp_col(l, SP_B2, 1, 64)
        for tt in range(ntt):
            ft, r_ft = f_r.next()
            P.ld("sp", ft[:, :TW], T["feats"][:, tt * TW:(tt + 1) * TW], writes=[r_ft])
            p0, r_p0 = self.pb[4]
            P.mm(p0[0:64, :TW], w1[:, :], ft[:, :TW], True, True, [r_w, r_ft], [r_p0])
            h1, r_h1 = h1_r.next()
            self.sin_layer(p0[0:64, :TW], b1, h1[:, :TW], TW, (v_r, t_r), [self.r_const], r_p0, r_h1)
            p1, r_p1 = self.pb[5]
            P.mm(p1[0:64, :TW], w2[:, :], h1[:, :TW], True, True, [r_w, r_h1], [r_p1])
            self.sin_layer(p1[0:64, :TW], b2, hid2[:, tt * TW:(tt + 1) * TW], TW, (v_r, t_r), [self.r_const], r_p1, r_hid2)
        Sv = T["S"].rearrange("(fc p) (o rs c) -> fc p o rs c", p=128, o=2, rs=2)
        AW = ntc * 128
        for o_ in range(2):
            mk2 = P.push()
            ksum = P.sbuf("ksum", [128, ntc, 512], BF16)
            kdiff = P.sbuf("kdiff", [128, ntc, 512], BF16)
            r_k = [Res() for _ in range(ntc)]
            dec_r = P.ring("dec", 2, [128, 512], F32)
            hf_r = P.ring("hf", 2, [128, 2, 512], F32)
            ab_r = P.ring("ab", 2, [128, 2, 512], F32)
            pn, r_pn = self.pb[6]
            for lc in range(ntc):
                dec, r_dec = dec_r.next()
                P.ld("sp", dec[:], T["decay"][lc * 128:(lc + 1) * 128, :], writes=[r_dec])
                hf, r_hf = hf_r.next()
                for d in range(2):
                    g = o_ * 2 + d
                    ph, r_ph = self.pb[d + 2 * (lc % 2)]
                    P.mm(ph[:, :], hid2[:, lc * 128:(lc + 1) * 128], w3[:, g * 512:(g + 1) * 512], True, True, [r_hid2, r_w], [r_ph])
                    P.tt(P.dve, hf[:, d, :], ph[:, :], dec[:], ALU.mult, [r_ph, r_dec], [r_hf])
                if lc == 0:
                    P.memset(P.dve, hf[0:1, 1, :], 0.0, [r_hf])
                ab, r_ab = ab_r.next()
                P.actv(ab[:], hf[:], AF.Abs, [r_hf], [r_ab])
                P.mm(pn[:, :], self.ones_f[:], ab[:, 0, :], lc == 0, False, [r_ab, self.r_const], [r_pn])
                P.mm(pn[:, :], self.ones_f[:], ab[:, 1, :], False, lc == ntc - 1, [r_ab, self.r_const], [r_pn])
                P.tt(P.pool, ksum[:, lc, :], hf[:, 0, :], hf[:, 1, :], ALU.add, [r_hf], [r_k[lc]])
                P.tt(P.pool, kdiff[:, lc, :], hf[:, 0, :], hf[:, 1, :], ALU.subtract, [r_hf], [r_k[lc]])
            rn = P.sbuf("rn", [128, 512], F32)
            r_rn = Res()
            P.recip(rn[:], pn[:, :], [r_pn], [r_rn])
            tbs = [(P.sbuf(f"ftb{i}", [128, 2 * AW], BF16), Res(), Res()) for i in range(2)]
            sst_r = P.ring("sst", 2, [128, 2, 512], F32)
            for fc in range(nfc):
                tb, r_tc, r_ts = tbs[fc % 2]
                P.ld("sp", tb[:, 0:AW], T["Fc"][fc].rearrange("p tc f -> p (tc f)"), writes=[r_tc])
                P.ld("act", tb[:, AW:2 * AW], T["Fs"][fc].rearrange("p tc f -> p (tc f)"), writes=[r_ts])
                Fc_t = tb[:, 0:AW].rearrange("p (tc f) -> p tc f", f=128)
                Fs_t = tb[:, AW:2 * AW].rearrange("p (tc f) -> p tc f", f=128)
                pr, r_pr = self.pb[0 + 2 * (fc % 2)]
                ps_, r_ps = self.pb[1 + 2 * (fc % 2)]
                for tc in range(ntc):
                    P.mm(pr[:, :], Fc_t[:, tc, :], ksum[:, tc, :], tc == 0, tc == ntc - 1, [r_tc, r_k[tc]], [r_pr])
                for tc in range(ntc):
                    P.mm(ps_[:, :], Fs_t[:, tc, :], kdiff[:, tc, :], tc == 0, tc == ntc - 1, [r_ts, r_k[tc]], [r_ps])
                sst, r_sst = sst_r.next()
                P.tt(P.dve, sst[:, 0, :], pr[:, :], rn[:], ALU.mult, [r_pr, r_rn], [r_sst])
                P.tt(P.dve, sst[:, 1, :], ps_[:, :], rn[:], ALU.mult, [r_ps, r_rn], [r_sst])
                P.ld("sp", Sv[fc, :, o_, :, :], sst[:], reads=[r_sst])
            P.pop(mk2)
        P.pop(mk)

    def head_norm_store(self, y_ap, W, head, l, rows0, c0, sq_r, st_r, ob_r, pbi, r_y, dyn_lat=None):
        P = self.P
        st, r_st = self.rms_rstd([y_ap], W, 128, sq_r, pbi, st_r, [r_y])
        ob, r_ob = ob_r.next()
        P.stt(ob[:, :W], y_ap, self.sp_col(l, SP_HG + head), st[:, :W], ALU.mult, ALU.mult, [r_y, r_st, self.r_const], [r_ob])
        if dyn_lat is None:
            P.ld("sp", self.YC[rows0:rows0 + 128, c0:c0 + W], ob[:, :W], reads=[r_ob])
        else:
            P.dma("sp", lambda e, ob=ob: e.dma_start(out=self.YC[rows0:rows0 + 128, bass.ds(self.rank(e) * SL + (CTX + dyn_lat), W)], in_=ob[:, :W]), reads=[r_ob])

    def phase_hyena(self, l, tag):
        P = self.P
        T = self.tab[tag]
        L, nfc, ntc, TW, ntt = T["L"], T["nfc"], T["ntc"], T["TW"], T["ntt"]
        off = 0 if tag == "C" else CTX
        self.hy_filter(l, T)
        mk = P.push()
        u_r = P.ring("hu", 2, [128, L + 2], F32)
        o_r = P.ring("ho", 2, [128, L], F32)
        for cc in range(12):
            u, r_u = u_r.next()
            P.memset(P.pool, u[:, 0:1], 0.0, [r_u])
            P.memset(P.pool, u[:, L + 1:L + 2], 0.0, [r_u])
            if tag == "C":
                P.ld("sp", u[:, 1:L + 1], self.UC[cc * 128:(cc + 1) * 128, 0:L], writes=[r_u])
            else:
                P.ld("sp", u[:, 1:L + 1].rearrange("p (s t) -> p s t", s=NR), self.Ulat(cc * 128), writes=[r_u])
            o, r_o = o_r.next()
            P.ts(P.dve, o[:, :], u[:, 0:L], self.sp_col(l, SP_HYC + 0 * 12 + cc), None, ALU.mult, None, [r_u, self.r_const], [r_o])
            P.stt(o[:, :], u[:, 1:L + 1], self.sp_col(l, SP_HYC + 1 * 12 + cc), o[:, :], ALU.mult, ALU.add, [r_u, r_o, self.r_const], [r_o])
            P.stt(o[:, :], u[:, 2:L + 2], self.sp_col(l, SP_HYC + 2 * 12 + cc), o[:, :], ALU.mult, ALU.add, [r_u, r_o, self.r_const], [r_o])
            dst = self.ZT[cc * 128:(cc + 1) * 128, off:off + L] if cc < 4 else self.PT[(cc - 4) * 128:(cc - 3) * 128, off:off + L]
            P.ld("act", dst, o[:, :], reads=[r_o])
        P.pop(mk)
        FG = 11 if nfc % 11 == 0 else nfc
        nfg = nfc // FG
        AW = max(ntc * 128, FG * TW)
        ZTv = self.ZT.rearrange("(c p) t -> p c t", p=128)
        Sv = T["S"].rearrange("(fc p) (o rs c) -> fc p o rs c", p=128, o=2, rs=2)
        for o_ in range(2):
            mk = P.push()
            zT = P.sbuf("zT", [128, ntc, 512], BF16)
            r_zT = [Res() for _ in range(ntc)]
            Yr = P.sbuf("Yr", [128, nfc, 512], BF16)
            Ys = P.sbuf("Ys", [128, nfc, 512], BF16)
            r_Y = [Res() for _ in range(nfc)]
            tbs = [(P.sbuf(f"tb{i}", [128, 2 * AW], BF16), Res(), Res()) for i in range(2)]
            zl_r = P.ring("zl", 1, [128, 4, TW], F32)
            for tt in range(ntt):
                zl, r_zl = zl_r.next()
                P.ld("sp", zl[:, :, :], ZTv[:, :, off + tt * TW:off + (tt + 1) * TW], writes=[r_zl])
                for s in range(TW // 128):
                    tc = tt * (TW // 128) + s
                    pt, r_pt = self.pb[tc % 4]
                    for cc in range(4):
                        P.tr(pt[:, cc * 128:(cc + 1) * 128], zl[:, cc, s * 128:(s + 1) * 128], self.ident[:], [r_zl, self.r_const], [r_pt])
                    P.copy(P.act if tc % 2 == 0 else P.dve, zT[:, tc, :], pt[:, :], [r_pt], [r_zT[tc]])
            S_r = P.ring("Sld", 2, [128, 2, 512], F32)
            t_rs = [P.ring(f"ty{i}", 1, [128, 512], F32) for i in range(4)]
            FW = ntc * 128
            for fc in range(nfc):
                tb, r_tc, r_ts = tbs[fc % 2]
                P.ld("sp", tb[:, 0:FW], T["Fc"][fc].rearrange("p tc f -> p (tc f)"), writes=[r_tc])
                P.ld("act", tb[:, AW:AW + FW], T["Fs"][fc].rearrange("p tc f -> p (tc f)"), writes=[r_ts])
                Fc_t = tb[:, 0:FW].rearrange("p (tc f) -> p tc f", f=128)
                Fs_t = tb[:, AW:AW + FW].rearrange("p (tc f) -> p tc f", f=128)
                St, r_S = S_r.next()
                P.ld("sp", St[:], Sv[fc, :, o_, :, :], writes=[r_S])
                pr, r_pr = self.pb[4 + 2 * (fc % 2)]
                ps_, r_ps = self.pb[5 + 2 * (fc % 2)]
                for tc in range(ntc):
                    P.mm(pr[:, :], Fc_t[:, tc, :], zT[:, tc, :], tc == 0, tc == ntc - 1, [r_tc, r_zT[tc]], [r_pr])
                for tc in range(ntc):
                    P.mm(ps_[:, :], Fs_t[:, tc, :], zT[:, tc, :], tc == 0, tc == ntc - 1, [r_ts, r_zT[tc]], [r_ps])
                (t1, r1), (t2, r2), (t3, r3), (t4, r4) = [r.next() for r in t_rs]
                P.tt(P.dve, t1[:], pr[:, :], St[:, 0, :], ALU.mult, [r_pr, r_S], [r1])
                P.tt(P.dve, t2[:], ps_[:, :], St[:, 1, :], ALU.mult, [r_ps, r_S], [r2])
                P.tt(P.dve, t3[:], pr[:, :], St[:, 1, :], ALU.mult, [r_pr, r_S], [r3])
                P.tt(P.dve, t4[:], ps_[:, :], St[:, 0, :], ALU.mult, [r_ps, r_S], [r4])
                P.tt(P.pool, Yr[:, fc, :], t1[:], t2[:], ALU.subtract, [r1, r2], [r_Y[fc]])
                P.tt(P.pool, Ys[:, fc, :], t3[:], t4[:], ALU.add, [r3, r4], [r_Y[fc]])
            zt_r = P.ring("zt", 2, [128, TW], F32)
            pp_r = P.ring("pp", 2, [128, TW], F32)
            tm_r = P.ring("tm", 2, [128, TW], F32)
            sq_r = P.ring("hsq", 2, [128, 512], BF16)
            st_r = P.ring("hst", 2, [128, 512], F32)
            ob_r = P.ring("hob", 2, [128, 512], BF16)
            k = 0
            for tt in range(ntt):
                c0 = off + tt * TW
                accs = [self.pb[cc] for cc in range(4)]
                for fg in range(nfg):
                    tb, r_tc, r_ts = tbs[k % 2]
                    k += 1
                    P.ld("sp", tb[:, 0:FG * TW], T["Ic"][tt][:, fg * FG:(fg + 1) * FG, :].rearrange("p f t -> p (f t)"), writes=[r_tc])
                    P.ld("act", tb[:, AW:AW + FG * TW], T["Is"][tt][:, fg * FG:(fg + 1) * FG, :].rearrange("p f t -> p (f t)"), writes=[r_ts])
                    Ic_t = tb[:, 0:FG * TW].rearrange("p (f t) -> p f t", t=TW)
                    Is_t = tb[:, AW:AW + FG * TW].rearrange("p (f t) -> p f t", t=TW)
                    for f_ in range(FG):
                        fc = fg * FG + f_
                        for cc in range(4):
                            pa, r_pa = accs[cc]
                            P.mm(pa[:, :TW], Yr[:, fc, cc * 128:(cc + 1) * 128], Ic_t[:, f_, :], fc == 0, False, [r_Y[fc], r_tc], [r_pa])
                            P.mm(pa[:, :TW], Ys[:, fc, cc * 128:(cc + 1) * 128], Is_t[:, f_, :], False, fc == nfc - 1, [r_Y[fc], r_ts], [r_pa])
                for cc in range(4):
                    pa, r_pa = accs[cc]
                    zt, r_zt = zt_r.next()
                    pp, r_pp = pp_r.next()
                    tm, r_tm = tm_r.next()
                    P.ld("sp", zt[:], self.ZT[cc * 128:(cc + 1) * 128, c0:c0 + TW], writes=[r_zt])
                    P.ld("act", pp[:], self.PT[o_ * 512 + cc * 128:o_ * 512 + (cc + 1) * 128, c0:c0 + TW], writes=[r_pp])
                    P.stt(tm[:], zt[:], self.sp_col(l, SP_HYB + o_ * 4 + cc), pa[:, :TW], ALU.mult, ALU.add, [r_zt, r_pa, self.r_const], [r_tm])
                    P.tt(P.pool, zt[:], tm[:], pp[:], ALU.mult, [r_tm, r_pp], [r_zt])
                    if o_ == 0:
                        P.ld("sp", self.ZT[cc * 128:(cc + 1) * 128, c0:c0 + TW], zt[:], reads=[r_zt])
                    else:
                        self.head_norm_store(zt[:], TW, cc, l, cc * 128, c0, sq_r, st_r, ob_r, 4 + cc, r_zt)
            P.pop(mk)


    def phase_lru(self, l, need_ctx):
        P = self.P
        li = self.lidx[l]
        mk = P.push()
        c8 = P.sbuf("c8", [128, 8], F32)
        r_c8 = Res()
        P.actv(c8[:], self.sp_col(l, SP_LAM, 8), AF.Exp, [self.r_const], [r_c8], scale=-1.0)
        P.actv(c8[:], c8[:], AF.Ln, [r_c8, self.r_const], [r_c8], bias=self.ones_f[:, 0:1], scale=1.0)
        P.ts(P.dve, c8[:], c8[:], -8.0, None, ALU.mult, None, [r_c8], [r_c8])
        wab = P.ring("wab", 2, [128, 4, 128], BF16)
        seqs = [(0, CTX, True), (CTX, SEQ, False)]
        bufs = {}
        for (off, Ls, isctx) in seqs:
            t = "c" if isctx else "l"
            bufs[isctx] = dict(
                xr=P.ring("xr" + t, 2, [128, Ls + 6], F32),
                xc=(P.sbuf("xc" + t, [128, Ls], F32), Res()),
                xcb=(P.sbuf("xcb" + t, [128, Ls], BF16), Res()),
                ra=(P.sbuf("ra" + t, [128, Ls], F32), Res()),
                ib=(P.sbuf("ib" + t, [128, Ls], F32), Res()),
                ta=(P.sbuf("ta" + t, [128, Ls], F32), Res()),
                h=[(P.sbuf(f"h{d}" + t, [128, Ls], F32), Res()) for d in range(2)],
            )
        sq_r = P.ring("lsq", 2, [128, 512], BF16)
        st_r = P.ring("lst", 2, [128, 512], F32)
        ob_r = P.ring("lob", 2, [128, 512], BF16)
        for n in range(4):
            wt, r_wt = wab.next()
            for d in range(2):
                P.ld("sp", wt[:, d * 2 + 0, :], self.LWA[li, d, n], writes=[r_wt])
                P.ld("sp", wt[:, d * 2 + 1, :], self.LWX[li, d, n], writes=[r_wt])
            for (off, Ls, isctx) in seqs:
                B = bufs[isctx]
                W = min(512, Ls)
                for d in range(2):
                    xr, r_xr = B["xr"].next()
                    P.memset(P.pool, xr[:, 0:3], 0.0, [r_xr])
                    P.memset(P.pool, xr[:, Ls + 3:Ls + 6], 0.0, [r_xr])
                    if isctx:
                        P.ld("sp", xr[:, 3:Ls + 3], self.UC[OFF_X + n * 128:OFF_X + (n + 1) * 128, 0:Ls], writes=[r_xr])
                    else:
                        P.ld("sp", xr[:, 3:Ls + 3].rearrange("p (s t) -> p s t", s=NR), self.Ulat(OFF_X + n * 128), writes=[r_xr])
                    xc, r_xc = B["xc"]
                    left = 3 if d == 0 else 0
                    for jj in range(4):
                        s0 = 3 + jj - left
                        wj = self.sp_col(l, SP_LC + d * 16 + jj * 4 + n)
                        if jj == 0:
                            P.ts(P.dve, xc[:, :], xr[:, s0:s0 + Ls], wj, None, ALU.mult, None, [r_xr, self.r_const], [r_xc])
                        else:
                            P.stt(xc[:, :], xr[:, s0:s0 + Ls], wj, xc[:, :], ALU.mult, ALU.add, [r_xr, r_xc, self.r_const], [r_xc])
                    xcb, r_xcb = B["xcb"]
                    P.copy(P.pool, xcb[:, :], xc[:, :], [r_xc], [r_xcb])
                    ra, r_ra = B["ra"]
                    ib, r_ib = B["ib"]
                    ta, r_ta = B["ta"]
                    for ti in range(Ls // W):
                        cs = slice(ti * W, (ti + 1) * W)
                        pa, r_pa = self.pb[(2 * ti) % 4]
                        px, r_px = self.pb[(2 * ti + 1) % 4]
                        P.mm(pa[:, :W], wt[:, d * 2 + 0, :], xcb[:, cs], True, True, [r_wt, r_xcb], [r_pa])
                        P.mm(px[:, :W], wt[:, d * 2 + 1, :], xcb[:, cs], True, True, [r_wt, r_xcb], [r_px])
                        P.actv(ra[:, cs], pa[:, :W], AF.Sigmoid, [r_pa, self.r_const], [r_ra], bias=self.sp_col(l, SP_BA + d * 4 + n))
                        P.actv(ib[:, cs], px[:, :W], AF.Sigmoid, [r_px, self.r_const], [r_ib], bias=self.sp_col(l, SP_BX + d * 4 + n))
                    P.actv(ra[:, :], ra[:, :], AF.Exp, [r_ra, r_c8], [r_ra], scale=c8[:, d * 4 + n:d * 4 + n + 1])
                    P.tt(P.pool, ta[:, :], ra[:, :], ra[:, :], ALU.mult, [r_ra], [r_ta])
                    P.ts(P.pool, ta[:, :], ta[:, :], -1.0, 1.0, ALU.mult, ALU.add, [r_ta], [r_ta])
                    P.actv(ta[:, :], ta[:, :], AF.Sqrt, [r_ta], [r_ta])
                    P.tt(P.dve, ib[:, :], ib[:, :], xc[:, :], ALU.mult, [r_ib, r_xc], [r_ib])
                    P.tt(P.dve, ib[:, :], ib[:, :], ta[:, :], ALU.mult, [r_ib, r_ta], [r_ib])
                    if not isctx:
                        f0 = Ls - 1 if d == 1 else 0
                        P.stt(ib[:, f0:f0 + 1], ra[:, f0:f0 + 1], self.lru_h0[:, d * 4 + n:d * 4 + n + 1], ib[:, f0:f0 + 1],
                              ALU.mult, ALU.add, [r_ra, r_ib, self.r_h0], [r_ib])
                    h, r_h = B["h"][d]
                    if d == 0:
                        P.op(P.dve, lambda e, h=h, ra=ra, ib=ib: e.tensor_tensor_scan(out=h[:, :], data0=ra[:, :], data1=ib[:, :], initial=0.0, op0=ALU.mult, op1=ALU.add), [r_ra, r_ib], [r_h])
                    else:
                        P.op(P.dve, lambda e, h=h, ra=ra, ib=ib: e.tensor_tensor_scan(out=h[:, ::-1], data0=ra[:, ::-1], data1=ib[:, ::-1], initial=0.0, op0=ALU.mult, op1=ALU.add), [r_ra, r_ib], [r_h])
                    if isctx:
                        f1 = Ls - 1 if d == 0 else 0
                        P.copy(P.dve, self.lru_h0[:, d * 4 + n:d * 4 + n + 1], h[:, f1:f1 + 1], [r_h], [self.r_h0])
                if isctx and not need_ctx:
                    continue
                xr, r_xr = B["xr"].next()
                if isctx:
                    P.ld("sp", xr[:, 0:Ls], self.UC[OFF_G + n * 128:OFF_G + (n + 1) * 128, 0:Ls], writes=[r_xr])
                else:
                    P.ld("sp", xr[:, 0:Ls].rearrange("p (s t) -> p s t", s=NR), self.Ulat(OFF_G + n * 128), writes=[r_xr])
                P.actv(xr[:, 0:Ls], xr[:, 0:Ls], AF.Gelu_apprx_tanh, [r_xr], [r_xr])
                (h0_, r_h0_), (h1_, r_h1_) = B["h"]
                P.tt(P.pool, h0_[:, :], h0_[:, :], h1_[:, :], ALU.add, [r_h0_, r_h1_], [r_h0_])
                P.tt(P.dve, h0_[:, :], h0_[:, :], xr[:, 0:Ls], ALU.mult, [r_h0_, r_xr], [r_h0_])
                for ti in range(Ls // W):
                    self.head_norm_store(h0_[:, ti * W:(ti + 1) * W], W, 4 + n, l, 512 + n * 128, off + ti * W, sq_r, st_r, ob_r, 4 + ti % 4, r_h0_)
        P.pop(mk)

    def phase_attn(self, l, need_ctx):
        P = self.P
        li = self.lidx[l]
        mk = P.push()
        ckv = P.sbuf("ckv", [128, 2, NT], BF16)
        r_ckv = Res()
        krope = P.sbuf("krope", [64, NT], BF16)
        r_kr = Res()
        cq = P.sbuf("cq", [128, 4, NTL], BF16)
        r_cq = Res()
        sq_r = P.ring("asq", 3, [128, 512], BF16)
        st_r = P.ring("ast", 2, [128, 512], F32)
        u2_r = P.ring("au2", 2, [128, 2, 512], F32)
        u4_r = P.ring("au4", 2, [128, 4, 512], F32)
        kr_r = P.ring("akr", 2, [64, 2, 512], F32)
        cs_r = P.ring("acs", 2, [64, 2, 512], F32)
        tk_r = P.ring("atk", 2, [64, 2, 512], F32)
        for (c0, W, j) in self.tok_tiles(True):
            u2, r_u2 = u2_r.next()
            P.ld("sp", u2[:, :, :W], self.Uap(OFF_KV, 256, c0, W).rearrange("(c p) t -> p c t", p=128), writes=[r_u2])
            st, r_st = self.rms_rstd([u2[:, c, :W] for c in range(2)], W, 256, sq_r, 7, st_r, [r_u2])
            for c in range(2):
                P.stt(ckv[:, c, c0:c0 + W], u2[:, c, :W], self.sp_col(l, SP_GKV + c), st[:, :W], ALU.mult, ALU.mult, [r_u2, r_st, self.r_const], [r_ckv])
            kr, r_krt = kr_r.next()
            P.ld("act", kr[:, 0, :W], self.Uap(OFF_KR, 64, c0, W), writes=[r_krt])
            if j == 0:
                P.ld("act", kr[:, 1, :W], self.Uap(OFF_KRR, 64, c0, W), writes=[r_krt])
                cs, r_cs = cs_r.next()
                P.ld("act", cs[:, :, :W], self.rope_cs[:, :, c0 - CTX:c0 - CTX + W], writes=[r_cs])
                tk, r_tk = tk_r.next()
                P.tt(P.pool, tk[:, :, :W], kr[:, :, :W], cs[:, :, :W], ALU.mult, [r_krt, r_cs], [r_tk])
                P.tt(P.pool, krope[:, c0:c0 + W], tk[:, 0, :W], tk[:, 1, :W], ALU.add, [r_tk], [r_kr])
            else:
                P.copy(P.pool, krope[:, c0:c0 + W], kr[:, 0, :W], [r_krt], [r_kr])
        qtiles = self.loc_tiles(need_ctx)
        for (c0, W, j) in qtiles:
            u4, r_u4 = u4_r.next()
            srcq = self.UC[OFF_Q:OFF_Q + 512, c0:c0 + W] if j == 1 else self.ULOC[OFF_Q:OFF_Q + 512, c0 - CTX:c0 - CTX + W]
            P.ld("sp", u4[:, :, :W], srcq.rearrange("(c p) t -> p c t", p=128), writes=[r_u4])
            st, r_st = self.rms_rstd([u4[:, c, :W] for c in range(4)], W, 512, sq_r, 6, st_r, [r_u4])
            for c in range(4):
                P.stt(cq[:, c, c0:c0 + W], u4[:, c, :W], self.sp_col(l, SP_GQ + c), st[:, :W], ALU.mult, ALU.mult, [r_u4, r_st, self.r_const], [r_cq])
        wkv_r = P.ring("wkv", 2, [128, 2, 256], BF16)
        wq_r = P.ring("wq", 2, [128, 4, 256], BF16)
        kn_r = P.ring("kn", 2, [128, NT], BF16)
        v_r = P.ring("vv", 2, [128, NT // 128, 128], BF16)
        qn_r = P.ring("qn", 2, [128, NTL], BF16)
        qr_r = P.ring("qr", 2, [64, NTL], BF16)
        pT_r = P.ring("pT", 4, [128, 512], BF16)
        ri_r = P.ring("ri", 2, [128, 512], F32)
        oo_r = P.ring("oo", 2, [128, 512], F32)
        ob_r = P.ring("aob", 2, [128, 512], BF16)
        WKV = self.WUKV[li].rearrange("(kc p) n -> p kc n", p=128)
        WQ = self.WUQ[li].rearrange("(kc p) n -> p kc n", p=128)
        kb = 0
        for h in range(8):
            wkv, r_wkv = wkv_r.next()
            wq, r_wq = wq_r.next()
            P.ld("sp", wkv[:], WKV[:, :, h * 256:(h + 1) * 256], writes=[r_wkv])
            P.ld("sp", wq[:], WQ[:, :, h * 256:(h + 1) * 256], writes=[r_wq])
            kn, r_kn = kn_r.next()
            vv, r_vv = v_r.next()
            qn, r_qn = qn_r.next()
            qr, r_qr = qr_r.next()
            for (c0, W, j) in self.tok_tiles(True):
                pt, r_pt = self.pb[kb % 4]
                kb += 1
                for kc in range(2):
                    P.mm(pt[:, :W], wkv[:, kc, 0:128], ckv[:, kc, c0:c0 + W], kc == 0, kc == 1, [r_wkv, r_ckv], [r_pt])
                P.copy(P.act, kn[:, c0:c0 + W], pt[:, :W], [r_pt], [r_kn])
            for (c0, W, j) in qtiles:
                pt, r_pt = self.pb[kb % 4]
                kb += 1
                for kc in range(4):
                    P.mm(pt[:, :W], wq[:, kc, 0:128], cq[:, kc, c0:c0 + W], kc == 0, kc == 3, [r_wq, r_cq], [r_pt])
                P.copy(P.dve, qn[:, c0:c0 + W], pt[:, :W], [r_pt], [r_qn])
                pt, r_pt = self.pb[kb % 4]
                kb += 1
                for kc in range(4):
                    P.mm(pt[0:64, :W], wq[:, kc, 128:192], cq[:, kc, c0:c0 + W], kc == 0, kc == 3, [r_wq, r_cq], [r_pt])
                if j == 0:
                    pt2, r_pt2 = self.pb[kb % 4]
                    kb += 1
                    for kc in range(4):
                        P.mm(pt2[0:64, :W], wq[:, kc, 192:256], cq[:, kc, c0:c0 + W], kc == 0, kc == 3, [r_wq, r_cq], [r_pt2])
                    cs, r_cs = cs_r.next()
                    P.dma("act", lambda e, cs=cs, c0=c0, W=W: e.dma_start(out=cs[:, :, :W], in_=self.rope_cs[:, :, bass.ds(self.rank(e) * SL + (c0 - CTX), W)]), writes=[r_cs])
                    tk, r_tk = tk_r.next()
                    P.tt(P.dve, tk[:, 0, :W], pt[0:64, :W], cs[:, 0, :W], ALU.mult, [r_pt, r_cs], [r_tk])
                    P.tt(P.dve, tk[:, 1, :W], pt2[0:64, :W], cs[:, 1, :W], ALU.mult, [r_pt2, r_cs], [r_tk])
                    P.tt(P.pool, qr[:, c0:c0 + W], tk[:, 0, :W], tk[:, 1, :W], ALU.add, [r_tk], [r_qr])
                else:
                    P.copy(P.dve, qr[:, c0:c0 + W], pt[0:64, :W], [r_pt], [r_qr])
            for kg in range(0, NT // 128, 4):
                nk4 = min(4, NT // 128 - kg)
                pt, r_pt = self.pb[kb % 4]
                kb += 1
                for q4 in range(nk4):
                    kc = kg + q4
                    for k2 in range(2):
                        P.mm(pt[:, q4 * 128:(q4 + 1) * 128], ckv[:, k2, kc * 128:(kc + 1) * 128], wkv[:, k2, 128:256], k2 == 0, k2 == 1, [r_ckv, r_wkv], [r_pt])
                P.copy(P.act, vv[:, kg:kg + nk4, :], pt[:, 0:nk4 * 128].rearrange("p (a b) -> p a b", b=128), [r_pt], [r_vv])
            for qi, (c0, W, j) in enumerate(qtiles):
                nk = 2 if j == 1 else NT // 128
                po, r_po = self.pb[4 + 2 * (qi % 2)]
                pl, r_pl = self.pb[5 + 2 * (qi % 2)]
                pend = None
                for kc in range(nk + 1):
                    if kc < nk:
                        pS, r_pS = self.pb[kc % 4]
                        P.mm(pS[:, :W], kn[:, kc * 128:(kc + 1) * 128], qn[:, c0:c0 + W], True, False, [r_kn, r_qn], [r_pS])
                        P.mm(pS[:, :W], krope[:, kc * 128:(kc + 1) * 128], qr[:, c0:c0 + W], False, True, [r_kr, r_qr], [r_pS])
                        pT, r_pT = pT_r.next()
                        P.actv(pT[:, :W], pS[:, :W], AF.Exp, [r_pS], [r_pT], scale=MLA_SCALE)
                    if pend is not None:
                        pkc, ppT, pr_pT = pend
                        P.mm(po[:, :W], vv[:, pkc, :], ppT[:, :W], pkc == 0, pkc == nk - 1, [r_vv, pr_pT], [r_po])
                        P.mm(pl[:, :W], self.ones_b[:], ppT[:, :W], pkc == 0, pkc == nk - 1, [self.r_const, pr_pT], [r_pl])
                    pend = (kc, pT, r_pT) if kc < nk else None
                ri, r_ri = ri_r.next()
                P.recip(ri[:, :W], pl[:, :W], [r_pl], [r_ri])
                oo, r_oo = oo_r.next()
                P.tt(P.dve, oo[:, :W], po[:, :W], ri[:, :W], ALU.mult, [r_po, r_ri], [r_oo])
                self.head_norm_store(oo[:, :W], W, 8 + h, l, 1024 + h * 128, c0, sq_r, st_r, ob_r, 4 + 2 * (qi % 2), r_oo,
                                     dyn_lat=(None if j == 1 else c0 - CTX))
        P.pop(mk)

    def phase_outproj(self, l, need_ctx):
        P = self.P
        li = self.lidx[l]
        mk = P.push()
        yc_r = P.ring("yc", 2, [128, 16, 512], BF16)
        w_r = P.ring("wo", 2, [128, 16, 512], BF16)
        mix_r = P.ring("mix", 1, [128, 16, 512], F32)
        xs_r = P.ring("oxs", 1, [128, 16, 512], F32)
        hb_r = P.ring("ohb", 1, [128, 16, 512], BF16)
        sq_r = P.ring("osq", 3, [128, 512], BF16)
        st_r = P.ring("ost", 2, [128, 512], F32)
        tmp_r = P.ring("otmp", 3, [128, 512], F32)
        YCv = self.YC.rearrange("(c p) t -> p c t", p=128)
        XTv = self.XT.rearrange("(c p) t -> p c t", p=128)
        H2v = self.H2.rearrange("(c p) t -> p c t", p=128)
        Wv = self.WOUT[li].rearrange("(kc p) n -> p kc n", p=128)
        kb = 0
        ltiles = self.loc_tiles(need_ctx)
        for (c0, W, j) in ltiles:
            yc, r_yc = yc_r.next()
            if j == 1:
                P.ld("sp", yc[:, :, :W], YCv[:, :, c0:c0 + W], writes=[r_yc])
            else:
                P.dma("sp", lambda e, yc=yc, c0=c0, W=W: e.dma_start(out=yc[:, :, :W], in_=YCv[:, :, bass.ds(self.rank(e) * SL + c0, W)]), writes=[r_yc])
            xs, r_xs = xs_r.next()
            P.ld("sp", xs[:, :, :W], XTv[:, :, c0:c0 + W], writes=[r_xs])
            mix, r_mix = mix_r.next()
            for g in range(4):
                wt, r_wt = w_r.next()
                P.ld("act", wt[:], Wv[:, :, g * 512:(g + 1) * 512], writes=[r_wt])
                for m in range(4):
                    pt, r_pt = self.pb[kb % 6]
                    kb += 1
                    for kc in range(16):
                        P.mm(pt[:, :W], wt[:, kc, m * 128:(m + 1) * 128], yc[:, kc, :W], kc == 0, kc == 15, [r_wt, r_yc], [r_pt])
                    P.copy(P.act if kb % 2 == 0 else P.dve, mix[:, g * 4 + m, :W], pt[:, :W], [r_pt], [r_mix])
            st, r_st = self.rms_rstd([mix[:, c, :W] for c in range(16)], W, D, sq_r, 7, st_r, [r_mix])
            for c in range(16):
                tmp, r_tmp = tmp_r.next()
                P.tt(P.pool, tmp[:, :W], mix[:, c, :W], st[:, :W], ALU.mult, [r_mix, r_st], [r_tmp])
                P.stt(xs[:, c, :W], tmp[:, :W], self.mod(2, c, j), xs[:, c, :W], ALU.mult, ALU.add, [r_tmp, r_xs, self.r_mods], [r_xs])
            P.ld("sp", XTv[:, :, c0:c0 + W], xs[:, :, :W], reads=[r_xs])
            st, r_st = self.rms_rstd([xs[:, c, :W] for c in range(16)], W, D, sq_r, 6, st_r, [r_xs])
            hb, r_hb = hb_r.next()
            for c in range(16):
                tmp, r_tmp = tmp_r.next()
                P.tt(P.dve, tmp[:, :W], xs[:, c, :W], st[:, :W], ALU.mult, [r_xs, r_st], [r_tmp])
                P.actv(hb[:, c, :W], tmp[:, :W], AF.Identity, [r_tmp, self.r_mods], [r_hb], bias=self.mod(4, c, j), scale=self.mod(3, c, j))
            P.ld("sp", H2v[:, :, c0:c0 + W], hb[:, :, :W], reads=[r_hb])
            HBv = self.HB.rearrange("(c p) k -> p c k", p=128)
            if c0 == CTX:
                P.ld("act", HBv[:, :, 0:1], hb[:, :, 0:1], reads=[r_hb], slow=True)
            if c0 + W == NTL:
                P.ld("act", HBv[:, :, 1:2], hb[:, :, W - 1:W], reads=[r_hb], slow=True)
        P.pop(mk)
        self.allgather([(self.HB, self.HG)])
        P.ld("sp", self.HGP[D:(NR + 1) * D, :], self.HG[:, :])
        P.barrier()

    def phase_ffn(self, l, need_ctx):
        P = self.P
        li = self.lidx[l]
        mk = P.push()
        OT = 342
        tiles = []
        if need_ctx:
            tiles.append((0, CTX, 0, CTX, 1))
        for lo in range(0, SL, OT):
            tiles.append((CTX, SL, lo, min(SL, lo + OT), 0))
        WT = OT + 2
        h2_r = P.ring("fh2", 1, [128, 16, WT], BF16)
        act = P.sbuf("fact", [128, 44, OT], BF16)
        r_act = [Res() for _ in range(44)]
        wu_r = P.ring("fwu", 2, [128, 2, 16, 256], BF16)
        wd_r = P.ring("fwd", 2, [128, 44, 128], BF16)
        f_r = P.ring("ff", 1, [128, 16, OT], F32)
        xs_r = P.ring("fxs", 1, [128, 16, OT], F32)
        gc_r = P.ring("fgc", 2, [128, OT], F32)
        vc_r = P.ring("fvc", 2, [128, OT], F32)
        sq_r = P.ring("fsq", 3, [128, 512], BF16)
        st_r = P.ring("fst_", 2, [128, 512], F32)
        tmp_r = P.ring("ftmp", 3, [128, OT], F32)
        XTv = self.XT.rearrange("(c p) t -> p c t", p=128)
        H2v = self.H2.rearrange("(c p) t -> p c t", p=128)
        WU = self.WUP[li].rearrange("(kc p) n -> p kc n", p=128)
        WD = self.WDN[li].rearrange("(kc p) n -> p kc n", p=128)
        kb = 0
        for (off, Ls, lo, hi, j) in tiles:
            Wo = hi - lo
            Wt = Wo + 2
            h2, r_h2 = h2_r.next()
            a, b = lo - 1, hi + 1
            ca, cb = 0, Wt
            HGPv = self.HGP.rearrange("(c p) k -> p c k", p=128)
            if a < 0:
                if j == 1:
                    P.memset(P.pool, h2[:, :, 0:1], 0.0, [r_h2])
                else:
                    P.dma("sp", lambda e, h2=h2: e.dma_start(out=h2[:, :, 0:1], in_=HGPv[:, bass.ds(self.rank(e) * 16, 16), 1:2], allow_slow_non_contiguous=True), writes=[r_h2])
                a, ca = 0, 1
            if b > Ls:
                if j == 1:
                    P.memset(P.pool, h2[:, :, Wt - 1:Wt], 0.0, [r_h2])
                else:
                    P.dma("sp", lambda e, h2=h2, Wt=Wt: e.dma_start(out=h2[:, :, Wt - 1:Wt], in_=HGPv[:, bass.ds((self.rank(e) + 2) * 16, 16), 0:1], allow_slow_non_contiguous=True), writes=[r_h2])
                b, cb = Ls, Wt - 1
            P.ld("sp", h2[:, :, ca:cb], H2v[:, :, off + a:off + b], writes=[r_h2])
            for i in range(44):
                if i % 2 == 0:
                    wu, r_wu = wu_r.next()
                    P.ld("act", wu[:, 0, :, :], WU[:, :, i * 128:i * 128 + 256], writes=[r_wu])
                    P.ld("act", wu[:, 1, :, :], WU[:, :, DFF + i * 128:DFF + i * 128 + 256], writes=[r_wu])
                s0 = (i % 2) * 128
                pg, r_pg = self.pb[(2 * i) % 6]
                pv, r_pv = self.pb[(2 * i + 1) % 6]
                for kc in range(16):
                    P.mm(pg[:, :Wt], wu[:, 0, kc, s0:s0 + 128], h2[:, kc, :Wt], kc == 0, kc == 15, [r_wu, r_h2], [r_pg])
                for kc in range(16):
                    P.mm(pv[:, :Wt], wu[:, 1, kc, s0:s0 + 128], h2[:, kc, :Wt], kc == 0, kc == 15, [r_wu, r_h2], [r_pv])
                gc, r_gc = gc_r.next()
                vc, r_vc = vc_r.next()
                for (pp, r_pp, oc, r_oc, ci) in ((pg, r_pg, gc, r_gc, i), (pv, r_pv, vc, r_vc, 44 + i)):
                    P.ts(P.dve, oc[:, :Wo], pp[:, 0:Wo], self.sp_col(l, SP_FC + 0 * 88 + ci), None, ALU.mult, None, [r_pp, self.r_const], [r_oc])
                    P.stt(oc[:, :Wo], pp[:, 1:Wo + 1], self.sp_col(l, SP_FC + 1 * 88 + ci), oc[:, :Wo], ALU.mult, ALU.add, [r_pp, r_oc, self.r_const], [r_oc])
                    P.stt(oc[:, :Wo], pp[:, 2:Wo + 2], self.sp_col(l, SP_FC + 2 * 88 + ci), oc[:, :Wo], ALU.mult, ALU.add, [r_pp, r_oc, self.r_const], [r_oc])
                P.actv(gc[:, :Wo], gc[:, :Wo], AF.Gelu_apprx_tanh, [r_gc], [r_gc])
                P.tt(P.pool, act[:, i, :Wo], gc[:, :Wo], vc[:, :Wo], ALU.mult, [r_gc, r_vc], [r_act[i]])
            f, r_f = f_r.next()
            for m in range(16):
                wd, r_wd = wd_r.next()
                P.ld("act", wd[:], WD[:, :, m * 128:(m + 1) * 128], writes=[r_wd])
                pd, r_pd = self.pb[6 + m % 2]
                for kc in range(44):
                    P.mm(pd[:, :Wo], wd[:, kc, :], act[:, kc, :Wo], kc == 0, kc == 43, [r_wd, r_act[kc]], [r_pd])
                P.copy(P.act if m % 2 == 0 else P.dve, f[:, m, :Wo], pd[:, :Wo], [r_pd], [r_f])
            st, r_st = self.rms_rstd([f[:, c, :Wo] for c in range(16)], Wo, D, sq_r, 0, st_r, [r_f])
            xs, r_xs = xs_r.next()
            P.ld("sp", xs[:, :, :Wo], XTv[:, :, off + lo:off + hi], writes=[r_xs])
            for c in range(16):
                tmp, r_tmp = tmp_r.next()
                P.tt(P.pool, tmp[:, :Wo], f[:, c, :Wo], st[:, :Wo], ALU.mult, [r_f, r_st], [r_tmp])
                P.stt(xs[:, c, :Wo], tmp[:, :Wo], self.mod(5, c, j), xs[:, c, :Wo], ALU.mult, ALU.add, [r_tmp, r_xs, self.r_mods], [r_xs])
            P.ld("sp", XTv[:, :, off + lo:off + hi], xs[:, :, :Wo], reads=[r_xs])
        P.pop(mk)


def make_inmaps(inp, n_cores=8, layers=None):
    c = _consts()
    layers = list(range(DEPTH)) if layers is None else list(layers)
    shared = {k: np.ascontiguousarray(np.asarray(inp[k], dtype=np.float32)[layers]) for k in (
        "ada_w", "w_in", "hy_w1", "hy_w2", "hy_w3", "lru_wa", "lru_wx", "mla_wuq", "mla_wukv", "w_out", "ffn_up", "ffn_down")}
    shared["smallp"] = _pack_small({k: np.asarray(v, dtype=np.float32) for k, v in inp.items()}).reshape(128, DEPTH * NSP)
    shared.update(c)
    maps = []
    for core in range(n_cores):
        b = core // NR
        r = core % NR
        m = dict(shared)
        m["x"] = np.ascontiguousarray(np.asarray(inp["x"][b, r * SL:(r + 1) * SL], dtype=np.float32))
        m["ctx"] = np.ascontiguousarray(np.asarray(inp["ctx"][b], dtype=np.float32))
        cv = np.stack([np.asarray(inp["c"][b], dtype=np.float32), np.asarray(inp["c_ctx"], dtype=np.float32)], axis=-1)
        m["cvec"] = np.ascontiguousarray(cv.reshape(16, 128, 2).transpose(1, 0, 2).reshape(128, 32))
        maps.append(m)
    return maps


_NC_CACHE = {}


def kernel(**inputs):
    if "nc" not in _NC_CACHE:
        _NC_CACHE["nc"] = Builder(range(DEPTH)).build()
    nc = _NC_CACHE["nc"]
    maps = make_inmaps(inputs)
    res = run_bass_kernel_spmd(nc, maps, core_ids=list(range(8)))
    out = np.stack([np.concatenate([np.asarray(res.results[b * NR + r]["out"]) for r in range(NR)], axis=0) for b in range(2)], axis=0)
    return out.astype(np.float32)
```

```python
import math
import numpy as np
import ml_dtypes
import concourse.bass as bass
import concourse.mybir as mybir
from concourse.bass_utils import run_bass_kernel_spmd

F32 = mybir.dt.float32
BF16 = mybir.dt.bfloat16
AF = mybir.ActivationFunctionType
ALU = mybir.AluOpType

D = 2048
SEQ = 4096
CTX = 256
NT = SEQ + CTX
DEPTH = 4
NR = 4
SL = SEQ // NR
NTL = CTX + SL
GROUPS = [[0, 1, 2, 3], [4, 5, 6, 7]]
NOWN = 27
UCH = 14
DIN = 3392
DINX = 3456
OFF_HY, OFF_G, OFF_Q, OFF_X, OFF_KV, OFF_KR, OFF_KRR = 0, 1536, 2048, 2560, 3072, 3328, 3392
DFF = 5632
EPS = 1e-6
MLA_SCALE = 192 ** -0.5
MAGIC = 12582912.0

SP_ADAB, SP_NG, SP_HYC, SP_HYB, SP_LC, SP_BA, SP_BX, SP_LAM = 0, 96, 160, 196, 204, 236, 244, 252
SP_GQ, SP_GKV, SP_HG, SP_FC, SP_B1, SP_B2, NSP = 260, 264, 266, 282, 546, 547, 548


class Res:
    __slots__ = ("name", "w", "r")

    def __init__(self, name=""):
        self.name = name
        self.w = None
        self.r = {}


class EngQ:
    def __init__(self, name, eng, inorder=False):
        self.name = name
        self.eng = eng
        self.inorder = inorder
        self.sem = None
        self.step = 1
        self.count = 0
        self.waited = {}
        self.thunks = []


class DmaSlot:
    def __init__(self, sem, name):
        self.sem = sem
        self.name = name
        self.count = 0
        self.step = 16
        self.inorder = False


class Ring:
    def __init__(self, items):
        self.items = items
        self.i = 0

    def next(self):
        it = self.items[self.i % len(self.items)]
        self.i += 1
        return it


class Prog:
    def __init__(self, nc, n_dma_slots=20, same_engine_sync=True):
        self.nc = nc
        self.same_engine_sync = same_engine_sync
        self.ctx = []
        self.pe = EngQ("pe", nc.tensor, inorder=True)
        self.dve = EngQ("dve", nc.vector)
        self.act = EngQ("act", nc.scalar)
        self.pool = EngQ("pool", nc.gpsimd)
        self.sp = EngQ("sp", nc.sync)
        self.engs = [self.pe, self.dve, self.act, self.pool, self.sp]
        for e in self.engs:
            e.sem = self._sem("s_" + e.name)
        self.slots = {}
        for qn in ("sp", "act", "pool"):
            self.slots[qn] = [DmaSlot(self._sem(f"d_{qn}{i}"), f"d_{qn}{i}") for i in range(n_dma_slots)]
        self.slot_rr = {"sp": 0, "act": 0, "pool": 0}
        self.n_inst = 0
        self.uid = 0

    def _sem(self, name):
        cm = self.nc.semaphore(name)
        s = cm.__enter__()
        self.ctx.append(cm)
        return s

    def push(self):
        return len(self.ctx)

    def pop(self, mark):
        self.barrier()
        while len(self.ctx) > mark:
            self.ctx.pop().__exit__(None, None, None)

    def sbuf(self, name, shape, dtype):
        self.uid += 1
        cm = self.nc.sbuf_tensor(f"{name}_{self.uid}", list(shape), dtype)
        t = cm.__enter__()
        self.ctx.append(cm)
        return t

    def psum(self, name, shape, dtype=F32):
        cm = self.nc.psum_tensor(name, list(shape), dtype)
        t = cm.__enter__()
        self.ctx.append(cm)
        return t

    def ring(self, name, n, shape, dtype):
        return Ring([(self.sbuf(f"{name}{i}", shape, dtype), Res(f"{name}{i}")) for i in range(n)])

    def _deps(self, reads, writes):
        deps = []
        for r in reads:
            if r.w is not None:
                deps.append(r.w)
        for w in writes:
            if w.w is not None:
                deps.append(w.w)
            deps.extend(w.r.values())
        return deps

    def _emit_waits(self, q, deps):
        need = {}
        for (src, c) in deps:
            if src is q and (q.inorder or not self.same_engine_sync):
                continue
            if q.waited.get(id(src), 0) >= c:
                continue
            if need.get(id(src), (None, 0))[1] < c:
                need[id(src)] = (src, c)
        for src, c in need.values():
            q.waited[id(src)] = c
            sem, val = src.sem, c * src.step
            q.thunks.append(lambda e, sem=sem, val=val: e.wait_ge(sem, val))

    def _mark(self, src, c, reads, writes):
        for r in reads:
            old = r.r.get(id(src))
            if old is None or old[1] < c:
                r.r[id(src)] = (src, c)
        for w in writes:
            w.w = (src, c)
            w.r = {}

    def op(self, q, fn, reads=(), writes=()):
        self._emit_waits(q, self._deps(reads, writes))
        q.count += 1
        sem = q.sem
        q.thunks.append(lambda e, fn=fn, sem=sem: fn(e).then_inc(sem, 1))
        self._mark(q, q.count, reads, writes)
        self.n_inst += 1

    def dma(self, qname, fn, reads=(), writes=()):
        q = {"sp": self.sp, "act": self.act, "pool": self.pool}[qname]
        slots = self.slots[qname]
        i = self.slot_rr[qname]
        self.slot_rr[qname] = (i + 1) % len(slots)
        slot = slots[i]
        deps = self._deps(reads, writes)
        if slot.count > 0:
            deps.append((slot, slot.count))
        self._emit_waits(q, deps)
        slot.count += 1
        sem = slot.sem
        q.thunks.append(lambda e, fn=fn, sem=sem: fn(e).then_inc(sem, 16))
        self._mark(slot, slot.count, reads, writes)
        self.n_inst += 1

    def barrier(self):
        srcs = list(self.engs)
        for qn in self.slots:
            srcs.extend(self.slots[qn])
        deps = [(s, s.count) for s in srcs if s.count > 0]
        for q in self.engs:
            need = [(s, c) for (s, c) in deps]
            keep_sync = self.same_engine_sync
            self.same_engine_sync = True
            self._emit_waits(q, need)
            self.same_engine_sync = keep_sync

    def mm(self, out, lhsT, rhs, start, stop, reads, writes):
        self.op(self.pe, lambda e: e.matmul(out, lhsT=lhsT, rhs=rhs, start=start, stop=stop), reads, writes)

    def tr(self, out, in_, ident, reads, writes):
        self.op(self.pe, lambda e: e.transpose(out, in_, ident), reads, writes)

    def tt(self, q, out, in0, in1, op, reads, writes):
        self.op(q, lambda e: e.tensor_tensor(out=out, in0=in0, in1=in1, op=op), reads, writes)

    def ts(self, q, out, in0, s1, s2, op0, op1, reads, writes):
        if op1 is None:
            self.op(q, lambda e: e.tensor_scalar(out=out, in0=in0, scalar1=s1, scalar2=None, op0=op0), reads, writes)
        else:
            self.op(q, lambda e: e.tensor_scalar(out=out, in0=in0, scalar1=s1, scalar2=s2, op0=op0, op1=op1), reads, writes)

    def stt(self, out, in0, scalar, in1, op0, op1, reads, writes):
        self.op(self.dve, lambda e: e.scalar_tensor_tensor(out=out, in0=in0, scalar=scalar, in1=in1, op0=op0, op1=op1), reads, writes)

    def actv(self, out, in_, func, reads, writes, bias=None, scale=None):
        kw = {}
        if bias is not None:
            kw["bias"] = bias
        if scale is not None:
            kw["scale"] = scale
        self.op(self.act, lambda e: e.activation(out=out, in_=in_, func=func, **kw), reads, writes)

    def copy(self, q, out, in_, reads, writes):
        if q is self.act:
            self.op(q, lambda e: e.activation(out=out, in_=in_, func=AF.Copy), reads, writes)
        else:
            self.op(q, lambda e: e.tensor_copy(out=out, in_=in_), reads, writes)

    def recip(self, out, in_, reads, writes):
        self.op(self.dve, lambda e: e.reciprocal(out=out, in_=in_), reads, writes)

    def memset(self, q, ap, val, writes):
        self.op(q, lambda e: e.memset(ap, val), (), writes)

    def ld(self, qname, out, in_, reads=(), writes=(), slow=False):
        if slow:
            self.dma(qname, lambda e: e.dma_start(out=out, in_=in_, allow_slow_non_contiguous=True), reads, writes)
        else:
            self.dma(qname, lambda e: e.dma_start(out=out, in_=in_), reads, writes)

    def finish(self, final_res):
        deps = [r.w for r in final_res if r.w is not None]
        self._emit_waits(self.sp, deps)
        self.barrier()
        nc = self.nc
        with nc.Block() as block:
            @block.sync
            def _(e):
                for t in self.sp.thunks:
                    t(e)

            @block.tensor
            def _(e):
                for t in self.pe.thunks:
                    t(e)

            @block.vector
            def _(e):
                for t in self.dve.thunks:
                    t(e)

            @block.scalar
            def _(e):
                for t in self.act.thunks:
                    t(e)

            @block.gpsimd
            def _(e):
                for t in self.pool.thunks:
                    t(e)
        while self.ctx:
            self.ctx.pop().__exit__(None, None, None)


_CONST_CACHE = {}


def _bf(a):
    return np.ascontiguousarray(a.astype(np.float32)).astype(ml_dtypes.bfloat16)


def _dft_tables(L):
    N = 2 * L
    nfc = (L + 1 + 127) // 128
    FP = nfc * 128
    a = np.arange(FP, dtype=np.int64)
    t = np.arange(L, dtype=np.int64)
    ang = 2.0 * np.pi * ((t[:, None] * a[None, :]) % N).astype(np.float64) / N
    valid = (a <= L).astype(np.float64)
    C = np.cos(ang) * valid
    S = np.sin(ang) * valid
    ntc = L // 128
    Fc = C.reshape(ntc, 128, nfc, 128).transpose(2, 1, 0, 3)
    Fs = S.reshape(ntc, 128, nfc, 128).transpose(2, 1, 0, 3)
    w = np.where((a == 0) | (a == L), 1.0, 2.0) * valid / N
    IcM = C.T * w[:, None]
    IsM = S.T * w[:, None]
    TW = min(512, L)
    ntt = L // TW
    Ic = IcM.reshape(nfc, 128, ntt, TW).transpose(2, 1, 0, 3)
    Is = IsM.reshape(nfc, 128, ntt, TW).transpose(2, 1, 0, 3)
    return _bf(Fc), _bf(Fs), _bf(Ic), _bf(Is)


def _hy_consts(L):
    t = (np.arange(L, dtype=np.float32) / np.float32(L)).astype(np.float32)
    ang = (2.0 * math.pi * t[:, None] * np.arange(1, 17, dtype=np.float32)).astype(np.float32)
    feats = np.concatenate([t[:, None], np.sin(ang), np.cos(ang)], axis=-1).astype(np.float32)
    deltas = np.abs(np.linspace(math.log(1e-2) / 1.5, math.log(1e-2) / 0.3, 512, dtype=np.float32))
    decay = np.exp(-t[:, None] * deltas[None, :]).astype(np.float32)
    return np.ascontiguousarray(feats.T), decay


def _rope_tables():
    row = np.repeat(np.arange(SEQ // 64, dtype=np.float32), 64)
    col = np.tile(np.arange(64, dtype=np.float32), SEQ // 64)
    inv = (10000.0 ** (-np.arange(16, dtype=np.float32) / 16)).astype(np.float32)
    ang = np.concatenate([row[:, None] * inv, col[:, None] * inv], axis=-1).astype(np.float32)
    cs = np.zeros((64, 2, SEQ), np.float32)
    cs[:32, 0] = np.cos(ang).T
    cs[32:, 0] = np.cos(ang).T
    cs[:32, 1] = np.sin(ang).T
    cs[32:, 1] = np.sin(ang).T
    return cs


def _consts():
    if _CONST_CACHE:
        return _CONST_CACHE
    c = {}
    c["ident_f"] = np.eye(128, dtype=np.float32)
    c["rope_cs"] = _rope_tables()
    for L, tag in ((SEQ, "L"), (CTX, "C")):
        Fc, Fs, Ic, Is = _dft_tables(L)
        c["Fc" + tag], c["Fs" + tag], c["Ic" + tag], c["Is" + tag] = Fc, Fs, Ic, Is
        f, d = _hy_consts(L)
        c["feats" + tag], c["decay" + tag] = f, d
    _CONST_CACHE.update(c)
    return _CONST_CACHE


def _pack_small(inp):
    sp = np.zeros((128, DEPTH, NSP), np.float32)

    def put(col, arr):
        n = arr.shape[1] // 128
        sp[:, :, col:col + n] = arr.reshape(DEPTH, n, 128).transpose(2, 0, 1)

    put(SP_ADAB, inp["ada_b"])
    put(SP_NG, inp["norm_g"].reshape(DEPTH, 4 * D))
    put(SP_HYC, inp["hy_conv"].reshape(DEPTH, 3 * 1536))
    put(SP_HYB, inp["hy_bias"].reshape(DEPTH, 2 * 512))
    put(SP_LC, inp["lru_conv"].reshape(DEPTH, 2 * 4 * 512))
    put(SP_BA, inp["lru_ba"].reshape(DEPTH, 1024))
    put(SP_BX, inp["lru_bx"].reshape(DEPTH, 1024))
    put(SP_LAM, inp["lru_lam"].reshape(DEPTH, 1024))
    put(SP_GQ, inp["mla_gq"])
    put(SP_GKV, inp["mla_gkv"])
    put(SP_HG, inp["head_g"])
    put(SP_FC, inp["ffn_conv"].reshape(DEPTH, 3 * 2 * DFF))
    sp[:64, :, SP_B1] = inp["hy_b1"].T
    sp[:64, :, SP_B2] = inp["hy_b2"].T
    return sp


class Builder:
    def __init__(self, layers, dbg=(), phases=None, same_engine_sync=True):
        self.layers = list(layers)
        self.lidx = {l: i for i, l in enumerate(self.layers)}
        NL = len(self.layers)
        self.dbg = set(dbg)
        self.phases = phases
        nc = self.nc = bass.Bass("TRN2", target_bir_lowering=False)
        self.I = {}
        din = self.din
        self.x_in = din("x", [SL, D])
        self.ctx_in = din("ctx", [CTX, D])
        self.cvec = din("cvec", [128, 32])
        self.smallp_in = din("smallp", [128, DEPTH * NSP])
        self.ownp_in = din("ownp", [128, DEPTH * NOWN])
        self.w3own = din("hy_w3_own", [NL, 64, 512])
        self.lwa_own = din("lru_wa_own", [NL, 2, 128, 128])
        self.lwx_own = din("lru_wx_own", [NL, 2, 128, 128])
        self.ada_w = din("ada_w", [NL, D, 6 * D])
        self.w_in = din("w_in", [NL, D, DIN])
        self.hy_w1 = din("hy_w1", [NL, 33, 64])
        self.hy_w2 = din("hy_w2", [NL, 64, 64])
        self.mla_wuq = din("mla_wuq", [NL, 512, 1536])
        self.mla_wukv = din("mla_wukv", [NL, 256, 2048])
        self.w_out = din("w_out", [NL, D, D])
        self.ffn_up = din("ffn_up", [NL, D, 2 * DFF])
        self.ffn_down = din("ffn_down", [NL, DFF, D])
        self.ident_in = din("ident_f", [128, 128])
        self.rope_cs = din("rope_cs", [64, 2, SEQ])
        self.rope_q = din("rope_q", [64, 2, SL])
        self.tab = {}
        for L, tag in ((SEQ, "L"), (CTX, "C")):
            nfc = (L + 1 + 127) // 128
            ntc = L // 128
            TW = min(512, L)
            self.tab[tag] = dict(
                L=L, nfc=nfc, ntc=ntc, TW=TW, ntt=L // TW,
                Fc=din("Fc" + tag, [nfc, 128, ntc, 128], BF16), Fs=din("Fs" + tag, [nfc, 128, ntc, 128], BF16),
                Ic=din("Ic" + tag, [L // TW, 128, nfc, TW], BF16), Is=din("Is" + tag, [L // TW, 128, nfc, TW], BF16),
                feats=din("feats" + tag, [33, L]), decay=din("decay" + tag, [L, 128]),
                S=self.dscr("S" + tag, [nfc * 128, 512], F32),
            )
        self.out = nc.dram_tensor("out", [SL, D], F32, kind="ExternalOutput").ap()
        ds = self.dscr
        self.XT = ds("XT", [D, NTL], F32)
        self.UC = ds("UC", [DINX, CTX], F32)
        self.ULOC = ds("ULOC", [UCH * 256, SL], F32)
        self.UG = ds("UG", [UCH * NR * 256, SL], F32)
        self.HB = ds("HB", [D, 2], BF16)
        self.HG = ds("HG", [NR * D, 2], BF16)
        self.HGP = ds("HGP", [(NR + 2) * D, 2], BF16)
        self.ZT = ds("ZT", [128, NT], F32)
        self.PT = ds("PT", [256, NT], F32)
        self.YIN = [ds(f"YIN{k}", [NR * 128, NTL], BF16) for k in range(2)]
        self.YG = [ds(f"YG{k}", [2 * NR * 256, NTL], BF16) for k in range(2)]
        self.YM = ds("YM", [1024, NTL], BF16)
        self.H2 = ds("H2", [D, NTL], BF16)
        self.WIN = ds("WIN", [NL, D, DINX], BF16)
        self.WUQ = ds("WUQ", [NL, 512, 2048], BF16)
        self.WUKV = ds("WUKV", [NL, 256, 2048], BF16)
        self.WOUT = ds("WOUT", [NL, D, D], BF16)
        self.WUP = ds("WUP", [NL, D, 2 * DFF], BF16)
        self.WDN = ds("WDN", [NL, DFF, D], BF16)
        self.P = Prog(nc, same_engine_sync=same_engine_sync)
        self.r_out = Res("out")
        self._rank = {}

    def din(self, name, shape, dt=F32):
        self.I[name] = (list(shape), dt)
        return self.nc.dram_tensor(name, list(shape), dt, kind="ExternalInput").ap()

    def dscr(self, name, shape, dt):
        kind = "ExternalOutput" if name in self.dbg else "Internal"
        return self.nc.dram_tensor(name, list(shape), dt, kind=kind).ap()

    def rank(self, e):
        key = id(e)
        if key not in self._rank:
            self._rank[key] = e.snap(e.partition_id() % NR)
        return self._rank[key]

    def rv(self, e, kind):
        key = (kind, id(e))
        if key not in self._rank:
            r = self.rank(e)
            expr = {"x": (r // 2) * 8 + (r % 2), "128": r * 128, "16": r * 16}[kind]
            self._rank[key] = e.snap(expr)
        return self._rank[key]

    def Uap(self, row0, nrows, c0, W):
        if c0 < CTX:
            return self.UC[row0:row0 + nrows, c0:c0 + W]
        t0 = c0 - CTX
        s_, tl = t0 // SL, t0 % SL
        assert tl + W <= SL
        k, rr = row0 // 256, row0 % 256
        assert rr + nrows <= 256
        base = (k * NR + s_) * 256 + rr
        return self.UG[base:base + nrows, tl:tl + W]

    def Ulat(self, row0):
        k, rr = row0 // 256, row0 % 256
        return self.UG.rearrange("(k s r) t -> k r s t", s=NR, r=256)[k][rr:rr + 128, :, :]

    def loc_tiles(self, need_ctx=True):
        tiles = [(0, CTX, 1)] if need_ctx else []
        tiles += [(CTX + i * 512, 512, 0) for i in range(SL // 512)]
        return tiles

    def allgather(self, pairs):
        P = self.P
        P.barrier()
        for (src, dst) in pairs:
            P.op(P.pool, lambda e, src=src, dst=dst: e.collective_compute("AllGather", ALU.bypass, replica_groups=GROUPS, ins=[src.opt()], outs=[dst.opt()]), (), ())
        P.barrier()

    def on(self, ph):
        return self.phases is None or ph in self.phases

    def build(self):
        P = self.P
        self.setup()
        for l in self.layers:
            last = (l == DEPTH - 1)
            if self.on("M"):
                self.phase_mod(l)
            if self.on("A"):
                self.phase_inproj(l)
            if self.on("H"):
                if not last:
                    self.phase_hyena(l, "C")
                self.phase_hyena(l, "L")
            if self.on("R"):
                self.phase_lru(l, not last)
                self.allgather([(self.YIN[k][q * 256:(q + 1) * 256, :], self.YG[k][q * 1024:(q + 1) * 1024, :]) for k in range(2) for q in range(2)])
            if self.on("T"):
                self.phase_attn(l, not last)
            if self.on("O"):
                self.phase_outproj(l, not last)
            if self.on("F"):
                self.phase_ffn(l, not last)
        if self.on("Z"):
            self.final()
        P.finish([self.r_out])
        return self.nc

    def setup(self):
        P = self.P
        self.pb = [(P.psum(f"pb{i}", [128, 512]), Res(f"pb{i}")) for i in range(8)]
        self.ident = P.sbuf("ident", [128, 128], F32)
        self.r_const = Res("const")
        P.ld("sp", self.ident[:], self.ident_in[:, :], writes=[self.r_const])
        self.ones_b = P.sbuf("ones_b", [128, 128], BF16)
        self.ones_f = P.sbuf("ones_f", [128, 128], F32)
        P.memset(P.dve, self.ones_b[:], 1.0, [self.r_const])
        P.memset(P.dve, self.ones_f[:], 1.0, [self.r_const])
        self.smallp = P.sbuf("smallp", [128, DEPTH * NSP], F32)
        P.ld("sp", self.smallp[:], self.smallp_in[:, :], writes=[self.r_const])
        self.scv = P.sbuf("scv", [128, 32], F32)
        cvt = P.sbuf("cvt", [128, 32], F32)
        P.ld("sp", cvt[:], self.cvec[:, :], writes=[self.r_const])
        P.actv(self.scv[:], cvt[:], AF.Silu, [self.r_const], [self.r_const])
        self.eps_t = P.sbuf("eps_t", [128, 1], F32)
        P.memset(P.dve, self.eps_t[:], EPS, [self.r_const])
        self.ownp = P.sbuf("ownp", [128, DEPTH * NOWN], F32)
        P.ld("sp", self.ownp[:], self.ownp_in[:, :], writes=[self.r_const])
        self.mods = P.sbuf("mods", [128, 6 * 32], F32)
        self.r_mods = Res("mods")
        self.lru_h0 = P.sbuf("lru_h0", [128, 8], F32)
        self.r_h0 = Res("h0")
        for l in self.layers:
            for (src, dst, rows, cols, dcols) in (
                (self.w_in[self.lidx[l]], self.WIN[self.lidx[l]], D, DIN, DINX),
                (self.w_out[self.lidx[l]], self.WOUT[self.lidx[l]], D, D, D),
                (self.ffn_up[self.lidx[l]], self.WUP[self.lidx[l]], D, 2 * DFF, 2 * DFF),
                (self.ffn_down[self.lidx[l]], self.WDN[self.lidx[l]], DFF, D, D),
                (self.mla_wukv[self.lidx[l]], self.WUKV[self.lidx[l]], 256, 2048, 2048),
            ):
                cw = 1408 if cols % 1408 == 0 else (1696 if cols % 1696 == 0 else 1024)
                assert cols % cw == 0 and cw <= 2048
                for r0 in range(0, rows, 1024):
                    r1 = min(rows, r0 + 1024)
                    for c0 in range(0, cols, cw):
                        P.ld("pool", dst[r0:r1, c0:c0 + cw], src[r0:r1, c0:c0 + cw])
            for h in range(8):
                P.ld("pool", self.WUQ[self.lidx[l]][:, h * 256:h * 256 + 192], self.mla_wuq[self.lidx[l]][:, h * 192:(h + 1) * 192])
            mk = P.push()
            wk = P.sbuf("wk", [128, 16, 64], F32)
            wr = P.sbuf("wr", [128, 16, 64], BF16)
            r_wk, r_wr = Res(), Res()
            P.ld("sp", wk[:], self.w_in[self.lidx[l]].rearrange("(kc p) n -> p kc n", p=128)[:, :, OFF_KR:OFF_KR + 64], writes=[r_wk])
            P.ts(P.dve, wr[:, :, 0:32], wk[:, :, 32:64], -1.0, None, ALU.mult, None, [r_wk], [r_wr])
            P.copy(P.dve, wr[:, :, 32:64], wk[:, :, 0:32], [r_wk], [r_wr])
            P.ld("sp", self.WIN[self.lidx[l]].rearrange("(kc p) n -> p kc n", p=128)[:, :, OFF_KRR:OFF_KRR + 64], wr[:], reads=[r_wr])
            qk = P.sbuf("qk", [128, 4, 8, 64], F32)
            qr = P.sbuf("qr", [128, 4, 8, 64], BF16)
            r_qk, r_qr = Res(), Res()
            srcv = self.mla_wuq[self.lidx[l]].rearrange("(kc p) (h n) -> p kc h n", p=128, n=192)
            dstv = self.WUQ[self.lidx[l]].rearrange("(kc p) (h n) -> p kc h n", p=128, n=256)
            for kc in range(4):
                P.ld("sp", qk[:, kc, :, :], srcv[:, kc, :, 128:192], writes=[r_qk])
            P.ts(P.dve, qr[:, :, :, 0:32], qk[:, :, :, 32:64], -1.0, None, ALU.mult, None, [r_qk], [r_qr])
            P.copy(P.dve, qr[:, :, :, 32:64], qk[:, :, :, 0:32], [r_qk], [r_qr])
            for kc in range(4):
                P.ld("sp", dstv[:, kc, :, 192:256], qr[:, kc, :, :], reads=[r_qr])
            P.pop(mk)
        mk = P.push()
        xin = P.ring("xin", 2, [128, D], F32)
        xst = P.ring("xst", 2, [128, 16, 128], F32)
        XTv = self.XT.rearrange("(c p) t -> p c t", p=128)
        k = 0
        for ti in range(NTL // 128):
            src = self.ctx_in[ti * 128:(ti + 1) * 128, :] if ti < 2 else self.x_in[(ti - 2) * 128:(ti - 1) * 128, :]
            xt, r_xt = xin.next()
            P.ld("sp" if ti % 2 == 0 else "act", xt[:], src, writes=[r_xt])
            st, r_st = xst.next()
            for g in range(4):
                pt, r_pt = self.pb[k % 8]
                k += 1
                for j in range(4):
                    c = g * 4 + j
                    P.tr(pt[:, j * 128:(j + 1) * 128], xt[:, c * 128:(c + 1) * 128], self.ident[:], [r_xt, self.r_const], [r_pt])
                P.copy(P.act if g % 2 == 0 else P.dve, st[:, g * 4:(g + 1) * 4, :], pt[:].rearrange("p (j t) -> p j t", j=4), [r_pt], [r_st])
            P.ld("sp" if ti % 2 == 1 else "act", XTv[:, :, ti * 128:(ti + 1) * 128], st[:], reads=[r_st])
        P.pop(mk)
        mk = P.push()
        zt = P.sbuf("zpad", [128, 16, 2], BF16)
        r_z = Res()
        P.memset(P.dve, zt[:], 0.0, [r_z])
        for blk in (0, NR + 1):
            P.ld("sp", self.HGP[blk * D:(blk + 1) * D, :].rearrange("(c p) k -> p c k", p=128), zt[:], reads=[r_z])
        P.pop(mk)

    def final(self):
        P = self.P
        mk = P.push()
        xin = P.ring("fin", 2, [128, 16, 128], F32)
        xst = P.ring("fst", 2, [128, D], F32)
        XTv = self.XT.rearrange("(c p) t -> p c t", p=128)
        k = 0
        for ti in range(SL // 128):
            xt, r_xt = xin.next()
            P.ld("sp" if ti % 2 == 0 else "act", xt[:], XTv[:, :, CTX + ti * 128:CTX + (ti + 1) * 128], writes=[r_xt])
            st, r_st = xst.next()
            for g in range(4):
                pt, r_pt = self.pb[k % 8]
                k += 1
                for j in range(4):
                    c = g * 4 + j
                    P.tr(pt[:, j * 128:(j + 1) * 128], xt[:, c, :], self.ident[:], [r_xt, self.r_const], [r_pt])
                P.copy(P.act if g % 2 == 0 else P.dve, st[:, g * 512:(g + 1) * 512], pt[:], [r_pt], [r_st])
            P.ld("sp" if ti % 2 == 1 else "act", self.out[ti * 128:(ti + 1) * 128, :], st[:], reads=[r_st], writes=[self.r_out])
        P.pop(mk)

    def sp_col(self, l, col, n=1, parts=128):
        base = l * NSP + col
        return self.smallp[0:parts, base:base + n]

    def rms_rstd(self, chunks, W, n_feat, sq_ring, pbi, st_ring, reads):
        P = self.P
        pt, r_pt = self.pb[pbi]
        for i, ap in enumerate(chunks):
            sq, r_sq = sq_ring.next()
            P.actv(sq[:, :W], ap, AF.Square, reads, [r_sq])
            P.mm(pt[:, :W], self.ones_b[:], sq[:, :W], i == 0, i == len(chunks) - 1, [r_sq, self.r_const], [r_pt])
        st, r_st = st_ring.next()
        P.actv(st[:, :W], pt[:, :W], AF.Sqrt, [r_pt], [r_st], bias=self.eps_t[:], scale=1.0 / n_feat)
        P.recip(st[:, :W], st[:, :W], [r_st], [r_st])
        return st, r_st

    def phase_mod(self, l):
        P = self.P
        mk = P.push()
        wr = P.ring("adaw", 2, [128, 16, 512], F32)
        modT = P.sbuf("modT", [128, 96, 2], F32)
        r_modT = Res()
        pm, r_pm = self.pb[0]
        src = self.ada_w[self.lidx[l]].rearrange("(kc p) n -> p kc n", p=128)
        scv = self.scv[:].rearrange("p (kc j) -> p kc j", j=2)
        for g in range(24):
            wt, r_wt = wr.next()
            P.ld("sp" if g % 2 == 0 else "act", wt[:], src[:, :, g * 512:(g + 1) * 512], writes=[r_wt])
            for m in range(4):
                col = (g * 4 + m) * 2
                for kc in range(16):
                    P.mm(pm[:, col:col + 2], wt[:, kc, m * 128:(m + 1) * 128], scv[:, kc, :], kc == 0, kc == 15, [r_wt, self.r_const], [r_pm])
        pmv = pm[:, 0:192].rearrange("p (m j) -> p m j", j=2)
        for j in range(2):
            P.tt(P.dve, modT[:, :, j], pmv[:, :, j], self.sp_col(l, SP_ADAB, 96), ALU.add, [r_pm, self.r_const], [r_modT])
        mods = self.mods[:].rearrange("p (k c j) -> p k c j", k=6, j=2)
        for j in range(2):
            for half, (sh, sc, g, nga, ngb) in enumerate(((0, 1, 2, 0, 1), (3, 4, 5, 2, 3))):
                ng_a = self.sp_col(l, SP_NG + nga * 16, 16)
                ng_b = self.sp_col(l, SP_NG + ngb * 16, 16)
                A, B, G = mods[:, half * 3 + 0, :, j], mods[:, half * 3 + 1, :, j], mods[:, half * 3 + 2, :, j]
                P.stt(A, modT[:, sc * 16:(sc + 1) * 16, j], 1.0, ng_a, ALU.add, ALU.mult, [r_modT, self.r_const], [self.r_mods])
                P.copy(P.dve, B, modT[:, sh * 16:(sh + 1) * 16, j], [r_modT], [self.r_mods])
                P.tt(P.dve, G, modT[:, g * 16:(g + 1) * 16, j], ng_b, ALU.mult, [r_modT, self.r_const], [self.r_mods])
        P.pop(mk)

    def mod(self, k, c, j):
        i = (k * 16 + c) * 2 + j
        return self.mods[:, i:i + 1]

    def tok_tiles(self, need_ctx=True):
        tiles = [(0, CTX, 1)] if need_ctx else []
        tiles += [(CTX + i * 512, 512, 0) for i in range(SEQ // 512)]
        return tiles

    def phase_inproj(self, l):
        P = self.P
        mk = P.push()
        xs_r = P.ring("xs", 2, [128, 16, 512], F32)
        hb_r = P.ring("hb", 2, [128, 16, 512], BF16)
        sq_r = P.ring("sq", 3, [128, 512], BF16)
        tmp_r = P.ring("tmp", 3, [128, 512], F32)
        st_r = P.ring("st", 2, [128, 512], F32)
        w_r = P.ring("wg", 2, [128, 16, 512], BF16)
        ev_r = P.ring("ev", 4, [128, 512], F32)
        XTv = self.XT.rearrange("(c p) t -> p c t", p=128)
        Wv = self.WIN[self.lidx[l]].rearrange("(kc p) n -> p kc n", p=128)
        kb = 0
        for (c0, W, j) in self.loc_tiles():
            xs, r_xs = xs_r.next()
            P.ld("sp", xs[:, :, :W], XTv[:, :, c0:c0 + W], writes=[r_xs])
            st, r_st = self.rms_rstd([xs[:, c, :W] for c in range(16)], W, D, sq_r, 7, st_r, [r_xs])
            hb, r_hb = hb_r.next()
            for c in range(16):
                tmp, r_tmp = tmp_r.next()
                P.tt(P.dve, tmp[:, :W], xs[:, c, :W], st[:, :W], ALU.mult, [r_xs, r_st], [r_tmp])
                P.actv(hb[:, c, :W], tmp[:, :W], AF.Identity, [r_tmp, self.r_mods], [r_hb], bias=self.mod(1, c, j), scale=self.mod(0, c, j))
            for g in range(7):
                ncol = 512 if g < 6 else DINX - 6 * 512
                wt, r_wt = w_r.next()
                P.ld("act", wt[:, :, :ncol], Wv[:, :, g * 512:g * 512 + ncol], writes=[r_wt])
                for m in range(ncol // 128):
                    pt, r_pt = self.pb[kb % 6]
                    kb += 1
                    for kc in range(16):
                        P.mm(pt[:, :W], wt[:, kc, m * 128:(m + 1) * 128], hb[:, kc, :W], kc == 0, kc == 15, [r_wt, r_hb], [r_pt])
                    ev, r_ev = ev_r.next()
                    P.copy(P.act if kb % 2 == 0 else P.dve, ev[:, :W], pt[:, :W], [r_pt], [r_ev])
                    row = g * 512 + m * 128
                    dstU = self.UC[row:row + 128, c0:c0 + W] if j == 1 else self.ULOC[row:row + 128, c0 - CTX:c0 - CTX + W]
                    P.ld("sp", dstU, ev[:, :W], reads=[r_ev])
        P.pop(mk)
        self.allgather([(self.ULOC[k * 256:(k + 1) * 256, :], self.UG[k * NR * 256:(k + 1) * NR * 256, :]) for k in range(UCH)])


    def sin_layer(self, ps_ap, bias_ap, out_ap, W, rings, reads, r_ps, r_out):
        P = self.P
        v, r_v = rings[0].next()
        t1, r_t1 = rings[1].next()
        P.ts(P.dve, v[:, :W], ps_ap, bias_ap, None, ALU.add, None, [r_ps] + reads, [r_v])
        P.ts(P.dve, t1[:, :W], v[:, :W], 1.0 / (2 * math.pi), MAGIC, ALU.mult, ALU.add, [r_v], [r_t1])
        P.ts(P.dve, t1[:, :W], t1[:, :W], -MAGIC, None, ALU.add, None, [r_t1], [r_t1])
        P.stt(v[:, :W], t1[:, :W], -2 * math.pi, v[:, :W], ALU.mult, ALU.add, [r_t1, r_v], [r_v])
        P.ts(P.dve, v[:, :W], v[:, :W], -3.1415925, 3.1415925, ALU.max, ALU.min, [r_v], [r_v])
        P.actv(out_ap, v[:, :W], AF.Sin, [r_v], [r_out])

    def hy_filter(self, l, T):
        P = self.P
        li = self.lidx[l]
        L, nfc, ntc, TW, ntt = T["L"], T["nfc"], T["ntc"], T["TW"], T["ntt"]
        mk = P.push()
        w1 = P.sbuf("hw1", [33, 64], F32)
        w2 = P.sbuf("hw2", [64, 64], F32)
        w3 = P.sbuf("hw3", [64, 4, 128], F32)
        r_w = Res()
        P.ld("sp", w1[:], self.hy_w1[li], writes=[r_w])
        P.ld("sp", w2[:], self.hy_w2[li], writes=[r_w])
        P.ld("sp", w3[:].rearrange("p g c -> p (g c)"), self.w3own[li], writes=[r_w])
        dec = P.sbuf("dec", [128, ntc, 128], F32)
        r_dec = Res()
        P.ld("act", dec[:], T["decay"].rearrange("(tc p) c -> p tc c", p=128), writes=[r_dec])
        hid2 = P.sbuf("hid2", [64, L], F32)
        r_hid2 = Res()
        f_r = P.ring("ft", 2, [33, 512], F32)
        h1_r = P.ring("h1", 2, [64, 512], F32)
        v_r = P.ring("sv", 2, [64, 512], F32)
        t_r = P.ring("st1", 2, [64, 512], F32)
        b1 = self.sp_col(l, SP_B1, 1, 64)
        b2 = self.sp_col(l, SP_B2, 1, 64)
        for tt in range(ntt):
            ft, r_ft = f_r.next()
            P.ld("sp", ft[:, :TW], T["feats"][:, tt * TW:(tt + 1) * TW], writes=[r_ft])
            p0, r_p0 = self.pb[4]
            P.mm(p0[0:64, :TW], w1[:, :], ft[:, :TW], True, True, [r_w, r_ft], [r_p0])
            h1, r_h1 = h1_r.next()
            self.sin_layer(p0[0:64, :TW], b1, h1[:, :TW], TW, (v_r, t_r), [self.r_const], r_p0, r_h1)
            p1, r_p1 = self.pb[5]
            P.mm(p1[0:64, :TW], w2[:, :], h1[:, :TW], True, True, [r_w, r_h1], [r_p1])
            self.sin_layer(p1[0:64, :TW], b2, hid2[:, tt * TW:(tt + 1) * TW], TW, (v_r, t_r), [self.r_const], r_p1, r_hid2)
        ksum = P.sbuf("ksum", [128, ntc, 256], BF16)
        kdiff = P.sbuf("kdiff", [128, ntc, 256], BF16)
        r_k = [Res() for _ in range(ntc)]
        hf_r = P.ring("hf", 2, [128, 4, 128], F32)
        ab_r = P.ring("ab", 2, [128, 512], F32)
        pn, r_pn = self.pb[6]
        w3f = w3[:].rearrange("p g c -> p (g c)")
        for lc in range(ntc):
            hf, r_hf = hf_r.next()
            ph, r_ph = self.pb[lc % 4]
            P.mm(ph[:, :], hid2[:, lc * 128:(lc + 1) * 128], w3f, True, True, [r_hid2, r_w], [r_ph])
            for g in range(4):
                P.tt(P.dve, hf[:, g, :], ph[:, g * 128:(g + 1) * 128], dec[:, lc, :], ALU.mult, [r_ph, r_dec], [r_hf])
            if lc == 0:
                P.memset(P.dve, hf[0:1, 1, :], 0.0, [r_hf])
                P.memset(P.dve, hf[0:1, 3, :], 0.0, [r_hf])
            ab, r_ab = ab_r.next()
            P.actv(ab[:], hf[:].rearrange("p g c -> p (g c)"), AF.Abs, [r_hf], [r_ab])
            P.mm(pn[:, :], self.ones_f[:], ab[:], lc == 0, lc == ntc - 1, [r_ab, self.r_const], [r_pn])
            for o_ in range(2):
                P.tt(P.pool, ksum[:, lc, o_ * 128:(o_ + 1) * 128], hf[:, 2 * o_, :], hf[:, 2 * o_ + 1, :], ALU.add, [r_hf], [r_k[lc]])
                P.tt(P.pool, kdiff[:, lc, o_ * 128:(o_ + 1) * 128], hf[:, 2 * o_, :], hf[:, 2 * o_ + 1, :], ALU.subtract, [r_hf], [r_k[lc]])
        rn = P.sbuf("rn", [128, 256], F32)
        r_rn = Res()
        pns = P.sbuf("pns", [128, 512], F32)
        r_pns = Res()
        P.copy(P.dve, pns[:], pn[:, :], [r_pn], [r_pns])
        pnv = pns[:].rearrange("p (o d c) -> p o d c", o=2, d=2)
        rnv = rn[:].rearrange("p (o c) -> p o c", o=2)
        P.tt(P.dve, rnv, pnv[:, :, 0, :], pnv[:, :, 1, :], ALU.add, [r_pns], [r_rn])
        P.recip(rn[:], rn[:], [r_rn], [r_rn])
        AW = ntc * 128
        tbs = [(P.sbuf(f"ftb{i}", [128, 2 * AW], BF16), Res(), Res()) for i in range(2)]
        sst_r = P.ring("sst", 2, [128, 512], F32)
        for fc in range(nfc):
            tb, r_tc, r_ts = tbs[fc % 2]
            P.ld("sp", tb[:, 0:AW], T["Fc"][fc].rearrange("p tc f -> p (tc f)"), writes=[r_tc])
            P.ld("act", tb[:, AW:2 * AW], T["Fs"][fc].rearrange("p tc f -> p (tc f)"), writes=[r_ts])
            Fc_t = tb[:, 0:AW].rearrange("p (tc f) -> p tc f", f=128)
            Fs_t = tb[:, AW:2 * AW].rearrange("p (tc f) -> p tc f", f=128)
            pr, r_pr = self.pb[0 + 2 * (fc % 2)]
            ps_, r_ps = self.pb[1 + 2 * (fc % 2)]
            for tc in range(ntc):
                P.mm(pr[:, 0:256], Fc_t[:, tc, :], ksum[:, tc, :], tc == 0, tc == ntc - 1, [r_tc, r_k[tc]], [r_pr])
            for tc in range(ntc):
                P.mm(ps_[:, 0:256], Fs_t[:, tc, :], kdiff[:, tc, :], tc == 0, tc == ntc - 1, [r_ts, r_k[tc]], [r_ps])
            sst, r_sst = sst_r.next()
            P.tt(P.dve, sst[:, 0:256], pr[:, 0:256], rn[:], ALU.mult, [r_pr, r_rn], [r_sst])
            P.tt(P.dve, sst[:, 256:512], ps_[:, 0:256], rn[:], ALU.mult, [r_ps, r_rn], [r_sst])
            P.ld("sp", T["S"][fc * 128:(fc + 1) * 128, :], sst[:], reads=[r_sst])
        P.pop(mk)

    def own(self, l, col):
        i = l * NOWN + col
        return self.ownp[:, i:i + 1]

    def head_norm_store(self, y_ap, W, l, hg_ap, dsts, sq_r, st_r, ob_r, pbi, r_y):
        P = self.P
        st, r_st = self.rms_rstd([y_ap], W, 128, sq_r, pbi, st_r, [r_y])
        ob, r_ob = ob_r.next()
        P.stt(ob[:, :W], y_ap, hg_ap, st[:, :W], ALU.mult, ALU.mult, [r_y, r_st, self.r_const], [r_ob])
        for k, d in enumerate(dsts):
            P.dma("sp" if k % 2 == 0 else "act", lambda e, d=d, ob=ob: e.dma_start(out=d(e), in_=ob[:, :W]), reads=[r_ob])

    def yin_dsts(self, kind, c0, W):
        Y = self.YIN[kind]
        if c0 < CTX:
            return [(lambda e, d=d: Y[d * 128:(d + 1) * 128, c0:c0 + W]) for d in range(NR)]
        t0 = c0 - CTX
        d, lc = t0 // SL, CTX + t0 % SL
        return [lambda e: Y[d * 128:(d + 1) * 128, lc:lc + W]]


    def ld_u_own(self, dst2d, A, isctx, L, col0, writes, qn="sp"):
        P = self.P
        if isctx:
            P.dma(qn, lambda e: e.dma_start(out=dst2d[:, col0:col0 + L], in_=self.UC[A:A + 512, 0:L][bass.ds(self.rv(e, "128"), 128), :]), writes=writes)
        else:
            win = self.UG[A * NR:A * NR + 2048, :].rearrange("(x i) t -> i x t", i=128)
            P.dma(qn, lambda e: e.dma_start(out=dst2d[:, col0:col0 + SEQ].rearrange("p (s t) -> p s t", s=NR), in_=win[:, bass.ds(self.rv(e, "x"), NR, 2), :]), writes=writes)

    def phase_hyena(self, l, tag):
        P = self.P
        T = self.tab[tag]
        L, nfc, ntc, TW, ntt = T["L"], T["nfc"], T["ntc"], T["TW"], T["ntt"]
        off = 0 if tag == "C" else CTX
        self.hy_filter(l, T)
        mk = P.push()
        u_r = P.ring("hu", 2, [128, L + 2], F32)
        o_r = P.ring("ho", 2, [128, L], F32)
        for p_ in range(3):
            u, r_u = u_r.next()
            P.memset(P.pool, u[:, 0:1], 0.0, [r_u])
            P.memset(P.pool, u[:, L + 1:L + 2], 0.0, [r_u])
            self.ld_u_own(u, p_ * 512, tag == "C", L, 1, [r_u])
            o, r_o = o_r.next()
            P.ts(P.dve, o[:, :], u[:, 0:L], self.own(l, p_ * 3 + 0), None, ALU.mult, None, [r_u, self.r_const], [r_o])
            P.stt(o[:, :], u[:, 1:L + 1], self.own(l, p_ * 3 + 1), o[:, :], ALU.mult, ALU.add, [r_u, r_o, self.r_const], [r_o])
            P.stt(o[:, :], u[:, 2:L + 2], self.own(l, p_ * 3 + 2), o[:, :], ALU.mult, ALU.add, [r_u, r_o, self.r_const], [r_o])
            dst = self.ZT[:, off:off + L] if p_ == 0 else self.PT[(p_ - 1) * 128:p_ * 128, off:off + L]
            P.ld("act", dst, o[:, :], reads=[r_o])
        P.pop(mk)
        FG = 11 if nfc % 11 == 0 else nfc
        nfg = nfc // FG
        AW = max(ntc * 128, FG * TW)
        for o_ in range(2):
            mk = P.push()
            zT = P.sbuf("zT", [128, ntc, 128], BF16)
            r_zT = [Res() for _ in range(ntc)]
            Yr = P.sbuf("Yr", [128, nfc, 128], BF16)
            Ys = P.sbuf("Ys", [128, nfc, 128], BF16)
            r_Y = [Res() for _ in range(nfc)]
            tbs = [(P.sbuf(f"tb{i}", [128, 2 * AW], BF16), Res(), Res()) for i in range(3)]
            zl_r = P.ring("zl", 2, [128, TW], F32)
            for tt in range(ntt):
                zl, r_zl = zl_r.next()
                P.ld("sp", zl[:, :], self.ZT[:, off + tt * TW:off + (tt + 1) * TW], writes=[r_zl])
                nsub = TW // 128
                pt, r_pt = self.pb[tt % 4]
                for s_ in range(nsub):
                    P.tr(pt[:, s_ * 128:(s_ + 1) * 128], zl[:, s_ * 128:(s_ + 1) * 128], self.ident[:], [r_zl, self.r_const], [r_pt])
                P.copy(P.act if tt % 2 == 0 else P.dve, zT[:, tt * nsub:(tt + 1) * nsub, :], pt[:, 0:TW].rearrange("p (a b) -> p a b", b=128), [r_pt], [r_zT[tt * nsub]])
                for s_ in range(1, nsub):
                    r_zT[tt * nsub + s_] = r_zT[tt * nsub]
            S_r = P.ring("Sld", 3, [128, 512], F32)
            t_rs = [P.ring(f"ty{i}", 2, [128, 128], F32) for i in range(4)]
            FW = ntc * 128
            for fc in range(nfc):
                tb, r_tc, r_ts = tbs[fc % 3]
                P.ld("sp", tb[:, 0:FW], T["Fc"][fc].rearrange("p tc f -> p (tc f)"), writes=[r_tc])
                P.ld("act", tb[:, AW:AW + FW], T["Fs"][fc].rearrange("p tc f -> p (tc f)"), writes=[r_ts])
                Fc_t = tb[:, 0:FW].rearrange("p (tc f) -> p tc f", f=128)
                Fs_t = tb[:, AW:AW + FW].rearrange("p (tc f) -> p tc f", f=128)
                St, r_S = S_r.next()
                P.ld("sp", St[:], T["S"][fc * 128:(fc + 1) * 128, :], writes=[r_S])
                Sr = St[:, o_ * 128:(o_ + 1) * 128]
                Ss = St[:, 256 + o_ * 128:256 + (o_ + 1) * 128]
                pr, r_pr = self.pb[4 + 2 * (fc % 2)]
                ps_, r_ps = self.pb[5 + 2 * (fc % 2)]
                for tc in range(ntc):
                    P.mm(pr[:, 0:128], Fc_t[:, tc, :], zT[:, tc, :], tc == 0, tc == ntc - 1, [r_tc, r_zT[tc]], [r_pr])
                for tc in range(ntc):
                    P.mm(ps_[:, 0:128], Fs_t[:, tc, :], zT[:, tc, :], tc == 0, tc == ntc - 1, [r_ts, r_zT[tc]], [r_ps])
                (t1, r1), (t2, r2), (t3, r3), (t4, r4) = [r.next() for r in t_rs]
                P.tt(P.dve, t1[:], pr[:, 0:128], Sr, ALU.mult, [r_pr, r_S], [r1])
                P.tt(P.dve, t2[:], ps_[:, 0:128], Ss, ALU.mult, [r_ps, r_S], [r2])
                P.tt(P.dve, t3[:], pr[:, 0:128], Ss, ALU.mult, [r_pr, r_S], [r3])
                P.tt(P.dve, t4[:], ps_[:, 0:128], Sr, ALU.mult, [r_ps, r_S], [r4])
                P.tt(P.pool, Yr[:, fc, :], t1[:], t2[:], ALU.subtract, [r1, r2], [r_Y[fc]])
                P.tt(P.pool, Ys[:, fc, :], t3[:], t4[:], ALU.add, [r3, r4], [r_Y[fc]])
            zt_r = P.ring("zt", 2, [128, TW], F32)
            pp_r = P.ring("pp", 2, [128, TW], F32)
            tm_r = P.ring("tm", 2, [128, TW], F32)
            sq_r = P.ring("hsq", 2, [128, 512], BF16)
            st_r = P.ring("hst", 2, [128, 512], F32)
            ob_r = P.ring("hob", 2, [128, 512], BF16)
            k = 0
            for tt in range(ntt):
                c0 = off + tt * TW
                pa, r_pa = self.pb[tt % 4]
                for fg in range(nfg):
                    tb, r_tc, r_ts = tbs[k % 3]
                    k += 1
                    P.ld("sp", tb[:, 0:FG * TW], T["Ic"][tt][:, fg * FG:(fg + 1) * FG, :].rearrange("p f t -> p (f t)"), writes=[r_tc])
                    P.ld("act", tb[:, AW:AW + FG * TW], T["Is"][tt][:, fg * FG:(fg + 1) * FG, :].rearrange("p f t -> p (f t)"), writes=[r_ts])
                    Ic_t = tb[:, 0:FG * TW].rearrange("p (f t) -> p f t", t=TW)
                    Is_t = tb[:, AW:AW + FG * TW].rearrange("p (f t) -> p f t", t=TW)
                    for f_ in range(FG):
                        fc = fg * FG + f_
                        P.mm(pa[:, :TW], Yr[:, fc, :], Ic_t[:, f_, :], fc == 0, False, [r_Y[fc], r_tc], [r_pa])
                        P.mm(pa[:, :TW], Ys[:, fc, :], Is_t[:, f_, :], False, fc == nfc - 1, [r_Y[fc], r_ts], [r_pa])
                zt, r_zt = zt_r.next()
                pp, r_pp = pp_r.next()
                tm, r_tm = tm_r.next()
                P.ld("sp", zt[:], self.ZT[:, c0:c0 + TW], writes=[r_zt])
                P.ld("act", pp[:], self.PT[o_ * 128:(o_ + 1) * 128, c0:c0 + TW], writes=[r_pp])
                P.stt(tm[:], zt[:], self.own(l, 9 + o_), pa[:, :TW], ALU.mult, ALU.add, [r_zt, r_pa, self.r_const], [r_tm])
                P.tt(P.pool, zt[:], tm[:], pp[:], ALU.mult, [r_tm, r_pp], [r_zt])
                if o_ == 0:
                    P.ld("sp", self.ZT[:, c0:c0 + TW], zt[:], reads=[r_zt])
                else:
                    self.head_norm_store(zt[:], TW, l, self.own(l, 11), self.yin_dsts(0, c0, TW), sq_r, st_r, ob_r, 4 + tt % 4, r_zt)
            P.pop(mk)

    def phase_lru(self, l, need_ctx):
        P = self.P
        li = self.lidx[l]
        mk = P.push()
        c8 = P.sbuf("c8", [128, 2], F32)
        r_c8 = Res()
        P.actv(c8[:], self.ownp[:, l * NOWN + 25:l * NOWN + 27], AF.Exp, [self.r_const], [r_c8], scale=-1.0)
        P.actv(c8[:], c8[:], AF.Ln, [r_c8, self.r_const], [r_c8], bias=self.ones_f[:, 0:1], scale=1.0)
        P.ts(P.dve, c8[:], c8[:], -8.0, None, ALU.mult, None, [r_c8], [r_c8])
        wt = P.sbuf("wab", [128, 4, 128], BF16)
        r_wt = Res()
        for d in range(2):
            for k, Wsrc in enumerate((self.lwa_own, self.lwx_own)):
                P.ld("pool", wt[:, d * 2 + k, :], Wsrc[li, d], writes=[r_wt])
        seqs = [(0, CTX, True), (CTX, SEQ, False)]
        sq_r = P.ring("lsq", 2, [128, 512], BF16)
        st_r = P.ring("lst", 2, [128, 512], F32)
        ob_r = P.ring("lob", 2, [128, 512], BF16)
        for (off, Ls, isctx) in seqs:
            mk2 = P.push()
            t = "c" if isctx else "l"
            xr_r = P.ring("xr" + t, 2, [128, Ls + 6], F32)
            xc, r_xc = P.sbuf("xc" + t, [128, Ls], F32), Res()
            xcb, r_xcb = P.sbuf("xcb" + t, [128, Ls], BF16), Res()
            ra, r_ra = P.sbuf("ra" + t, [128, Ls], F32), Res()
            ib, r_ib = P.sbuf("ib" + t, [128, Ls], F32), Res()
            ta, r_ta = P.sbuf("ta" + t, [128, Ls], F32), Res()
            hs = [(P.sbuf(f"h{d}" + t, [128, Ls], F32), Res()) for d in range(2)]
            W = min(512, Ls)
            for d in range(2):
                xr, r_xr = xr_r.next()
                P.memset(P.pool, xr[:, 0:3], 0.0, [r_xr])
                P.memset(P.pool, xr[:, Ls + 3:Ls + 6], 0.0, [r_xr])
                self.ld_u_own(xr, OFF_X, isctx, Ls, 3, [r_xr], qn="act")
                left = 3 if d == 0 else 0
                for jj in range(4):
                    s0 = 3 + jj - left
                    wj = self.own(l, 13 + d * 4 + jj)
                    if jj == 0:
                        P.ts(P.dve, xc[:, :], xr[:, s0:s0 + Ls], wj, None, ALU.mult, None, [r_xr, self.r_const], [r_xc])
                    else:
                        P.stt(xc[:, :], xr[:, s0:s0 + Ls], wj, xc[:, :], ALU.mult, ALU.add, [r_xr, r_xc, self.r_const], [r_xc])
                P.copy(P.pool, xcb[:, :], xc[:, :], [r_xc], [r_xcb])
                for ti in range(Ls // W):
                    cs = slice(ti * W, (ti + 1) * W)
                    pa, r_pa = self.pb[(2 * ti) % 4]
                    px, r_px = self.pb[(2 * ti + 1) % 4]
                    P.mm(pa[:, :W], wt[:, d * 2 + 0, :], xcb[:, cs], True, True, [r_wt, r_xcb], [r_pa])
                    P.mm(px[:, :W], wt[:, d * 2 + 1, :], xcb[:, cs], True, True, [r_wt, r_xcb], [r_px])
                    P.actv(ra[:, cs], pa[:, :W], AF.Sigmoid, [r_pa, self.r_const], [r_ra], bias=self.own(l, 21 + d))
                    P.actv(ib[:, cs], px[:, :W], AF.Sigmoid, [r_px, self.r_const], [r_ib], bias=self.own(l, 23 + d))
                P.actv(ra[:, :], ra[:, :], AF.Exp, [r_ra, r_c8], [r_ra], scale=c8[:, d:d + 1])
                P.tt(P.pool, ta[:, :], ra[:, :], ra[:, :], ALU.mult, [r_ra], [r_ta])
                P.ts(P.pool, ta[:, :], ta[:, :], -1.0, 1.0, ALU.mult, ALU.add, [r_ta], [r_ta])
                P.actv(ta[:, :], ta[:, :], AF.Sqrt, [r_ta], [r_ta])
                P.tt(P.dve, ib[:, :], ib[:, :], xc[:, :], ALU.mult, [r_ib, r_xc], [r_ib])
                P.tt(P.dve, ib[:, :], ib[:, :], ta[:, :], ALU.mult, [r_ib, r_ta], [r_ib])
                if not isctx:
                    f0 = Ls - 1 if d == 1 else 0
                    P.stt(ib[:, f0:f0 + 1], ra[:, f0:f0 + 1], self.lru_h0[:, d:d + 1], ib[:, f0:f0 + 1],
                          ALU.mult, ALU.add, [r_ra, r_ib, self.r_h0], [r_ib])
                h, r_h = hs[d]
                if d == 0:
                    P.op(P.dve, lambda e, h=h, ra=ra, ib=ib: e.tensor_tensor_scan(out=h[:, :], data0=ra[:, :], data1=ib[:, :], initial=0.0, op0=ALU.mult, op1=ALU.add), [r_ra, r_ib], [r_h])
                else:
                    P.op(P.dve, lambda e, h=h, ra=ra, ib=ib: e.tensor_tensor_scan(out=h[:, ::-1], data0=ra[:, ::-1], data1=ib[:, ::-1], initial=0.0, op0=ALU.mult, op1=ALU.add), [r_ra, r_ib], [r_h])
                if isctx:
                    f1 = Ls - 1 if d == 0 else 0
                    P.copy(P.dve, self.lru_h0[:, d:d + 1], h[:, f1:f1 + 1], [r_h], [self.r_h0])
            if not (isctx and not need_ctx):
                xr, r_xr = xr_r.next()
                self.ld_u_own(xr, OFF_G, isctx, Ls, 0, [r_xr], qn="act")
                P.actv(xr[:, 0:Ls], xr[:, 0:Ls], AF.Gelu_apprx_tanh, [r_xr], [r_xr])
                (h0_, r_h0_), (h1_, r_h1_) = hs
                P.tt(P.pool, h0_[:, :], h0_[:, :], h1_[:, :], ALU.add, [r_h0_, r_h1_], [r_h0_])
                P.tt(P.dve, h0_[:, :], h0_[:, :], xr[:, 0:Ls], ALU.mult, [r_h0_, r_xr], [r_h0_])
                for ti in range(Ls // W):
                    self.head_norm_store(h0_[:, ti * W:(ti + 1) * W], W, l, self.own(l, 12), self.yin_dsts(1, off + ti * W, W), sq_r, st_r, ob_r, 4 + ti % 4, r_h0_)
            P.pop(mk2)
        P.pop(mk)

    def phase_attn(self, l, need_ctx):
        P = self.P
        li = self.lidx[l]
        mk = P.push()
        ckv = P.sbuf("ckv", [128, 2, NT], BF16)
        r_ckv = Res()
        krope = P.sbuf("krope", [64, NT], BF16)
        r_kr = Res()
        cq = P.sbuf("cq", [128, 4, NTL], BF16)
        r_cq = Res()
        sq_r = P.ring("asq", 3, [128, 512], BF16)
        st_r = P.ring("ast", 2, [128, 512], F32)
        u2_r = P.ring("au2", 2, [128, 2, 512], F32)
        u4_r = P.ring("au4", 2, [128, 4, 512], F32)
        kr_r = P.ring("akr", 2, [64, 2, 512], F32)
        cs_r = P.ring("acs", 2, [64, 2, 512], F32)
        tk_r = P.ring("atk", 2, [64, 2, 512], F32)
        for (c0, W, j) in self.tok_tiles(True):
            u2, r_u2 = u2_r.next()
            P.ld("sp", u2[:, :, :W], self.Uap(OFF_KV, 256, c0, W).rearrange("(c p) t -> p c t", p=128), writes=[r_u2])
            st, r_st = self.rms_rstd([u2[:, c, :W] for c in range(2)], W, 256, sq_r, 7, st_r, [r_u2])
            for c in range(2):
                P.stt(ckv[:, c, c0:c0 + W], u2[:, c, :W], self.sp_col(l, SP_GKV + c), st[:, :W], ALU.mult, ALU.mult, [r_u2, r_st, self.r_const], [r_ckv])
            kr, r_krt = kr_r.next()
            P.ld("act", kr[:, 0, :W], self.Uap(OFF_KR, 64, c0, W), writes=[r_krt])
            if j == 0:
                P.ld("act", kr[:, 1, :W], self.Uap(OFF_KRR, 64, c0, W), writes=[r_krt])
                cs, r_cs = cs_r.next()
                P.ld("act", cs[:, :, :W], self.rope_cs[:, :, c0 - CTX:c0 - CTX + W], writes=[r_cs])
                tk, r_tk = tk_r.next()
                P.tt(P.pool, tk[:, :, :W], kr[:, :, :W], cs[:, :, :W], ALU.mult, [r_krt, r_cs], [r_tk])
                P.tt(P.pool, krope[:, c0:c0 + W], tk[:, 0, :W], tk[:, 1, :W], ALU.add, [r_tk], [r_kr])
            else:
                P.copy(P.pool, krope[:, c0:c0 + W], kr[:, 0, :W], [r_krt], [r_kr])
        qtiles = self.loc_tiles(need_ctx)
        for (c0, W, j) in qtiles:
            u4, r_u4 = u4_r.next()
            srcq = self.UC[OFF_Q:OFF_Q + 512, c0:c0 + W] if j == 1 else self.ULOC[OFF_Q:OFF_Q + 512, c0 - CTX:c0 - CTX + W]
            P.ld("sp", u4[:, :, :W], srcq.rearrange("(c p) t -> p c t", p=128), writes=[r_u4])
            st, r_st = self.rms_rstd([u4[:, c, :W] for c in range(4)], W, 512, sq_r, 6, st_r, [r_u4])
            for c in range(4):
                P.stt(cq[:, c, c0:c0 + W], u4[:, c, :W], self.sp_col(l, SP_GQ + c), st[:, :W], ALU.mult, ALU.mult, [r_u4, r_st, self.r_const], [r_cq])
        wkv_r = P.ring("wkv", 2, [128, 2, 256], BF16)
        wq_r = P.ring("wq", 2, [128, 4, 256], BF16)
        kn_r = P.ring("kn", 2, [128, NT], BF16)
        v_r = P.ring("vv", 2, [128, NT // 128, 128], BF16)
        qn_r = P.ring("qn", 2, [128, NTL], BF16)
        qr_r = P.ring("qr", 2, [64, NTL], BF16)
        pT_r = P.ring("pT", 4, [128, 512], BF16)
        ri_r = P.ring("ri", 2, [128, 512], F32)
        oo_r = P.ring("oo", 2, [128, 512], F32)
        ob_r = P.ring("aob", 2, [128, 512], BF16)
        WKV = self.WUKV[li].rearrange("(kc p) n -> p kc n", p=128)
        WQ = self.WUQ[li].rearrange("(kc p) n -> p kc n", p=128)
        kb = 0
        for h in range(8):
            wkv, r_wkv = wkv_r.next()
            wq, r_wq = wq_r.next()
            P.ld("sp", wkv[:], WKV[:, :, h * 256:(h + 1) * 256], writes=[r_wkv])
            P.ld("sp", wq[:], WQ[:, :, h * 256:(h + 1) * 256], writes=[r_wq])
            kn, r_kn = kn_r.next()
            vv, r_vv = v_r.next()
            qn, r_qn = qn_r.next()
            qr, r_qr = qr_r.next()
            for (c0, W, j) in self.tok_tiles(True):
                pt, r_pt = self.pb[kb % 4]
                kb += 1
                for kc in range(2):
                    P.mm(pt[:, :W], wkv[:, kc, 0:128], ckv[:, kc, c0:c0 + W], kc == 0, kc == 1, [r_wkv, r_ckv], [r_pt])
                P.copy(P.act, kn[:, c0:c0 + W], pt[:, :W], [r_pt], [r_kn])
            for (c0, W, j) in qtiles:
                pt, r_pt = self.pb[kb % 4]
                kb += 1
                for kc in range(4):
                    P.mm(pt[:, :W], wq[:, kc, 0:128], cq[:, kc, c0:c0 + W], kc == 0, kc == 3, [r_wq, r_cq], [r_pt])
                P.copy(P.dve, qn[:, c0:c0 + W], pt[:, :W], [r_pt], [r_qn])
                pt, r_pt = self.pb[kb % 4]
                kb += 1
                for kc in range(4):
                    P.mm(pt[0:64, :W], wq[:, kc, 128:192], cq[:, kc, c0:c0 + W], kc == 0, kc == 3, [r_wq, r_cq], [r_pt])
                if j == 0:
                    pt2, r_pt2 = self.pb[kb % 4]
                    kb += 1
                    for kc in range(4):
                        P.mm(pt2[0:64, :W], wq[:, kc, 192:256], cq[:, kc, c0:c0 + W], kc == 0, kc == 3, [r_wq, r_cq], [r_pt2])
                    cs, r_cs = cs_r.next()
                    P.ld("act", cs[:, :, :W], self.rope_q[:, :, c0 - CTX:c0 - CTX + W], writes=[r_cs])
                    tk, r_tk = tk_r.next()
                    P.tt(P.dve, tk[:, 0, :W], pt[0:64, :W], cs[:, 0, :W], ALU.mult, [r_pt, r_cs], [r_tk])
                    P.tt(P.dve, tk[:, 1, :W], pt2[0:64, :W], cs[:, 1, :W], ALU.mult, [r_pt2, r_cs], [r_tk])
                    P.tt(P.pool, qr[:, c0:c0 + W], tk[:, 0, :W], tk[:, 1, :W], ALU.add, [r_tk], [r_qr])
                else:
                    P.copy(P.dve, qr[:, c0:c0 + W], pt[0:64, :W], [r_pt], [r_qr])
            for kg in range(0, NT // 128, 4):
                nk4 = min(4, NT // 128 - kg)
                pt, r_pt = self.pb[kb % 4]
                kb += 1
                for q4 in range(nk4):
                    kc = kg + q4
                    for k2 in range(2):
                        P.mm(pt[:, q4 * 128:(q4 + 1) * 128], ckv[:, k2, kc * 128:(kc + 1) * 128], wkv[:, k2, 128:256], k2 == 0, k2 == 1, [r_ckv, r_wkv], [r_pt])
                P.copy(P.act, vv[:, kg:kg + nk4, :], pt[:, 0:nk4 * 128].rearrange("p (a b) -> p a b", b=128), [r_pt], [r_vv])
            for qi, (c0, W, j) in enumerate(qtiles):
                nk = 2 if j == 1 else NT // 128
                po, r_po = self.pb[4 + 2 * (qi % 2)]
                pl, r_pl = self.pb[5 + 2 * (qi % 2)]
                pend = None
                for kc in range(nk + 1):
                    if kc < nk:
                        pS, r_pS = self.pb[kc % 4]
                        P.mm(pS[:, :W], kn[:, kc * 128:(kc + 1) * 128], qn[:, c0:c0 + W], True, False, [r_kn, r_qn], [r_pS])
                        P.mm(pS[:, :W], krope[:, kc * 128:(kc + 1) * 128], qr[:, c0:c0 + W], False, True, [r_kr, r_qr], [r_pS])
                        pT, r_pT = pT_r.next()
                        P.actv(pT[:, :W], pS[:, :W], AF.Exp, [r_pS], [r_pT], scale=MLA_SCALE)
                    if pend is not None:
                        pkc, ppT, pr_pT = pend
                        P.mm(po[:, :W], vv[:, pkc, :], ppT[:, :W], pkc == 0, pkc == nk - 1, [r_vv, pr_pT], [r_po])
                        P.mm(pl[:, :W], self.ones_b[:], ppT[:, :W], pkc == 0, pkc == nk - 1, [self.r_const, pr_pT], [r_pl])
                    pend = (kc, pT, r_pT) if kc < nk else None
                ri, r_ri = ri_r.next()
                P.recip(ri[:, :W], pl[:, :W], [r_pl], [r_ri])
                oo, r_oo = oo_r.next()
                P.tt(P.dve, oo[:, :W], po[:, :W], ri[:, :W], ALU.mult, [r_po, r_ri], [r_oo])
                dsts = [lambda e, h=h, c0=c0, W=W: self.YM[h * 128:(h + 1) * 128, c0:c0 + W]]
                self.head_norm_store(oo[:, :W], W, l, self.sp_col(l, SP_HG + 8 + h), dsts, sq_r, st_r, ob_r, 4 + 2 * (qi % 2), r_oo)
        P.pop(mk)

    def phase_outproj(self, l, need_ctx):
        P = self.P
        li = self.lidx[l]
        mk = P.push()
        yc_r = P.ring("yc", 2, [128, 16, 512], BF16)
        w_r = P.ring("wo", 2, [128, 16, 512], BF16)
        mix_r = P.ring("mix", 1, [128, 16, 512], F32)
        xs_r = P.ring("oxs", 1, [128, 16, 512], F32)
        hb_r = P.ring("ohb", 1, [128, 16, 512], BF16)
        sq_r = P.ring("osq", 3, [128, 512], BF16)
        st_r = P.ring("ost", 2, [128, 512], F32)
        tmp_r = P.ring("otmp", 3, [128, 512], F32)
        YMv = self.YM.rearrange("(c p) t -> p c t", p=128)
        XTv = self.XT.rearrange("(c p) t -> p c t", p=128)
        H2v = self.H2.rearrange("(c p) t -> p c t", p=128)
        Wv = self.WOUT[li].rearrange("(kc p) n -> p kc n", p=128)
        kb = 0
        ltiles = self.loc_tiles(need_ctx)
        for (c0, W, j) in ltiles:
            yc, r_yc = yc_r.next()
            P.ld("sp", yc[:, 8:16, :W], YMv[:, :, c0:c0 + W], writes=[r_yc])
            for kind in range(2):
                def ldy(e, yc=yc, c0=c0, W=W, kind=kind):
                    win = self.YG[kind].rearrange("(x i) t -> i x t", i=128)
                    return e.dma_start(out=yc[:, kind * 4:(kind + 1) * 4, :W], in_=win[:, bass.ds(self.rv(e, "x"), NR, 2), c0:c0 + W])
                P.dma("pool", ldy, writes=[r_yc])
            xs, r_xs = xs_r.next()
            P.ld("sp", xs[:, :, :W], XTv[:, :, c0:c0 + W], writes=[r_xs])
            mix, r_mix = mix_r.next()
            for g in range(4):
                wt, r_wt = w_r.next()
                P.ld("act", wt[:], Wv[:, :, g * 512:(g + 1) * 512], writes=[r_wt])
                for m in range(4):
                    pt, r_pt = self.pb[kb % 6]
                    kb += 1
                    for kc in range(16):
                        P.mm(pt[:, :W], wt[:, kc, m * 128:(m + 1) * 128], yc[:, kc, :W], kc == 0, kc == 15, [r_wt, r_yc], [r_pt])
                    P.copy(P.act if kb % 2 == 0 else P.dve, mix[:, g * 4 + m, :W], pt[:, :W], [r_pt], [r_mix])
            st, r_st = self.rms_rstd([mix[:, c, :W] for c in range(16)], W, D, sq_r, 7, st_r, [r_mix])
            for c in range(16):
                tmp, r_tmp = tmp_r.next()
                P.tt(P.pool, tmp[:, :W], mix[:, c, :W], st[:, :W], ALU.mult, [r_mix, r_st], [r_tmp])
                P.stt(xs[:, c, :W], tmp[:, :W], self.mod(2, c, j), xs[:, c, :W], ALU.mult, ALU.add, [r_tmp, r_xs, self.r_mods], [r_xs])
            P.ld("sp", XTv[:, :, c0:c0 + W], xs[:, :, :W], reads=[r_xs])
            st, r_st = self.rms_rstd([xs[:, c, :W] for c in range(16)], W, D, sq_r, 6, st_r, [r_xs])
            hb, r_hb = hb_r.next()
            for c in range(16):
                tmp, r_tmp = tmp_r.next()
                P.tt(P.dve, tmp[:, :W], xs[:, c, :W], st[:, :W], ALU.mult, [r_xs, r_st], [r_tmp])
                P.actv(hb[:, c, :W], tmp[:, :W], AF.Identity, [r_tmp, self.r_mods], [r_hb], bias=self.mod(4, c, j), scale=self.mod(3, c, j))
            P.ld("sp", H2v[:, :, c0:c0 + W], hb[:, :, :W], reads=[r_hb])
            HBv = self.HB.rearrange("(c p) k -> p c k", p=128)
            if c0 == CTX:
                P.ld("act", HBv[:, :, 0:1], hb[:, :, 0:1], reads=[r_hb], slow=True)
            if c0 + W == NTL:
                P.ld("act", HBv[:, :, 1:2], hb[:, :, W - 1:W], reads=[r_hb], slow=True)
        P.pop(mk)
        self.allgather([(self.HB, self.HG)])
        P.ld("sp", self.HGP[D:(NR + 1) * D, :], self.HG[:, :])
        P.barrier()

    def phase_ffn(self, l, need_ctx):
        P = self.P
        li = self.lidx[l]
        mk = P.push()
        OT = 342
        tiles = []
        if need_ctx:
            tiles.append((0, CTX, 0, CTX, 1))
        for lo in range(0, SL, OT):
            tiles.append((CTX, SL, lo, min(SL, lo + OT), 0))
        WT = OT + 2
        h2_r = P.ring("fh2", 1, [128, 16, WT], BF16)
        act = P.sbuf("fact", [128, 44, OT], BF16)
        r_act = [Res() for _ in range(44)]
        wu_r = P.ring("fwu", 2, [128, 2, 16, 256], BF16)
        wd_r = P.ring("fwd", 2, [128, 44, 128], BF16)
        f_r = P.ring("ff", 1, [128, 16, OT], F32)
        xs_r = P.ring("fxs", 1, [128, 16, OT], F32)
        gc_r = P.ring("fgc", 2, [128, OT], F32)
        vc_r = P.ring("fvc", 2, [128, OT], F32)
        sq_r = P.ring("fsq", 3, [128, 512], BF16)
        st_r = P.ring("fst_", 2, [128, 512], F32)
        tmp_r = P.ring("ftmp", 3, [128, OT], F32)
        XTv = self.XT.rearrange("(c p) t -> p c t", p=128)
        H2v = self.H2.rearrange("(c p) t -> p c t", p=128)
        WU = self.WUP[li].rearrange("(kc p) n -> p kc n", p=128)
        WD = self.WDN[li].rearrange("(kc p) n -> p kc n", p=128)
        kb = 0
        for (off, Ls, lo, hi, j) in tiles:
            Wo = hi - lo
            Wt = Wo + 2
            h2, r_h2 = h2_r.next()
            a, b = lo - 1, hi + 1
            ca, cb = 0, Wt
            HGPv = self.HGP.rearrange("(c p) k -> p c k", p=128)
            if a < 0:
                if j == 1:
                    P.memset(P.pool, h2[:, :, 0:1], 0.0, [r_h2])
                else:
                    P.dma("act", lambda e, h2=h2: e.dma_start(out=h2[:, :, 0:1], in_=HGPv[:, 0:64, 1:2][:, bass.ds(self.rv(e, "16"), 16), :], allow_slow_non_contiguous=True), writes=[r_h2])
                a, ca = 0, 1
            if b > Ls:
                if j == 1:
                    P.memset(P.pool, h2[:, :, Wt - 1:Wt], 0.0, [r_h2])
                else:
                    P.dma("act", lambda e, h2=h2, Wt=Wt: e.dma_start(out=h2[:, :, Wt - 1:Wt], in_=HGPv[:, 32:96, 0:1][:, bass.ds(self.rv(e, "16"), 16), :], allow_slow_non_contiguous=True), writes=[r_h2])
                b, cb = Ls, Wt - 1
            P.ld("sp", h2[:, :, ca:cb], H2v[:, :, off + a:off + b], writes=[r_h2])
            for i in range(44):
                if i % 2 == 0:
                    wu, r_wu = wu_r.next()
                    P.ld("act", wu[:, 0, :, :], WU[:, :, i * 128:i * 128 + 256], writes=[r_wu])
                    P.ld("act", wu[:, 1, :, :], WU[:, :, DFF + i * 128:DFF + i * 128 + 256], writes=[r_wu])
                s0 = (i % 2) * 128
                pg, r_pg = self.pb[(2 * i) % 6]
                pv, r_pv = self.pb[(2 * i + 1) % 6]
                for kc in range(16):
                    P.mm(pg[:, :Wt], wu[:, 0, kc, s0:s0 + 128], h2[:, kc, :Wt], kc == 0, kc == 15, [r_wu, r_h2], [r_pg])
                for kc in range(16):
                    P.mm(pv[:, :Wt], wu[:, 1, kc, s0:s0 + 128], h2[:, kc, :Wt], kc == 0, kc == 15, [r_wu, r_h2], [r_pv])
                gc, r_gc = gc_r.next()
                vc, r_vc = vc_r.next()
                for (pp, r_pp, oc, r_oc, ci) in ((pg, r_pg, gc, r_gc, i), (pv, r_pv, vc, r_vc, 44 + i)):
                    P.ts(P.dve, oc[:, :Wo], pp[:, 0:Wo], self.sp_col(l, SP_FC + 0 * 88 + ci), None, ALU.mult, None, [r_pp, self.r_const], [r_oc])
                    P.stt(oc[:, :Wo], pp[:, 1:Wo + 1], self.sp_col(l, SP_FC + 1 * 88 + ci), oc[:, :Wo], ALU.mult, ALU.add, [r_pp, r_oc, self.r_const], [r_oc])
                    P.stt(oc[:, :Wo], pp[:, 2:Wo + 2], self.sp_col(l, SP_FC + 2 * 88 + ci), oc[:, :Wo], ALU.mult, ALU.add, [r_pp, r_oc, self.r_const], [r_oc])
                P.actv(gc[:, :Wo], gc[:, :Wo], AF.Gelu_apprx_tanh, [r_gc], [r_gc])
                P.tt(P.pool, act[:, i, :Wo], gc[:, :Wo], vc[:, :Wo], ALU.mult, [r_gc, r_vc], [r_act[i]])
            f, r_f = f_r.next()
            for m in range(16):
                wd, r_wd = wd_r.next()
                P.ld("act", wd[:], WD[:, :, m * 128:(m + 1) * 128], writes=[r_wd])
                pd, r_pd = self.pb[6 + m % 2]
                for kc in range(44):
                    P.mm(pd[:, :Wo], wd[:, kc, :], act[:, kc, :Wo], kc == 0, kc == 43, [r_wd, r_act[kc]], [r_pd])
                P.copy(P.act if m % 2 == 0 else P.dve, f[:, m, :Wo], pd[:, :Wo], [r_pd], [r_f])
            st, r_st = self.rms_rstd([f[:, c, :Wo] for c in range(16)], Wo, D, sq_r, 0, st_r, [r_f])
            xs, r_xs = xs_r.next()
            P.ld("sp", xs[:, :, :Wo], XTv[:, :, off + lo:off + hi], writes=[r_xs])
            for c in range(16):
                tmp, r_tmp = tmp_r.next()
                P.tt(P.pool, tmp[:, :Wo], f[:, c, :Wo], st[:, :Wo], ALU.mult, [r_f, r_st], [r_tmp])
                P.stt(xs[:, c, :Wo], tmp[:, :Wo], self.mod(5, c, j), xs[:, c, :Wo], ALU.mult, ALU.add, [r_tmp, r_xs, self.r_mods], [r_xs])
            P.ld("sp", XTv[:, :, off + lo:off + hi], xs[:, :, :Wo], reads=[r_xs])
        P.pop(mk)


def make_inmaps(inp, n_cores=8, layers=None):
    c = _consts()
    layers = list(range(DEPTH)) if layers is None else list(layers)
    shared = {k: np.ascontiguousarray(np.asarray(inp[k], dtype=np.float32)[layers]) for k in (
        "ada_w", "w_in", "hy_w1", "hy_w2", "mla_wuq", "mla_wukv", "w_out", "ffn_up", "ffn_down")}
    shared["smallp"] = _pack_small({k: np.asarray(v, dtype=np.float32) for k, v in inp.items()}).reshape(128, DEPTH * NSP)
    shared.update({k: v for k, v in c.items() if not k.startswith("decay")})
    spf = shared["smallp"].reshape(128, DEPTH, NSP)
    own_map = []
    for p_ in range(3):
        for j in range(3):
            own_map.append(SP_HYC + j * 12 + p_ * 4)
    own_map += [SP_HYB + 0, SP_HYB + 4, SP_HG, SP_HG + 4]
    for d in range(2):
        for j in range(4):
            own_map.append(SP_LC + d * 16 + j * 4)
    own_map += [SP_BA, SP_BA + 4, SP_BX, SP_BX + 4, SP_LAM, SP_LAM + 4]
    w3 = np.asarray(inp["hy_w3"], dtype=np.float32)[layers].reshape(len(layers), 64, 4, 4, 128)
    lwa = np.asarray(inp["lru_wa"], dtype=np.float32)[layers]
    lwx = np.asarray(inp["lru_wx"], dtype=np.float32)[layers]
    maps = []
    for core in range(n_cores):
        b = core // NR
        r = core % NR
        m = dict(shared)
        m["x"] = np.ascontiguousarray(np.asarray(inp["x"][b, r * SL:(r + 1) * SL], dtype=np.float32))
        m["ownp"] = np.ascontiguousarray(spf[:, :, [cc + r for cc in own_map]].reshape(128, DEPTH * NOWN))
        m["hy_w3_own"] = np.ascontiguousarray(w3[:, :, :, r, :].reshape(len(layers), 64, 512))
        m["lru_wa_own"] = np.ascontiguousarray(lwa[:, :, r])
        m["lru_wx_own"] = np.ascontiguousarray(lwx[:, :, r])
        m["rope_q"] = np.ascontiguousarray(c["rope_cs"][:, :, r * SL:(r + 1) * SL])
        m["decayL"] = np.ascontiguousarray(c["decayL"][:, r * 128:(r + 1) * 128])
        m["decayC"] = np.ascontiguousarray(c["decayC"][:, r * 128:(r + 1) * 128])
        m["ctx"] = np.ascontiguousarray(np.asarray(inp["ctx"][b], dtype=np.float32))
        cv = np.stack([np.asarray(inp["c"][b], dtype=np.float32), np.asarray(inp["c_ctx"], dtype=np.float32)], axis=-1)
        m["cvec"] = np.ascontiguousarray(cv.reshape(16, 128, 2).transpose(1, 0, 2).reshape(128, 32))
        maps.append(m)
    return maps


_NC_CACHE = {}


def kernel(**inputs):
    if "nc" not in _NC_CACHE:
        _NC_CACHE["nc"] = Builder(range(DEPTH)).build()
    nc = _NC_CACHE["nc"]
    maps = make_inmaps(inputs)
    res = run_bass_kernel_spmd(nc, maps, core_ids=list(range(8)))
    out = np.stack([np.concatenate([np.asarray(res.results[b * NR + r]["out"]) for r in range(NR)], axis=0) for b in range(2)], axis=0)
    return out.astype(np.float32)
```

```python
import math
import numpy as np
import ml_dtypes
import concourse.bass as bass
import concourse.mybir as mybir
from concourse.bass_utils import run_bass_kernel_spmd

F32 = mybir.dt.float32
BF16 = mybir.dt.bfloat16
AF = mybir.ActivationFunctionType
ALU = mybir.AluOpType

D = 2048
SEQ = 4096
CTX = 256
NT = SEQ + CTX
DEPTH = 4
NR = 4
SL = SEQ // NR
NTL = CTX + SL
GROUPS = [[0, 1, 2, 3], [4, 5, 6, 7]]
NOWN = 27
UCH = 14
DIN = 3392
DINX = 3456
OFF_HY, OFF_G, OFF_Q, OFF_X, OFF_KV, OFF_KR, OFF_KRR = 0, 1536, 2048, 2560, 3072, 3328, 3392
DFF = 5632
EPS = 1e-6
MLA_SCALE = 192 ** -0.5
MAGIC = 12582912.0

SP_ADAB, SP_NG, SP_HYC, SP_HYB, SP_LC, SP_BA, SP_BX, SP_LAM = 0, 96, 160, 196, 204, 236, 244, 252
SP_GQ, SP_GKV, SP_HG, SP_FC, SP_B1, SP_B2, NSP = 260, 264, 266, 282, 546, 547, 548


class Res:
    __slots__ = ("name", "w", "r")

    def __init__(self, name=""):
        self.name = name
        self.w = None
        self.r = {}


class EngQ:
    def __init__(self, name, eng, inorder=False):
        self.name = name
        self.eng = eng
        self.inorder = inorder
        self.sem = None
        self.step = 1
        self.count = 0
        self.waited = {}
        self.thunks = []


class DmaSlot:
    def __init__(self, sem, name):
        self.sem = sem
        self.name = name
        self.count = 0
        self.step = 16
        self.inorder = False


class Ring:
    def __init__(self, items):
        self.items = items
        self.i = 0

    def next(self):
        it = self.items[self.i % len(self.items)]
        self.i += 1
        return it


class Prog:
    def __init__(self, nc, n_dma_slots=20, same_engine_sync=True):
        self.nc = nc
        self.same_engine_sync = same_engine_sync
        self.ctx = []
        self.pe = EngQ("pe", nc.tensor, inorder=True)
        self.dve = EngQ("dve", nc.vector)
        self.act = EngQ("act", nc.scalar)
        self.pool = EngQ("pool", nc.gpsimd)
        self.sp = EngQ("sp", nc.sync)
        self.engs = [self.pe, self.dve, self.act, self.pool, self.sp]
        for e in self.engs:
            e.sem = self._sem("s_" + e.name)
        self.slots = {}
        for qn in ("sp", "act", "pool"):
            self.slots[qn] = [DmaSlot(self._sem(f"d_{qn}{i}"), f"d_{qn}{i}") for i in range(n_dma_slots)]
        self.slot_rr = {"sp": 0, "act": 0, "pool": 0}
        self.n_inst = 0
        self.uid = 0

    def _sem(self, name):
        cm = self.nc.semaphore(name)
        s = cm.__enter__()
        self.ctx.append(cm)
        return s

    def push(self):
        return len(self.ctx)

    def pop(self, mark):
        self.barrier()
        while len(self.ctx) > mark:
            self.ctx.pop().__exit__(None, None, None)

    def sbuf(self, name, shape, dtype):
        self.uid += 1
        cm = self.nc.sbuf_tensor(f"{name}_{self.uid}", list(shape), dtype)
        t = cm.__enter__()
        self.ctx.append(cm)
        return t

    def psum(self, name, shape, dtype=F32):
        cm = self.nc.psum_tensor(name, list(shape), dtype)
        t = cm.__enter__()
        self.ctx.append(cm)
        return t

    def ring(self, name, n, shape, dtype):
        return Ring([(self.sbuf(f"{name}{i}", shape, dtype), Res(f"{name}{i}")) for i in range(n)])

    def _deps(self, reads, writes):
        deps = []
        for r in reads:
            if r.w is not None:
                deps.append(r.w)
        for w in writes:
            if w.w is not None:
                deps.append(w.w)
            deps.extend(w.r.values())
        return deps

    def _emit_waits(self, q, deps):
        need = {}
        for (src, c) in deps:
            if src is q and (q.inorder or not self.same_engine_sync):
                continue
            if q.waited.get(id(src), 0) >= c:
                continue
            if need.get(id(src), (None, 0))[1] < c:
                need[id(src)] = (src, c)
        for src, c in need.values():
            q.waited[id(src)] = c
            sem, val = src.sem, c * src.step
            q.thunks.append(lambda e, sem=sem, val=val: e.wait_ge(sem, val))

    def _mark(self, src, c, reads, writes):
        for r in reads:
            old = r.r.get(id(src))
            if old is None or old[1] < c:
                r.r[id(src)] = (src, c)
        for w in writes:
            w.w = (src, c)
            w.r = {}

    def op(self, q, fn, reads=(), writes=()):
        self._emit_waits(q, self._deps(reads, writes))
        q.count += 1
        sem = q.sem
        q.thunks.append(lambda e, fn=fn, sem=sem: fn(e).then_inc(sem, 1))
        self._mark(q, q.count, reads, writes)
        self.n_inst += 1

    def dma(self, qname, fn, reads=(), writes=()):
        q = {"sp": self.sp, "act": self.act, "pool": self.pool}[qname]
        slots = self.slots[qname]
        i = self.slot_rr[qname]
        self.slot_rr[qname] = (i + 1) % len(slots)
        slot = slots[i]
        deps = self._deps(reads, writes)
        if slot.count > 0:
            deps.append((slot, slot.count))
        self._emit_waits(q, deps)
        slot.count += 1
        sem = slot.sem
        q.thunks.append(lambda e, fn=fn, sem=sem: fn(e).then_inc(sem, 16))
        self._mark(slot, slot.count, reads, writes)
        self.n_inst += 1

    def barrier(self):
        srcs = list(self.engs)
        for qn in self.slots:
            srcs.extend(self.slots[qn])
        deps = [(s, s.count) for s in srcs if s.count > 0]
        for q in self.engs:
            need = [(s, c) for (s, c) in deps]
            keep_sync = self.same_engine_sync
            self.same_engine_sync = True
            self._emit_waits(q, need)
            self.same_engine_sync = keep_sync

    def mm(self, out, lhsT, rhs, start, stop, reads, writes):
        self.op(self.pe, lambda e: e.matmul(out, lhsT=lhsT, rhs=rhs, start=start, stop=stop), reads, writes)

    def tr(self, out, in_, ident, reads, writes):
        self.op(self.pe, lambda e: e.transpose(out, in_, ident), reads, writes)

    def tt(self, q, out, in0, in1, op, reads, writes):
        self.op(q, lambda e: e.tensor_tensor(out=out, in0=in0, in1=in1, op=op), reads, writes)

    def ts(self, q, out, in0, s1, s2, op0, op1, reads, writes):
        if op1 is None:
            self.op(q, lambda e: e.tensor_scalar(out=out, in0=in0, scalar1=s1, scalar2=None, op0=op0), reads, writes)
        else:
            self.op(q, lambda e: e.tensor_scalar(out=out, in0=in0, scalar1=s1, scalar2=s2, op0=op0, op1=op1), reads, writes)

    def stt(self, out, in0, scalar, in1, op0, op1, reads, writes):
        self.op(self.dve, lambda e: e.scalar_tensor_tensor(out=out, in0=in0, scalar=scalar, in1=in1, op0=op0, op1=op1), reads, writes)

    def actv(self, out, in_, func, reads, writes, bias=None, scale=None):
        kw = {}
        if bias is not None:
            kw["bias"] = bias
        if scale is not None:
            kw["scale"] = scale
        self.op(self.act, lambda e: e.activation(out=out, in_=in_, func=func, **kw), reads, writes)

    def copy(self, q, out, in_, reads, writes):
        if q is self.act:
            self.op(q, lambda e: e.activation(out=out, in_=in_, func=AF.Copy), reads, writes)
        else:
            self.op(q, lambda e: e.tensor_copy(out=out, in_=in_), reads, writes)

    def recip(self, out, in_, reads, writes):
        self.op(self.dve, lambda e: e.reciprocal(out=out, in_=in_), reads, writes)

    def memset(self, q, ap, val, writes):
        self.op(q, lambda e: e.memset(ap, val), (), writes)

    def ld(self, qname, out, in_, reads=(), writes=(), slow=False):
        if slow:
            self.dma(qname, lambda e: e.dma_start(out=out, in_=in_, allow_slow_non_contiguous=True), reads, writes)
        else:
            self.dma(qname, lambda e: e.dma_start(out=out, in_=in_), reads, writes)

    def finish(self, final_res):
        deps = [r.w for r in final_res if r.w is not None]
        self._emit_waits(self.sp, deps)
        self.barrier()
        nc = self.nc
        with nc.Block() as block:
            @block.sync
            def _(e):
                for t in self.sp.thunks:
                    t(e)

            @block.tensor
            def _(e):
                for t in self.pe.thunks:
                    t(e)

            @block.vector
            def _(e):
                for t in self.dve.thunks:
                    t(e)

            @block.scalar
            def _(e):
                for t in self.act.thunks:
                    t(e)

            @block.gpsimd
            def _(e):
                for t in self.pool.thunks:
                    t(e)
        while self.ctx:
            self.ctx.pop().__exit__(None, None, None)


_CONST_CACHE = {}


def _bf(a):
    return np.ascontiguousarray(a.astype(np.float32)).astype(ml_dtypes.bfloat16)


def _dft_tables(L):
    N = 2 * L
    nfc = (L + 1 + 127) // 128
    FP = nfc * 128
    a = np.arange(FP, dtype=np.int64)
    t = np.arange(L, dtype=np.int64)
    ang = 2.0 * np.pi * ((t[:, None] * a[None, :]) % N).astype(np.float64) / N
    valid = (a <= L).astype(np.float64)
    C = np.cos(ang) * valid
    S = np.sin(ang) * valid
    ntc = L // 128
    Fc = C.reshape(ntc, 128, nfc, 128).transpose(2, 1, 0, 3)
    Fs = S.reshape(ntc, 128, nfc, 128).transpose(2, 1, 0, 3)
    w = np.where((a == 0) | (a == L), 1.0, 2.0) * valid / N
    IcM = C.T * w[:, None]
    IsM = S.T * w[:, None]
    TW = min(512, L)
    ntt = L // TW
    Ic = IcM.reshape(nfc, 128, ntt, TW).transpose(2, 1, 0, 3)
    Is = IsM.reshape(nfc, 128, ntt, TW).transpose(2, 1, 0, 3)
    return _bf(Fc), _bf(Fs), _bf(Ic), _bf(Is)


def _hy_consts(L):
    t = (np.arange(L, dtype=np.float32) / np.float32(L)).astype(np.float32)
    ang = (2.0 * math.pi * t[:, None] * np.arange(1, 17, dtype=np.float32)).astype(np.float32)
    feats = np.concatenate([t[:, None], np.sin(ang), np.cos(ang)], axis=-1).astype(np.float32)
    deltas = np.abs(np.linspace(math.log(1e-2) / 1.5, math.log(1e-2) / 0.3, 512, dtype=np.float32))
    decay = np.exp(-t[:, None] * deltas[None, :]).astype(np.float32)
    return np.ascontiguousarray(feats.T), decay


def _rope_tables():
    row = np.repeat(np.arange(SEQ // 64, dtype=np.float32), 64)
    col = np.tile(np.arange(64, dtype=np.float32), SEQ // 64)
    inv = (10000.0 ** (-np.arange(16, dtype=np.float32) / 16)).astype(np.float32)
    ang = np.concatenate([row[:, None] * inv, col[:, None] * inv], axis=-1).astype(np.float32)
    cs = np.zeros((64, 2, SEQ), np.float32)
    cs[:32, 0] = np.cos(ang).T
    cs[32:, 0] = np.cos(ang).T
    cs[:32, 1] = np.sin(ang).T
    cs[32:, 1] = np.sin(ang).T
    return cs


def _consts():
    if _CONST_CACHE:
        return _CONST_CACHE
    c = {}
    c["ident_f"] = np.eye(128, dtype=np.float32)
    c["rope_cs"] = _rope_tables()
    for L, tag in ((SEQ, "L"), (CTX, "C")):
        Fc, Fs, Ic, Is = _dft_tables(L)
        c["Fc" + tag], c["Fs" + tag], c["Ic" + tag], c["Is" + tag] = Fc, Fs, Ic, Is
        f, d = _hy_consts(L)
        c["feats" + tag], c["decay" + tag] = f, d
    _CONST_CACHE.update(c)
    return _CONST_CACHE


def _pack_small(inp):
    sp = np.zeros((128, DEPTH, NSP), np.float32)

    def put(col, arr):
        n = arr.shape[1] // 128
        sp[:, :, col:col + n] = arr.reshape(DEPTH, n, 128).transpose(2, 0, 1)

    put(SP_ADAB, inp["ada_b"])
    put(SP_NG, inp["norm_g"].reshape(DEPTH, 4 * D))
    put(SP_HYC, inp["hy_conv"].reshape(DEPTH, 3 * 1536))
    put(SP_HYB, inp["hy_bias"].reshape(DEPTH, 2 * 512))
    put(SP_LC, inp["lru_conv"].reshape(DEPTH, 2 * 4 * 512))
    put(SP_BA, inp["lru_ba"].reshape(DEPTH, 1024))
    put(SP_BX, inp["lru_bx"].reshape(DEPTH, 1024))
    put(SP_LAM, inp["lru_lam"].reshape(DEPTH, 1024))
    put(SP_GQ, inp["mla_gq"])
    put(SP_GKV, inp["mla_gkv"])
    put(SP_HG, inp["head_g"])
    put(SP_FC, inp["ffn_conv"].reshape(DEPTH, 3 * 2 * DFF))
    sp[:64, :, SP_B1] = inp["hy_b1"].T
    sp[:64, :, SP_B2] = inp["hy_b2"].T
    return sp


class Builder:
    def __init__(self, layers, dbg=(), phases=None, same_engine_sync=True):
        self.layers = list(layers)
        self.lidx = {l: i for i, l in enumerate(self.layers)}
        NL = len(self.layers)
        self.dbg = set(dbg)
        self.phases = phases
        nc = self.nc = bass.Bass("TRN2", target_bir_lowering=False)
        self.I = {}
        din = self.din
        self.x_in = din("x", [SL, D])
        self.ctx_in = din("ctx", [CTX, D])
        self.cvec = din("cvec", [128, 32])
        self.smallp_in = din("smallp", [128, DEPTH * NSP])
        self.ownp_in = din("ownp", [128, DEPTH * NOWN])
        self.w3own = din("hy_w3_own", [NL, 64, 512])
        self.lwa_own = din("lru_wa_own", [NL, 2, 128, 128])
        self.lwx_own = din("lru_wx_own", [NL, 2, 128, 128])
        self.ada_w = din("ada_w_own", [NL, D, 6 * D // NR])
        self.w_in = din("w_in", [NL, D, DIN])
        self.hy_w1 = din("hy_w1", [NL, 33, 64])
        self.hy_w2 = din("hy_w2", [NL, 64, 64])
        self.mla_wuq = din("mla_wuq", [NL, 512, 1536])
        self.mla_wukv = din("mla_wukv", [NL, 256, 2048])
        self.w_out = din("w_out", [NL, D, D])
        self.ffn_up = din("ffn_up", [NL, D, 2 * DFF])
        self.ffn_down = din("ffn_down", [NL, DFF, D])
        self.ident_in = din("ident_f", [128, 128])
        self.rope_cs = din("rope_cs", [64, 2, SEQ])
        self.rope_q = din("rope_q", [64, 2, SL])
        self.tab = {}
        for L, tag in ((SEQ, "L"), (CTX, "C")):
            nfc = (L + 1 + 127) // 128
            ntc = L // 128
            TW = min(512, L)
            self.tab[tag] = dict(
                L=L, nfc=nfc, ntc=ntc, TW=TW, ntt=L // TW,
                Fc=din("Fc" + tag, [nfc, 128, ntc, 128], BF16), Fs=din("Fs" + tag, [nfc, 128, ntc, 128], BF16),
                Ic=din("Ic" + tag, [L // TW, 128, nfc, TW], BF16), Is=din("Is" + tag, [L // TW, 128, nfc, TW], BF16),
                feats=din("feats" + tag, [33, L]), decay=din("decay" + tag, [L, 128]),
                S=self.dscr("S" + tag, [NL, nfc * 128, 512], F32),
            )
        self.out = nc.dram_tensor("out", [SL, D], F32, kind="ExternalOutput").ap()
        ds = self.dscr
        self.XT = ds("XT", [D, NTL], F32)
        self.UC = ds("UC", [DINX, CTX], F32)
        self.ULOC = ds("ULOC", [UCH * 256, SL], F32)
        self.UG = ds("UG", [UCH * NR * 256, SL], F32)
        self.MIN = ds("MIN", [128, 48], F32)
        self.MG = ds("MG", [NR * 128, 48], F32)
        self.HB = ds("HB", [D, 2], BF16)
        self.HG = ds("HG", [NR * D, 2], BF16)
        self.HGP = ds("HGP", [(NR + 2) * D, 2], BF16)
        self.ZT = ds("ZT", [128, NT], F32)
        self.PT = ds("PT", [256, NT], F32)
        self.YIN = [ds(f"YIN{k}", [NR * 128, NTL], BF16) for k in range(2)]
        self.YG = [ds(f"YG{k}", [2 * NR * 256, NTL], BF16) for k in range(2)]
        self.YM = ds("YM", [1024, NTL], BF16)
        self.H2 = ds("H2", [D, NTL], BF16)
        self.WIN = ds("WIN", [NL, D, DINX], BF16)
        self.WUQ = ds("WUQ", [NL, 512, 2048], BF16)
        self.WUKV = ds("WUKV", [NL, 256, 2048], BF16)
        self.WOUT = ds("WOUT", [NL, D, D], BF16)
        self.WUP = ds("WUP", [NL, D, 2 * DFF], BF16)
        self.WDN = ds("WDN", [NL, DFF, D], BF16)
        self.P = Prog(nc, same_engine_sync=same_engine_sync)
        self.r_out = Res("out")
        self._rank = {}

    def din(self, name, shape, dt=F32):
        self.I[name] = (list(shape), dt)
        return self.nc.dram_tensor(name, list(shape), dt, kind="ExternalInput").ap()

    def dscr(self, name, shape, dt):
        kind = "ExternalOutput" if name in self.dbg else "Internal"
        return self.nc.dram_tensor(name, list(shape), dt, kind=kind).ap()

    def rank(self, e):
        key = id(e)
        if key not in self._rank:
            self._rank[key] = e.snap(e.partition_id() % NR)
        return self._rank[key]

    def rv(self, e, kind):
        key = (kind, id(e))
        if key not in self._rank:
            r = self.rank(e)
            expr = {"x": (r // 2) * 8 + (r % 2), "128": r * 128, "16": r * 16}[kind]
            self._rank[key] = e.snap(expr)
        return self._rank[key]

    def Uap(self, row0, nrows, c0, W):
        if c0 < CTX:
            return self.UC[row0:row0 + nrows, c0:c0 + W]
        t0 = c0 - CTX
        s_, tl = t0 // SL, t0 % SL
        assert tl + W <= SL
        k, rr = row0 // 256, row0 % 256
        assert rr + nrows <= 256
        base = (k * NR + s_) * 256 + rr
        return self.UG[base:base + nrows, tl:tl + W]

    def Ulat(self, row0):
        k, rr = row0 // 256, row0 % 256
        return self.UG.rearrange("(k s r) t -> k r s t", s=NR, r=256)[k][rr:rr + 128, :, :]

    def loc_tiles(self, need_ctx=True):
        tiles = [(0, CTX, 1)] if need_ctx else []
        tiles += [(CTX + i * 512, 512, 0) for i in range(SL // 512)]
        return tiles

    def allgather(self, pairs):
        P = self.P
        P.barrier()
        for (src, dst) in pairs:
            P.op(P.pool, lambda e, src=src, dst=dst: e.collective_compute("AllGather", ALU.bypass, replica_groups=GROUPS, ins=[src.opt()], outs=[dst.opt()]), (), ())
        P.barrier()

    def on(self, ph):
        return self.phases is None or ph in self.phases

    def build(self):
        P = self.P
        self.setup()
        for l in self.layers:
            last = (l == DEPTH - 1)
            if self.on("M"):
                self.phase_mod(l)
            if self.on("A"):
                self.phase_inproj(l)
            if self.on("H"):
                if not last:
                    self.phase_hyena(l, "C")
                self.phase_hyena(l, "L")
            if self.on("R"):
                self.phase_lru(l, not last)
                self.allgather([(self.YIN[k][q * 256:(q + 1) * 256, :], self.YG[k][q * 1024:(q + 1) * 1024, :]) for k in range(2) for q in range(2)])
            if self.on("T"):
                self.phase_attn(l, not last)
            if self.on("O"):
                self.phase_outproj(l, not last)
            if self.on("F"):
                self.phase_ffn(l, not last)
        if self.on("Z"):
            self.final()
        P.finish([self.r_out])
        return self.nc

    def setup(self):
        P = self.P
        self.pb = [(P.psum(f"pb{i}", [128, 512]), Res(f"pb{i}")) for i in range(8)]
        self.ident = P.sbuf("ident", [128, 128], F32)
        self.r_const = Res("const")
        P.ld("sp", self.ident[:], self.ident_in[:, :], writes=[self.r_const])
        self.ones_b = P.sbuf("ones_b", [128, 128], BF16)
        self.ones_f = P.sbuf("ones_f", [128, 128], F32)
        P.memset(P.dve, self.ones_b[:], 1.0, [self.r_const])
        P.memset(P.dve, self.ones_f[:], 1.0, [self.r_const])
        self.smallp = P.sbuf("smallp", [128, DEPTH * NSP], F32)
        P.ld("sp", self.smallp[:], self.smallp_in[:, :], writes=[self.r_const])
        self.scv = P.sbuf("scv", [128, 32], F32)
        cvt = P.sbuf("cvt", [128, 32], F32)
        P.ld("sp", cvt[:], self.cvec[:, :], writes=[self.r_const])
        P.actv(self.scv[:], cvt[:], AF.Silu, [self.r_const], [self.r_const])
        self.eps_t = P.sbuf("eps_t", [128, 1], F32)
        P.memset(P.dve, self.eps_t[:], EPS, [self.r_const])
        self.ownp = P.sbuf("ownp", [128, DEPTH * NOWN], F32)
        P.ld("sp", self.ownp[:], self.ownp_in[:, :], writes=[self.r_const])
        self.mods = P.sbuf("mods", [128, 6 * 32], F32)
        self.r_mods = Res("mods")
        self.lru_h0 = P.sbuf("lru_h0", [128, 8], F32)
        self.r_h0 = Res("h0")
        for l in self.layers:
            for (src, dst, rows, cols, dcols) in (
                (self.w_in[self.lidx[l]], self.WIN[self.lidx[l]], D, DIN, DINX),
                (self.w_out[self.lidx[l]], self.WOUT[self.lidx[l]], D, D, D),
                (self.ffn_up[self.lidx[l]], self.WUP[self.lidx[l]], D, 2 * DFF, 2 * DFF),
                (self.ffn_down[self.lidx[l]], self.WDN[self.lidx[l]], DFF, D, D),
                (self.mla_wukv[self.lidx[l]], self.WUKV[self.lidx[l]], 256, 2048, 2048),
            ):
                cw = 1408 if cols % 1408 == 0 else (1696 if cols % 1696 == 0 else 1024)
                assert cols % cw == 0 and cw <= 2048
                for r0 in range(0, rows, 1024):
                    r1 = min(rows, r0 + 1024)
                    for c0 in range(0, cols, cw):
                        P.ld("pool", dst[r0:r1, c0:c0 + cw], src[r0:r1, c0:c0 + cw])
            for h in range(8):
                P.ld("pool", self.WUQ[self.lidx[l]][:, h * 256:h * 256 + 192], self.mla_wuq[self.lidx[l]][:, h * 192:(h + 1) * 192])
            mk = P.push()
            wk = P.sbuf("wk", [128, 16, 64], F32)
            wr = P.sbuf("wr", [128, 16, 64], BF16)
            r_wk, r_wr = Res(), Res()
            P.ld("sp", wk[:], self.w_in[self.lidx[l]].rearrange("(kc p) n -> p kc n", p=128)[:, :, OFF_KR:OFF_KR + 64], writes=[r_wk])
            P.ts(P.dve, wr[:, :, 0:32], wk[:, :, 32:64], -1.0, None, ALU.mult, None, [r_wk], [r_wr])
            P.copy(P.dve, wr[:, :, 32:64], wk[:, :, 0:32], [r_wk], [r_wr])
            P.ld("sp", self.WIN[self.lidx[l]].rearrange("(kc p) n -> p kc n", p=128)[:, :, OFF_KRR:OFF_KRR + 64], wr[:], reads=[r_wr])
            qk = P.sbuf("qk", [128, 4, 8, 64], F32)
            qr = P.sbuf("qr", [128, 4, 8, 64], BF16)
            r_qk, r_qr = Res(), Res()
            srcv = self.mla_wuq[self.lidx[l]].rearrange("(kc p) (h n) -> p kc h n", p=128, n=192)
            dstv = self.WUQ[self.lidx[l]].rearrange("(kc p) (h n) -> p kc h n", p=128, n=256)
            for kc in range(4):
                P.ld("sp", qk[:, kc, :, :], srcv[:, kc, :, 128:192], writes=[r_qk])
            P.ts(P.dve, qr[:, :, :, 0:32], qk[:, :, :, 32:64], -1.0, None, ALU.mult, None, [r_qk], [r_qr])
            P.copy(P.dve, qr[:, :, :, 32:64], qk[:, :, :, 0:32], [r_qk], [r_qr])
            for kc in range(4):
                P.ld("sp", dstv[:, kc, :, 192:256], qr[:, kc, :, :], reads=[r_qr])
            P.pop(mk)
        mk = P.push()
        xin = P.ring("xin", 2, [128, D], F32)
        xst = P.ring("xst", 2, [128, 16, 128], F32)
        XTv = self.XT.rearrange("(c p) t -> p c t", p=128)
        k = 0
        for ti in range(NTL // 128):
            src = self.ctx_in[ti * 128:(ti + 1) * 128, :] if ti < 2 else self.x_in[(ti - 2) * 128:(ti - 1) * 128, :]
            xt, r_xt = xin.next()
            P.ld("sp" if ti % 2 == 0 else "act", xt[:], src, writes=[r_xt])
            st, r_st = xst.next()
            for g in range(4):
                pt, r_pt = self.pb[k % 8]
                k += 1
                for j in range(4):
                    c = g * 4 + j
                    P.tr(pt[:, j * 128:(j + 1) * 128], xt[:, c * 128:(c + 1) * 128], self.ident[:], [r_xt, self.r_const], [r_pt])
                P.copy(P.act if g % 2 == 0 else P.dve, st[:, g * 4:(g + 1) * 4, :], pt[:].rearrange("p (j t) -> p j t", j=4), [r_pt], [r_st])
            P.ld("sp" if ti % 2 == 1 else "act", XTv[:, :, ti * 128:(ti + 1) * 128], st[:], reads=[r_st])
        P.pop(mk)
        if self.on("H"):
            ls = self.layers
            for i in range(0, len(ls), 2):
                self.hy_filter_all(self.tab["L"], ls[i:i + 2])
            lc_ = [l for l in ls if l < DEPTH - 1]
            for i in range(0, len(lc_), 2):
                self.hy_filter_all(self.tab["C"], lc_[i:i + 2])
        mk = P.push()
        zt = P.sbuf("zpad", [128, 16, 2], BF16)
        r_z = Res()
        P.memset(P.dve, zt[:], 0.0, [r_z])
        for blk in (0, NR + 1):
            P.ld("sp", self.HGP[blk * D:(blk + 1) * D, :].rearrange("(c p) k -> p c k", p=128), zt[:], reads=[r_z])
        P.pop(mk)

    def final(self):
        P = self.P
        mk = P.push()
        xin = P.ring("fin", 2, [128, 16, 128], F32)
        xst = P.ring("fst", 2, [128, D], F32)
        XTv = self.XT.rearrange("(c p) t -> p c t", p=128)
        k = 0
        for ti in range(SL // 128):
            xt, r_xt = xin.next()
            P.ld("sp" if ti % 2 == 0 else "act", xt[:], XTv[:, :, CTX + ti * 128:CTX + (ti + 1) * 128], writes=[r_xt])
            st, r_st = xst.next()
            for g in range(4):
                pt, r_pt = self.pb[k % 8]
                k += 1
                for j in range(4):
                    c = g * 4 + j
                    P.tr(pt[:, j * 128:(j + 1) * 128], xt[:, c, :], self.ident[:], [r_xt, self.r_const], [r_pt])
                P.copy(P.act if g % 2 == 0 else P.dve, st[:, g * 512:(g + 1) * 512], pt[:], [r_pt], [r_st])
            P.ld("sp" if ti % 2 == 1 else "act", self.out[ti * 128:(ti + 1) * 128, :], st[:], reads=[r_st], writes=[self.r_out])
        P.pop(mk)

    def sp_col(self, l, col, n=1, parts=128):
        base = l * NSP + col
        return self.smallp[0:parts, base:base + n]

    def rms_rstd(self, chunks, W, n_feat, sq_ring, pbi, st_ring, reads):
        P = self.P
        pt, r_pt = self.pb[pbi]
        for i, ap in enumerate(chunks):
            sq, r_sq = sq_ring.next()
            P.actv(sq[:, :W], ap, AF.Square, reads, [r_sq])
            P.mm(pt[:, :W], self.ones_b[:], sq[:, :W], i == 0, i == len(chunks) - 1, [r_sq, self.r_const], [r_pt])
        st, r_st = st_ring.next()
        P.actv(st[:, :W], pt[:, :W], AF.Sqrt, [r_pt], [r_st], bias=self.eps_t[:], scale=1.0 / n_feat)
        P.recip(st[:, :W], st[:, :W], [r_st], [r_st])
        return st, r_st

    def phase_mod(self, l):
        P = self.P
        mk = P.push()
        wr = P.ring("adaw", 2, [128, 16, 512], F32)
        modT = P.sbuf("modT", [128, 96, 2], F32)
        r_modT = Res()
        pm, r_pm = self.pb[0]
        src = self.ada_w[self.lidx[l]].rearrange("(kc p) n -> p kc n", p=128)
        scv = self.scv[:].rearrange("p (kc j) -> p kc j", j=2)
        for g in range(6):
            wt, r_wt = wr.next()
            P.ld("sp" if g % 2 == 0 else "act", wt[:], src[:, :, g * 512:(g + 1) * 512], writes=[r_wt])
            for m in range(4):
                col = (g * 4 + m) * 2
                for kc in range(16):
                    P.mm(pm[:, col:col + 2], wt[:, kc, m * 128:(m + 1) * 128], scv[:, kc, :], kc == 0, kc == 15, [r_wt, self.r_const], [r_pm])
        mloc = P.sbuf("mloc", [128, 48], F32)
        r_ml = Res()
        P.copy(P.dve, mloc[:], pm[:, 0:48], [r_pm], [r_ml])
        P.ld("sp", self.MIN[:, :], mloc[:], reads=[r_ml])
        self.allgather([(self.MIN, self.MG)])
        raw = P.sbuf("mraw", [128, 96, 2], F32)
        r_raw = Res()
        P.ld("sp", raw[:].rearrange("p (s m) j -> p s (m j)", s=NR), self.MG.rearrange("(s p) c -> p s c", p=128), writes=[r_raw])
        for j in range(2):
            P.tt(P.dve, modT[:, :, j], raw[:, :, j], self.sp_col(l, SP_ADAB, 96), ALU.add, [r_raw, self.r_const], [r_modT])
        mods = self.mods[:].rearrange("p (k c j) -> p k c j", k=6, j=2)
        for j in range(2):
            for half, (sh, sc, g, nga, ngb) in enumerate(((0, 1, 2, 0, 1), (3, 4, 5, 2, 3))):
                ng_a = self.sp_col(l, SP_NG + nga * 16, 16)
                ng_b = self.sp_col(l, SP_NG + ngb * 16, 16)
                A, B, G = mods[:, half * 3 + 0, :, j], mods[:, half * 3 + 1, :, j], mods[:, half * 3 + 2, :, j]
                P.stt(A, modT[:, sc * 16:(sc + 1) * 16, j], 1.0, ng_a, ALU.add, ALU.mult, [r_modT, self.r_const], [self.r_mods])
                P.copy(P.dve, B, modT[:, sh * 16:(sh + 1) * 16, j], [r_modT], [self.r_mods])
                P.tt(P.dve, G, modT[:, g * 16:(g + 1) * 16, j], ng_b, ALU.mult, [r_modT, self.r_const], [self.r_mods])
        P.pop(mk)

    def mod(self, k, c, j):
        i = (k * 16 + c) * 2 + j
        return self.mods[:, i:i + 1]

    def tok_tiles(self, need_ctx=True):
        tiles = [(0, CTX, 1)] if need_ctx else []
        tiles += [(CTX + i * 512, 512, 0) for i in range(SEQ // 512)]
        return tiles

    def phase_inproj(self, l):
        P = self.P
        mk = P.push()
        xs_r = P.ring("xs", 2, [128, 16, 512], F32)
        hb_r = P.ring("hb", 2, [128, 16, 512], BF16)
        sq_r = P.ring("sq", 3, [128, 512], BF16)
        tmp_r = P.ring("tmp", 3, [128, 512], F32)
        st_r = P.ring("st", 2, [128, 512], F32)
        w_r = P.ring("wg", 2, [128, 16, 512], BF16)
        ev_r = P.ring("ev", 4, [128, 512], F32)
        XTv = self.XT.rearrange("(c p) t -> p c t", p=128)
        Wv = self.WIN[self.lidx[l]].rearrange("(kc p) n -> p kc n", p=128)
        kb = 0
        for (c0, W, j) in self.loc_tiles():
            xs, r_xs = xs_r.next()
            P.ld("sp", xs[:, :, :W], XTv[:, :, c0:c0 + W], writes=[r_xs])
            st, r_st = self.rms_rstd([xs[:, c, :W] for c in range(16)], W, D, sq_r, 7, st_r, [r_xs])
            hb, r_hb = hb_r.next()
            for c in range(16):
                tmp, r_tmp = tmp_r.next()
                P.tt(P.dve, tmp[:, :W], xs[:, c, :W], st[:, :W], ALU.mult, [r_xs, r_st], [r_tmp])
                P.actv(hb[:, c, :W], tmp[:, :W], AF.Identity, [r_tmp, self.r_mods], [r_hb], bias=self.mod(1, c, j), scale=self.mod(0, c, j))
            for g in range(7):
                ncol = 512 if g < 6 else DINX - 6 * 512
                wt, r_wt = w_r.next()
                P.ld("act", wt[:, :, :ncol], Wv[:, :, g * 512:g * 512 + ncol], writes=[r_wt])
                for m in range(ncol // 128):
                    pt, r_pt = self.pb[kb % 6]
                    kb += 1
                    for kc in range(16):
                        P.mm(pt[:, :W], wt[:, kc, m * 128:(m + 1) * 128], hb[:, kc, :W], kc == 0, kc == 15, [r_wt, r_hb], [r_pt])
                    ev, r_ev = ev_r.next()
                    P.copy(P.act if kb % 2 == 0 else P.dve, ev[:, :W], pt[:, :W], [r_pt], [r_ev])
                    row = g * 512 + m * 128
                    dstU = self.UC[row:row + 128, c0:c0 + W] if j == 1 else self.ULOC[row:row + 128, c0 - CTX:c0 - CTX + W]
                    P.ld("sp", dstU, ev[:, :W], reads=[r_ev])
        P.pop(mk)
        self.allgather([(self.ULOC[k * 256:(k + 1) * 256, :], self.UG[k * NR * 256:(k + 1) * NR * 256, :]) for k in range(UCH) if k not in (8, 9)])


    def sin_layer(self, ps_ap, bias_ap, out_ap, W, rings, reads, r_ps, r_out):
        P = self.P
        v, r_v = rings[0].next()
        t1, r_t1 = rings[1].next()
        P.ts(P.dve, v[:, :W], ps_ap, bias_ap, None, ALU.add, None, [r_ps] + reads, [r_v])
        P.ts(P.dve, t1[:, :W], v[:, :W], 1.0 / (2 * math.pi), MAGIC, ALU.mult, ALU.add, [r_v], [r_t1])
        P.ts(P.dve, t1[:, :W], t1[:, :W], -MAGIC, None, ALU.add, None, [r_t1], [r_t1])
        P.stt(v[:, :W], t1[:, :W], -2 * math.pi, v[:, :W], ALU.mult, ALU.add, [r_t1, r_v], [r_v])
        P.ts(P.dve, v[:, :W], v[:, :W], -3.1415925, 3.1415925, ALU.max, ALU.min, [r_v], [r_v])
        P.actv(out_ap, v[:, :W], AF.Sin, [r_v], [r_out])

    def hy_filter_all(self, T, layers):
        P = self.P
        L, nfc, ntc, TW, ntt = T["L"], T["nfc"], T["ntc"], T["TW"], T["ntt"]
        G = len(layers)
        mk = P.push()
        ksum = P.sbuf("ksum", [128, ntc, G * 256], BF16)
        kdiff = P.sbuf("kdiff", [128, ntc, G * 256], BF16)
        r_k = [Res() for _ in range(ntc)]
        rn = P.sbuf("rn", [128, G * 256], F32)
        r_rn = Res()
        dec = P.sbuf("dec", [128, ntc, 128], F32)
        r_dec = Res()
        P.ld("act", dec[:], T["decay"].rearrange("(tc p) c -> p tc c", p=128), writes=[r_dec])
        mk1 = P.push()
        w1 = P.sbuf("hw1", [33, 64], F32)
        w2 = P.sbuf("hw2", [64, 64], F32)
        w3 = P.sbuf("hw3", [64, 4, 128], F32)
        r_w = Res()
        hid2 = P.sbuf("hid2", [64, L], F32)
        r_hid2 = Res()
        f_r = P.ring("ft", 2, [33, 512], F32)
        h1_r = P.ring("h1", 2, [64, 512], F32)
        v_r = P.ring("sv", 2, [64, 512], F32)
        t_r = P.ring("st1", 2, [64, 512], F32)
        hf_r = P.ring("hf", 2, [128, 4, 128], F32)
        ab_r = P.ring("ab", 2, [128, 512], F32)
        pns = P.sbuf("pns", [128, 512], F32)
        r_pns = Res()
        for gi, l in enumerate(layers):
            li = self.lidx[l]
            P.ld("sp", w1[:], self.hy_w1[li], writes=[r_w])
            P.ld("sp", w2[:], self.hy_w2[li], writes=[r_w])
            P.ld("sp", w3[:].rearrange("p g c -> p (g c)"), self.w3own[li], writes=[r_w])
            b1 = self.sp_col(l, SP_B1, 1, 64)
            b2 = self.sp_col(l, SP_B2, 1, 64)
            for tt in range(ntt):
                ft, r_ft = f_r.next()
                P.ld("sp", ft[:, :TW], T["feats"][:, tt * TW:(tt + 1) * TW], writes=[r_ft])
                p0, r_p0 = self.pb[4]
                P.mm(p0[0:64, :TW], w1[:, :], ft[:, :TW], True, True, [r_w, r_ft], [r_p0])
                h1, r_h1 = h1_r.next()
                self.sin_layer(p0[0:64, :TW], b1, h1[:, :TW], TW, (v_r, t_r), [self.r_const], r_p0, r_h1)
                p1, r_p1 = self.pb[5]
                P.mm(p1[0:64, :TW], w2[:, :], h1[:, :TW], True, True, [r_w, r_h1], [r_p1])
                self.sin_layer(p1[0:64, :TW], b2, hid2[:, tt * TW:(tt + 1) * TW], TW, (v_r, t_r), [self.r_const], r_p1, r_hid2)
            pn, r_pn = self.pb[6]
            w3f = w3[:].rearrange("p g c -> p (g c)")
            for lc in range(ntc):
                hf, r_hf = hf_r.next()
                ph, r_ph = self.pb[lc % 4]
                P.mm(ph[:, :], hid2[:, lc * 128:(lc + 1) * 128], w3f, True, True, [r_hid2, r_w], [r_ph])
                for g in range(4):
                    P.tt(P.dve, hf[:, g, :], ph[:, g * 128:(g + 1) * 128], dec[:, lc, :], ALU.mult, [r_ph, r_dec], [r_hf])
                if lc == 0:
                    P.memset(P.dve, hf[0:1, 1, :], 0.0, [r_hf])
                    P.memset(P.dve, hf[0:1, 3, :], 0.0, [r_hf])
                ab, r_ab = ab_r.next()
                P.actv(ab[:], hf[:].rearrange("p g c -> p (g c)"), AF.Abs, [r_hf], [r_ab])
                P.mm(pn[:, :], self.ones_f[:], ab[:], lc == 0, lc == ntc - 1, [r_ab, self.r_const], [r_pn])
                for o_ in range(2):
                    cs = slice(gi * 256 + o_ * 128, gi * 256 + (o_ + 1) * 128)
                    P.tt(P.pool, ksum[:, lc, cs], hf[:, 2 * o_, :], hf[:, 2 * o_ + 1, :], ALU.add, [r_hf], [r_k[lc]])
                    P.tt(P.pool, kdiff[:, lc, cs], hf[:, 2 * o_, :], hf[:, 2 * o_ + 1, :], ALU.subtract, [r_hf], [r_k[lc]])
            P.copy(P.dve, pns[:], pn[:, :], [r_pn], [r_pns])
            pnv = pns[:].rearrange("p (o d c) -> p o d c", o=2, d=2)
            rnv = rn[:, gi * 256:(gi + 1) * 256].rearrange("p (o c) -> p o c", o=2)
            P.tt(P.dve, rnv, pnv[:, :, 0, :], pnv[:, :, 1, :], ALU.add, [r_pns], [r_rn])
            P.recip(rn[:, gi * 256:(gi + 1) * 256], rn[:, gi * 256:(gi + 1) * 256], [r_rn], [r_rn])
        P.pop(mk1)
        AW = ntc * 128
        tbs = [(P.sbuf(f"ftb{i}", [128, 2 * AW], BF16), Res(), Res()) for i in range(2)]
        sst_r = P.ring("sst", 3, [128, 512], F32)
        for fc in range(nfc):
            tb, r_tc, r_ts = tbs[fc % 2]
            P.ld("sp", tb[:, 0:AW], T["Fc"][fc].rearrange("p tc f -> p (tc f)"), writes=[r_tc])
            P.ld("act", tb[:, AW:2 * AW], T["Fs"][fc].rearrange("p tc f -> p (tc f)"), writes=[r_ts])
            Fc_t = tb[:, 0:AW].rearrange("p (tc f) -> p tc f", f=128)
            Fs_t = tb[:, AW:2 * AW].rearrange("p (tc f) -> p tc f", f=128)
            for gi, l in enumerate(layers):
                li = self.lidx[l]
                gs = slice(gi * 256, (gi + 1) * 256)
                pr, r_pr = self.pb[(4 * fc + 2 * gi) % 8]
                ps_, r_ps = self.pb[(4 * fc + 2 * gi + 1) % 8]
                for tc in range(ntc):
                    P.mm(pr[:, 0:256], Fc_t[:, tc, :], ksum[:, tc, gs], tc == 0, tc == ntc - 1, [r_tc, r_k[tc]], [r_pr])
                for tc in range(ntc):
                    P.mm(ps_[:, 0:256], Fs_t[:, tc, :], kdiff[:, tc, gs], tc == 0, tc == ntc - 1, [r_ts, r_k[tc]], [r_ps])
                sst, r_sst = sst_r.next()
                P.tt(P.dve, sst[:, 0:256], pr[:, 0:256], rn[:, gs], ALU.mult, [r_pr, r_rn], [r_sst])
                P.tt(P.dve, sst[:, 256:512], ps_[:, 0:256], rn[:, gs], ALU.mult, [r_ps, r_rn], [r_sst])
                P.ld("sp", T["S"][li][fc * 128:(fc + 1) * 128, :], sst[:], reads=[r_sst])
        P.pop(mk)

    def own(self, l, col):
        i = l * NOWN + col
        return self.ownp[:, i:i + 1]

    def head_norm_store(self, y_ap, W, l, hg_ap, dsts, sq_r, st_r, ob_r, pbi, r_y):
        P = self.P
        st, r_st = self.rms_rstd([y_ap], W, 128, sq_r, pbi, st_r, [r_y])
        ob, r_ob = ob_r.next()
        P.stt(ob[:, :W], y_ap, hg_ap, st[:, :W], ALU.mult, ALU.mult, [r_y, r_st, self.r_const], [r_ob])
        for k, d in enumerate(dsts):
            P.dma("sp" if k % 2 == 0 else "act", lambda e, d=d, ob=ob: e.dma_start(out=d(e), in_=ob[:, :W]), reads=[r_ob])

    def yin_dsts(self, kind, c0, W):
        Y = self.YIN[kind]
        if c0 < CTX:
            return [(lambda e, d=d: Y[d * 128:(d + 1) * 128, c0:c0 + W]) for d in range(NR)]
        t0 = c0 - CTX
        d, lc = t0 // SL, CTX + t0 % SL
        return [lambda e: Y[d * 128:(d + 1) * 128, lc:lc + W]]


    def ld_u_own(self, dst2d, A, isctx, L, col0, writes, qn="sp"):
        P = self.P
        if isctx:
            P.dma(qn, lambda e: e.dma_start(out=dst2d[:, col0:col0 + L], in_=self.UC[A:A + 512, 0:L][bass.ds(self.rv(e, "128"), 128), :]), writes=writes)
        else:
            win = self.UG[A * NR:A * NR + 2048, :].rearrange("(x i) t -> i x t", i=128)
            P.dma(qn, lambda e: e.dma_start(out=dst2d[:, col0:col0 + SEQ].rearrange("p (s t) -> p s t", s=NR), in_=win[:, bass.ds(self.rv(e, "x"), NR, 2), :]), writes=writes)

    def phase_hyena(self, l, tag):
        P = self.P
        T = self.tab[tag]
        L, nfc, ntc, TW, ntt = T["L"], T["nfc"], T["ntc"], T["TW"], T["ntt"]
        off = 0 if tag == "C" else CTX
        mk = P.push()
        u_r = P.ring("hu", 2, [128, L + 2], F32)
        o_r = P.ring("ho", 2, [128, L], F32)
        for p_ in range(3):
            u, r_u = u_r.next()
            P.memset(P.pool, u[:, 0:1], 0.0, [r_u])
            P.memset(P.pool, u[:, L + 1:L + 2], 0.0, [r_u])
            self.ld_u_own(u, p_ * 512, tag == "C", L, 1, [r_u])
            o, r_o = o_r.next()
            P.ts(P.dve, o[:, :], u[:, 0:L], self.own(l, p_ * 3 + 0), None, ALU.mult, None, [r_u, self.r_const], [r_o])
            P.stt(o[:, :], u[:, 1:L + 1], self.own(l, p_ * 3 + 1), o[:, :], ALU.mult, ALU.add, [r_u, r_o, self.r_const], [r_o])
            P.stt(o[:, :], u[:, 2:L + 2], self.own(l, p_ * 3 + 2), o[:, :], ALU.mult, ALU.add, [r_u, r_o, self.r_const], [r_o])
            dst = self.ZT[:, off:off + L] if p_ == 0 else self.PT[(p_ - 1) * 128:p_ * 128, off:off + L]
            P.ld("act", dst, o[:, :], reads=[r_o])
        P.pop(mk)
        FG = 11 if nfc % 11 == 0 else nfc
        nfg = nfc // FG
        AW = max(ntc * 128, FG * TW)
        for o_ in range(2):
            mk = P.push()
            zT = P.sbuf("zT", [128, ntc, 128], BF16)
            r_zT = [Res() for _ in range(ntc)]
            Yr = P.sbuf("Yr", [128, nfc, 128], BF16)
            Ys = P.sbuf("Ys", [128, nfc, 128], BF16)
            r_Y = [Res() for _ in range(nfc)]
            tbs = [(P.sbuf(f"tb{i}", [128, 2 * AW], BF16), Res(), Res()) for i in range(3)]
            zl_r = P.ring("zl", 2, [128, TW], F32)
            for tt in range(ntt):
                zl, r_zl = zl_r.next()
                P.ld("sp", zl[:, :], self.ZT[:, off + tt * TW:off + (tt + 1) * TW], writes=[r_zl])
                nsub = TW // 128
                pt, r_pt = self.pb[tt % 4]
                for s_ in range(nsub):
                    P.tr(pt[:, s_ * 128:(s_ + 1) * 128], zl[:, s_ * 128:(s_ + 1) * 128], self.ident[:], [r_zl, self.r_const], [r_pt])
                P.copy(P.act if tt % 2 == 0 else P.dve, zT[:, tt * nsub:(tt + 1) * nsub, :], pt[:, 0:TW].rearrange("p (a b) -> p a b", b=128), [r_pt], [r_zT[tt * nsub]])
                for s_ in range(1, nsub):
                    r_zT[tt * nsub + s_] = r_zT[tt * nsub]
            S_r = P.ring("Sld", 3, [128, 512], F32)
            t_rs = [P.ring(f"ty{i}", 2, [128, 128], F32) for i in range(4)]
            FW = ntc * 128
            for fc in range(nfc):
                tb, r_tc, r_ts = tbs[fc % 3]
                P.ld("sp", tb[:, 0:FW], T["Fc"][fc].rearrange("p tc f -> p (tc f)"), writes=[r_tc])
                P.ld("act", tb[:, AW:AW + FW], T["Fs"][fc].rearrange("p tc f -> p (tc f)"), writes=[r_ts])
                Fc_t = tb[:, 0:FW].rearrange("p (tc f) -> p tc f", f=128)
                Fs_t = tb[:, AW:AW + FW].rearrange("p (tc f) -> p tc f", f=128)
                St, r_S = S_r.next()
                P.ld("sp", St[:], T["S"][self.lidx[l]][fc * 128:(fc + 1) * 128, :], writes=[r_S])
                Sr = St[:, o_ * 128:(o_ + 1) * 128]
                Ss = St[:, 256 + o_ * 128:256 + (o_ + 1) * 128]
                pr, r_pr = self.pb[4 + 2 * (fc % 2)]
                ps_, r_ps = self.pb[5 + 2 * (fc % 2)]
                for tc in range(ntc):
                    P.mm(pr[:, 0:128], Fc_t[:, tc, :], zT[:, tc, :], tc == 0, tc == ntc - 1, [r_tc, r_zT[tc]], [r_pr])
                for tc in range(ntc):
                    P.mm(ps_[:, 0:128], Fs_t[:, tc, :], zT[:, tc, :], tc == 0, tc == ntc - 1, [r_ts, r_zT[tc]], [r_ps])
                (t1, r1), (t2, r2), (t3, r3), (t4, r4) = [r.next() for r in t_rs]
                P.tt(P.dve, t1[:], pr[:, 0:128], Sr, ALU.mult, [r_pr, r_S], [r1])
                P.tt(P.dve, t2[:], ps_[:, 0:128], Ss, ALU.mult, [r_ps, r_S], [r2])
                P.tt(P.dve, t3[:], pr[:, 0:128], Ss, ALU.mult, [r_pr, r_S], [r3])
                P.tt(P.dve, t4[:], ps_[:, 0:128], Sr, ALU.mult, [r_ps, r_S], [r4])
                P.tt(P.pool, Yr[:, fc, :], t1[:], t2[:], ALU.subtract, [r1, r2], [r_Y[fc]])
                P.tt(P.pool, Ys[:, fc, :], t3[:], t4[:], ALU.add, [r3, r4], [r_Y[fc]])
            zt_r = P.ring("zt", 2, [128, TW], F32)
            pp_r = P.ring("pp", 2, [128, TW], F32)
            tm_r = P.ring("tm", 2, [128, TW], F32)
            sq_r = P.ring("hsq", 2, [128, 512], BF16)
            st_r = P.ring("hst", 2, [128, 512], F32)
            ob_r = P.ring("hob", 2, [128, 512], BF16)
            k = 0
            for tt in range(ntt):
                c0 = off + tt * TW
                pa, r_pa = self.pb[tt % 4]
                for fg in range(nfg):
                    tb, r_tc, r_ts = tbs[k % 3]
                    k += 1
                    P.ld("sp", tb[:, 0:FG * TW], T["Ic"][tt][:, fg * FG:(fg + 1) * FG, :].rearrange("p f t -> p (f t)"), writes=[r_tc])
                    P.ld("act", tb[:, AW:AW + FG * TW], T["Is"][tt][:, fg * FG:(fg + 1) * FG, :].rearrange("p f t -> p (f t)"), writes=[r_ts])
                    Ic_t = tb[:, 0:FG * TW].rearrange("p (f t) -> p f t", t=TW)
                    Is_t = tb[:, AW:AW + FG * TW].rearrange("p (f t) -> p f t", t=TW)
                    for f_ in range(FG):
                        fc = fg * FG + f_
                        P.mm(pa[:, :TW], Yr[:, fc, :], Ic_t[:, f_, :], fc == 0, False, [r_Y[fc], r_tc], [r_pa])
                        P.mm(pa[:, :TW], Ys[:, fc, :], Is_t[:, f_, :], False, fc == nfc - 1, [r_Y[fc], r_ts], [r_pa])
                zt, r_zt = zt_r.next()
                pp, r_pp = pp_r.next()
                tm, r_tm = tm_r.next()
                P.ld("sp", zt[:], self.ZT[:, c0:c0 + TW], writes=[r_zt])
                P.ld("act", pp[:], self.PT[o_ * 128:(o_ + 1) * 128, c0:c0 + TW], writes=[r_pp])
                P.stt(tm[:], zt[:], self.own(l, 9 + o_), pa[:, :TW], ALU.mult, ALU.add, [r_zt, r_pa, self.r_const], [r_tm])
                P.tt(P.pool, zt[:], tm[:], pp[:], ALU.mult, [r_tm, r_pp], [r_zt])
                if o_ == 0:
                    P.ld("sp", self.ZT[:, c0:c0 + TW], zt[:], reads=[r_zt])
                else:
                    self.head_norm_store(zt[:], TW, l, self.own(l, 11), self.yin_dsts(0, c0, TW), sq_r, st_r, ob_r, 4 + tt % 4, r_zt)
            P.pop(mk)

    def phase_lru(self, l, need_ctx):
        P = self.P
        li = self.lidx[l]
        mk = P.push()
        c8 = P.sbuf("c8", [128, 2], F32)
        r_c8 = Res()
        P.actv(c8[:], self.ownp[:, l * NOWN + 25:l * NOWN + 27], AF.Exp, [self.r_const], [r_c8], scale=-1.0)
        P.actv(c8[:], c8[:], AF.Ln, [r_c8, self.r_const], [r_c8], bias=self.ones_f[:, 0:1], scale=1.0)
        P.ts(P.dve, c8[:], c8[:], -8.0, None, ALU.mult, None, [r_c8], [r_c8])
        wt = P.sbuf("wab", [128, 4, 128], BF16)
        r_wt = Res()
        for d in range(2):
            for k, Wsrc in enumerate((self.lwa_own, self.lwx_own)):
                P.ld("pool", wt[:, d * 2 + k, :], Wsrc[li, d], writes=[r_wt])
        seqs = [(0, CTX, True), (CTX, SEQ, False)]
        sq_r = P.ring("lsq", 2, [128, 512], BF16)
        st_r = P.ring("lst", 2, [128, 512], F32)
        ob_r = P.ring("lob", 2, [128, 512], BF16)
        for (off, Ls, isctx) in seqs:
            mk2 = P.push()
            t = "c" if isctx else "l"
            xr_r = P.ring("xr" + t, 2, [128, Ls + 6], F32)
            xc, r_xc = P.sbuf("xc" + t, [128, Ls], F32), Res()
            xcb, r_xcb = P.sbuf("xcb" + t, [128, Ls], BF16), Res()
            ra, r_ra = P.sbuf("ra" + t, [128, Ls], F32), Res()
            ib, r_ib = P.sbuf("ib" + t, [128, Ls], F32), Res()
            ta, r_ta = P.sbuf("ta" + t, [128, Ls], F32), Res()
            hs = [(P.sbuf(f"h{d}" + t, [128, Ls], F32), Res()) for d in range(2)]
            W = min(512, Ls)
            for d in range(2):
                xr, r_xr = xr_r.next()
                P.memset(P.pool, xr[:, 0:3], 0.0, [r_xr])
                P.memset(P.pool, xr[:, Ls + 3:Ls + 6], 0.0, [r_xr])
                self.ld_u_own(xr, OFF_X, isctx, Ls, 3, [r_xr], qn="act")
                left = 3 if d == 0 else 0
                for jj in range(4):
                    s0 = 3 + jj - left
                    wj = self.own(l, 13 + d * 4 + jj)
                    if jj == 0:
                        P.ts(P.dve, xc[:, :], xr[:, s0:s0 + Ls], wj, None, ALU.mult, None, [r_xr, self.r_const], [r_xc])
                    else:
                        P.stt(xc[:, :], xr[:, s0:s0 + Ls], wj, xc[:, :], ALU.mult, ALU.add, [r_xr, r_xc, self.r_const], [r_xc])
                P.copy(P.pool, xcb[:, :], xc[:, :], [r_xc], [r_xcb])
                for ti in range(Ls // W):
                    cs = slice(ti * W, (ti + 1) * W)
                    pa, r_pa = self.pb[(2 * ti) % 4]
                    px, r_px = self.pb[(2 * ti + 1) % 4]
                    P.mm(pa[:, :W], wt[:, d * 2 + 0, :], xcb[:, cs], True, True, [r_wt, r_xcb], [r_pa])
                    P.mm(px[:, :W], wt[:, d * 2 + 1, :], xcb[:, cs], True, True, [r_wt, r_xcb], [r_px])
                    P.actv(ra[:, cs], pa[:, :W], AF.Sigmoid, [r_pa, self.r_const], [r_ra], bias=self.own(l, 21 + d))
                    P.actv(ib[:, cs], px[:, :W], AF.Sigmoid, [r_px, self.r_const], [r_ib], bias=self.own(l, 23 + d))
                P.actv(ra[:, :], ra[:, :], AF.Exp, [r_ra, r_c8], [r_ra], scale=c8[:, d:d + 1])
                P.tt(P.pool, ta[:, :], ra[:, :], ra[:, :], ALU.mult, [r_ra], [r_ta])
                P.ts(P.pool, ta[:, :], ta[:, :], -1.0, 1.0, ALU.mult, ALU.add, [r_ta], [r_ta])
                P.actv(ta[:, :], ta[:, :], AF.Sqrt, [r_ta], [r_ta])
                P.tt(P.dve, ib[:, :], ib[:, :], xc[:, :], ALU.mult, [r_ib, r_xc], [r_ib])
                P.tt(P.dve, ib[:, :], ib[:, :], ta[:, :], ALU.mult, [r_ib, r_ta], [r_ib])
                if not isctx:
                    f0 = Ls - 1 if d == 1 else 0
                    P.stt(ib[:, f0:f0 + 1], ra[:, f0:f0 + 1], self.lru_h0[:, d:d + 1], ib[:, f0:f0 + 1],
                          ALU.mult, ALU.add, [r_ra, r_ib, self.r_h0], [r_ib])
                h, r_h = hs[d]
                if d == 0:
                    P.op(P.dve, lambda e, h=h, ra=ra, ib=ib: e.tensor_tensor_scan(out=h[:, :], data0=ra[:, :], data1=ib[:, :], initial=0.0, op0=ALU.mult, op1=ALU.add), [r_ra, r_ib], [r_h])
                else:
                    P.op(P.dve, lambda e, h=h, ra=ra, ib=ib: e.tensor_tensor_scan(out=h[:, ::-1], data0=ra[:, ::-1], data1=ib[:, ::-1], initial=0.0, op0=ALU.mult, op1=ALU.add), [r_ra, r_ib], [r_h])
                if isctx:
                    f1 = Ls - 1 if d == 0 else 0
                    P.copy(P.dve, self.lru_h0[:, d:d + 1], h[:, f1:f1 + 1], [r_h], [self.r_h0])
            if not (isctx and not need_ctx):
                xr, r_xr = xr_r.next()
                self.ld_u_own(xr, OFF_G, isctx, Ls, 0, [r_xr], qn="act")
                P.actv(xr[:, 0:Ls], xr[:, 0:Ls], AF.Gelu_apprx_tanh, [r_xr], [r_xr])
                (h0_, r_h0_), (h1_, r_h1_) = hs
                P.tt(P.pool, h0_[:, :], h0_[:, :], h1_[:, :], ALU.add, [r_h0_, r_h1_], [r_h0_])
                P.tt(P.dve, h0_[:, :], h0_[:, :], xr[:, 0:Ls], ALU.mult, [r_h0_, r_xr], [r_h0_])
                for ti in range(Ls // W):
                    self.head_norm_store(h0_[:, ti * W:(ti + 1) * W], W, l, self.own(l, 12), self.yin_dsts(1, off + ti * W, W), sq_r, st_r, ob_r, 4 + ti % 4, r_h0_)
            P.pop(mk2)
        P.pop(mk)

    def phase_attn(self, l, need_ctx):
        P = self.P
        li = self.lidx[l]
        mk = P.push()
        ckv = P.sbuf("ckv", [128, 2, NT], BF16)
        r_ckv = Res()
        krope = P.sbuf("krope", [64, NT], BF16)
        r_kr = Res()
        cq = P.sbuf("cq", [128, 4, NTL], BF16)
        r_cq = Res()
        sq_r = P.ring("asq", 3, [128, 512], BF16)
        st_r = P.ring("ast", 2, [128, 512], F32)
        u2_r = P.ring("au2", 2, [128, 2, 512], F32)
        u4_r = P.ring("au4", 2, [128, 4, 512], F32)
        kr_r = P.ring("akr", 2, [64, 2, 512], F32)
        cs_r = P.ring("acs", 2, [64, 2, 512], F32)
        tk_r = P.ring("atk", 2, [64, 2, 512], F32)
        for (c0, W, j) in self.tok_tiles(True):
            u2, r_u2 = u2_r.next()
            P.ld("sp", u2[:, :, :W], self.Uap(OFF_KV, 256, c0, W).rearrange("(c p) t -> p c t", p=128), writes=[r_u2])
            st, r_st = self.rms_rstd([u2[:, c, :W] for c in range(2)], W, 256, sq_r, 7, st_r, [r_u2])
            for c in range(2):
                P.stt(ckv[:, c, c0:c0 + W], u2[:, c, :W], self.sp_col(l, SP_GKV + c), st[:, :W], ALU.mult, ALU.mult, [r_u2, r_st, self.r_const], [r_ckv])
            kr, r_krt = kr_r.next()
            P.ld("act", kr[:, 0, :W], self.Uap(OFF_KR, 64, c0, W), writes=[r_krt])
            if j == 0:
                P.ld("act", kr[:, 1, :W], self.Uap(OFF_KRR, 64, c0, W), writes=[r_krt])
                cs, r_cs = cs_r.next()
                P.ld("act", cs[:, :, :W], self.rope_cs[:, :, c0 - CTX:c0 - CTX + W], writes=[r_cs])
                tk, r_tk = tk_r.next()
                P.tt(P.pool, tk[:, :, :W], kr[:, :, :W], cs[:, :, :W], ALU.mult, [r_krt, r_cs], [r_tk])
                P.tt(P.pool, krope[:, c0:c0 + W], tk[:, 0, :W], tk[:, 1, :W], ALU.add, [r_tk], [r_kr])
            else:
                P.copy(P.pool, krope[:, c0:c0 + W], kr[:, 0, :W], [r_krt], [r_kr])
        qtiles = self.loc_tiles(need_ctx)
        for (c0, W, j) in qtiles:
            u4, r_u4 = u4_r.next()
            srcq = self.UC[OFF_Q:OFF_Q + 512, c0:c0 + W] if j == 1 else self.ULOC[OFF_Q:OFF_Q + 512, c0 - CTX:c0 - CTX + W]
            P.ld("sp", u4[:, :, :W], srcq.rearrange("(c p) t -> p c t", p=128), writes=[r_u4])
            st, r_st = self.rms_rstd([u4[:, c, :W] for c in range(4)], W, 512, sq_r, 6, st_r, [r_u4])
            for c in range(4):
                P.stt(cq[:, c, c0:c0 + W], u4[:, c, :W], self.sp_col(l, SP_GQ + c), st[:, :W], ALU.mult, ALU.mult, [r_u4, r_st, self.r_const], [r_cq])
        wkv_r = P.ring("wkv", 2, [128, 2, 256], BF16)
        wq_r = P.ring("wq", 2, [128, 4, 256], BF16)
        kn_r = P.ring("kn", 2, [128, NT], BF16)
        v_r = P.ring("vv", 2, [128, NT // 128, 128], BF16)
        qn_r = P.ring("qn", 2, [128, NTL], BF16)
        qr_r = P.ring("qr", 2, [64, NTL], BF16)
        pT_r = P.ring("pT", 4, [128, 512], BF16)
        ri_r = P.ring("ri", 2, [128, 512], F32)
        oo_r = P.ring("oo", 2, [128, 512], F32)
        ob_r = P.ring("aob", 2, [128, 512], BF16)
        WKV = self.WUKV[li].rearrange("(kc p) n -> p kc n", p=128)
        WQ = self.WUQ[li].rearrange("(kc p) n -> p kc n", p=128)
        kb = 0
        for h in range(8):
            wkv, r_wkv = wkv_r.next()
            wq, r_wq = wq_r.next()
            P.ld("sp", wkv[:], WKV[:, :, h * 256:(h + 1) * 256], writes=[r_wkv])
            P.ld("sp", wq[:], WQ[:, :, h * 256:(h + 1) * 256], writes=[r_wq])
            kn, r_kn = kn_r.next()
            vv, r_vv = v_r.next()
            qn, r_qn = qn_r.next()
            qr, r_qr = qr_r.next()
            for (c0, W, j) in self.tok_tiles(True):
                pt, r_pt = self.pb[kb % 4]
                kb += 1
                for kc in range(2):
                    P.mm(pt[:, :W], wkv[:, kc, 0:128], ckv[:, kc, c0:c0 + W], kc == 0, kc == 1, [r_wkv, r_ckv], [r_pt])
                P.copy(P.act, kn[:, c0:c0 + W], pt[:, :W], [r_pt], [r_kn])
            for (c0, W, j) in qtiles:
                pt, r_pt = self.pb[kb % 4]
                kb += 1
                for kc in range(4):
                    P.mm(pt[:, :W], wq[:, kc, 0:128], cq[:, kc, c0:c0 + W], kc == 0, kc == 3, [r_wq, r_cq], [r_pt])
                P.copy(P.dve, qn[:, c0:c0 + W], pt[:, :W], [r_pt], [r_qn])
                pt, r_pt = self.pb[kb % 4]
                kb += 1
                for kc in range(4):
                    P.mm(pt[0:64, :W], wq[:, kc, 128:192], cq[:, kc, c0:c0 + W], kc == 0, kc == 3, [r_wq, r_cq], [r_pt])
                if j == 0:
                    pt2, r_pt2 = self.pb[kb % 4]
                    kb += 1
                    for kc in range(4):
                        P.mm(pt2[0:64, :W], wq[:, kc, 192:256], cq[:, kc, c0:c0 + W], kc == 0, kc == 3, [r_wq, r_cq], [r_pt2])
                    cs, r_cs = cs_r.next()
                    P.ld("act", cs[:, :, :W], self.rope_q[:, :, c0 - CTX:c0 - CTX + W], writes=[r_cs])
                    tk, r_tk = tk_r.next()
                    P.tt(P.dve, tk[:, 0, :W], pt[0:64, :W], cs[:, 0, :W], ALU.mult, [r_pt, r_cs], [r_tk])
                    P.tt(P.dve, tk[:, 1, :W], pt2[0:64, :W], cs[:, 1, :W], ALU.mult, [r_pt2, r_cs], [r_tk])
                    P.tt(P.pool, qr[:, c0:c0 + W], tk[:, 0, :W], tk[:, 1, :W], ALU.add, [r_tk], [r_qr])
                else:
                    P.copy(P.dve, qr[:, c0:c0 + W], pt[0:64, :W], [r_pt], [r_qr])
            for kg in range(0, NT // 128, 4):
                nk4 = min(4, NT // 128 - kg)
                pt, r_pt = self.pb[kb % 4]
                kb += 1
                for q4 in range(nk4):
                    kc = kg + q4
                    for k2 in range(2):
                        P.mm(pt[:, q4 * 128:(q4 + 1) * 128], ckv[:, k2, kc * 128:(kc + 1) * 128], wkv[:, k2, 128:256], k2 == 0, k2 == 1, [r_ckv, r_wkv], [r_pt])
                P.copy(P.act, vv[:, kg:kg + nk4, :], pt[:, 0:nk4 * 128].rearrange("p (a b) -> p a b", b=128), [r_pt], [r_vv])
            for qi, (c0, W, j) in enumerate(qtiles):
                nk = 2 if j == 1 else NT // 128
                po, r_po = self.pb[4 + 2 * (qi % 2)]
                pl, r_pl = self.pb[5 + 2 * (qi % 2)]
                pend = None
                for kc in range(nk + 1):
                    if kc < nk:
                        pS, r_pS = self.pb[kc % 4]
                        P.mm(pS[:, :W], kn[:, kc * 128:(kc + 1) * 128], qn[:, c0:c0 + W], True, False, [r_kn, r_qn], [r_pS])
                        P.mm(pS[:, :W], krope[:, kc * 128:(kc + 1) * 128], qr[:, c0:c0 + W], False, True, [r_kr, r_qr], [r_pS])
                        pT, r_pT = pT_r.next()
                        P.actv(pT[:, :W], pS[:, :W], AF.Exp, [r_pS], [r_pT], scale=MLA_SCALE)
                    if pend is not None:
                        pkc, ppT, pr_pT = pend
                        P.mm(po[:, :W], vv[:, pkc, :], ppT[:, :W], pkc == 0, pkc == nk - 1, [r_vv, pr_pT], [r_po])
                        P.mm(pl[:, :W], self.ones_b[:], ppT[:, :W], pkc == 0, pkc == nk - 1, [self.r_const, pr_pT], [r_pl])
                    pend = (kc, pT, r_pT) if kc < nk else None
                ri, r_ri = ri_r.next()
                P.recip(ri[:, :W], pl[:, :W], [r_pl], [r_ri])
                oo, r_oo = oo_r.next()
                P.tt(P.dve, oo[:, :W], po[:, :W], ri[:, :W], ALU.mult, [r_po, r_ri], [r_oo])
                dsts = [lambda e, h=h, c0=c0, W=W: self.YM[h * 128:(h + 1) * 128, c0:c0 + W]]
                self.head_norm_store(oo[:, :W], W, l, self.sp_col(l, SP_HG + 8 + h), dsts, sq_r, st_r, ob_r, 4 + 2 * (qi % 2), r_oo)
        P.pop(mk)

    def phase_outproj(self, l, need_ctx):
        P = self.P
        li = self.lidx[l]
        mk = P.push()
        yc_r = P.ring("yc", 2, [128, 16, 512], BF16)
        w_r = P.ring("wo", 2, [128, 16, 512], BF16)
        mix_r = P.ring("mix", 1, [128, 16, 512], F32)
        xs_r = P.ring("oxs", 1, [128, 16, 512], F32)
        hb_r = P.ring("ohb", 1, [128, 16, 512], BF16)
        sq_r = P.ring("osq", 3, [128, 512], BF16)
        st_r = P.ring("ost", 2, [128, 512], F32)
        tmp_r = P.ring("otmp", 3, [128, 512], F32)
        YMv = self.YM.rearrange("(c p) t -> p c t", p=128)
        XTv = self.XT.rearrange("(c p) t -> p c t", p=128)
        H2v = self.H2.rearrange("(c p) t -> p c t", p=128)
        Wv = self.WOUT[li].rearrange("(kc p) n -> p kc n", p=128)
        kb = 0
        ltiles = self.loc_tiles(need_ctx)
        for (c0, W, j) in ltiles:
            yc, r_yc = yc_r.next()
            P.ld("sp", yc[:, 8:16, :W], YMv[:, :, c0:c0 + W], writes=[r_yc])
            for kind in range(2):
                def ldy(e, yc=yc, c0=c0, W=W, kind=kind):
                    win = self.YG[kind].rearrange("(x i) t -> i x t", i=128)
                    return e.dma_start(out=yc[:, kind * 4:(kind + 1) * 4, :W], in_=win[:, bass.ds(self.rv(e, "x"), NR, 2), c0:c0 + W])
                P.dma("pool", ldy, writes=[r_yc])
            xs, r_xs = xs_r.next()
            P.ld("sp", xs[:, :, :W], XTv[:, :, c0:c0 + W], writes=[r_xs])
            mix, r_mix = mix_r.next()
            for g in range(4):
                wt, r_wt = w_r.next()
                P.ld("act", wt[:], Wv[:, :, g * 512:(g + 1) * 512], writes=[r_wt])
                for m in range(4):
                    pt, r_pt = self.pb[kb % 6]
                    kb += 1
                    for kc in range(16):
                        P.mm(pt[:, :W], wt[:, kc, m * 128:(m + 1) * 128], yc[:, kc, :W], kc == 0, kc == 15, [r_wt, r_yc], [r_pt])
                    P.copy(P.act if kb % 2 == 0 else P.dve, mix[:, g * 4 + m, :W], pt[:, :W], [r_pt], [r_mix])
            st, r_st = self.rms_rstd([mix[:, c, :W] for c in range(16)], W, D, sq_r, 7, st_r, [r_mix])
            for c in range(16):
                tmp, r_tmp = tmp_r.next()
                P.tt(P.pool, tmp[:, :W], mix[:, c, :W], st[:, :W], ALU.mult, [r_mix, r_st], [r_tmp])
                P.stt(xs[:, c, :W], tmp[:, :W], self.mod(2, c, j), xs[:, c, :W], ALU.mult, ALU.add, [r_tmp, r_xs, self.r_mods], [r_xs])
            P.ld("sp", XTv[:, :, c0:c0 + W], xs[:, :, :W], reads=[r_xs])
            st, r_st = self.rms_rstd([xs[:, c, :W] for c in range(16)], W, D, sq_r, 6, st_r, [r_xs])
            hb, r_hb = hb_r.next()
            for c in range(16):
                tmp, r_tmp = tmp_r.next()
                P.tt(P.dve, tmp[:, :W], xs[:, c, :W], st[:, :W], ALU.mult, [r_xs, r_st], [r_tmp])
                P.actv(hb[:, c, :W], tmp[:, :W], AF.Identity, [r_tmp, self.r_mods], [r_hb], bias=self.mod(4, c, j), scale=self.mod(3, c, j))
            P.ld("sp", H2v[:, :, c0:c0 + W], hb[:, :, :W], reads=[r_hb])
            HBv = self.HB.rearrange("(c p) k -> p c k", p=128)
            if c0 == CTX:
                P.ld("act", HBv[:, :, 0:1], hb[:, :, 0:1], reads=[r_hb], slow=True)
            if c0 + W == NTL:
                P.ld("act", HBv[:, :, 1:2], hb[:, :, W - 1:W], reads=[r_hb], slow=True)
        P.pop(mk)
        self.allgather([(self.HB, self.HG)])
        P.ld("sp", self.HGP[D:(NR + 1) * D, :], self.HG[:, :])
        P.barrier()

    def phase_ffn(self, l, need_ctx):
        P = self.P
        li = self.lidx[l]
        mk = P.push()
        OT = 342
        tiles = []
        if need_ctx:
            tiles.append((0, CTX, 0, CTX, 1))
        for lo in range(0, SL, OT):
            tiles.append((CTX, SL, lo, min(SL, lo + OT), 0))
        WT = OT + 2
        h2_r = P.ring("fh2", 1, [128, 16, WT], BF16)
        act = P.sbuf("fact", [128, 44, OT], BF16)
        r_act = [Res() for _ in range(44)]
        wu_r = P.ring("fwu", 2, [128, 2, 16, 256], BF16)
        wd_r = P.ring("fwd", 2, [128, 44, 128], BF16)
        f_r = P.ring("ff", 1, [128, 16, OT], F32)
        xs_r = P.ring("fxs", 1, [128, 16, OT], F32)
        gc_r = P.ring("fgc", 2, [128, OT], F32)
        vc_r = P.ring("fvc", 2, [128, OT], F32)
        sq_r = P.ring("fsq", 3, [128, 512], BF16)
        st_r = P.ring("fst_", 2, [128, 512], F32)
        tmp_r = P.ring("ftmp", 3, [128, OT], F32)
        XTv = self.XT.rearrange("(c p) t -> p c t", p=128)
        H2v = self.H2.rearrange("(c p) t -> p c t", p=128)
        WU = self.WUP[li].rearrange("(kc p) n -> p kc n", p=128)
        WD = self.WDN[li].rearrange("(kc p) n -> p kc n", p=128)
        kb = 0
        for (off, Ls, lo, hi, j) in tiles:
            Wo = hi - lo
            Wt = Wo + 2
            h2, r_h2 = h2_r.next()
            a, b = lo - 1, hi + 1
            ca, cb = 0, Wt
            HGPv = self.HGP.rearrange("(c p) k -> p c k", p=128)
            if a < 0:
                if j == 1:
                    P.memset(P.pool, h2[:, :, 0:1], 0.0, [r_h2])
                else:
                    P.dma("act", lambda e, h2=h2: e.dma_start(out=h2[:, :, 0:1], in_=HGPv[:, 0:64, 1:2][:, bass.ds(self.rv(e, "16"), 16), :], allow_slow_non_contiguous=True), writes=[r_h2])
                a, ca = 0, 1
            if b > Ls:
                if j == 1:
                    P.memset(P.pool, h2[:, :, Wt - 1:Wt], 0.0, [r_h2])
                else:
                    P.dma("act", lambda e, h2=h2, Wt=Wt: e.dma_start(out=h2[:, :, Wt - 1:Wt], in_=HGPv[:, 32:96, 0:1][:, bass.ds(self.rv(e, "16"), 16), :], allow_slow_non_contiguous=True), writes=[r_h2])
                b, cb = Ls, Wt - 1
            P.ld("sp", h2[:, :, ca:cb], H2v[:, :, off + a:off + b], writes=[r_h2])
            for i in range(44):
                if i % 2 == 0:
                    wu, r_wu = wu_r.next()
                    P.ld("act", wu[:, 0, :, :], WU[:, :, i * 128:i * 128 + 256], writes=[r_wu])
                    P.ld("act", wu[:, 1, :, :], WU[:, :, DFF + i * 128:DFF + i * 128 + 256], writes=[r_wu])
                s0 = (i % 2) * 128
                pg, r_pg = self.pb[(2 * i) % 6]
                pv, r_pv = self.pb[(2 * i + 1) % 6]
                for kc in range(16):
                    P.mm(pg[:, :Wt], wu[:, 0, kc, s0:s0 + 128], h2[:, kc, :Wt], kc == 0, kc == 15, [r_wu, r_h2], [r_pg])
                for kc in range(16):
                    P.mm(pv[:, :Wt], wu[:, 1, kc, s0:s0 + 128], h2[:, kc, :Wt], kc == 0, kc == 15, [r_wu, r_h2], [r_pv])
                gc, r_gc = gc_r.next()
                vc, r_vc = vc_r.next()
                for (pp, r_pp, oc, r_oc, ci) in ((pg, r_pg, gc, r_gc, i), (pv, r_pv, vc, r_vc, 44 + i)):
                    P.ts(P.dve, oc[:, :Wo], pp[:, 0:Wo], self.sp_col(l, SP_FC + 0 * 88 + ci), None, ALU.mult, None, [r_pp, self.r_const], [r_oc])
                    P.stt(oc[:, :Wo], pp[:, 1:Wo + 1], self.sp_col(l, SP_FC + 1 * 88 + ci), oc[:, :Wo], ALU.mult, ALU.add, [r_pp, r_oc, self.r_const], [r_oc])
                    P.stt(oc[:, :Wo], pp[:, 2:Wo + 2], self.sp_col(l, SP_FC + 2 * 88 + ci), oc[:, :Wo], ALU.mult, ALU.add, [r_pp, r_oc, self.r_const], [r_oc])
                P.actv(gc[:, :Wo], gc[:, :Wo], AF.Gelu_apprx_tanh, [r_gc], [r_gc])
                P.tt(P.pool, act[:, i, :Wo], gc[:, :Wo], vc[:, :Wo], ALU.mult, [r_gc, r_vc], [r_act[i]])
            f, r_f = f_r.next()
            for m in range(16):
                wd, r_wd = wd_r.next()
                P.ld("act", wd[:], WD[:, :, m * 128:(m + 1) * 128], writes=[r_wd])
                pd, r_pd = self.pb[6 + m % 2]
                for kc in range(44):
                    P.mm(pd[:, :Wo], wd[:, kc, :], act[:, kc, :Wo], kc == 0, kc == 43, [r_wd, r_act[kc]], [r_pd])
                P.copy(P.act if m % 2 == 0 else P.dve, f[:, m, :Wo], pd[:, :Wo], [r_pd], [r_f])
            st, r_st = self.rms_rstd([f[:, c, :Wo] for c in range(16)], Wo, D, sq_r, 0, st_r, [r_f])
            xs, r_xs = xs_r.next()
            P.ld("sp", xs[:, :, :Wo], XTv[:, :, off + lo:off + hi], writes=[r_xs])
            for c in range(16):
                tmp, r_tmp = tmp_r.next()
                P.tt(P.pool, tmp[:, :Wo], f[:, c, :Wo], st[:, :Wo], ALU.mult, [r_f, r_st], [r_tmp])
                P.stt(xs[:, c, :Wo], tmp[:, :Wo], self.mod(5, c, j), xs[:, c, :Wo], ALU.mult, ALU.add, [r_tmp, r_xs, self.r_mods], [r_xs])
            P.ld("sp", XTv[:, :, off + lo:off + hi], xs[:, :, :Wo], reads=[r_xs])
        P.pop(mk)


def make_inmaps(inp, n_cores=8, layers=None):
    c = _consts()
    layers = list(range(DEPTH)) if layers is None else list(layers)
    shared = {k: np.ascontiguousarray(np.asarray(inp[k], dtype=np.float32)[layers]) for k in (
        "w_in", "hy_w1", "hy_w2", "mla_wuq", "mla_wukv", "w_out", "ffn_up", "ffn_down")}
    adaw = np.asarray(inp["ada_w"], dtype=np.float32)
    shared["smallp"] = _pack_small({k: np.asarray(v, dtype=np.float32) for k, v in inp.items()}).reshape(128, DEPTH * NSP)
    shared.update({k: v for k, v in c.items() if not k.startswith("decay")})
    spf = shared["smallp"].reshape(128, DEPTH, NSP)
    own_map = []
    for p_ in range(3):
        for j in range(3):
            own_map.append(SP_HYC + j * 12 + p_ * 4)
    own_map += [SP_HYB + 0, SP_HYB + 4, SP_HG, SP_HG + 4]
    for d in range(2):
        for j in range(4):
            own_map.append(SP_LC + d * 16 + j * 4)
    own_map += [SP_BA, SP_BA + 4, SP_BX, SP_BX + 4, SP_LAM, SP_LAM + 4]
    w3 = np.asarray(inp["hy_w3"], dtype=np.float32)[layers].reshape(len(layers), 64, 4, 4, 128)
    lwa = np.asarray(inp["lru_wa"], dtype=np.float32)[layers]
    lwx = np.asarray(inp["lru_wx"], dtype=np.float32)[layers]
    maps = []
    for core in range(n_cores):
        b = core // NR
        r = core % NR
        m = dict(shared)
        m["x"] = np.ascontiguousarray(np.asarray(inp["x"][b, r * SL:(r + 1) * SL], dtype=np.float32))
        m["ownp"] = np.ascontiguousarray(spf[:, :, [cc + r for cc in own_map]].reshape(128, DEPTH * NOWN))
        m["hy_w3_own"] = np.ascontiguousarray(w3[:, :, :, r, :].reshape(len(layers), 64, 512))
        m["lru_wa_own"] = np.ascontiguousarray(lwa[:, :, r])
        m["lru_wx_own"] = np.ascontiguousarray(lwx[:, :, r])
        m["ada_w_own"] = np.ascontiguousarray(adaw[layers][:, :, r * (6 * D // NR):(r + 1) * (6 * D // NR)])
        m["rope_q"] = np.ascontiguousarray(c["rope_cs"][:, :, r * SL:(r + 1) * SL])
        m["decayL"] = np.ascontiguousarray(c["decayL"][:, r * 128:(r + 1) * 128])
        m["decayC"] = np.ascontiguousarray(c["decayC"][:, r * 128:(r + 1) * 128])
        m["ctx"] = np.ascontiguousarray(np.asarray(inp["ctx"][b], dtype=np.float32))
        cv = np.stack([np.asarray(inp["c"][b], dtype=np.float32), np.asarray(inp["c_ctx"], dtype=np.float32)], axis=-1)
        m["cvec"] = np.ascontiguousarray(cv.reshape(16, 128, 2).transpose(1, 0, 2).reshape(128, 32))
        maps.append(m)
    return maps


_NC_CACHE = {}


def kernel(**inputs):
    if "nc" not in _NC_CACHE:
        _NC_CACHE["nc"] = Builder(range(DEPTH)).build()
    nc = _NC_CACHE["nc"]
    maps = make_inmaps(inputs)
    res = run_bass_kernel_spmd(nc, maps, core_ids=list(range(8)))
    out = np.stack([np.concatenate([np.asarray(res.results[b * NR + r]["out"]) for r in range(NR)], axis=0) for b in range(2)], axis=0)
    return out.astype(np.float32)
```

```python
import math
import numpy as np
import ml_dtypes
import concourse.bass as bass
import concourse.mybir as mybir
from concourse.bass_utils import run_bass_kernel_spmd

F32 = mybir.dt.float32
BF16 = mybir.dt.bfloat16
AF = mybir.ActivationFunctionType
ALU = mybir.AluOpType

D = 2048
SEQ = 4096
CTX = 256
NT = SEQ + CTX
DEPTH = 4
NR = 4
SL = SEQ // NR
NTL = CTX + SL
GROUPS = [[0, 1, 2, 3], [4, 5, 6, 7]]
NOWN = 27
UCH = 14
DIN = 3392
DINX = 3456
OFF_HY, OFF_G, OFF_Q, OFF_X, OFF_KV, OFF_KR, OFF_KRR = 0, 1536, 2048, 2560, 3072, 3328, 3392
DFF = 5632
EPS = 1e-6
MLA_SCALE = 192 ** -0.5
MAGIC = 12582912.0

SP_ADAB, SP_NG, SP_HYC, SP_HYB, SP_LC, SP_BA, SP_BX, SP_LAM = 0, 96, 160, 196, 204, 236, 244, 252
SP_GQ, SP_GKV, SP_HG, SP_FC, SP_B1, SP_B2, NSP = 260, 264, 266, 282, 546, 547, 548


class Res:
    __slots__ = ("name", "w", "r")

    def __init__(self, name=""):
        self.name = name
        self.w = None
        self.r = {}


class EngQ:
    def __init__(self, name, eng, inorder=False):
        self.name = name
        self.eng = eng
        self.inorder = inorder
        self.sem = None
        self.step = 1
        self.count = 0
        self.waited = {}
        self.thunks = []


class DmaSlot:
    def __init__(self, sem, name):
        self.sem = sem
        self.name = name
        self.count = 0
        self.step = 16
        self.inorder = False


class Ring:
    def __init__(self, items):
        self.items = items
        self.i = 0

    def next(self):
        it = self.items[self.i % len(self.items)]
        self.i += 1
        return it


class Prog:
    def __init__(self, nc, n_dma_slots=20, same_engine_sync=True):
        self.nc = nc
        self.same_engine_sync = same_engine_sync
        self.ctx = []
        self.pe = EngQ("pe", nc.tensor, inorder=True)
        self.dve = EngQ("dve", nc.vector)
        self.act = EngQ("act", nc.scalar)
        self.pool = EngQ("pool", nc.gpsimd)
        self.sp = EngQ("sp", nc.sync)
        self.engs = [self.pe, self.dve, self.act, self.pool, self.sp]
        for e in self.engs:
            e.sem = self._sem("s_" + e.name)
        self.slots = {}
        for qn in ("sp", "act", "pool"):
            self.slots[qn] = [DmaSlot(self._sem(f"d_{qn}{i}"), f"d_{qn}{i}") for i in range(n_dma_slots)]
        self.slot_rr = {"sp": 0, "act": 0, "pool": 0}
        self.n_inst = 0
        self.uid = 0

    def _sem(self, name):
        cm = self.nc.semaphore(name)
        s = cm.__enter__()
        self.ctx.append(cm)
        return s

    def push(self):
        return len(self.ctx)

    def pop(self, mark):
        self.barrier()
        while len(self.ctx) > mark:
            self.ctx.pop().__exit__(None, None, None)

    def sbuf(self, name, shape, dtype):
        self.uid += 1
        cm = self.nc.sbuf_tensor(f"{name}_{self.uid}", list(shape), dtype)
        t = cm.__enter__()
        self.ctx.append(cm)
        return t

    def psum(self, name, shape, dtype=F32):
        cm = self.nc.psum_tensor(name, list(shape), dtype)
        t = cm.__enter__()
        self.ctx.append(cm)
        return t

    def ring(self, name, n, shape, dtype):
        return Ring([(self.sbuf(f"{name}{i}", shape, dtype), Res(f"{name}{i}")) for i in range(n)])

    def _deps(self, reads, writes):
        deps = []
        for r in reads:
            if r.w is not None:
                deps.append(r.w)
        for w in writes:
            if w.w is not None:
                deps.append(w.w)
            deps.extend(w.r.values())
        return deps

    def _emit_waits(self, q, deps):
        need = {}
        for (src, c) in deps:
            if src is q and (q.inorder or not self.same_engine_sync):
                continue
            if q.waited.get(id(src), 0) >= c:
                continue
            if need.get(id(src), (None, 0))[1] < c:
                need[id(src)] = (src, c)
        for src, c in need.values():
            q.waited[id(src)] = c
            sem, val = src.sem, c * src.step
            q.thunks.append(lambda e, sem=sem, val=val: e.wait_ge(sem, val))

    def _mark(self, src, c, reads, writes):
        for r in reads:
            old = r.r.get(id(src))
            if old is None or old[1] < c:
                r.r[id(src)] = (src, c)
        for w in writes:
            w.w = (src, c)
            w.r = {}

    def op(self, q, fn, reads=(), writes=()):
        self._emit_waits(q, self._deps(reads, writes))
        q.count += 1
        sem = q.sem
        q.thunks.append(lambda e, fn=fn, sem=sem: fn(e).then_inc(sem, 1))
        self._mark(q, q.count, reads, writes)
        self.n_inst += 1

    def dma(self, qname, fn, reads=(), writes=()):
        q = {"sp": self.sp, "act": self.act, "pool": self.pool}[qname]
        slots = self.slots[qname]
        i = self.slot_rr[qname]
        self.slot_rr[qname] = (i + 1) % len(slots)
        slot = slots[i]
        deps = self._deps(reads, writes)
        if slot.count > 0:
            deps.append((slot, slot.count))
        self._emit_waits(q, deps)
        slot.count += 1
        sem = slot.sem
        q.thunks.append(lambda e, fn=fn, sem=sem: fn(e).then_inc(sem, 16))
        self._mark(slot, slot.count, reads, writes)
        self.n_inst += 1

    def barrier(self):
        srcs = list(self.engs)
        for qn in self.slots:
            srcs.extend(self.slots[qn])
        deps = [(s, s.count) for s in srcs if s.count > 0]
        for q in self.engs:
            need = [(s, c) for (s, c) in deps]
            keep_sync = self.same_engine_sync
            self.same_engine_sync = True
            self._emit_waits(q, need)
            self.same_engine_sync = keep_sync

    def mm(self, out, lhsT, rhs, start, stop, reads, writes):
        self.op(self.pe, lambda e: e.matmul(out, lhsT=lhsT, rhs=rhs, start=start, stop=stop), reads, writes)

    def tr(self, out, in_, ident, reads, writes):
        self.op(self.pe, lambda e: e.transpose(out, in_, ident), reads, writes)

    def tt(self, q, out, in0, in1, op, reads, writes):
        self.op(q, lambda e: e.tensor_tensor(out=out, in0=in0, in1=in1, op=op), reads, writes)

    def ts(self, q, out, in0, s1, s2, op0, op1, reads, writes):
        if op1 is None:
            self.op(q, lambda e: e.tensor_scalar(out=out, in0=in0, scalar1=s1, scalar2=None, op0=op0), reads, writes)
        else:
            self.op(q, lambda e: e.tensor_scalar(out=out, in0=in0, scalar1=s1, scalar2=s2, op0=op0, op1=op1), reads, writes)

    def stt(self, out, in0, scalar, in1, op0, op1, reads, writes):
        self.op(self.dve, lambda e: e.scalar_tensor_tensor(out=out, in0=in0, scalar=scalar, in1=in1, op0=op0, op1=op1), reads, writes)

    def actv(self, out, in_, func, reads, writes, bias=None, scale=None):
        kw = {}
        if bias is not None:
            kw["bias"] = bias
        if scale is not None:
            kw["scale"] = scale
        self.op(self.act, lambda e: e.activation(out=out, in_=in_, func=func, **kw), reads, writes)

    def copy(self, q, out, in_, reads, writes):
        if q is self.act:
            self.op(q, lambda e: e.activation(out=out, in_=in_, func=AF.Copy), reads, writes)
        else:
            self.op(q, lambda e: e.tensor_copy(out=out, in_=in_), reads, writes)

    def recip(self, out, in_, reads, writes):
        self.op(self.dve, lambda e: e.reciprocal(out=out, in_=in_), reads, writes)

    def memset(self, q, ap, val, writes):
        self.op(q, lambda e: e.memset(ap, val), (), writes)

    def ld(self, qname, out, in_, reads=(), writes=(), slow=False):
        if slow:
            self.dma(qname, lambda e: e.dma_start(out=out, in_=in_, allow_slow_non_contiguous=True), reads, writes)
        else:
            self.dma(qname, lambda e: e.dma_start(out=out, in_=in_), reads, writes)

    def finish(self, final_res):
        deps = [r.w for r in final_res if r.w is not None]
        self._emit_waits(self.sp, deps)
        self.barrier()
        nc = self.nc
        with nc.Block() as block:
            @block.sync
            def _(e):
                for t in self.sp.thunks:
                    t(e)

            @block.tensor
            def _(e):
                for t in self.pe.thunks:
                    t(e)

            @block.vector
            def _(e):
                for t in self.dve.thunks:
                    t(e)

            @block.scalar
            def _(e):
                for t in self.act.thunks:
                    t(e)

            @block.gpsimd
            def _(e):
                for t in self.pool.thunks:
                    t(e)
        while self.ctx:
            self.ctx.pop().__exit__(None, None, None)


_CONST_CACHE = {}


def _bf(a):
    return np.ascontiguousarray(a.astype(np.float32)).astype(ml_dtypes.bfloat16)


def _dft_tables(L):
    N = 2 * L
    nfc = (L + 1 + 127) // 128
    FP = nfc * 128
    a = np.arange(FP, dtype=np.int64)
    t = np.arange(L, dtype=np.int64)
    ang = 2.0 * np.pi * ((t[:, None] * a[None, :]) % N).astype(np.float64) / N
    valid = (a <= L).astype(np.float64)
    C = np.cos(ang) * valid
    S = np.sin(ang) * valid
    ntc = L // 128
    Fc = C.reshape(ntc, 128, nfc, 128).transpose(2, 1, 0, 3)
    Fs = S.reshape(ntc, 128, nfc, 128).transpose(2, 1, 0, 3)
    w = np.where((a == 0) | (a == L), 1.0, 2.0) * valid / N
    IcM = C.T * w[:, None]
    IsM = S.T * w[:, None]
    TW = min(512, L)
    ntt = L // TW
    Ic = IcM.reshape(nfc, 128, ntt, TW).transpose(2, 1, 0, 3)
    Is = IsM.reshape(nfc, 128, ntt, TW).transpose(2, 1, 0, 3)
    return _bf(Fc), _bf(Fs), _bf(Ic), _bf(Is)


def _hy_consts(L):
    t = (np.arange(L, dtype=np.float32) / np.float32(L)).astype(np.float32)
    ang = (2.0 * math.pi * t[:, None] * np.arange(1, 17, dtype=np.float32)).astype(np.float32)
    feats = np.concatenate([t[:, None], np.sin(ang), np.cos(ang)], axis=-1).astype(np.float32)
    deltas = np.abs(np.linspace(math.log(1e-2) / 1.5, math.log(1e-2) / 0.3, 512, dtype=np.float32))
    decay = np.exp(-t[:, None] * deltas[None, :]).astype(np.float32)
    return np.ascontiguousarray(feats.T), decay


def _rope_tables():
    row = np.repeat(np.arange(SEQ // 64, dtype=np.float32), 64)
    col = np.tile(np.arange(64, dtype=np.float32), SEQ // 64)
    inv = (10000.0 ** (-np.arange(16, dtype=np.float32) / 16)).astype(np.float32)
    ang = np.concatenate([row[:, None] * inv, col[:, None] * inv], axis=-1).astype(np.float32)
    cs = np.zeros((64, 2, SEQ), np.float32)
    cs[:32, 0] = np.cos(ang).T
    cs[32:, 0] = np.cos(ang).T
    cs[:32, 1] = np.sin(ang).T
    cs[32:, 1] = np.sin(ang).T
    return cs


def _consts():
    if _CONST_CACHE:
        return _CONST_CACHE
    c = {}
    c["ident_f"] = np.eye(128, dtype=np.float32)
    c["rope_cs"] = _rope_tables()
    for L, tag in ((SEQ, "L"), (CTX, "C")):
        Fc, Fs, Ic, Is = _dft_tables(L)
        c["Fc" + tag], c["Fs" + tag], c["Ic" + tag], c["Is" + tag] = Fc, Fs, Ic, Is
        f, d = _hy_consts(L)
        c["feats" + tag], c["decay" + tag] = f, d
    _CONST_CACHE.update(c)
    return _CONST_CACHE


def _pack_small(inp):
    sp = np.zeros((128, DEPTH, NSP), np.float32)

    def put(col, arr):
        n = arr.shape[1] // 128
        sp[:, :, col:col + n] = arr.reshape(DEPTH, n, 128).transpose(2, 0, 1)

    put(SP_ADAB, inp["ada_b"])
    put(SP_NG, inp["norm_g"].reshape(DEPTH, 4 * D))
    put(SP_HYC, inp["hy_conv"].reshape(DEPTH, 3 * 1536))
    put(SP_HYB, inp["hy_bias"].reshape(DEPTH, 2 * 512))
    put(SP_LC, inp["lru_conv"].reshape(DEPTH, 2 * 4 * 512))
    put(SP_BA, inp["lru_ba"].reshape(DEPTH, 1024))
    put(SP_BX, inp["lru_bx"].reshape(DEPTH, 1024))
    put(SP_LAM, inp["lru_lam"].reshape(DEPTH, 1024))
    put(SP_GQ, inp["mla_gq"])
    put(SP_GKV, inp["mla_gkv"])
    put(SP_HG, inp["head_g"])
    put(SP_FC, inp["ffn_conv"].reshape(DEPTH, 3 * 2 * DFF))
    sp[:64, :, SP_B1] = inp["hy_b1"].T
    sp[:64, :, SP_B2] = inp["hy_b2"].T
    return sp


class Builder:
    def __init__(self, layers, dbg=(), phases=None, same_engine_sync=True):
        self.layers = list(layers)
        self.lidx = {l: i for i, l in enumerate(self.layers)}
        NL = len(self.layers)
        self.dbg = set(dbg)
        self.phases = phases
        nc = self.nc = bass.Bass("TRN2", target_bir_lowering=False)
        self.I = {}
        din = self.din
        self.x_in = din("x", [SL, D])
        self.ctx_in = din("ctx", [CTX, D])
        self.cvec = din("cvec", [128, 32])
        self.smallp_in = din("smallp", [128, DEPTH * NSP])
        self.ownp_in = din("ownp", [128, DEPTH * NOWN])
        self.w3own = din("hy_w3_own", [NL, 64, 512])
        self.lwa_own = din("lru_wa_own", [NL, 2, 128, 128])
        self.lwx_own = din("lru_wx_own", [NL, 2, 128, 128])
        self.ada_w = din("ada_w_own", [NL, D, 6 * D // NR])
        self.w_in = din("w_in", [NL, D, DIN])
        self.hy_w1 = din("hy_w1", [NL, 33, 64])
        self.hy_w2 = din("hy_w2", [NL, 64, 64])
        self.mla_wuq = din("mla_wuq", [NL, 512, 1536])
        self.mla_wukv = din("mla_wukv", [NL, 256, 2048])
        self.w_out = din("w_out", [NL, D, D])
        self.ffn_up = din("ffn_up", [NL, D, 2 * DFF])
        self.ffn_down = din("ffn_down", [NL, DFF, D])
        self.ident_in = din("ident_f", [128, 128])
        self.rope_cs = din("rope_cs", [64, 2, SEQ])
        self.rope_q = din("rope_q", [64, 2, SL])
        self.tab = {}
        for L, tag in ((SEQ, "L"), (CTX, "C")):
            nfc = (L + 1 + 127) // 128
            ntc = L // 128
            TW = min(512, L)
            self.tab[tag] = dict(
                L=L, nfc=nfc, ntc=ntc, TW=TW, ntt=L // TW,
                Fc=din("Fc" + tag, [nfc, 128, ntc, 128], BF16), Fs=din("Fs" + tag, [nfc, 128, ntc, 128], BF16),
                Ic=din("Ic" + tag, [L // TW, 128, nfc, TW], BF16), Is=din("Is" + tag, [L // TW, 128, nfc, TW], BF16),
                feats=din("feats" + tag, [33, L]), decay=din("decay" + tag, [L, 128]),
                S=self.dscr("S" + tag, [NL, nfc * 128, 512], F32),
            )
        self.out = nc.dram_tensor("out", [SL, D], F32, kind="ExternalOutput").ap()
        ds = self.dscr
        self.XT = ds("XT", [D, NTL], F32)
        self.UC = ds("UC", [DINX, CTX], F32)
        self.ULOC = ds("ULOC", [UCH * 256, SL], F32)
        self.UG = ds("UG", [UCH * NR * 256, SL], F32)
        self.MIN = ds("MIN", [128, 48], F32)
        self.MG = ds("MG", [NR * 128, 48], F32)
        self.HB = ds("HB", [D, 2], BF16)
        self.HG = ds("HG", [NR * D, 2], BF16)
        self.HGP = ds("HGP", [(NR + 2) * D, 2], BF16)
        self.ZT = ds("ZT", [128, NT], F32)
        self.PT = ds("PT", [256, NT], F32)
        self.YIN = [ds(f"YIN{k}", [NR * 128, NTL], BF16) for k in range(2)]
        self.YG = [ds(f"YG{k}", [2 * NR * 256, NTL], BF16) for k in range(2)]
        self.YM = ds("YM", [1024, NTL], BF16)
        self.H2 = ds("H2", [D, NTL], BF16)
        self.WIN = ds("WIN", [NL, D, DINX], BF16)
        self.WUQ = ds("WUQ", [NL, 512, 2048], BF16)
        self.WUKV = ds("WUKV", [NL, 256, 2048], BF16)
        self.WOUT = ds("WOUT", [NL, D, D], BF16)
        self.WUP = ds("WUP", [NL, D, 2 * DFF], BF16)
        self.WDN = ds("WDN", [NL, DFF, D], BF16)
        self.P = Prog(nc, same_engine_sync=same_engine_sync)
        self.r_out = Res("out")
        self._rank = {}

    def din(self, name, shape, dt=F32):
        self.I[name] = (list(shape), dt)
        return self.nc.dram_tensor(name, list(shape), dt, kind="ExternalInput").ap()

    def dscr(self, name, shape, dt):
        kind = "ExternalOutput" if name in self.dbg else "Internal"
        return self.nc.dram_tensor(name, list(shape), dt, kind=kind).ap()

    def rank(self, e):
        key = id(e)
        if key not in self._rank:
            self._rank[key] = e.snap(e.partition_id() % NR)
        return self._rank[key]

    def rv(self, e, kind):
        key = (kind, id(e))
        if key not in self._rank:
            r = self.rank(e)
            expr = {"x": (r // 2) * 8 + (r % 2), "128": r * 128, "16": r * 16}[kind]
            self._rank[key] = e.snap(expr)
        return self._rank[key]

    def Uap(self, row0, nrows, c0, W):
        if c0 < CTX:
            return self.UC[row0:row0 + nrows, c0:c0 + W]
        t0 = c0 - CTX
        s_, tl = t0 // SL, t0 % SL
        assert tl + W <= SL
        k, rr = row0 // 256, row0 % 256
        assert rr + nrows <= 256
        base = (k * NR + s_) * 256 + rr
        return self.UG[base:base + nrows, tl:tl + W]

    def Ulat(self, row0):
        k, rr = row0 // 256, row0 % 256
        return self.UG.rearrange("(k s r) t -> k r s t", s=NR, r=256)[k][rr:rr + 128, :, :]

    def loc_tiles(self, need_ctx=True):
        tiles = [(0, CTX, 1)] if need_ctx else []
        tiles += [(CTX + i * 512, 512, 0) for i in range(SL // 512)]
        return tiles

    def allgather(self, pairs):
        P = self.P
        P.barrier()
        for (src, dst) in pairs:
            P.op(P.pool, lambda e, src=src, dst=dst: e.collective_compute("AllGather", ALU.bypass, replica_groups=GROUPS, ins=[src.opt()], outs=[dst.opt()]), (), ())
        P.barrier()

    def on(self, ph):
        return self.phases is None or ph in self.phases

    def build(self):
        P = self.P
        self.setup()
        for l in self.layers:
            last = (l == DEPTH - 1)
            if self.on("M"):
                self.phase_mod(l)
            if self.on("A"):
                self.phase_inproj(l)
            if self.on("H"):
                if not last:
                    self.phase_hyena(l, "C")
                self.phase_hyena(l, "L")
            if self.on("R"):
                self.phase_lru(l, not last)
                self.allgather([(self.YIN[k][q * 256:(q + 1) * 256, :], self.YG[k][q * 1024:(q + 1) * 1024, :]) for k in range(2) for q in range(2)])
            if self.on("T"):
                self.phase_attn(l, not last)
            if self.on("O"):
                self.phase_outproj(l, not last)
            if self.on("F"):
                self.phase_ffn(l, not last)
        if self.on("Z"):
            self.final()
        P.finish([self.r_out])
        return self.nc

    def setup(self):
        P = self.P
        self.pb = [(P.psum(f"pb{i}", [128, 512]), Res(f"pb{i}")) for i in range(8)]
        self.ident = P.sbuf("ident", [128, 128], F32)
        self.r_const = Res("const")
        P.ld("sp", self.ident[:], self.ident_in[:, :], writes=[self.r_const])
        self.ones_b = P.sbuf("ones_b", [128, 128], BF16)
        self.ones_f = P.sbuf("ones_f", [128, 128], F32)
        P.memset(P.dve, self.ones_b[:], 1.0, [self.r_const])
        P.memset(P.dve, self.ones_f[:], 1.0, [self.r_const])
        self.smallp = P.sbuf("smallp", [128, DEPTH * NSP], F32)
        P.ld("sp", self.smallp[:], self.smallp_in[:, :], writes=[self.r_const])
        self.scv = P.sbuf("scv", [128, 32], F32)
        cvt = P.sbuf("cvt", [128, 32], F32)
        P.ld("sp", cvt[:], self.cvec[:, :], writes=[self.r_const])
        P.actv(self.scv[:], cvt[:], AF.Silu, [self.r_const], [self.r_const])
        self.eps_t = P.sbuf("eps_t", [128, 1], F32)
        P.memset(P.dve, self.eps_t[:], EPS, [self.r_const])
        self.ownp = P.sbuf("ownp", [128, DEPTH * NOWN], F32)
        P.ld("sp", self.ownp[:], self.ownp_in[:, :], writes=[self.r_const])
        self.mods = P.sbuf("mods", [128, 6 * 32], F32)
        self.r_mods = Res("mods")
        self.lru_h0 = P.sbuf("lru_h0", [128, 8], F32)
        self.r_h0 = Res("h0")
        for l in self.layers:
            for (src, dst, rows, cols, dcols) in (
                (self.w_in[self.lidx[l]], self.WIN[self.lidx[l]], D, DIN, DINX),
                (self.w_out[self.lidx[l]], self.WOUT[self.lidx[l]], D, D, D),
                (self.ffn_up[self.lidx[l]], self.WUP[self.lidx[l]], D, 2 * DFF, 2 * DFF),
                (self.ffn_down[self.lidx[l]], self.WDN[self.lidx[l]], DFF, D, D),
                (self.mla_wukv[self.lidx[l]], self.WUKV[self.lidx[l]], 256, 2048, 2048),
            ):
                cw = 1408 if cols % 1408 == 0 else (1696 if cols % 1696 == 0 else 1024)
                assert cols % cw == 0 and cw <= 2048
                for r0 in range(0, rows, 1024):
                    r1 = min(rows, r0 + 1024)
                    for c0 in range(0, cols, cw):
                        P.ld("pool", dst[r0:r1, c0:c0 + cw], src[r0:r1, c0:c0 + cw])
            for h in range(8):
                P.ld("pool", self.WUQ[self.lidx[l]][:, h * 256:h * 256 + 192], self.mla_wuq[self.lidx[l]][:, h * 192:(h + 1) * 192])
            mk = P.push()
            wk = P.sbuf("wk", [128, 16, 64], F32)
            wr = P.sbuf("wr", [128, 16, 64], BF16)
            r_wk, r_wr = Res(), Res()
            P.ld("sp", wk[:], self.w_in[self.lidx[l]].rearrange("(kc p) n -> p kc n", p=128)[:, :, OFF_KR:OFF_KR + 64], writes=[r_wk])
            P.ts(P.dve, wr[:, :, 0:32], wk[:, :, 32:64], -1.0, None, ALU.mult, None, [r_wk], [r_wr])
            P.copy(P.dve, wr[:, :, 32:64], wk[:, :, 0:32], [r_wk], [r_wr])
            P.ld("sp", self.WIN[self.lidx[l]].rearrange("(kc p) n -> p kc n", p=128)[:, :, OFF_KRR:OFF_KRR + 64], wr[:], reads=[r_wr])
            qk = P.sbuf("qk", [128, 4, 8, 64], F32)
            qr = P.sbuf("qr", [128, 4, 8, 64], BF16)
            r_qk, r_qr = Res(), Res()
            srcv = self.mla_wuq[self.lidx[l]].rearrange("(kc p) (h n) -> p kc h n", p=128, n=192)
            dstv = self.WUQ[self.lidx[l]].rearrange("(kc p) (h n) -> p kc h n", p=128, n=256)
            for kc in range(4):
                P.ld("sp", qk[:, kc, :, :], srcv[:, kc, :, 128:192], writes=[r_qk])
            P.ts(P.dve, qr[:, :, :, 0:32], qk[:, :, :, 32:64], -1.0, None, ALU.mult, None, [r_qk], [r_qr])
            P.copy(P.dve, qr[:, :, :, 32:64], qk[:, :, :, 0:32], [r_qk], [r_qr])
            for kc in range(4):
                P.ld("sp", dstv[:, kc, :, 192:256], qr[:, kc, :, :], reads=[r_qr])
            P.pop(mk)
        mk = P.push()
        xin = P.ring("xin", 2, [128, D], F32)
        xst = P.ring("xst", 2, [128, 16, 128], F32)
        XTv = self.XT.rearrange("(c p) t -> p c t", p=128)
        k = 0
        for ti in range(NTL // 128):
            src = self.ctx_in[ti * 128:(ti + 1) * 128, :] if ti < 2 else self.x_in[(ti - 2) * 128:(ti - 1) * 128, :]
            xt, r_xt = xin.next()
            P.ld("sp" if ti % 2 == 0 else "act", xt[:], src, writes=[r_xt])
            st, r_st = xst.next()
            for g in range(4):
                pt, r_pt = self.pb[k % 8]
                k += 1
                for j in range(4):
                    c = g * 4 + j
                    P.tr(pt[:, j * 128:(j + 1) * 128], xt[:, c * 128:(c + 1) * 128], self.ident[:], [r_xt, self.r_const], [r_pt])
                P.copy(P.act if g % 2 == 0 else P.dve, st[:, g * 4:(g + 1) * 4, :], pt[:].rearrange("p (j t) -> p j t", j=4), [r_pt], [r_st])
            P.ld("sp" if ti % 2 == 1 else "act", XTv[:, :, ti * 128:(ti + 1) * 128], st[:], reads=[r_st])
        P.pop(mk)
        if self.on("H"):
            ls = self.layers
            for i in range(0, len(ls), 2):
                self.hy_filter_all(self.tab["L"], ls[i:i + 2])
            lc_ = [l for l in ls if l < DEPTH - 1]
            for i in range(0, len(lc_), 2):
                self.hy_filter_all(self.tab["C"], lc_[i:i + 2])
        mk = P.push()
        zt = P.sbuf("zpad", [128, 16, 2], BF16)
        r_z = Res()
        P.memset(P.dve, zt[:], 0.0, [r_z])
        for blk in (0, NR + 1):
            P.ld("sp", self.HGP[blk * D:(blk + 1) * D, :].rearrange("(c p) k -> p c k", p=128), zt[:], reads=[r_z])
        P.pop(mk)

    def final(self):
        P = self.P
        mk = P.push()
        xin = P.ring("fin", 2, [128, 16, 128], F32)
        xst = P.ring("fst", 2, [128, D], F32)
        XTv = self.XT.rearrange("(c p) t -> p c t", p=128)
        k = 0
        for ti in range(SL // 128):
            xt, r_xt = xin.next()
            P.ld("sp" if ti % 2 == 0 else "act", xt[:], XTv[:, :, CTX + ti * 128:CTX + (ti + 1) * 128], writes=[r_xt])
            st, r_st = xst.next()
            for g in range(4):
                pt, r_pt = self.pb[k % 8]
                k += 1
                for j in range(4):
                    c = g * 4 + j
                    P.tr(pt[:, j * 128:(j + 1) * 128], xt[:, c, :], self.ident[:], [r_xt, self.r_const], [r_pt])
                P.copy(P.act if g % 2 == 0 else P.dve, st[:, g * 512:(g + 1) * 512], pt[:], [r_pt], [r_st])
            P.ld("sp" if ti % 2 == 1 else "act", self.out[ti * 128:(ti + 1) * 128, :], st[:], reads=[r_st], writes=[self.r_out])
        P.pop(mk)

    def sp_col(self, l, col, n=1, parts=128):
        base = l * NSP + col
        return self.smallp[0:parts, base:base + n]

    def rms_rstd(self, chunks, W, n_feat, sq_ring, pbi, st_ring, reads):
        P = self.P
        pt, r_pt = self.pb[pbi]
        for i, ap in enumerate(chunks):
            sq, r_sq = sq_ring.next()
            P.actv(sq[:, :W], ap, AF.Square, reads, [r_sq])
            P.mm(pt[:, :W], self.ones_b[:], sq[:, :W], i == 0, i == len(chunks) - 1, [r_sq, self.r_const], [r_pt])
        st, r_st = st_ring.next()
        P.actv(st[:, :W], pt[:, :W], AF.Sqrt, [r_pt], [r_st], bias=self.eps_t[:], scale=1.0 / n_feat)
        P.recip(st[:, :W], st[:, :W], [r_st], [r_st])
        return st, r_st

    def phase_mod(self, l):
        P = self.P
        mk = P.push()
        wr = P.ring("adaw", 2, [128, 16, 512], F32)
        modT = P.sbuf("modT", [128, 96, 2], F32)
        r_modT = Res()
        pm, r_pm = self.pb[0]
        src = self.ada_w[self.lidx[l]].rearrange("(kc p) n -> p kc n", p=128)
        scv = self.scv[:].rearrange("p (kc j) -> p kc j", j=2)
        for g in range(6):
            wt, r_wt = wr.next()
            P.ld("sp" if g % 2 == 0 else "act", wt[:], src[:, :, g * 512:(g + 1) * 512], writes=[r_wt])
            for m in range(4):
                col = (g * 4 + m) * 2
                for kc in range(16):
                    P.mm(pm[:, col:col + 2], wt[:, kc, m * 128:(m + 1) * 128], scv[:, kc, :], kc == 0, kc == 15, [r_wt, self.r_const], [r_pm])
        mloc = P.sbuf("mloc", [128, 48], F32)
        r_ml = Res()
        P.copy(P.dve, mloc[:], pm[:, 0:48], [r_pm], [r_ml])
        P.ld("sp", self.MIN[:, :], mloc[:], reads=[r_ml])
        self.allgather([(self.MIN, self.MG)])
        raw = P.sbuf("mraw", [128, 96, 2], F32)
        r_raw = Res()
        P.ld("sp", raw[:].rearrange("p (s m) j -> p s (m j)", s=NR), self.MG.rearrange("(s p) c -> p s c", p=128), writes=[r_raw])
        for j in range(2):
            P.tt(P.dve, modT[:, :, j], raw[:, :, j], self.sp_col(l, SP_ADAB, 96), ALU.add, [r_raw, self.r_const], [r_modT])
        mods = self.mods[:].rearrange("p (k c j) -> p k c j", k=6, j=2)
        for j in range(2):
            for half, (sh, sc, g, nga, ngb) in enumerate(((0, 1, 2, 0, 1), (3, 4, 5, 2, 3))):
                ng_a = self.sp_col(l, SP_NG + nga * 16, 16)
                ng_b = self.sp_col(l, SP_NG + ngb * 16, 16)
                A, B, G = mods[:, half * 3 + 0, :, j], mods[:, half * 3 + 1, :, j], mods[:, half * 3 + 2, :, j]
                P.stt(A, modT[:, sc * 16:(sc + 1) * 16, j], 1.0, ng_a, ALU.add, ALU.mult, [r_modT, self.r_const], [self.r_mods])
                P.copy(P.dve, B, modT[:, sh * 16:(sh + 1) * 16, j], [r_modT], [self.r_mods])
                P.tt(P.dve, G, modT[:, g * 16:(g + 1) * 16, j], ng_b, ALU.mult, [r_modT, self.r_const], [self.r_mods])
        P.pop(mk)

    def mod(self, k, c, j):
        i = (k * 16 + c) * 2 + j
        return self.mods[:, i:i + 1]

    def tok_tiles(self, need_ctx=True):
        tiles = [(0, CTX, 1)] if need_ctx else []
        tiles += [(CTX + i * 512, 512, 0) for i in range(SEQ // 512)]
        return tiles

    def phase_inproj(self, l):
        P = self.P
        mk = P.push()
        xs_r = P.ring("xs", 2, [128, 16, 512], F32)
        hb_r = P.ring("hb", 2, [128, 16, 512], BF16)
        sq_r = P.ring("sq", 3, [128, 512], BF16)
        tmp_r = P.ring("tmp", 3, [128, 512], F32)
        st_r = P.ring("st", 2, [128, 512], F32)
        w_r = P.ring("wg", 2, [128, 16, 512], BF16)
        ev_r = P.ring("ev", 4, [128, 512], F32)
        XTv = self.XT.rearrange("(c p) t -> p c t", p=128)
        Wv = self.WIN[self.lidx[l]].rearrange("(kc p) n -> p kc n", p=128)
        kb = 0
        for (c0, W, j) in self.loc_tiles():
            xs, r_xs = xs_r.next()
            P.ld("sp", xs[:, :, :W], XTv[:, :, c0:c0 + W], writes=[r_xs])
            st, r_st = self.rms_rstd([xs[:, c, :W] for c in range(16)], W, D, sq_r, 7, st_r, [r_xs])
            hb, r_hb = hb_r.next()
            for c in range(16):
                tmp, r_tmp = tmp_r.next()
                P.tt(P.dve, tmp[:, :W], xs[:, c, :W], st[:, :W], ALU.mult, [r_xs, r_st], [r_tmp])
                P.actv(hb[:, c, :W], tmp[:, :W], AF.Identity, [r_tmp, self.r_mods], [r_hb], bias=self.mod(1, c, j), scale=self.mod(0, c, j))
            for g in range(7):
                ncol = 512 if g < 6 else DINX - 6 * 512
                wt, r_wt = w_r.next()
                P.ld("act", wt[:, :, :ncol], Wv[:, :, g * 512:g * 512 + ncol], writes=[r_wt])
                for m in range(ncol // 128):
                    pt, r_pt = self.pb[kb % 6]
                    kb += 1
                    for kc in range(16):
                        P.mm(pt[:, :W], wt[:, kc, m * 128:(m + 1) * 128], hb[:, kc, :W], kc == 0, kc == 15, [r_wt, r_hb], [r_pt])
                    ev, r_ev = ev_r.next()
                    P.copy(P.act if kb % 2 == 0 else P.dve, ev[:, :W], pt[:, :W], [r_pt], [r_ev])
                    row = g * 512 + m * 128
                    dstU = self.UC[row:row + 128, c0:c0 + W] if j == 1 else self.ULOC[row:row + 128, c0 - CTX:c0 - CTX + W]
                    P.ld("sp", dstU, ev[:, :W], reads=[r_ev])
        P.pop(mk)
        self.allgather([(self.ULOC[k * 256:(k + 1) * 256, :], self.UG[k * NR * 256:(k + 1) * NR * 256, :]) for k in range(UCH) if k not in (8, 9)])


    def sin_layer(self, ps_ap, bias_ap, out_ap, W, rings, reads, r_ps, r_out):
        P = self.P
        v, r_v = rings[0].next()
        t1, r_t1 = rings[1].next()
        P.ts(P.dve, v[:, :W], ps_ap, bias_ap, None, ALU.add, None, [r_ps] + reads, [r_v])
        P.ts(P.dve, t1[:, :W], v[:, :W], 1.0 / (2 * math.pi), MAGIC, ALU.mult, ALU.add, [r_v], [r_t1])
        P.ts(P.dve, t1[:, :W], t1[:, :W], -MAGIC, None, ALU.add, None, [r_t1], [r_t1])
        P.stt(v[:, :W], t1[:, :W], -2 * math.pi, v[:, :W], ALU.mult, ALU.add, [r_t1, r_v], [r_v])
        P.ts(P.dve, v[:, :W], v[:, :W], -3.1415925, 3.1415925, ALU.max, ALU.min, [r_v], [r_v])
        P.actv(out_ap, v[:, :W], AF.Sin, [r_v], [r_out])

    def hy_filter_all(self, T, layers):
        P = self.P
        L, nfc, ntc, TW, ntt = T["L"], T["nfc"], T["ntc"], T["TW"], T["ntt"]
        G = len(layers)
        mk = P.push()
        ksum = P.sbuf("ksum", [128, ntc, G * 256], BF16)
        kdiff = P.sbuf("kdiff", [128, ntc, G * 256], BF16)
        r_k = [Res() for _ in range(ntc)]
        rn = P.sbuf("rn", [128, G * 256], F32)
        r_rn = Res()
        dec = P.sbuf("dec", [128, ntc, 128], F32)
        r_dec = Res()
        P.ld("act", dec[:], T["decay"].rearrange("(tc p) c -> p tc c", p=128), writes=[r_dec])
        mk1 = P.push()
        w1 = P.sbuf("hw1", [33, 64], F32)
        w2 = P.sbuf("hw2", [64, 64], F32)
        w3 = P.sbuf("hw3", [64, 4, 128], F32)
        r_w = Res()
        hid2 = P.sbuf("hid2", [64, L], F32)
        r_hid2 = Res()
        f_r = P.ring("ft", 2, [33, 512], F32)
        h1_r = P.ring("h1", 2, [64, 512], F32)
        v_r = P.ring("sv", 2, [64, 512], F32)
        t_r = P.ring("st1", 2, [64, 512], F32)
        hf_r = P.ring("hf", 2, [128, 4, 128], F32)
        ab_r = P.ring("ab", 2, [128, 512], F32)
        pns = P.sbuf("pns", [128, 512], F32)
        r_pns = Res()
        for gi, l in enumerate(layers):
            li = self.lidx[l]
            P.ld("sp", w1[:], self.hy_w1[li], writes=[r_w])
            P.ld("sp", w2[:], self.hy_w2[li], writes=[r_w])
            P.ld("sp", w3[:].rearrange("p g c -> p (g c)"), self.w3own[li], writes=[r_w])
            b1 = self.sp_col(l, SP_B1, 1, 64)
            b2 = self.sp_col(l, SP_B2, 1, 64)
            for tt in range(ntt):
                ft, r_ft = f_r.next()
                P.ld("sp", ft[:, :TW], T["feats"][:, tt * TW:(tt + 1) * TW], writes=[r_ft])
                p0, r_p0 = self.pb[4]
                P.mm(p0[0:64, :TW], w1[:, :], ft[:, :TW], True, True, [r_w, r_ft], [r_p0])
                h1, r_h1 = h1_r.next()
                self.sin_layer(p0[0:64, :TW], b1, h1[:, :TW], TW, (v_r, t_r), [self.r_const], r_p0, r_h1)
                p1, r_p1 = self.pb[5]
                P.mm(p1[0:64, :TW], w2[:, :], h1[:, :TW], True, True, [r_w, r_h1], [r_p1])
                self.sin_layer(p1[0:64, :TW], b2, hid2[:, tt * TW:(tt + 1) * TW], TW, (v_r, t_r), [self.r_const], r_p1, r_hid2)
            pn, r_pn = self.pb[6]
            w3f = w3[:].rearrange("p g c -> p (g c)")
            for lc in range(ntc):
                hf, r_hf = hf_r.next()
                ph, r_ph = self.pb[lc % 4]
                P.mm(ph[:, :], hid2[:, lc * 128:(lc + 1) * 128], w3f, True, True, [r_hid2, r_w], [r_ph])
                for g in range(4):
                    P.tt(P.dve, hf[:, g, :], ph[:, g * 128:(g + 1) * 128], dec[:, lc, :], ALU.mult, [r_ph, r_dec], [r_hf])
                if lc == 0:
                    P.memset(P.dve, hf[0:1, 1, :], 0.0, [r_hf])
                    P.memset(P.dve, hf[0:1, 3, :], 0.0, [r_hf])
                ab, r_ab = ab_r.next()
                P.actv(ab[:], hf[:].rearrange("p g c -> p (g c)"), AF.Abs, [r_hf], [r_ab])
                P.mm(pn[:, :], self.ones_f[:], ab[:], lc == 0, lc == ntc - 1, [r_ab, self.r_const], [r_pn])
                for o_ in range(2):
                    cs = slice(gi * 256 + o_ * 128, gi * 256 + (o_ + 1) * 128)
                    P.tt(P.pool, ksum[:, lc, cs], hf[:, 2 * o_, :], hf[:, 2 * o_ + 1, :], ALU.add, [r_hf], [r_k[lc]])
                    P.tt(P.pool, kdiff[:, lc, cs], hf[:, 2 * o_, :], hf[:, 2 * o_ + 1, :], ALU.subtract, [r_hf], [r_k[lc]])
            P.copy(P.dve, pns[:], pn[:, :], [r_pn], [r_pns])
            pnv = pns[:].rearrange("p (o d c) -> p o d c", o=2, d=2)
            rnv = rn[:, gi * 256:(gi + 1) * 256].rearrange("p (o c) -> p o c", o=2)
            P.tt(P.dve, rnv, pnv[:, :, 0, :], pnv[:, :, 1, :], ALU.add, [r_pns], [r_rn])
            P.recip(rn[:, gi * 256:(gi + 1) * 256], rn[:, gi * 256:(gi + 1) * 256], [r_rn], [r_rn])
        P.pop(mk1)
        AW = ntc * 128
        tbs = [(P.sbuf(f"ftb{i}", [128, 2 * AW], BF16), Res(), Res()) for i in range(2)]
        sst_r = P.ring("sst", 3, [128, 512], F32)
        for fc in range(nfc):
            tb, r_tc, r_ts = tbs[fc % 2]
            P.ld("sp", tb[:, 0:AW], T["Fc"][fc].rearrange("p tc f -> p (tc f)"), writes=[r_tc])
            P.ld("act", tb[:, AW:2 * AW], T["Fs"][fc].rearrange("p tc f -> p (tc f)"), writes=[r_ts])
            Fc_t = tb[:, 0:AW].rearrange("p (tc f) -> p tc f", f=128)
            Fs_t = tb[:, AW:2 * AW].rearrange("p (tc f) -> p tc f", f=128)
            for gi, l in enumerate(layers):
                li = self.lidx[l]
                gs = slice(gi * 256, (gi + 1) * 256)
                pr, r_pr = self.pb[(4 * fc + 2 * gi) % 8]
                ps_, r_ps = self.pb[(4 * fc + 2 * gi + 1) % 8]
                for tc in range(ntc):
                    P.mm(pr[:, 0:256], Fc_t[:, tc, :], ksum[:, tc, gs], tc == 0, tc == ntc - 1, [r_tc, r_k[tc]], [r_pr])
                for tc in range(ntc):
                    P.mm(ps_[:, 0:256], Fs_t[:, tc, :], kdiff[:, tc, gs], tc == 0, tc == ntc - 1, [r_ts, r_k[tc]], [r_ps])
                sst, r_sst = sst_r.next()
                P.tt(P.dve, sst[:, 0:256], pr[:, 0:256], rn[:, gs], ALU.mult, [r_pr, r_rn], [r_sst])
                P.tt(P.dve, sst[:, 256:512], ps_[:, 0:256], rn[:, gs], ALU.mult, [r_ps, r_rn], [r_sst])
                P.ld("sp", T["S"][li][fc * 128:(fc + 1) * 128, :], sst[:], reads=[r_sst])
        P.pop(mk)

    def own(self, l, col):
        i = l * NOWN + col
        return self.ownp[:, i:i + 1]

    def head_norm_store(self, y_ap, W, l, hg_ap, dsts, sq_r, st_r, ob_r, pbi, r_y):
        P = self.P
        st, r_st = self.rms_rstd([y_ap], W, 128, sq_r, pbi, st_r, [r_y])
        ob, r_ob = ob_r.next()
        P.stt(ob[:, :W], y_ap, hg_ap, st[:, :W], ALU.mult, ALU.mult, [r_y, r_st, self.r_const], [r_ob])
        for k, d in enumerate(dsts):
            P.dma("sp" if k % 2 == 0 else "act", lambda e, d=d, ob=ob: e.dma_start(out=d(e), in_=ob[:, :W]), reads=[r_ob])

    def yin_dsts(self, kind, c0, W):
        Y = self.YIN[kind]
        if c0 < CTX:
            return [(lambda e, d=d: Y[d * 128:(d + 1) * 128, c0:c0 + W]) for d in range(NR)]
        t0 = c0 - CTX
        d, lc = t0 // SL, CTX + t0 % SL
        return [lambda e: Y[d * 128:(d + 1) * 128, lc:lc + W]]


    def ld_u_own(self, dst2d, A, isctx, L, col0, writes, qn="sp"):
        P = self.P
        if isctx:
            P.dma(qn, lambda e: e.dma_start(out=dst2d[:, col0:col0 + L], in_=self.UC[A:A + 512, 0:L][bass.ds(self.rv(e, "128"), 128), :]), writes=writes)
        else:
            win = self.UG[A * NR:A * NR + 2048, :].rearrange("(x i) t -> i x t", i=128)
            P.dma(qn, lambda e: e.dma_start(out=dst2d[:, col0:col0 + SEQ].rearrange("p (s t) -> p s t", s=NR), in_=win[:, bass.ds(self.rv(e, "x"), NR, 2), :]), writes=writes)

    def phase_hyena(self, l, tag):
        P = self.P
        T = self.tab[tag]
        L, nfc, ntc, TW, ntt = T["L"], T["nfc"], T["ntc"], T["TW"], T["ntt"]
        off = 0 if tag == "C" else CTX
        mk = P.push()
        u_r = P.ring("hu", 2, [128, L + 2], F32)
        o_r = P.ring("ho", 2, [128, L], F32)
        for p_ in range(3):
            u, r_u = u_r.next()
            P.memset(P.pool, u[:, 0:1], 0.0, [r_u])
            P.memset(P.pool, u[:, L + 1:L + 2], 0.0, [r_u])
            self.ld_u_own(u, p_ * 512, tag == "C", L, 1, [r_u])
            o, r_o = o_r.next()
            P.ts(P.dve, o[:, :], u[:, 0:L], self.own(l, p_ * 3 + 0), None, ALU.mult, None, [r_u, self.r_const], [r_o])
            P.stt(o[:, :], u[:, 1:L + 1], self.own(l, p_ * 3 + 1), o[:, :], ALU.mult, ALU.add, [r_u, r_o, self.r_const], [r_o])
            P.stt(o[:, :], u[:, 2:L + 2], self.own(l, p_ * 3 + 2), o[:, :], ALU.mult, ALU.add, [r_u, r_o, self.r_const], [r_o])
            dst = self.ZT[:, off:off + L] if p_ == 0 else self.PT[(p_ - 1) * 128:p_ * 128, off:off + L]
            P.ld("act", dst, o[:, :], reads=[r_o])
        P.pop(mk)
        FG = 11 if nfc % 11 == 0 else nfc
        nfg = nfc // FG
        AW = max(ntc * 128, FG * TW)
        for o_ in range(2):
            mk = P.push()
            zT = P.sbuf("zT", [128, ntc, 128], BF16)
            r_zT = [Res() for _ in range(ntc)]
            Yr = P.sbuf("Yr", [128, nfc, 128], BF16)
            Ys = P.sbuf("Ys", [128, nfc, 128], BF16)
            r_Y = [Res() for _ in range(nfc)]
            tbs = [(P.sbuf(f"tb{i}", [128, 2 * AW], BF16), Res(), Res()) for i in range(3)]
            zl_r = P.ring("zl", 2, [128, TW], F32)
            for tt in range(ntt):
                zl, r_zl = zl_r.next()
                P.ld("sp", zl[:, :], self.ZT[:, off + tt * TW:off + (tt + 1) * TW], writes=[r_zl])
                nsub = TW // 128
                pt, r_pt = self.pb[tt % 4]
                for s_ in range(nsub):
                    P.tr(pt[:, s_ * 128:(s_ + 1) * 128], zl[:, s_ * 128:(s_ + 1) * 128], self.ident[:], [r_zl, self.r_const], [r_pt])
                P.copy(P.act if tt % 2 == 0 else P.dve, zT[:, tt * nsub:(tt + 1) * nsub, :], pt[:, 0:TW].rearrange("p (a b) -> p a b", b=128), [r_pt], [r_zT[tt * nsub]])
                for s_ in range(1, nsub):
                    r_zT[tt * nsub + s_] = r_zT[tt * nsub]
            S_r = P.ring("Sld", 3, [128, 512], F32)
            t_rs = [P.ring(f"ty{i}", 2, [128, 128], F32) for i in range(4)]
            FW = ntc * 128
            for fc in range(nfc):
                tb, r_tc, r_ts = tbs[fc % 3]
                P.ld("sp", tb[:, 0:FW], T["Fc"][fc].rearrange("p tc f -> p (tc f)"), writes=[r_tc])
                P.ld("act", tb[:, AW:AW + FW], T["Fs"][fc].rearrange("p tc f -> p (tc f)"), writes=[r_ts])
                Fc_t = tb[:, 0:FW].rearrange("p (tc f) -> p tc f", f=128)
                Fs_t = tb[:, AW:AW + FW].rearrange("p (tc f) -> p tc f", f=128)
                St, r_S = S_r.next()
                P.ld("sp", St[:], T["S"][self.lidx[l]][fc * 128:(fc + 1) * 128, :], writes=[r_S])
                Sr = St[:, o_ * 128:(o_ + 1) * 128]
                Ss = St[:, 256 + o_ * 128:256 + (o_ + 1) * 128]
                pr, r_pr = self.pb[4 + 2 * (fc % 2)]
                ps_, r_ps = self.pb[5 + 2 * (fc % 2)]
                for tc in range(ntc):
                    P.mm(pr[:, 0:128], Fc_t[:, tc, :], zT[:, tc, :], tc == 0, tc == ntc - 1, [r_tc, r_zT[tc]], [r_pr])
                for tc in range(ntc):
                    P.mm(ps_[:, 0:128], Fs_t[:, tc, :], zT[:, tc, :], tc == 0, tc == ntc - 1, [r_ts, r_zT[tc]], [r_ps])
                (t1, r1), (t2, r2), (t3, r3), (t4, r4) = [r.next() for r in t_rs]
                P.tt(P.dve, t1[:], pr[:, 0:128], Sr, ALU.mult, [r_pr, r_S], [r1])
                P.tt(P.dve, t2[:], ps_[:, 0:128], Ss, ALU.mult, [r_ps, r_S], [r2])
                P.tt(P.dve, t3[:], pr[:, 0:128], Ss, ALU.mult, [r_pr, r_S], [r3])
                P.tt(P.dve, t4[:], ps_[:, 0:128], Sr, ALU.mult, [r_ps, r_S], [r4])
                P.tt(P.pool, Yr[:, fc, :], t1[:], t2[:], ALU.subtract, [r1, r2], [r_Y[fc]])
                P.tt(P.pool, Ys[:, fc, :], t3[:], t4[:], ALU.add, [r3, r4], [r_Y[fc]])
            zt_r = P.ring("zt", 2, [128, TW], F32)
            pp_r = P.ring("pp", 2, [128, TW], F32)
            tm_r = P.ring("tm", 2, [128, TW], F32)
            sq_r = P.ring("hsq", 2, [128, 512], BF16)
            st_r = P.ring("hst", 2, [128, 512], F32)
            ob_r = P.ring("hob", 2, [128, 512], BF16)
            k = 0
            for tt in range(ntt):
                c0 = off + tt * TW
                pa, r_pa = self.pb[tt % 4]
                for fg in range(nfg):
                    tb, r_tc, r_ts = tbs[k % 3]
                    k += 1
                    P.ld("sp", tb[:, 0:FG * TW], T["Ic"][tt][:, fg * FG:(fg + 1) * FG, :].rearrange("p f t -> p (f t)"), writes=[r_tc])
                    P.ld("act", tb[:, AW:AW + FG * TW], T["Is"][tt][:, fg * FG:(fg + 1) * FG, :].rearrange("p f t -> p (f t)"), writes=[r_ts])
                    Ic_t = tb[:, 0:FG * TW].rearrange("p (f t) -> p f t", t=TW)
                    Is_t = tb[:, AW:AW + FG * TW].rearrange("p (f t) -> p f t", t=TW)
                    for f_ in range(FG):
                        fc = fg * FG + f_
                        P.mm(pa[:, :TW], Yr[:, fc, :], Ic_t[:, f_, :], fc == 0, False, [r_Y[fc], r_tc], [r_pa])
                        P.mm(pa[:, :TW], Ys[:, fc, :], Is_t[:, f_, :], False, fc == nfc - 1, [r_Y[fc], r_ts], [r_pa])
                zt, r_zt = zt_r.next()
                pp, r_pp = pp_r.next()
                tm, r_tm = tm_r.next()
                P.ld("sp", zt[:], self.ZT[:, c0:c0 + TW], writes=[r_zt])
                P.ld("act", pp[:], self.PT[o_ * 128:(o_ + 1) * 128, c0:c0 + TW], writes=[r_pp])
                P.stt(tm[:], zt[:], self.own(l, 9 + o_), pa[:, :TW], ALU.mult, ALU.add, [r_zt, r_pa, self.r_const], [r_tm])
                P.tt(P.pool, zt[:], tm[:], pp[:], ALU.mult, [r_tm, r_pp], [r_zt])
                if o_ == 0:
                    P.ld("sp", self.ZT[:, c0:c0 + TW], zt[:], reads=[r_zt])
                else:
                    self.head_norm_store(zt[:], TW, l, self.own(l, 11), self.yin_dsts(0, c0, TW), sq_r, st_r, ob_r, 4 + tt % 4, r_zt)
            P.pop(mk)

    def phase_lru(self, l, need_ctx):
        P = self.P
        li = self.lidx[l]
        mk = P.push()
        c8 = P.sbuf("c8", [128, 2], F32)
        r_c8 = Res()
        P.actv(c8[:], self.ownp[:, l * NOWN + 25:l * NOWN + 27], AF.Exp, [self.r_const], [r_c8], scale=-1.0)
        P.actv(c8[:], c8[:], AF.Ln, [r_c8, self.r_const], [r_c8], bias=self.ones_f[:, 0:1], scale=1.0)
        P.ts(P.dve, c8[:], c8[:], -8.0, None, ALU.mult, None, [r_c8], [r_c8])
        wt = P.sbuf("wab", [128, 4, 128], BF16)
        r_wt = Res()
        for d in range(2):
            for k, Wsrc in enumerate((self.lwa_own, self.lwx_own)):
                P.ld("pool", wt[:, d * 2 + k, :], Wsrc[li, d], writes=[r_wt])
        seqs = [(0, CTX, True), (CTX, SEQ, False)]
        sq_r = P.ring("lsq", 2, [128, 512], BF16)
        st_r = P.ring("lst", 2, [128, 512], F32)
        ob_r = P.ring("lob", 2, [128, 512], BF16)
        for (off, Ls, isctx) in seqs:
            mk2 = P.push()
            t = "c" if isctx else "l"
            xr_r = P.ring("xr" + t, 2, [128, Ls + 6], F32)
            xc, r_xc = P.sbuf("xc" + t, [128, Ls], F32), Res()
            xcb, r_xcb = P.sbuf("xcb" + t, [128, Ls], BF16), Res()
            ra, r_ra = P.sbuf("ra" + t, [128, Ls], F32), Res()
            ib, r_ib = P.sbuf("ib" + t, [128, Ls], F32), Res()
            ta, r_ta = P.sbuf("ta" + t, [128, Ls], F32), Res()
            hs = [(P.sbuf(f"h{d}" + t, [128, Ls], F32), Res()) for d in range(2)]
            W = min(512, Ls)
            for d in range(2):
                xr, r_xr = xr_r.next()
                P.memset(P.pool, xr[:, 0:3], 0.0, [r_xr])
                P.memset(P.pool, xr[:, Ls + 3:Ls + 6], 0.0, [r_xr])
                self.ld_u_own(xr, OFF_X, isctx, Ls, 3, [r_xr], qn="act")
                left = 3 if d == 0 else 0
                for jj in range(4):
                    s0 = 3 + jj - left
                    wj = self.own(l, 13 + d * 4 + jj)
                    if jj == 0:
                        P.ts(P.dve, xc[:, :], xr[:, s0:s0 + Ls], wj, None, ALU.mult, None, [r_xr, self.r_const], [r_xc])
                    else:
                        P.stt(xc[:, :], xr[:, s0:s0 + Ls], wj, xc[:, :], ALU.mult, ALU.add, [r_xr, r_xc, self.r_const], [r_xc])
                P.copy(P.pool, xcb[:, :], xc[:, :], [r_xc], [r_xcb])
                for ti in range(Ls // W):
                    cs = slice(ti * W, (ti + 1) * W)
                    pa, r_pa = self.pb[(2 * ti) % 4]
                    px, r_px = self.pb[(2 * ti + 1) % 4]
                    P.mm(pa[:, :W], wt[:, d * 2 + 0, :], xcb[:, cs], True, True, [r_wt, r_xcb], [r_pa])
                    P.mm(px[:, :W], wt[:, d * 2 + 1, :], xcb[:, cs], True, True, [r_wt, r_xcb], [r_px])
                    P.actv(ra[:, cs], pa[:, :W], AF.Sigmoid, [r_pa, self.r_const], [r_ra], bias=self.own(l, 21 + d))
                    P.actv(ib[:, cs], px[:, :W], AF.Sigmoid, [r_px, self.r_const], [r_ib], bias=self.own(l, 23 + d))
                P.actv(ra[:, :], ra[:, :], AF.Exp, [r_ra, r_c8], [r_ra], scale=c8[:, d:d + 1])
                P.tt(P.pool, ta[:, :], ra[:, :], ra[:, :], ALU.mult, [r_ra], [r_ta])
                P.ts(P.pool, ta[:, :], ta[:, :], -1.0, 1.0, ALU.mult, ALU.add, [r_ta], [r_ta])
                P.actv(ta[:, :], ta[:, :], AF.Sqrt, [r_ta], [r_ta])
                P.tt(P.dve, ib[:, :], ib[:, :], xc[:, :], ALU.mult, [r_ib, r_xc], [r_ib])
                P.tt(P.dve, ib[:, :], ib[:, :], ta[:, :], ALU.mult, [r_ib, r_ta], [r_ib])
                if not isctx:
                    f0 = Ls - 1 if d == 1 else 0
                    P.stt(ib[:, f0:f0 + 1], ra[:, f0:f0 + 1], self.lru_h0[:, d:d + 1], ib[:, f0:f0 + 1],
                          ALU.mult, ALU.add, [r_ra, r_ib, self.r_h0], [r_ib])
                h, r_h = hs[d]
                if d == 0:
                    P.op(P.dve, lambda e, h=h, ra=ra, ib=ib: e.tensor_tensor_scan(out=h[:, :], data0=ra[:, :], data1=ib[:, :], initial=0.0, op0=ALU.mult, op1=ALU.add), [r_ra, r_ib], [r_h])
                else:
                    P.op(P.dve, lambda e, h=h, ra=ra, ib=ib: e.tensor_tensor_scan(out=h[:, ::-1], data0=ra[:, ::-1], data1=ib[:, ::-1], initial=0.0, op0=ALU.mult, op1=ALU.add), [r_ra, r_ib], [r_h])
                if isctx:
                    f1 = Ls - 1 if d == 0 else 0
                    P.copy(P.dve, self.lru_h0[:, d:d + 1], h[:, f1:f1 + 1], [r_h], [self.r_h0])
            if not (isctx and not need_ctx):
                xr, r_xr = xr_r.next()
                self.ld_u_own(xr, OFF_G, isctx, Ls, 0, [r_xr], qn="act")
                P.actv(xr[:, 0:Ls], xr[:, 0:Ls], AF.Gelu_apprx_tanh, [r_xr], [r_xr])
                (h0_, r_h0_), (h1_, r_h1_) = hs
                P.tt(P.pool, h0_[:, :], h0_[:, :], h1_[:, :], ALU.add, [r_h0_, r_h1_], [r_h0_])
                P.tt(P.dve, h0_[:, :], h0_[:, :], xr[:, 0:Ls], ALU.mult, [r_h0_, r_xr], [r_h0_])
                for ti in range(Ls // W):
                    self.head_norm_store(h0_[:, ti * W:(ti + 1) * W], W, l, self.own(l, 12), self.yin_dsts(1, off + ti * W, W), sq_r, st_r, ob_r, 4 + ti % 4, r_h0_)
            P.pop(mk2)
        P.pop(mk)

    def phase_attn(self, l, need_ctx):
        P = self.P
        li = self.lidx[l]
        mk = P.push()
        ckv = P.sbuf("ckv", [128, 2, NT], BF16)
        r_ckv = Res()
        krope = P.sbuf("krope", [64, NT], BF16)
        r_kr = Res()
        cq = P.sbuf("cq", [128, 4, NTL], BF16)
        r_cq = Res()
        sq_r = P.ring("asq", 3, [128, 512], BF16)
        st_r = P.ring("ast", 2, [128, 512], F32)
        u2_r = P.ring("au2", 2, [128, 2, 512], F32)
        u4_r = P.ring("au4", 2, [128, 4, 512], F32)
        kr_r = P.ring("akr", 2, [64, 2, 512], F32)
        cs_r = P.ring("acs", 2, [64, 2, 512], F32)
        tk_r = P.ring("atk", 2, [64, 2, 512], F32)
        for (c0, W, j) in self.tok_tiles(True):
            u2, r_u2 = u2_r.next()
            P.ld("sp", u2[:, :, :W], self.Uap(OFF_KV, 256, c0, W).rearrange("(c p) t -> p c t", p=128), writes=[r_u2])
            st, r_st = self.rms_rstd([u2[:, c, :W] for c in range(2)], W, 256, sq_r, 7, st_r, [r_u2])
            for c in range(2):
                P.stt(ckv[:, c, c0:c0 + W], u2[:, c, :W], self.sp_col(l, SP_GKV + c), st[:, :W], ALU.mult, ALU.mult, [r_u2, r_st, self.r_const], [r_ckv])
            kr, r_krt = kr_r.next()
            P.ld("act", kr[:, 0, :W], self.Uap(OFF_KR, 64, c0, W), writes=[r_krt])
            if j == 0:
                P.ld("act", kr[:, 1, :W], self.Uap(OFF_KRR, 64, c0, W), writes=[r_krt])
                cs, r_cs = cs_r.next()
                P.ld("act", cs[:, :, :W], self.rope_cs[:, :, c0 - CTX:c0 - CTX + W], writes=[r_cs])
                tk, r_tk = tk_r.next()
                P.tt(P.pool, tk[:, :, :W], kr[:, :, :W], cs[:, :, :W], ALU.mult, [r_krt, r_cs], [r_tk])
                P.tt(P.pool, krope[:, c0:c0 + W], tk[:, 0, :W], tk[:, 1, :W], ALU.add, [r_tk], [r_kr])
            else:
                P.copy(P.pool, krope[:, c0:c0 + W], kr[:, 0, :W], [r_krt], [r_kr])
        qtiles = self.loc_tiles(need_ctx)
        for (c0, W, j) in qtiles:
            u4, r_u4 = u4_r.next()
            srcq = self.UC[OFF_Q:OFF_Q + 512, c0:c0 + W] if j == 1 else self.ULOC[OFF_Q:OFF_Q + 512, c0 - CTX:c0 - CTX + W]
            P.ld("sp", u4[:, :, :W], srcq.rearrange("(c p) t -> p c t", p=128), writes=[r_u4])
            st, r_st = self.rms_rstd([u4[:, c, :W] for c in range(4)], W, 512, sq_r, 6, st_r, [r_u4])
            for c in range(4):
                P.stt(cq[:, c, c0:c0 + W], u4[:, c, :W], self.sp_col(l, SP_GQ + c), st[:, :W], ALU.mult, ALU.mult, [r_u4, r_st, self.r_const], [r_cq])
        wkv_r = P.ring("wkv", 2, [128, 2, 256], BF16)
        wq_r = P.ring("wq", 2, [128, 4, 256], BF16)
        kn_r = P.ring("kn", 2, [128, NT], BF16)
        v_r = P.ring("vv", 2, [128, NT // 128, 128], BF16)
        qn_r = P.ring("qn", 2, [128, NTL], BF16)
        qr_r = P.ring("qr", 2, [64, NTL], BF16)
        pT_r = P.ring("pT", 4, [128, 512], BF16)
        ri_r = P.ring("ri", 2, [128, 512], F32)
        oo_r = P.ring("oo", 2, [128, 512], F32)
        ob_r = P.ring("aob", 2, [128, 512], BF16)
        WKV = self.WUKV[li].rearrange("(kc p) n -> p kc n", p=128)
        WQ = self.WUQ[li].rearrange("(kc p) n -> p kc n", p=128)
        kb = 0
        for h in range(8):
            wkv, r_wkv = wkv_r.next()
            wq, r_wq = wq_r.next()
            P.ld("sp", wkv[:], WKV[:, :, h * 256:(h + 1) * 256], writes=[r_wkv])
            P.ld("sp", wq[:], WQ[:, :, h * 256:(h + 1) * 256], writes=[r_wq])
            kn, r_kn = kn_r.next()
            vv, r_vv = v_r.next()
            qn, r_qn = qn_r.next()
            qr, r_qr = qr_r.next()
            for (c0, W, j) in self.tok_tiles(True):
                pt, r_pt = self.pb[kb % 4]
                kb += 1
                for kc in range(2):
                    P.mm(pt[:, :W], wkv[:, kc, 0:128], ckv[:, kc, c0:c0 + W], kc == 0, kc == 1, [r_wkv, r_ckv], [r_pt])
                P.copy(P.act, kn[:, c0:c0 + W], pt[:, :W], [r_pt], [r_kn])
            for (c0, W, j) in qtiles:
                pt, r_pt = self.pb[kb % 4]
                kb += 1
                for kc in range(4):
                    P.mm(pt[:, :W], wq[:, kc, 0:128], cq[:, kc, c0:c0 + W], kc == 0, kc == 3, [r_wq, r_cq], [r_pt])
                P.copy(P.dve, qn[:, c0:c0 + W], pt[:, :W], [r_pt], [r_qn])
                pt, r_pt = self.pb[kb % 4]
                kb += 1
                for kc in range(4):
                    P.mm(pt[0:64, :W], wq[:, kc, 128:192], cq[:, kc, c0:c0 + W], kc == 0, kc == 3, [r_wq, r_cq], [r_pt])
                if j == 0:
                    pt2, r_pt2 = self.pb[kb % 4]
                    kb += 1
                    for kc in range(4):
                        P.mm(pt2[0:64, :W], wq[:, kc, 192:256], cq[:, kc, c0:c0 + W], kc == 0, kc == 3, [r_wq, r_cq], [r_pt2])
                    cs, r_cs = cs_r.next()
                    P.ld("act", cs[:, :, :W], self.rope_q[:, :, c0 - CTX:c0 - CTX + W], writes=[r_cs])
                    tk, r_tk = tk_r.next()
                    P.tt(P.dve, tk[:, 0, :W], pt[0:64, :W], cs[:, 0, :W], ALU.mult, [r_pt, r_cs], [r_tk])
                    P.tt(P.dve, tk[:, 1, :W], pt2[0:64, :W], cs[:, 1, :W], ALU.mult, [r_pt2, r_cs], [r_tk])
                    P.tt(P.pool, qr[:, c0:c0 + W], tk[:, 0, :W], tk[:, 1, :W], ALU.add, [r_tk], [r_qr])
                else:
                    P.copy(P.dve, qr[:, c0:c0 + W], pt[0:64, :W], [r_pt], [r_qr])
            for kg in range(0, NT // 128, 4):
                nk4 = min(4, NT // 128 - kg)
                pt, r_pt = self.pb[kb % 4]
                kb += 1
                for q4 in range(nk4):
                    kc = kg + q4
                    for k2 in range(2):
                        P.mm(pt[:, q4 * 128:(q4 + 1) * 128], ckv[:, k2, kc * 128:(kc + 1) * 128], wkv[:, k2, 128:256], k2 == 0, k2 == 1, [r_ckv, r_wkv], [r_pt])
                P.copy(P.act, vv[:, kg:kg + nk4, :], pt[:, 0:nk4 * 128].rearrange("p (a b) -> p a b", b=128), [r_pt], [r_vv])
            for qi, (c0, W, j) in enumerate(qtiles):
                nk = 2 if j == 1 else NT // 128
                po, r_po = self.pb[4 + 2 * (qi % 2)]
                pl, r_pl = self.pb[5 + 2 * (qi % 2)]
                pend = None
                for kc in range(nk + 1):
                    if kc < nk:
                        pS, r_pS = self.pb[kc % 4]
                        P.mm(pS[:, :W], kn[:, kc * 128:(kc + 1) * 128], qn[:, c0:c0 + W], True, False, [r_kn, r_qn], [r_pS])
                        P.mm(pS[:, :W], krope[:, kc * 128:(kc + 1) * 128], qr[:, c0:c0 + W], False, True, [r_kr, r_qr], [r_pS])
                        pT, r_pT = pT_r.next()
                        P.actv(pT[:, :W], pS[:, :W], AF.Exp, [r_pS], [r_pT], scale=MLA_SCALE)
                    if pend is not None:
                        pkc, ppT, pr_pT = pend
                        P.mm(po[:, :W], vv[:, pkc, :], ppT[:, :W], pkc == 0, pkc == nk - 1, [r_vv, pr_pT], [r_po])
                        P.mm(pl[:, :W], self.ones_b[:], ppT[:, :W], pkc == 0, pkc == nk - 1, [self.r_const, pr_pT], [r_pl])
                    pend = (kc, pT, r_pT) if kc < nk else None
                ri, r_ri = ri_r.next()
                P.recip(ri[:, :W], pl[:, :W], [r_pl], [r_ri])
                oo, r_oo = oo_r.next()
                P.tt(P.dve, oo[:, :W], po[:, :W], ri[:, :W], ALU.mult, [r_po, r_ri], [r_oo])
                dsts = [lambda e, h=h, c0=c0, W=W: self.YM[h * 128:(h + 1) * 128, c0:c0 + W]]
                self.head_norm_store(oo[:, :W], W, l, self.sp_col(l, SP_HG + 8 + h), dsts, sq_r, st_r, ob_r, 4 + 2 * (qi % 2), r_oo)
        P.pop(mk)

    def phase_outproj(self, l, need_ctx):
        P = self.P
        li = self.lidx[l]
        mk = P.push()
        yc_r = P.ring("yc", 2, [128, 16, 512], BF16)
        w_r = P.ring("wo", 2, [128, 16, 512], BF16)
        mix_r = P.ring("mix", 1, [128, 16, 512], F32)
        xs_r = P.ring("oxs", 1, [128, 16, 512], F32)
        hb_r = P.ring("ohb", 1, [128, 16, 512], BF16)
        sq_r = P.ring("osq", 3, [128, 512], BF16)
        st_r = P.ring("ost", 2, [128, 512], F32)
        tmp_r = P.ring("otmp", 3, [128, 512], F32)
        YMv = self.YM.rearrange("(c p) t -> p c t", p=128)
        XTv = self.XT.rearrange("(c p) t -> p c t", p=128)
        H2v = self.H2.rearrange("(c p) t -> p c t", p=128)
        Wv = self.WOUT[li].rearrange("(kc p) n -> p kc n", p=128)
        kb = 0
        ltiles = self.loc_tiles(need_ctx)
        for (c0, W, j) in ltiles:
            yc, r_yc = yc_r.next()
            P.ld("sp", yc[:, 8:16, :W], YMv[:, :, c0:c0 + W], writes=[r_yc])
            for kind in range(2):
                def ldy(e, yc=yc, c0=c0, W=W, kind=kind):
                    win = self.YG[kind].rearrange("(x i) t -> i x t", i=128)
                    return e.dma_start(out=yc[:, kind * 4:(kind + 1) * 4, :W], in_=win[:, bass.ds(self.rv(e, "x"), NR, 2), c0:c0 + W])
                P.dma("pool", ldy, writes=[r_yc])
            xs, r_xs = xs_r.next()
            P.ld("sp", xs[:, :, :W], XTv[:, :, c0:c0 + W], writes=[r_xs])
            mix, r_mix = mix_r.next()
            for g in range(4):
                wt, r_wt = w_r.next()
                P.ld("act", wt[:], Wv[:, :, g * 512:(g + 1) * 512], writes=[r_wt])
                for m in range(4):
                    pt, r_pt = self.pb[kb % 6]
                    kb += 1
                    for kc in range(16):
                        P.mm(pt[:, :W], wt[:, kc, m * 128:(m + 1) * 128], yc[:, kc, :W], kc == 0, kc == 15, [r_wt, r_yc], [r_pt])
                    P.copy(P.act if kb % 2 == 0 else P.dve, mix[:, g * 4 + m, :W], pt[:, :W], [r_pt], [r_mix])
            st, r_st = self.rms_rstd([mix[:, c, :W] for c in range(16)], W, D, sq_r, 7, st_r, [r_mix])
            for c in range(16):
                tmp, r_tmp = tmp_r.next()
                P.tt(P.pool, tmp[:, :W], mix[:, c, :W], st[:, :W], ALU.mult, [r_mix, r_st], [r_tmp])
                P.stt(xs[:, c, :W], tmp[:, :W], self.mod(2, c, j), xs[:, c, :W], ALU.mult, ALU.add, [r_tmp, r_xs, self.r_mods], [r_xs])
            P.ld("sp", XTv[:, :, c0:c0 + W], xs[:, :, :W], reads=[r_xs])
            st, r_st = self.rms_rstd([xs[:, c, :W] for c in range(16)], W, D, sq_r, 6, st_r, [r_xs])
            hb, r_hb = hb_r.next()
            for c in range(16):
                tmp, r_tmp = tmp_r.next()
                P.tt(P.dve, tmp[:, :W], xs[:, c, :W], st[:, :W], ALU.mult, [r_xs, r_st], [r_tmp])
                P.actv(hb[:, c, :W], tmp[:, :W], AF.Identity, [r_tmp, self.r_mods], [r_hb], bias=self.mod(4, c, j), scale=self.mod(3, c, j))
            P.ld("sp", H2v[:, :, c0:c0 + W], hb[:, :, :W], reads=[r_hb])
            HBv = self.HB.rearrange("(c p) k -> p c k", p=128)
            if c0 == CTX:
                P.ld("act", HBv[:, :, 0:1], hb[:, :, 0:1], reads=[r_hb], slow=True)
            if c0 + W == NTL:
                P.ld("act", HBv[:, :, 1:2], hb[:, :, W - 1:W], reads=[r_hb], slow=True)
        P.pop(mk)
        self.allgather([(self.HB, self.HG)])
        P.ld("sp", self.HGP[D:(NR + 1) * D, :], self.HG[:, :])
        P.barrier()

    def phase_ffn(self, l, need_ctx):
        P = self.P
        li = self.lidx[l]
        mk = P.push()
        OT = 342
        tiles = []
        if need_ctx:
            tiles.append((0, CTX, 0, CTX, 1))
        for lo in range(0, SL, OT):
            tiles.append((CTX, SL, lo, min(SL, lo + OT), 0))
        groups = [tiles[i:i + 2] for i in range(0, len(tiles), 2)]
        WT = OT + 2
        h2_r = P.ring("fh2", 2, [128, 16, WT], BF16)
        acts = [(P.sbuf(f"fact{t}", [128, 44, OT], BF16), [Res() for _ in range(44)]) for t in range(2)]
        wu_r = P.ring("fwu", 2, [128, 2, 16, 256], BF16)
        wd_r = P.ring("fwd", 2, [128, 44, 128], BF16)
        f_r = P.ring("ff", 1, [128, 16, OT], F32)
        xs_r = P.ring("fxs", 1, [128, 16, OT], F32)
        gc_r = P.ring("fgc", 2, [128, OT], F32)
        vc_r = P.ring("fvc", 2, [128, OT], F32)
        sq_r = P.ring("fsq", 2, [128, 512], BF16)
        st_r = P.ring("fst_", 1, [128, 512], F32)
        tmp_r = P.ring("ftmp", 2, [128, OT], F32)
        XTv = self.XT.rearrange("(c p) t -> p c t", p=128)
        H2v = self.H2.rearrange("(c p) t -> p c t", p=128)
        HGPv = self.HGP.rearrange("(c p) k -> p c k", p=128)
        WU = self.WUP[li].rearrange("(kc p) n -> p kc n", p=128)
        WD = self.WDN[li].rearrange("(kc p) n -> p kc n", p=128)
        for grp in groups:
            h2s = []
            for (off, Ls, lo, hi, j) in grp:
                Wo = hi - lo
                Wt = Wo + 2
                h2, r_h2 = h2_r.next()
                a, b = lo - 1, hi + 1
                ca, cb = 0, Wt
                if a < 0:
                    if j == 1:
                        P.memset(P.pool, h2[:, :, 0:1], 0.0, [r_h2])
                    else:
                        P.dma("act", lambda e, h2=h2: e.dma_start(out=h2[:, :, 0:1], in_=HGPv[:, 0:64, 1:2][:, bass.ds(self.rv(e, "16"), 16), :], allow_slow_non_contiguous=True), writes=[r_h2])
                    a, ca = 0, 1
                if b > Ls:
                    if j == 1:
                        P.memset(P.pool, h2[:, :, Wt - 1:Wt], 0.0, [r_h2])
                    else:
                        P.dma("act", lambda e, h2=h2, Wt=Wt: e.dma_start(out=h2[:, :, Wt - 1:Wt], in_=HGPv[:, 32:96, 0:1][:, bass.ds(self.rv(e, "16"), 16), :], allow_slow_non_contiguous=True), writes=[r_h2])
                    b, cb = Ls, Wt - 1
                P.ld("sp", h2[:, :, ca:cb], H2v[:, :, off + a:off + b], writes=[r_h2])
                h2s.append((h2, r_h2, Wo, Wt))
            for i in range(44):
                if i % 2 == 0:
                    wu, r_wu = wu_r.next()
                    P.ld("act", wu[:, 0, :, :], WU[:, :, i * 128:i * 128 + 256], writes=[r_wu])
                    P.ld("sp", wu[:, 1, :, :], WU[:, :, DFF + i * 128:DFF + i * 128 + 256], writes=[r_wu])
                s0 = (i % 2) * 128
                for t, (h2, r_h2, Wo, Wt) in enumerate(h2s):
                    pg, r_pg = self.pb[(i % 2) * 4 + 2 * t]
                    pv, r_pv = self.pb[(i % 2) * 4 + 2 * t + 1]
                    for kc in range(16):
                        P.mm(pg[:, :Wt], wu[:, 0, kc, s0:s0 + 128], h2[:, kc, :Wt], kc == 0, kc == 15, [r_wu, r_h2], [r_pg])
                    for kc in range(16):
                        P.mm(pv[:, :Wt], wu[:, 1, kc, s0:s0 + 128], h2[:, kc, :Wt], kc == 0, kc == 15, [r_wu, r_h2], [r_pv])
                for t, (h2, r_h2, Wo, Wt) in enumerate(h2s):
                    pg, r_pg = self.pb[(i % 2) * 4 + 2 * t]
                    pv, r_pv = self.pb[(i % 2) * 4 + 2 * t + 1]
                    act, r_act = acts[t]
                    gc, r_gc = gc_r.next()
                    vc, r_vc = vc_r.next()
                    for (pp, r_pp, oc, r_oc, ci) in ((pg, r_pg, gc, r_gc, i), (pv, r_pv, vc, r_vc, 44 + i)):
                        P.ts(P.dve, oc[:, :Wo], pp[:, 0:Wo], self.sp_col(l, SP_FC + 0 * 88 + ci), None, ALU.mult, None, [r_pp, self.r_const], [r_oc])
                        P.stt(oc[:, :Wo], pp[:, 1:Wo + 1], self.sp_col(l, SP_FC + 1 * 88 + ci), oc[:, :Wo], ALU.mult, ALU.add, [r_pp, r_oc, self.r_const], [r_oc])
                        P.stt(oc[:, :Wo], pp[:, 2:Wo + 2], self.sp_col(l, SP_FC + 2 * 88 + ci), oc[:, :Wo], ALU.mult, ALU.add, [r_pp, r_oc, self.r_const], [r_oc])
                    P.actv(gc[:, :Wo], gc[:, :Wo], AF.Gelu_apprx_tanh, [r_gc], [r_gc])
                    P.tt(P.pool, act[:, i, :Wo], gc[:, :Wo], vc[:, :Wo], ALU.mult, [r_gc, r_vc], [r_act[i]])
            for t, (off, Ls, lo, hi, j) in enumerate(grp):
                Wo = hi - lo
                act, r_act = acts[t]
                f, r_f = f_r.next()
                for m in range(16):
                    wd, r_wd = wd_r.next()
                    P.ld("act" if m % 2 == 0 else "sp", wd[:], WD[:, :, m * 128:(m + 1) * 128], writes=[r_wd])
                    pd, r_pd = self.pb[m % 2]
                    for kc in range(44):
                        P.mm(pd[:, :Wo], wd[:, kc, :], act[:, kc, :Wo], kc == 0, kc == 43, [r_wd, r_act[kc]], [r_pd])
                    P.copy(P.act if m % 2 == 0 else P.dve, f[:, m, :Wo], pd[:, :Wo], [r_pd], [r_f])
                st, r_st = self.rms_rstd([f[:, c, :Wo] for c in range(16)], Wo, D, sq_r, 2, st_r, [r_f])
                xs, r_xs = xs_r.next()
                P.ld("sp", xs[:, :, :Wo], XTv[:, :, off + lo:off + hi], writes=[r_xs])
                for c in range(16):
                    tmp, r_tmp = tmp_r.next()
                    P.tt(P.pool, tmp[:, :Wo], f[:, c, :Wo], st[:, :Wo], ALU.mult, [r_f, r_st], [r_tmp])
                    P.stt(xs[:, c, :Wo], tmp[:, :Wo], self.mod(5, c, j), xs[:, c, :Wo], ALU.mult, ALU.add, [r_tmp, r_xs, self.r_mods], [r_xs])
                P.ld("sp", XTv[:, :, off + lo:off + hi], xs[:, :, :Wo], reads=[r_xs])
        P.pop(mk)


def make_inmaps(inp, n_cores=8, layers=None):
    c = _consts()
    layers = list(range(DEPTH)) if layers is None else list(layers)
    shared = {k: np.ascontiguousarray(np.asarray(inp[k], dtype=np.float32)[layers]) for k in (
        "w_in", "hy_w1", "hy_w2", "mla_wuq", "mla_wukv", "w_out", "ffn_up", "ffn_down")}
    adaw = np.asarray(inp["ada_w"], dtype=np.float32)
    shared["smallp"] = _pack_small({k: np.asarray(v, dtype=np.float32) for k, v in inp.items()}).reshape(128, DEPTH * NSP)
    shared.update({k: v for k, v in c.items() if not k.startswith("decay")})
    spf = shared["smallp"].reshape(128, DEPTH, NSP)
    own_map = []
    for p_ in range(3):
        for j in range(3):
            own_map.append(SP_HYC + j * 12 + p_ * 4)
    own_map += [SP_HYB + 0, SP_HYB + 4, SP_HG, SP_HG + 4]
    for d in range(2):
        for j in range(4):
            own_map.append(SP_LC + d * 16 + j * 4)
    own_map += [SP_BA, SP_BA + 4, SP_BX, SP_BX + 4, SP_LAM, SP_LAM + 4]
    w3 = np.asarray(inp["hy_w3"], dtype=np.float32)[layers].reshape(len(layers), 64, 4, 4, 128)
    lwa = np.asarray(inp["lru_wa"], dtype=np.float32)[layers]
    lwx = np.asarray(inp["lru_wx"], dtype=np.float32)[layers]
    maps = []
    for core in range(n_cores):
        b = core // NR
        r = core % NR
        m = dict(shared)
        m["x"] = np.ascontiguousarray(np.asarray(inp["x"][b, r * SL:(r + 1) * SL], dtype=np.float32))
        m["ownp"] = np.ascontiguousarray(spf[:, :, [cc + r for cc in own_map]].reshape(128, DEPTH * NOWN))
        m["hy_w3_own"] = np.ascontiguousarray(w3[:, :, :, r, :].reshape(len(layers), 64, 512))
        m["lru_wa_own"] = np.ascontiguousarray(lwa[:, :, r])
        m["lru_wx_own"] = np.ascontiguousarray(lwx[:, :, r])
        m["ada_w_own"] = np.ascontiguousarray(adaw[layers][:, :, r * (6 * D // NR):(r + 1) * (6 * D // NR)])
        m["rope_q"] = np.ascontiguousarray(c["rope_cs"][:, :, r * SL:(r + 1) * SL])
        m["decayL"] = np.ascontiguousarray(c["decayL"][:, r * 128:(r + 1) * 128])
        m["decayC"] = np.ascontiguousarray(c["decayC"][:, r * 128:(r + 1) * 128])
        m["ctx"] = np.ascontiguousarray(np.asarray(inp["ctx"][b], dtype=np.float32))
        cv = np.stack([np.asarray(inp["c"][b], dtype=np.float32), np.asarray(inp["c_ctx"], dtype=np.float32)], axis=-1)
        m["cvec"] = np.ascontiguousarray(cv.reshape(16, 128, 2).transpose(1, 0, 2).reshape(128, 32))
        maps.append(m)
    return maps


_NC_CACHE = {}


def kernel(**inputs):
    if "nc" not in _NC_CACHE:
        _NC_CACHE["nc"] = Builder(range(DEPTH)).build()
    nc = _NC_CACHE["nc"]
    maps = make_inmaps(inputs)
    res = run_bass_kernel_spmd(nc, maps, core_ids=list(range(8)))
    out = np.stack([np.concatenate([np.asarray(res.results[b * NR + r]["out"]) for r in range(NR)], axis=0) for b in range(2)], axis=0)
    return out.astype(np.float32)
```

```python
import math
import numpy as np
import ml_dtypes
import concourse.bass as bass
import concourse.mybir as mybir
from concourse.bass_utils import run_bass_kernel_spmd

F32 = mybir.dt.float32
BF16 = mybir.dt.bfloat16
AF = mybir.ActivationFunctionType
ALU = mybir.AluOpType

D = 2048
SEQ = 4096
CTX = 256
NT = SEQ + CTX
DEPTH = 4
NR = 4
SL = SEQ // NR
NTL = CTX + SL
GROUPS = [[0, 1, 2, 3], [4, 5, 6, 7]]
NOWN = 27
UCH = 14
DIN = 3392
DINX = 3456
OFF_HY, OFF_G, OFF_Q, OFF_X, OFF_KV, OFF_KR, OFF_KRR = 0, 1536, 2048, 2560, 3072, 3328, 3392
DFF = 5632
EPS = 1e-6
MLA_SCALE = 192 ** -0.5
MAGIC = 12582912.0

SP_ADAB, SP_NG, SP_HYC, SP_HYB, SP_LC, SP_BA, SP_BX, SP_LAM = 0, 96, 160, 196, 204, 236, 244, 252
SP_GQ, SP_GKV, SP_HG, SP_FC, SP_B1, SP_B2, NSP = 260, 264, 266, 282, 546, 547, 548


class Res:
    __slots__ = ("name", "w", "r")

    def __init__(self, name=""):
        self.name = name
        self.w = None
        self.r = {}


class EngQ:
    def __init__(self, name, eng, inorder=False):
        self.name = name
        self.eng = eng
        self.inorder = inorder
        self.sem = None
        self.step = 1
        self.count = 0
        self.waited = {}
        self.thunks = []


class DmaSlot:
    def __init__(self, sem, name):
        self.sem = sem
        self.name = name
        self.count = 0
        self.step = 16
        self.inorder = False


class Ring:
    def __init__(self, items):
        self.items = items
        self.i = 0

    def next(self):
        it = self.items[self.i % len(self.items)]
        self.i += 1
        return it


class Prog:
    def __init__(self, nc, n_dma_slots=20, same_engine_sync=True):
        self.nc = nc
        self.same_engine_sync = same_engine_sync
        self.ctx = []
        self.pe = EngQ("pe", nc.tensor, inorder=True)
        self.dve = EngQ("dve", nc.vector)
        self.act = EngQ("act", nc.scalar)
        self.pool = EngQ("pool", nc.gpsimd)
        self.sp = EngQ("sp", nc.sync)
        self.engs = [self.pe, self.dve, self.act, self.pool, self.sp]
        for e in self.engs:
            e.sem = self._sem("s_" + e.name)
        self.slots = {}
        for qn in ("sp", "act", "pool"):
            self.slots[qn] = [DmaSlot(self._sem(f"d_{qn}{i}"), f"d_{qn}{i}") for i in range(n_dma_slots)]
        self.slot_rr = {"sp": 0, "act": 0, "pool": 0}
        self.n_inst = 0
        self.uid = 0

    def _sem(self, name):
        cm = self.nc.semaphore(name)
        s = cm.__enter__()
        self.ctx.append(cm)
        return s

    def push(self):
        return len(self.ctx)

    def pop(self, mark):
        self.barrier()
        while len(self.ctx) > mark:
            self.ctx.pop().__exit__(None, None, None)

    def sbuf(self, name, shape, dtype):
        self.uid += 1
        cm = self.nc.sbuf_tensor(f"{name}_{self.uid}", list(shape), dtype)
        t = cm.__enter__()
        self.ctx.append(cm)
        return t

    def psum(self, name, shape, dtype=F32):
        cm = self.nc.psum_tensor(name, list(shape), dtype)
        t = cm.__enter__()
        self.ctx.append(cm)
        return t

    def ring(self, name, n, shape, dtype):
        return Ring([(self.sbuf(f"{name}{i}", shape, dtype), Res(f"{name}{i}")) for i in range(n)])

    def _deps(self, reads, writes):
        deps = []
        for r in reads:
            if r.w is not None:
                deps.append(r.w)
        for w in writes:
            if w.w is not None:
                deps.append(w.w)
            deps.extend(w.r.values())
        return deps

    def _emit_waits(self, q, deps):
        need = {}
        for (src, c) in deps:
            if src is q and (q.inorder or not self.same_engine_sync):
                continue
            if q.waited.get(id(src), 0) >= c:
                continue
            if need.get(id(src), (None, 0))[1] < c:
                need[id(src)] = (src, c)
        for src, c in need.values():
            q.waited[id(src)] = c
            sem, val = src.sem, c * src.step
            q.thunks.append(lambda e, sem=sem, val=val: e.wait_ge(sem, val))

    def _mark(self, src, c, reads, writes):
        for r in reads:
            old = r.r.get(id(src))
            if old is None or old[1] < c:
                r.r[id(src)] = (src, c)
        for w in writes:
            w.w = (src, c)
            w.r = {}

    def op(self, q, fn, reads=(), writes=()):
        self._emit_waits(q, self._deps(reads, writes))
        q.count += 1
        sem = q.sem
        q.thunks.append(lambda e, fn=fn, sem=sem: fn(e).then_inc(sem, 1))
        self._mark(q, q.count, reads, writes)
        self.n_inst += 1

    def dma(self, qname, fn, reads=(), writes=()):
        q = {"sp": self.sp, "act": self.act, "pool": self.pool}[qname]
        slots = self.slots[qname]
        i = self.slot_rr[qname]
        self.slot_rr[qname] = (i + 1) % len(slots)
        slot = slots[i]
        deps = self._deps(reads, writes)
        if slot.count > 0:
            deps.append((slot, slot.count))
        self._emit_waits(q, deps)
        slot.count += 1
        sem = slot.sem
        q.thunks.append(lambda e, fn=fn, sem=sem: fn(e).then_inc(sem, 16))
        self._mark(slot, slot.count, reads, writes)
        self.n_inst += 1

    def barrier(self):
        srcs = list(self.engs)
        for qn in self.slots:
            srcs.extend(self.slots[qn])
        deps = [(s, s.count) for s in srcs if s.count > 0]
        for q in self.engs:
            need = [(s, c) for (s, c) in deps]
            keep_sync = self.same_engine_sync
            self.same_engine_sync = True
            self._emit_waits(q, need)
            self.same_engine_sync = keep_sync

    def mm(self, out, lhsT, rhs, start, stop, reads, writes):
        self.op(self.pe, lambda e: e.matmul(out, lhsT=lhsT, rhs=rhs, start=start, stop=stop), reads, writes)

    def tr(self, out, in_, ident, reads, writes):
        self.op(self.pe, lambda e: e.transpose(out, in_, ident), reads, writes)

    def tt(self, q, out, in0, in1, op, reads, writes):
        self.op(q, lambda e: e.tensor_tensor(out=out, in0=in0, in1=in1, op=op), reads, writes)

    def ts(self, q, out, in0, s1, s2, op0, op1, reads, writes):
        if op1 is None:
            self.op(q, lambda e: e.tensor_scalar(out=out, in0=in0, scalar1=s1, scalar2=None, op0=op0), reads, writes)
        else:
            self.op(q, lambda e: e.tensor_scalar(out=out, in0=in0, scalar1=s1, scalar2=s2, op0=op0, op1=op1), reads, writes)

    def stt(self, out, in0, scalar, in1, op0, op1, reads, writes):
        self.op(self.dve, lambda e: e.scalar_tensor_tensor(out=out, in0=in0, scalar=scalar, in1=in1, op0=op0, op1=op1), reads, writes)

    def actv(self, out, in_, func, reads, writes, bias=None, scale=None):
        kw = {}
        if bias is not None:
            kw["bias"] = bias
        if scale is not None:
            kw["scale"] = scale
        self.op(self.act, lambda e: e.activation(out=out, in_=in_, func=func, **kw), reads, writes)

    def copy(self, q, out, in_, reads, writes):
        if q is self.act:
            self.op(q, lambda e: e.activation(out=out, in_=in_, func=AF.Copy), reads, writes)
        else:
            self.op(q, lambda e: e.tensor_copy(out=out, in_=in_), reads, writes)

    def recip(self, out, in_, reads, writes):
        self.op(self.dve, lambda e: e.reciprocal(out=out, in_=in_), reads, writes)

    def memset(self, q, ap, val, writes):
        self.op(q, lambda e: e.memset(ap, val), (), writes)

    def ld(self, qname, out, in_, reads=(), writes=(), slow=False):
        if slow:
            self.dma(qname, lambda e: e.dma_start(out=out, in_=in_, allow_slow_non_contiguous=True), reads, writes)
        else:
            self.dma(qname, lambda e: e.dma_start(out=out, in_=in_), reads, writes)

    def finish(self, final_res):
        deps = [r.w for r in final_res if r.w is not None]
        self._emit_waits(self.sp, deps)
        self.barrier()
        nc = self.nc
        with nc.Block() as block:
            @block.sync
            def _(e):
                for t in self.sp.thunks:
                    t(e)

            @block.tensor
            def _(e):
                for t in self.pe.thunks:
                    t(e)

            @block.vector
            def _(e):
                for t in self.dve.thunks:
                    t(e)

            @block.scalar
            def _(e):
                for t in self.act.thunks:
                    t(e)

            @block.gpsimd
            def _(e):
                for t in self.pool.thunks:
                    t(e)
        while self.ctx:
            self.ctx.pop().__exit__(None, None, None)


_CONST_CACHE = {}


def _bf(a):
    return np.ascontiguousarray(a.astype(np.float32)).astype(ml_dtypes.bfloat16)


def _dft_tables(L):
    N = 2 * L
    H = L // 2
    nfcA = (H + 1 + 127) // 128
    nfcB = H // 128
    nfc = nfcA + nfcB
    a = np.arange(nfcA * 128, dtype=np.int64)
    t = np.arange(L, dtype=np.int64)
    ang = 2.0 * np.pi * ((t[:, None] * a[None, :]) % N).astype(np.float64) / N
    validA = (a <= H).astype(np.float64)
    C = np.cos(ang) * validA
    S = np.sin(ang) * validA
    ntc = L // 128
    Fc = C.reshape(ntc, 128, nfcA, 128).transpose(2, 1, 0, 3)
    Fs = S.reshape(ntc, 128, nfcA, 128).transpose(2, 1, 0, 3)
    freq = np.concatenate([a, L - np.arange(nfcB * 128, dtype=np.int64)])
    valid = np.concatenate([validA, np.ones(nfcB * 128)])
    w = np.where((freq == 0) | (freq == L), 1.0, 2.0) * valid / N
    angI = 2.0 * np.pi * ((freq[:, None] * t[None, :]) % N).astype(np.float64) / N
    IcM = np.cos(angI) * w[:, None]
    IsM = np.sin(angI) * w[:, None]
    IsM[nfcA * 128:] *= -1.0
    TW = min(512, L)
    ntt = L // TW
    Ic = IcM.reshape(nfc, 128, ntt, TW).transpose(2, 1, 0, 3)
    Is = IsM.reshape(nfc, 128, ntt, TW).transpose(2, 1, 0, 3)
    return _bf(Fc), _bf(Fs), _bf(Ic), _bf(Is)


def _hy_consts(L):
    t = (np.arange(L, dtype=np.float32) / np.float32(L)).astype(np.float32)
    ang = (2.0 * math.pi * t[:, None] * np.arange(1, 17, dtype=np.float32)).astype(np.float32)
    feats = np.concatenate([t[:, None], np.sin(ang), np.cos(ang)], axis=-1).astype(np.float32)
    deltas = np.abs(np.linspace(math.log(1e-2) / 1.5, math.log(1e-2) / 0.3, 512, dtype=np.float32))
    decay = np.exp(-t[:, None] * deltas[None, :]).astype(np.float32)
    return np.ascontiguousarray(feats.T), decay


def _rope_tables():
    row = np.repeat(np.arange(SEQ // 64, dtype=np.float32), 64)
    col = np.tile(np.arange(64, dtype=np.float32), SEQ // 64)
    inv = (10000.0 ** (-np.arange(16, dtype=np.float32) / 16)).astype(np.float32)
    ang = np.concatenate([row[:, None] * inv, col[:, None] * inv], axis=-1).astype(np.float32)
    cs = np.zeros((64, 2, SEQ), np.float32)
    cs[:32, 0] = np.cos(ang).T
    cs[32:, 0] = np.cos(ang).T
    cs[:32, 1] = np.sin(ang).T
    cs[32:, 1] = np.sin(ang).T
    return cs


def _consts():
    if _CONST_CACHE:
        return _CONST_CACHE
    c = {}
    c["ident_f"] = np.eye(128, dtype=np.float32)
    c["sgn"] = np.where(np.arange(128) % 2 == 0, 1.0, -1.0).astype(np.float32).reshape(128, 1)
    c["rope_cs"] = _rope_tables()
    for L, tag in ((SEQ, "L"), (CTX, "C")):
        Fc, Fs, Ic, Is = _dft_tables(L)
        c["Fc" + tag], c["Fs" + tag], c["Ic" + tag], c["Is" + tag] = Fc, Fs, Ic, Is
        f, d = _hy_consts(L)
        c["feats" + tag], c["decay" + tag] = f, d
    _CONST_CACHE.update(c)
    return _CONST_CACHE


def _pack_small(inp):
    sp = np.zeros((128, DEPTH, NSP), np.float32)

    def put(col, arr):
        n = arr.shape[1] // 128
        sp[:, :, col:col + n] = arr.reshape(DEPTH, n, 128).transpose(2, 0, 1)

    put(SP_ADAB, inp["ada_b"])
    put(SP_NG, inp["norm_g"].reshape(DEPTH, 4 * D))
    put(SP_HYC, inp["hy_conv"].reshape(DEPTH, 3 * 1536))
    put(SP_HYB, inp["hy_bias"].reshape(DEPTH, 2 * 512))
    put(SP_LC, inp["lru_conv"].reshape(DEPTH, 2 * 4 * 512))
    put(SP_BA, inp["lru_ba"].reshape(DEPTH, 1024))
    put(SP_BX, inp["lru_bx"].reshape(DEPTH, 1024))
    put(SP_LAM, inp["lru_lam"].reshape(DEPTH, 1024))
    put(SP_GQ, inp["mla_gq"])
    put(SP_GKV, inp["mla_gkv"])
    put(SP_HG, inp["head_g"])
    put(SP_FC, inp["ffn_conv"].reshape(DEPTH, 3 * 2 * DFF))
    sp[:64, :, SP_B1] = inp["hy_b1"].T
    sp[:64, :, SP_B2] = inp["hy_b2"].T
    return sp


class Builder:
    def __init__(self, layers, dbg=(), phases=None, same_engine_sync=True):
        self.layers = list(layers)
        self.lidx = {l: i for i, l in enumerate(self.layers)}
        NL = len(self.layers)
        self.dbg = set(dbg)
        self.phases = phases
        nc = self.nc = bass.Bass("TRN2", target_bir_lowering=False)
        self.I = {}
        din = self.din
        self.x_in = din("x", [SL, D])
        self.ctx_in = din("ctx", [CTX, D])
        self.cvec = din("cvec", [128, 32])
        self.smallp_in = din("smallp", [128, DEPTH * NSP])
        self.ownp_in = din("ownp", [128, DEPTH * NOWN])
        self.w3own = din("hy_w3_own", [NL, 64, 512])
        self.lwa_own = din("lru_wa_own", [NL, 2, 128, 128])
        self.lwx_own = din("lru_wx_own", [NL, 2, 128, 128])
        self.ada_w = din("ada_w_own", [NL, D, 6 * D // NR])
        self.w_in = din("w_in", [NL, D, DIN])
        self.hy_w1 = din("hy_w1", [NL, 33, 64])
        self.hy_w2 = din("hy_w2", [NL, 64, 64])
        self.mla_wuq = din("mla_wuq", [NL, 512, 1536])
        self.mla_wukv = din("mla_wukv", [NL, 256, 2048])
        self.w_out = din("w_out", [NL, D, D])
        self.ffn_up = din("ffn_up", [NL, D, 2 * DFF])
        self.ffn_down = din("ffn_down", [NL, DFF, D])
        self.ident_in = din("ident_f", [128, 128])
        self.sgn_in = din("sgn", [128, 1])
        self.rope_cs = din("rope_cs", [64, 2, SEQ])
        self.rope_q = din("rope_q", [64, 2, SL])
        self.tab = {}
        for L, tag in ((SEQ, "L"), (CTX, "C")):
            nfcA = (L // 2 + 1 + 127) // 128
            nfcB = (L // 2) // 128
            nfc = nfcA + nfcB
            ntc = L // 128
            TW = min(512, L)
            self.tab[tag] = dict(
                L=L, nfc=nfc, nfcA=nfcA, nfcB=nfcB, ntc=ntc, TW=TW, ntt=L // TW,
                Fc=din("Fc" + tag, [nfcA, 128, ntc, 128], BF16), Fs=din("Fs" + tag, [nfcA, 128, ntc, 128], BF16),
                Ic=din("Ic" + tag, [L // TW, 128, nfc, TW], BF16), Is=din("Is" + tag, [L // TW, 128, nfc, TW], BF16),
                feats=din("feats" + tag, [33, L]), decay=din("decay" + tag, [L, 128]),
                S=self.dscr("S" + tag, [NL, nfc * 128, 512], F32),
            )
        self.out = nc.dram_tensor("out", [SL, D], F32, kind="ExternalOutput").ap()
        ds = self.dscr
        self.XT = ds("XT", [D, NTL], F32)
        self.UC = ds("UC", [DINX, CTX], F32)
        self.ULOC = ds("ULOC", [UCH * 256, SL], F32)
        self.UG = ds("UG", [UCH * NR * 256, SL], F32)
        self.MIN = ds("MIN", [128, 48], F32)
        self.MG = ds("MG", [NR * 128, 48], F32)
        self.HB = ds("HB", [D, 2], BF16)
        self.HG = ds("HG", [NR * D, 2], BF16)
        self.HGP = ds("HGP", [(NR + 2) * D, 2], BF16)
        self.ZT = ds("ZT", [128, NT], F32)
        self.PT = ds("PT", [256, NT], F32)
        self.YIN = [ds(f"YIN{k}", [NR * 128, NTL], BF16) for k in range(2)]
        self.YG = [ds(f"YG{k}", [2 * NR * 256, NTL], BF16) for k in range(2)]
        self.YM = ds("YM", [1024, NTL], BF16)
        self.H2 = ds("H2", [D, NTL], BF16)
        self.WIN = ds("WIN", [NL, D, DINX], BF16)
        self.WUQ = ds("WUQ", [NL, 512, 2048], BF16)
        self.WUKV = ds("WUKV", [NL, 256, 2048], BF16)
        self.WOUT = ds("WOUT", [NL, D, D], BF16)
        self.WUP = ds("WUP", [NL, D, 2 * DFF], BF16)
        self.WDN = ds("WDN", [NL, DFF, D], BF16)
        self.P = Prog(nc, same_engine_sync=same_engine_sync)
        self.r_out = Res("out")
        self._rank = {}

    def din(self, name, shape, dt=F32):
        self.I[name] = (list(shape), dt)
        return self.nc.dram_tensor(name, list(shape), dt, kind="ExternalInput").ap()

    def dscr(self, name, shape, dt):
        kind = "ExternalOutput" if name in self.dbg else "Internal"
        return self.nc.dram_tensor(name, list(shape), dt, kind=kind).ap()

    def rank(self, e):
        key = id(e)
        if key not in self._rank:
            self._rank[key] = e.snap(e.partition_id() % NR)
        return self._rank[key]

    def rv(self, e, kind):
        key = (kind, id(e))
        if key not in self._rank:
            r = self.rank(e)
            expr = {"x": (r // 2) * 8 + (r % 2), "128": r * 128, "16": r * 16}[kind]
            self._rank[key] = e.snap(expr)
        return self._rank[key]

    def Uap(self, row0, nrows, c0, W):
        if c0 < CTX:
            return self.UC[row0:row0 + nrows, c0:c0 + W]
        t0 = c0 - CTX
        s_, tl = t0 // SL, t0 % SL
        assert tl + W <= SL
        k, rr = row0 // 256, row0 % 256
        assert rr + nrows <= 256
        base = (k * NR + s_) * 256 + rr
        return self.UG[base:base + nrows, tl:tl + W]

    def Ulat(self, row0):
        k, rr = row0 // 256, row0 % 256
        return self.UG.rearrange("(k s r) t -> k r s t", s=NR, r=256)[k][rr:rr + 128, :, :]

    def loc_tiles(self, need_ctx=True):
        tiles = [(0, CTX, 1)] if need_ctx else []
        tiles += [(CTX + i * 512, 512, 0) for i in range(SL // 512)]
        return tiles

    def allgather(self, pairs):
        P = self.P
        P.barrier()
        for (src, dst) in pairs:
            P.op(P.pool, lambda e, src=src, dst=dst: e.collective_compute("AllGather", ALU.bypass, replica_groups=GROUPS, ins=[src.opt()], outs=[dst.opt()]), (), ())
        P.barrier()

    def on(self, ph):
        return self.phases is None or ph in self.phases

    def build(self):
        P = self.P
        self.setup()
        for l in self.layers:
            last = (l == DEPTH - 1)
            if self.on("M"):
                self.phase_mod(l)
            if self.on("A"):
                self.phase_inproj(l)
            if self.on("H"):
                if not last:
                    self.phase_hyena(l, "C")
                self.phase_hyena(l, "L")
            if self.on("R"):
                self.phase_lru(l, not last)
                self.allgather([(self.YIN[k][q * 256:(q + 1) * 256, :], self.YG[k][q * 1024:(q + 1) * 1024, :]) for k in range(2) for q in range(2)])
            if self.on("T"):
                self.phase_attn(l, not last)
            if self.on("O"):
                self.phase_outproj(l, not last)
            if self.on("F"):
                self.phase_ffn(l, not last)
        if self.on("Z"):
            self.final()
        P.finish([self.r_out])
        return self.nc

    def setup(self):
        P = self.P
        self.pb = [(P.psum(f"pb{i}", [128, 512]), Res(f"pb{i}")) for i in range(8)]
        self.ident = P.sbuf("ident", [128, 128], F32)
        self.r_const = Res("const")
        P.ld("sp", self.ident[:], self.ident_in[:, :], writes=[self.r_const])
        self.sgn = P.sbuf("sgn", [128, 1], F32)
        P.ld("sp", self.sgn[:], self.sgn_in[:, :], writes=[self.r_const])
        self.ones_b = P.sbuf("ones_b", [128, 128], BF16)
        self.ones_f = P.sbuf("ones_f", [128, 128], F32)
        P.memset(P.dve, self.ones_b[:], 1.0, [self.r_const])
        P.memset(P.dve, self.ones_f[:], 1.0, [self.r_const])
        self.smallp = P.sbuf("smallp", [128, DEPTH * NSP], F32)
        P.ld("sp", self.smallp[:], self.smallp_in[:, :], writes=[self.r_const])
        self.scv = P.sbuf("scv", [128, 32], F32)
        cvt = P.sbuf("cvt", [128, 32], F32)
        P.ld("sp", cvt[:], self.cvec[:, :], writes=[self.r_const])
        P.actv(self.scv[:], cvt[:], AF.Silu, [self.r_const], [self.r_const])
        self.eps_t = P.sbuf("eps_t", [128, 1], F32)
        P.memset(P.dve, self.eps_t[:], EPS, [self.r_const])
        self.ownp = P.sbuf("ownp", [128, DEPTH * NOWN], F32)
        P.ld("sp", self.ownp[:], self.ownp_in[:, :], writes=[self.r_const])
        self.mods = P.sbuf("mods", [128, 6 * 32], F32)
        self.r_mods = Res("mods")
        self.lru_h0 = P.sbuf("lru_h0", [128, 8], F32)
        self.r_h0 = Res("h0")
        for l in self.layers:
            for (src, dst, rows, cols, dcols) in (
                (self.w_in[self.lidx[l]], self.WIN[self.lidx[l]], D, DIN, DINX),
                (self.w_out[self.lidx[l]], self.WOUT[self.lidx[l]], D, D, D),
                (self.ffn_up[self.lidx[l]], self.WUP[self.lidx[l]], D, 2 * DFF, 2 * DFF),
                (self.ffn_down[self.lidx[l]], self.WDN[self.lidx[l]], DFF, D, D),
                (self.mla_wukv[self.lidx[l]], self.WUKV[self.lidx[l]], 256, 2048, 2048),
            ):
                cw = 1408 if cols % 1408 == 0 else (1696 if cols % 1696 == 0 else 1024)
                assert cols % cw == 0 and cw <= 2048
                for r0 in range(0, rows, 1024):
                    r1 = min(rows, r0 + 1024)
                    for c0 in range(0, cols, cw):
                        P.ld("pool", dst[r0:r1, c0:c0 + cw], src[r0:r1, c0:c0 + cw])
            for h in range(8):
                P.ld("pool", self.WUQ[self.lidx[l]][:, h * 256:h * 256 + 192], self.mla_wuq[self.lidx[l]][:, h * 192:(h + 1) * 192])
            mk = P.push()
            wk = P.sbuf("wk", [128, 16, 64], F32)
            wr = P.sbuf("wr", [128, 16, 64], BF16)
            r_wk, r_wr = Res(), Res()
            P.ld("sp", wk[:], self.w_in[self.lidx[l]].rearrange("(kc p) n -> p kc n", p=128)[:, :, OFF_KR:OFF_KR + 64], writes=[r_wk])
            P.ts(P.dve, wr[:, :, 0:32], wk[:, :, 32:64], -1.0, None, ALU.mult, None, [r_wk], [r_wr])
            P.copy(P.dve, wr[:, :, 32:64], wk[:, :, 0:32], [r_wk], [r_wr])
            P.ld("sp", self.WIN[self.lidx[l]].rearrange("(kc p) n -> p kc n", p=128)[:, :, OFF_KRR:OFF_KRR + 64], wr[:], reads=[r_wr])
            qk = P.sbuf("qk", [128, 4, 8, 64], F32)
            qr = P.sbuf("qr", [128, 4, 8, 64], BF16)
            r_qk, r_qr = Res(), Res()
            srcv = self.mla_wuq[self.lidx[l]].rearrange("(kc p) (h n) -> p kc h n", p=128, n=192)
            dstv = self.WUQ[self.lidx[l]].rearrange("(kc p) (h n) -> p kc h n", p=128, n=256)
            for kc in range(4):
                P.ld("sp", qk[:, kc, :, :], srcv[:, kc, :, 128:192], writes=[r_qk])
            P.ts(P.dve, qr[:, :, :, 0:32], qk[:, :, :, 32:64], -1.0, None, ALU.mult, None, [r_qk], [r_qr])
            P.copy(P.dve, qr[:, :, :, 32:64], qk[:, :, :, 0:32], [r_qk], [r_qr])
            for kc in range(4):
                P.ld("sp", dstv[:, kc, :, 192:256], qr[:, kc, :, :], reads=[r_qr])
            P.pop(mk)
        mk = P.push()
        xin = P.ring("xin", 2, [128, D], F32)
        xst = P.ring("xst", 2, [128, 16, 128], F32)
        XTv = self.XT.rearrange("(c p) t -> p c t", p=128)
        k = 0
        for ti in range(NTL // 128):
            src = self.ctx_in[ti * 128:(ti + 1) * 128, :] if ti < 2 else self.x_in[(ti - 2) * 128:(ti - 1) * 128, :]
            xt, r_xt = xin.next()
            P.ld("sp" if ti % 2 == 0 else "act", xt[:], src, writes=[r_xt])
            st, r_st = xst.next()
            for g in range(4):
                pt, r_pt = self.pb[k % 8]
                k += 1
                for j in range(4):
                    c = g * 4 + j
                    P.tr(pt[:, j * 128:(j + 1) * 128], xt[:, c * 128:(c + 1) * 128], self.ident[:], [r_xt, self.r_const], [r_pt])
                P.copy(P.act if g % 2 == 0 else P.dve, st[:, g * 4:(g + 1) * 4, :], pt[:].rearrange("p (j t) -> p j t", j=4), [r_pt], [r_st])
            P.ld("sp" if ti % 2 == 1 else "act", XTv[:, :, ti * 128:(ti + 1) * 128], st[:], reads=[r_st])
        P.pop(mk)
        if self.on("H"):
            ls = self.layers
            for i in range(0, len(ls), 2):
                self.hy_filter_all(self.tab["L"], ls[i:i + 2])
            lc_ = [l for l in ls if l < DEPTH - 1]
            for i in range(0, len(lc_), 2):
                self.hy_filter_all(self.tab["C"], lc_[i:i + 2])
        mk = P.push()
        zt = P.sbuf("zpad", [128, 16, 2], BF16)
        r_z = Res()
        P.memset(P.dve, zt[:], 0.0, [r_z])
        for blk in (0, NR + 1):
            P.ld("sp", self.HGP[blk * D:(blk + 1) * D, :].rearrange("(c p) k -> p c k", p=128), zt[:], reads=[r_z])
        P.pop(mk)

    def final(self):
        P = self.P
        mk = P.push()
        xin = P.ring("fin", 2, [128, 16, 128], F32)
        xst = P.ring("fst", 2, [128, D], F32)
        XTv = self.XT.rearrange("(c p) t -> p c t", p=128)
        k = 0
        for ti in range(SL // 128):
            xt, r_xt = xin.next()
            P.ld("sp" if ti % 2 == 0 else "act", xt[:], XTv[:, :, CTX + ti * 128:CTX + (ti + 1) * 128], writes=[r_xt])
            st, r_st = xst.next()
            for g in range(4):
                pt, r_pt = self.pb[k % 8]
                k += 1
                for j in range(4):
                    c = g * 4 + j
                    P.tr(pt[:, j * 128:(j + 1) * 128], xt[:, c, :], self.ident[:], [r_xt, self.r_const], [r_pt])
                P.copy(P.act if g % 2 == 0 else P.dve, st[:, g * 512:(g + 1) * 512], pt[:], [r_pt], [r_st])
            P.ld("sp" if ti % 2 == 1 else "act", self.out[ti * 128:(ti + 1) * 128, :], st[:], reads=[r_st], writes=[self.r_out])
        P.pop(mk)

    def sp_col(self, l, col, n=1, parts=128):
        base = l * NSP + col
        return self.smallp[0:parts, base:base + n]

    def rms_rstd(self, chunks, W, n_feat, sq_ring, pbi, st_ring, reads):
        P = self.P
        pt, r_pt = self.pb[pbi]
        for i, ap in enumerate(chunks):
            sq, r_sq = sq_ring.next()
            P.actv(sq[:, :W], ap, AF.Square, reads, [r_sq])
            P.mm(pt[:, :W], self.ones_b[:], sq[:, :W], i == 0, i == len(chunks) - 1, [r_sq, self.r_const], [r_pt])
        st, r_st = st_ring.next()
        P.actv(st[:, :W], pt[:, :W], AF.Sqrt, [r_pt], [r_st], bias=self.eps_t[:], scale=1.0 / n_feat)
        P.recip(st[:, :W], st[:, :W], [r_st], [r_st])
        return st, r_st

    def phase_mod(self, l):
        P = self.P
        mk = P.push()
        wr = P.ring("adaw", 2, [128, 16, 512], F32)
        modT = P.sbuf("modT", [128, 96, 2], F32)
        r_modT = Res()
        pm, r_pm = self.pb[0]
        src = self.ada_w[self.lidx[l]].rearrange("(kc p) n -> p kc n", p=128)
        scv = self.scv[:].rearrange("p (kc j) -> p kc j", j=2)
        for g in range(6):
            wt, r_wt = wr.next()
            P.ld("sp" if g % 2 == 0 else "act", wt[:], src[:, :, g * 512:(g + 1) * 512], writes=[r_wt])
            for m in range(4):
                col = (g * 4 + m) * 2
                for kc in range(16):
                    P.mm(pm[:, col:col + 2], wt[:, kc, m * 128:(m + 1) * 128], scv[:, kc, :], kc == 0, kc == 15, [r_wt, self.r_const], [r_pm])
        mloc = P.sbuf("mloc", [128, 48], F32)
        r_ml = Res()
        P.copy(P.dve, mloc[:], pm[:, 0:48], [r_pm], [r_ml])
        P.ld("sp", self.MIN[:, :], mloc[:], reads=[r_ml])
        self.allgather([(self.MIN, self.MG)])
        raw = P.sbuf("mraw", [128, 96, 2], F32)
        r_raw = Res()
        P.ld("sp", raw[:].rearrange("p (s m) j -> p s (m j)", s=NR), self.MG.rearrange("(s p) c -> p s c", p=128), writes=[r_raw])
        for j in range(2):
            P.tt(P.dve, modT[:, :, j], raw[:, :, j], self.sp_col(l, SP_ADAB, 96), ALU.add, [r_raw, self.r_const], [r_modT])
        mods = self.mods[:].rearrange("p (k c j) -> p k c j", k=6, j=2)
        for j in range(2):
            for half, (sh, sc, g, nga, ngb) in enumerate(((0, 1, 2, 0, 1), (3, 4, 5, 2, 3))):
                ng_a = self.sp_col(l, SP_NG + nga * 16, 16)
                ng_b = self.sp_col(l, SP_NG + ngb * 16, 16)
                A, B, G = mods[:, half * 3 + 0, :, j], mods[:, half * 3 + 1, :, j], mods[:, half * 3 + 2, :, j]
                P.stt(A, modT[:, sc * 16:(sc + 1) * 16, j], 1.0, ng_a, ALU.add, ALU.mult, [r_modT, self.r_const], [self.r_mods])
                P.copy(P.dve, B, modT[:, sh * 16:(sh + 1) * 16, j], [r_modT], [self.r_mods])
                P.tt(P.dve, G, modT[:, g * 16:(g + 1) * 16, j], ng_b, ALU.mult, [r_modT, self.r_const], [self.r_mods])
        P.pop(mk)

    def mod(self, k, c, j):
        i = (k * 16 + c) * 2 + j
        return self.mods[:, i:i + 1]

    def tok_tiles(self, need_ctx=True):
        tiles = [(0, CTX, 1)] if need_ctx else []
        tiles += [(CTX + i * 512, 512, 0) for i in range(SEQ // 512)]
        return tiles

    def phase_inproj(self, l):
        P = self.P
        mk = P.push()
        xs_r = P.ring("xs", 2, [128, 16, 512], F32)
        hb_r = P.ring("hb", 2, [128, 16, 512], BF16)
        sq_r = P.ring("sq", 3, [128, 512], BF16)
        tmp_r = P.ring("tmp", 3, [128, 512], F32)
        st_r = P.ring("st", 2, [128, 512], F32)
        w_r = P.ring("wg", 2, [128, 16, 512], BF16)
        ev_r = P.ring("ev", 4, [128, 512], F32)
        XTv = self.XT.rearrange("(c p) t -> p c t", p=128)
        Wv = self.WIN[self.lidx[l]].rearrange("(kc p) n -> p kc n", p=128)
        kb = 0
        for (c0, W, j) in self.loc_tiles():
            xs, r_xs = xs_r.next()
            P.ld("sp", xs[:, :, :W], XTv[:, :, c0:c0 + W], writes=[r_xs])
            st, r_st = self.rms_rstd([xs[:, c, :W] for c in range(16)], W, D, sq_r, 7, st_r, [r_xs])
            hb, r_hb = hb_r.next()
            for c in range(16):
                tmp, r_tmp = tmp_r.next()
                P.tt(P.dve, tmp[:, :W], xs[:, c, :W], st[:, :W], ALU.mult, [r_xs, r_st], [r_tmp])
                P.actv(hb[:, c, :W], tmp[:, :W], AF.Identity, [r_tmp, self.r_mods], [r_hb], bias=self.mod(1, c, j), scale=self.mod(0, c, j))
            for g in range(7):
                ncol = 512 if g < 6 else DINX - 6 * 512
                wt, r_wt = w_r.next()
                P.ld("act", wt[:, :, :ncol], Wv[:, :, g * 512:g * 512 + ncol], writes=[r_wt])
                for m in range(ncol // 128):
                    pt, r_pt = self.pb[kb % 6]
                    kb += 1
                    for kc in range(16):
                        P.mm(pt[:, :W], wt[:, kc, m * 128:(m + 1) * 128], hb[:, kc, :W], kc == 0, kc == 15, [r_wt, r_hb], [r_pt])
                    ev, r_ev = ev_r.next()
                    P.copy(P.act if kb % 2 == 0 else P.dve, ev[:, :W], pt[:, :W], [r_pt], [r_ev])
                    row = g * 512 + m * 128
                    dstU = self.UC[row:row + 128, c0:c0 + W] if j == 1 else self.ULOC[row:row + 128, c0 - CTX:c0 - CTX + W]
                    P.ld("sp", dstU, ev[:, :W], reads=[r_ev])
        P.pop(mk)
        self.allgather([(self.ULOC[k * 256:(k + 1) * 256, :], self.UG[k * NR * 256:(k + 1) * NR * 256, :]) for k in range(UCH) if k not in (8, 9)])


    def sin_layer(self, ps_ap, bias_ap, out_ap, W, rings, reads, r_ps, r_out):
        P = self.P
        v, r_v = rings[0].next()
        t1, r_t1 = rings[1].next()
        P.ts(P.dve, v[:, :W], ps_ap, bias_ap, None, ALU.add, None, [r_ps] + reads, [r_v])
        P.ts(P.dve, t1[:, :W], v[:, :W], 1.0 / (2 * math.pi), MAGIC, ALU.mult, ALU.add, [r_v], [r_t1])
        P.ts(P.dve, t1[:, :W], t1[:, :W], -MAGIC, None, ALU.add, None, [r_t1], [r_t1])
        P.stt(v[:, :W], t1[:, :W], -2 * math.pi, v[:, :W], ALU.mult, ALU.add, [r_t1, r_v], [r_v])
        P.ts(P.dve, v[:, :W], v[:, :W], -3.1415925, 3.1415925, ALU.max, ALU.min, [r_v], [r_v])
        P.actv(out_ap, v[:, :W], AF.Sin, [r_v], [r_out])

    def hy_filter_all(self, T, layers):
        P = self.P
        L, nfc, ntc, TW, ntt = T["L"], T["nfc"], T["ntc"], T["TW"], T["ntt"]
        G = len(layers)
        mk = P.push()
        nfcA, nfcB = T["nfcA"], T["nfcB"]
        ksum = P.sbuf("ksum", [128, ntc, G * 256], BF16)
        kdiff = P.sbuf("kdiff", [128, ntc, G * 256], BF16)
        ksa = P.sbuf("ksa", [128, ntc, G * 256], BF16)
        kda = P.sbuf("kda", [128, ntc, G * 256], BF16)
        r_k = [Res() for _ in range(ntc)]
        rn = P.sbuf("rn", [128, G * 256], F32)
        r_rn = Res()
        dec = P.sbuf("dec", [128, ntc, 128], F32)
        r_dec = Res()
        P.ld("act", dec[:], T["decay"].rearrange("(tc p) c -> p tc c", p=128), writes=[r_dec])
        mk1 = P.push()
        w1 = P.sbuf("hw1", [33, 64], F32)
        w2 = P.sbuf("hw2", [64, 64], F32)
        w3 = P.sbuf("hw3", [64, 4, 128], F32)
        r_w = Res()
        hid2 = P.sbuf("hid2", [64, L], F32)
        r_hid2 = Res()
        f_r = P.ring("ft", 2, [33, 512], F32)
        h1_r = P.ring("h1", 2, [64, 512], F32)
        v_r = P.ring("sv", 2, [64, 512], F32)
        t_r = P.ring("st1", 2, [64, 512], F32)
        hf_r = P.ring("hf", 2, [128, 4, 128], F32)
        ab_r = P.ring("ab", 2, [128, 512], F32)
        pns = P.sbuf("pns", [128, 512], F32)
        r_pns = Res()
        for gi, l in enumerate(layers):
            li = self.lidx[l]
            P.ld("sp", w1[:], self.hy_w1[li], writes=[r_w])
            P.ld("sp", w2[:], self.hy_w2[li], writes=[r_w])
            P.ld("sp", w3[:].rearrange("p g c -> p (g c)"), self.w3own[li], writes=[r_w])
            b1 = self.sp_col(l, SP_B1, 1, 64)
            b2 = self.sp_col(l, SP_B2, 1, 64)
            for tt in range(ntt):
                ft, r_ft = f_r.next()
                P.ld("sp", ft[:, :TW], T["feats"][:, tt * TW:(tt + 1) * TW], writes=[r_ft])
                p0, r_p0 = self.pb[4]
                P.mm(p0[0:64, :TW], w1[:, :], ft[:, :TW], True, True, [r_w, r_ft], [r_p0])
                h1, r_h1 = h1_r.next()
                self.sin_layer(p0[0:64, :TW], b1, h1[:, :TW], TW, (v_r, t_r), [self.r_const], r_p0, r_h1)
                p1, r_p1 = self.pb[5]
                P.mm(p1[0:64, :TW], w2[:, :], h1[:, :TW], True, True, [r_w, r_h1], [r_p1])
                self.sin_layer(p1[0:64, :TW], b2, hid2[:, tt * TW:(tt + 1) * TW], TW, (v_r, t_r), [self.r_const], r_p1, r_hid2)
            pn, r_pn = self.pb[6]
            w3f = w3[:].rearrange("p g c -> p (g c)")
            for lc in range(ntc):
                hf, r_hf = hf_r.next()
                ph, r_ph = self.pb[lc % 4]
                P.mm(ph[:, :], hid2[:, lc * 128:(lc + 1) * 128], w3f, True, True, [r_hid2, r_w], [r_ph])
                for g in range(4):
                    P.tt(P.dve, hf[:, g, :], ph[:, g * 128:(g + 1) * 128], dec[:, lc, :], ALU.mult, [r_ph, r_dec], [r_hf])
                if lc == 0:
                    P.memset(P.dve, hf[0:1, 1, :], 0.0, [r_hf])
                    P.memset(P.dve, hf[0:1, 3, :], 0.0, [r_hf])
                ab, r_ab = ab_r.next()
                P.actv(ab[:], hf[:].rearrange("p g c -> p (g c)"), AF.Abs, [r_hf], [r_ab])
                P.mm(pn[:, :], self.ones_f[:], ab[:], lc == 0, lc == ntc - 1, [r_ab, self.r_const], [r_pn])
                for o_ in range(2):
                    cs = slice(gi * 256 + o_ * 128, gi * 256 + (o_ + 1) * 128)
                    P.tt(P.pool, ksum[:, lc, cs], hf[:, 2 * o_, :], hf[:, 2 * o_ + 1, :], ALU.add, [r_hf], [r_k[lc]])
                    P.tt(P.pool, kdiff[:, lc, cs], hf[:, 2 * o_, :], hf[:, 2 * o_ + 1, :], ALU.subtract, [r_hf], [r_k[lc]])
                gsl = slice(gi * 256, (gi + 1) * 256)
                P.ts(P.pool, ksa[:, lc, gsl], ksum[:, lc, gsl], self.sgn[:, 0:1], None, ALU.mult, None, [r_k[lc], self.r_const], [r_k[lc]])
                P.ts(P.pool, kda[:, lc, gsl], kdiff[:, lc, gsl], self.sgn[:, 0:1], None, ALU.mult, None, [r_k[lc], self.r_const], [r_k[lc]])
            P.copy(P.dve, pns[:], pn[:, :], [r_pn], [r_pns])
            pnv = pns[:].rearrange("p (o d c) -> p o d c", o=2, d=2)
            rnv = rn[:, gi * 256:(gi + 1) * 256].rearrange("p (o c) -> p o c", o=2)
            P.tt(P.dve, rnv, pnv[:, :, 0, :], pnv[:, :, 1, :], ALU.add, [r_pns], [r_rn])
            P.recip(rn[:, gi * 256:(gi + 1) * 256], rn[:, gi * 256:(gi + 1) * 256], [r_rn], [r_rn])
        P.pop(mk1)
        AW = ntc * 128
        tbs = [(P.sbuf(f"ftb{i}", [128, 2 * AW], BF16), Res(), Res()) for i in range(2)]
        sst_r = P.ring("sst", 3, [128, 512], F32)
        for fc in range(nfcA):
            tb, r_tc, r_ts = tbs[fc % 2]
            P.ld("sp", tb[:, 0:AW], T["Fc"][fc].rearrange("p tc f -> p (tc f)"), writes=[r_tc])
            P.ld("act", tb[:, AW:2 * AW], T["Fs"][fc].rearrange("p tc f -> p (tc f)"), writes=[r_ts])
            Fc_t = tb[:, 0:AW].rearrange("p (tc f) -> p tc f", f=128)
            Fs_t = tb[:, AW:2 * AW].rearrange("p (tc f) -> p tc f", f=128)
            for gi, l in enumerate(layers):
                li = self.lidx[l]
                gs = slice(gi * 256, (gi + 1) * 256)
                halves = [(ksum, kdiff, fc)] + ([(ksa, kda, nfcA + fc)] if fc < nfcB else [])
                for hb, (km, kd, chunk) in enumerate(halves):
                    pr, r_pr = self.pb[gi * 4 + hb * 2]
                    ps_, r_ps = self.pb[gi * 4 + hb * 2 + 1]
                    for tc in range(ntc):
                        P.mm(pr[:, 0:256], Fc_t[:, tc, :], km[:, tc, gs], tc == 0, tc == ntc - 1, [r_tc, r_k[tc]], [r_pr])
                    for tc in range(ntc):
                        P.mm(ps_[:, 0:256], Fs_t[:, tc, :], kd[:, tc, gs], tc == 0, tc == ntc - 1, [r_ts, r_k[tc]], [r_ps])
                    sst, r_sst = sst_r.next()
                    P.tt(P.dve, sst[:, 0:256], pr[:, 0:256], rn[:, gs], ALU.mult, [r_pr, r_rn], [r_sst])
                    P.tt(P.dve, sst[:, 256:512], ps_[:, 0:256], rn[:, gs], ALU.mult, [r_ps, r_rn], [r_sst])
                    P.ld("sp", T["S"][li][chunk * 128:(chunk + 1) * 128, :], sst[:], reads=[r_sst])
        P.pop(mk)

    def own(self, l, col):
        i = l * NOWN + col
        return self.ownp[:, i:i + 1]

    def head_norm_store(self, y_ap, W, l, hg_ap, dsts, sq_r, st_r, ob_r, pbi, r_y):
        P = self.P
        st, r_st = self.rms_rstd([y_ap], W, 128, sq_r, pbi, st_r, [r_y])
        ob, r_ob = ob_r.next()
        P.stt(ob[:, :W], y_ap, hg_ap, st[:, :W], ALU.mult, ALU.mult, [r_y, r_st, self.r_const], [r_ob])
        for k, d in enumerate(dsts):
            P.dma("sp" if k % 2 == 0 else "act", lambda e, d=d, ob=ob: e.dma_start(out=d(e), in_=ob[:, :W]), reads=[r_ob])

    def yin_dsts(self, kind, c0, W):
        Y = self.YIN[kind]
        if c0 < CTX:
            return [(lambda e, d=d: Y[d * 128:(d + 1) * 128, c0:c0 + W]) for d in range(NR)]
        t0 = c0 - CTX
        d, lc = t0 // SL, CTX + t0 % SL
        return [lambda e: Y[d * 128:(d + 1) * 128, lc:lc + W]]


    def ld_u_own(self, dst2d, A, isctx, L, col0, writes, qn="sp"):
        P = self.P
        if isctx:
            P.dma(qn, lambda e: e.dma_start(out=dst2d[:, col0:col0 + L], in_=self.UC[A:A + 512, 0:L][bass.ds(self.rv(e, "128"), 128), :]), writes=writes)
        else:
            win = self.UG[A * NR:A * NR + 2048, :].rearrange("(x i) t -> i x t", i=128)
            P.dma(qn, lambda e: e.dma_start(out=dst2d[:, col0:col0 + SEQ].rearrange("p (s t) -> p s t", s=NR), in_=win[:, bass.ds(self.rv(e, "x"), NR, 2), :]), writes=writes)

    def phase_hyena(self, l, tag):
        P = self.P
        T = self.tab[tag]
        L, nfc, ntc, TW, ntt = T["L"], T["nfc"], T["ntc"], T["TW"], T["ntt"]
        off = 0 if tag == "C" else CTX
        mk = P.push()
        u_r = P.ring("hu", 2, [128, L + 2], F32)
        o_r = P.ring("ho", 2, [128, L], F32)
        for p_ in range(3):
            u, r_u = u_r.next()
            P.memset(P.pool, u[:, 0:1], 0.0, [r_u])
            P.memset(P.pool, u[:, L + 1:L + 2], 0.0, [r_u])
            self.ld_u_own(u, p_ * 512, tag == "C", L, 1, [r_u])
            o, r_o = o_r.next()
            P.ts(P.dve, o[:, :], u[:, 0:L], self.own(l, p_ * 3 + 0), None, ALU.mult, None, [r_u, self.r_const], [r_o])
            P.stt(o[:, :], u[:, 1:L + 1], self.own(l, p_ * 3 + 1), o[:, :], ALU.mult, ALU.add, [r_u, r_o, self.r_const], [r_o])
            P.stt(o[:, :], u[:, 2:L + 2], self.own(l, p_ * 3 + 2), o[:, :], ALU.mult, ALU.add, [r_u, r_o, self.r_const], [r_o])
            dst = self.ZT[:, off:off + L] if p_ == 0 else self.PT[(p_ - 1) * 128:p_ * 128, off:off + L]
            P.ld("act", dst, o[:, :], reads=[r_o])
        P.pop(mk)
        FG = 11 if nfc % 11 == 0 else nfc
        nfg = nfc // FG
        AW = max(ntc * 128, FG * TW)
        for o_ in range(2):
            mk = P.push()
            nfcA, nfcB = T["nfcA"], T["nfcB"]
            zT = P.sbuf("zT", [128, ntc, 256], BF16)
            r_zT = [Res() for _ in range(ntc)]
            Yr = P.sbuf("Yr", [128, nfc, 128], BF16)
            Ys = P.sbuf("Ys", [128, nfc, 128], BF16)
            r_Y = [Res() for _ in range(nfc)]
            tbs = [(P.sbuf(f"tb{i}", [128, 2 * AW], BF16), Res(), Res()) for i in range(3)]
            zl_r = P.ring("zl", 2, [128, TW], F32)
            for tt in range(ntt):
                zl, r_zl = zl_r.next()
                P.ld("sp", zl[:, :], self.ZT[:, off + tt * TW:off + (tt + 1) * TW], writes=[r_zl])
                nsub = TW // 128
                pt, r_pt = self.pb[tt % 4]
                for s_ in range(nsub):
                    P.tr(pt[:, s_ * 128:(s_ + 1) * 128], zl[:, s_ * 128:(s_ + 1) * 128], self.ident[:], [r_zl, self.r_const], [r_pt])
                P.copy(P.act if tt % 2 == 0 else P.dve, zT[:, tt * nsub:(tt + 1) * nsub, 0:128], pt[:, 0:TW].rearrange("p (a b) -> p a b", b=128), [r_pt], [r_zT[tt * nsub]])
                P.ts(P.pool, zT[:, tt * nsub:(tt + 1) * nsub, 128:256], zT[:, tt * nsub:(tt + 1) * nsub, 0:128], self.sgn[:, 0:1], None, ALU.mult, None,
                     [r_zT[tt * nsub], self.r_const], [r_zT[tt * nsub]])
                for s_ in range(1, nsub):
                    r_zT[tt * nsub + s_] = r_zT[tt * nsub]
            S_r = P.ring("Sld", 3, [128, 512], F32)
            t_rs = [P.ring(f"ty{i}", 2, [128, 128], F32) for i in range(4)]
            FW = ntc * 128
            for fc in range(nfcA):
                tb, r_tc, r_ts = tbs[fc % 3]
                P.ld("sp", tb[:, 0:FW], T["Fc"][fc].rearrange("p tc f -> p (tc f)"), writes=[r_tc])
                P.ld("act", tb[:, AW:AW + FW], T["Fs"][fc].rearrange("p tc f -> p (tc f)"), writes=[r_ts])
                Fc_t = tb[:, 0:FW].rearrange("p (tc f) -> p tc f", f=128)
                Fs_t = tb[:, AW:AW + FW].rearrange("p (tc f) -> p tc f", f=128)
                NW = 256 if fc < nfcB else 128
                pr, r_pr = self.pb[4 + 2 * (fc % 2)]
                ps_, r_ps = self.pb[5 + 2 * (fc % 2)]
                for tc in range(ntc):
                    P.mm(pr[:, 0:NW], Fc_t[:, tc, :], zT[:, tc, 0:NW], tc == 0, tc == ntc - 1, [r_tc, r_zT[tc]], [r_pr])
                for tc in range(ntc):
                    P.mm(ps_[:, 0:NW], Fs_t[:, tc, :], zT[:, tc, 0:NW], tc == 0, tc == ntc - 1, [r_ts, r_zT[tc]], [r_ps])
                for hb, chunk in enumerate([fc] + ([nfcA + fc] if fc < nfcB else [])):
                    St, r_S = S_r.next()
                    P.ld("sp", St[:], T["S"][self.lidx[l]][chunk * 128:(chunk + 1) * 128, :], writes=[r_S])
                    Sr = St[:, o_ * 128:(o_ + 1) * 128]
                    Ss = St[:, 256 + o_ * 128:256 + (o_ + 1) * 128]
                    cs = slice(hb * 128, (hb + 1) * 128)
                    (t1, r1), (t2, r2), (t3, r3), (t4, r4) = [r.next() for r in t_rs]
                    P.tt(P.dve, t1[:], pr[:, cs], Sr, ALU.mult, [r_pr, r_S], [r1])
                    P.tt(P.dve, t2[:], ps_[:, cs], Ss, ALU.mult, [r_ps, r_S], [r2])
                    P.tt(P.dve, t3[:], pr[:, cs], Ss, ALU.mult, [r_pr, r_S], [r3])
                    P.tt(P.dve, t4[:], ps_[:, cs], Sr, ALU.mult, [r_ps, r_S], [r4])
                    P.tt(P.pool, Yr[:, chunk, :], t1[:], t2[:], ALU.subtract, [r1, r2], [r_Y[chunk]])
                    P.tt(P.pool, Ys[:, chunk, :], t3[:], t4[:], ALU.add, [r3, r4], [r_Y[chunk]])
            zt_r = P.ring("zt", 2, [128, TW], F32)
            pp_r = P.ring("pp", 2, [128, TW], F32)
            tm_r = P.ring("tm", 2, [128, TW], F32)
            sq_r = P.ring("hsq", 2, [128, 512], BF16)
            st_r = P.ring("hst", 2, [128, 512], F32)
            ob_r = P.ring("hob", 2, [128, 512], BF16)
            k = 0
            for tt in range(ntt):
                c0 = off + tt * TW
                pa, r_pa = self.pb[tt % 4]
                for fg in range(nfg):
                    tb, r_tc, r_ts = tbs[k % 3]
                    k += 1
                    P.ld("sp", tb[:, 0:FG * TW], T["Ic"][tt][:, fg * FG:(fg + 1) * FG, :].rearrange("p f t -> p (f t)"), writes=[r_tc])
                    P.ld("act", tb[:, AW:AW + FG * TW], T["Is"][tt][:, fg * FG:(fg + 1) * FG, :].rearrange("p f t -> p (f t)"), writes=[r_ts])
                    Ic_t = tb[:, 0:FG * TW].rearrange("p (f t) -> p f t", t=TW)
                    Is_t = tb[:, AW:AW + FG * TW].rearrange("p (f t) -> p f t", t=TW)
                    for f_ in range(FG):
                        fc = fg * FG + f_
                        P.mm(pa[:, :TW], Yr[:, fc, :], Ic_t[:, f_, :], fc == 0, False, [r_Y[fc], r_tc], [r_pa])
                        P.mm(pa[:, :TW], Ys[:, fc, :], Is_t[:, f_, :], False, fc == nfc - 1, [r_Y[fc], r_ts], [r_pa])
                zt, r_zt = zt_r.next()
                pp, r_pp = pp_r.next()
                tm, r_tm = tm_r.next()
                P.ld("sp", zt[:], self.ZT[:, c0:c0 + TW], writes=[r_zt])
                P.ld("act", pp[:], self.PT[o_ * 128:(o_ + 1) * 128, c0:c0 + TW], writes=[r_pp])
                P.stt(tm[:], zt[:], self.own(l, 9 + o_), pa[:, :TW], ALU.mult, ALU.add, [r_zt, r_pa, self.r_const], [r_tm])
                P.tt(P.pool, zt[:], tm[:], pp[:], ALU.mult, [r_tm, r_pp], [r_zt])
                if o_ == 0:
                    P.ld("sp", self.ZT[:, c0:c0 + TW], zt[:], reads=[r_zt])
                else:
                    self.head_norm_store(zt[:], TW, l, self.own(l, 11), self.yin_dsts(0, c0, TW), sq_r, st_r, ob_r, 4 + tt % 4, r_zt)
            P.pop(mk)

    def phase_lru(self, l, need_ctx):
        P = self.P
        li = self.lidx[l]
        mk = P.push()
        c8 = P.sbuf("c8", [128, 2], F32)
        r_c8 = Res()
        P.actv(c8[:], self.ownp[:, l * NOWN + 25:l * NOWN + 27], AF.Exp, [self.r_const], [r_c8], scale=-1.0)
        P.actv(c8[:], c8[:], AF.Ln, [r_c8, self.r_const], [r_c8], bias=self.ones_f[:, 0:1], scale=1.0)
        P.ts(P.dve, c8[:], c8[:], -8.0, None, ALU.mult, None, [r_c8], [r_c8])
        wt = P.sbuf("wab", [128, 4, 128], BF16)
        r_wt = Res()
        for d in range(2):
            for k, Wsrc in enumerate((self.lwa_own, self.lwx_own)):
                P.ld("pool", wt[:, d * 2 + k, :], Wsrc[li, d], writes=[r_wt])
        seqs = [(0, CTX, True), (CTX, SEQ, False)]
        sq_r = P.ring("lsq", 2, [128, 512], BF16)
        st_r = P.ring("lst", 2, [128, 512], F32)
        ob_r = P.ring("lob", 2, [128, 512], BF16)
        for (off, Ls, isctx) in seqs:
            mk2 = P.push()
            t = "c" if isctx else "l"
            xr_r = P.ring("xr" + t, 2, [128, Ls + 6], F32)
            xc, r_xc = P.sbuf("xc" + t, [128, Ls], F32), Res()
            xcb, r_xcb = P.sbuf("xcb" + t, [128, Ls], BF16), Res()
            ra, r_ra = P.sbuf("ra" + t, [128, Ls], F32), Res()
            ib, r_ib = P.sbuf("ib" + t, [128, Ls], F32), Res()
            ta, r_ta = P.sbuf("ta" + t, [128, Ls], F32), Res()
            hs = [(P.sbuf(f"h{d}" + t, [128, Ls], F32), Res()) for d in range(2)]
            W = min(512, Ls)
            for d in range(2):
                xr, r_xr = xr_r.next()
                P.memset(P.pool, xr[:, 0:3], 0.0, [r_xr])
                P.memset(P.pool, xr[:, Ls + 3:Ls + 6], 0.0, [r_xr])
                self.ld_u_own(xr, OFF_X, isctx, Ls, 3, [r_xr], qn="act")
                left = 3 if d == 0 else 0
                for jj in range(4):
                    s0 = 3 + jj - left
                    wj = self.own(l, 13 + d * 4 + jj)
                    if jj == 0:
                        P.ts(P.dve, xc[:, :], xr[:, s0:s0 + Ls], wj, None, ALU.mult, None, [r_xr, self.r_const], [r_xc])
                    else:
                        P.stt(xc[:, :], xr[:, s0:s0 + Ls], wj, xc[:, :], ALU.mult, ALU.add, [r_xr, r_xc, self.r_const], [r_xc])
                P.copy(P.pool, xcb[:, :], xc[:, :], [r_xc], [r_xcb])
                for ti in range(Ls // W):
                    cs = slice(ti * W, (ti + 1) * W)
                    pa, r_pa = self.pb[(2 * ti) % 4]
                    px, r_px = self.pb[(2 * ti + 1) % 4]
                    P.mm(pa[:, :W], wt[:, d * 2 + 0, :], xcb[:, cs], True, True, [r_wt, r_xcb], [r_pa])
                    P.mm(px[:, :W], wt[:, d * 2 + 1, :], xcb[:, cs], True, True, [r_wt, r_xcb], [r_px])
                    P.actv(ra[:, cs], pa[:, :W], AF.Sigmoid, [r_pa, self.r_const], [r_ra], bias=self.own(l, 21 + d))
                    P.actv(ib[:, cs], px[:, :W], AF.Sigmoid, [r_px, self.r_const], [r_ib], bias=self.own(l, 23 + d))
                P.actv(ra[:, :], ra[:, :], AF.Exp, [r_ra, r_c8], [r_ra], scale=c8[:, d:d + 1])
                P.tt(P.pool, ta[:, :], ra[:, :], ra[:, :], ALU.mult, [r_ra], [r_ta])
                P.ts(P.pool, ta[:, :], ta[:, :], -1.0, 1.0, ALU.mult, ALU.add, [r_ta], [r_ta])
                P.actv(ta[:, :], ta[:, :], AF.Sqrt, [r_ta], [r_ta])
                P.tt(P.dve, ib[:, :], ib[:, :], xc[:, :], ALU.mult, [r_ib, r_xc], [r_ib])
                P.tt(P.dve, ib[:, :], ib[:, :], ta[:, :], ALU.mult, [r_ib, r_ta], [r_ib])
                if not isctx:
                    f0 = Ls - 1 if d == 1 else 0
                    P.stt(ib[:, f0:f0 + 1], ra[:, f0:f0 + 1], self.lru_h0[:, d:d + 1], ib[:, f0:f0 + 1],
                          ALU.mult, ALU.add, [r_ra, r_ib, self.r_h0], [r_ib])
                h, r_h = hs[d]
                if d == 0:
                    P.op(P.dve, lambda e, h=h, ra=ra, ib=ib: e.tensor_tensor_scan(out=h[:, :], data0=ra[:, :], data1=ib[:, :], initial=0.0, op0=ALU.mult, op1=ALU.add), [r_ra, r_ib], [r_h])
                else:
                    P.op(P.dve, lambda e, h=h, ra=ra, ib=ib: e.tensor_tensor_scan(out=h[:, ::-1], data0=ra[:, ::-1], data1=ib[:, ::-1], initial=0.0, op0=ALU.mult, op1=ALU.add), [r_ra, r_ib], [r_h])
                if isctx:
                    f1 = Ls - 1 if d == 0 else 0
                    P.copy(P.dve, self.lru_h0[:, d:d + 1], h[:, f1:f1 + 1], [r_h], [self.r_h0])
            if not (isctx and not need_ctx):
                xr, r_xr = xr_r.next()
                self.ld_u_own(xr, OFF_G, isctx, Ls, 0, [r_xr], qn="act")
                P.actv(xr[:, 0:Ls], xr[:, 0:Ls], AF.Gelu_apprx_tanh, [r_xr], [r_xr])
                (h0_, r_h0_), (h1_, r_h1_) = hs
                P.tt(P.pool, h0_[:, :], h0_[:, :], h1_[:, :], ALU.add, [r_h0_, r_h1_], [r_h0_])
                P.tt(P.dve, h0_[:, :], h0_[:, :], xr[:, 0:Ls], ALU.mult, [r_h0_, r_xr], [r_h0_])
                for ti in range(Ls // W):
                    self.head_norm_store(h0_[:, ti * W:(ti + 1) * W], W, l, self.own(l, 12), self.yin_dsts(1, off + ti * W, W), sq_r, st_r, ob_r, 4 + ti % 4, r_h0_)
            P.pop(mk2)
        P.pop(mk)

    def phase_attn(self, l, need_ctx):
        P = self.P
        li = self.lidx[l]
        mk = P.push()
        ckv = P.sbuf("ckv", [128, 2, NT], BF16)
        r_ckv = Res()
        krope = P.sbuf("krope", [64, NT], BF16)
        r_kr = Res()
        cq = P.sbuf("cq", [128, 4, NTL], BF16)
        r_cq = Res()
        sq_r = P.ring("asq", 3, [128, 512], BF16)
        st_r = P.ring("ast", 2, [128, 512], F32)
        u2_r = P.ring("au2", 2, [128, 2, 512], F32)
        u4_r = P.ring("au4", 2, [128, 4, 512], F32)
        kr_r = P.ring("akr", 2, [64, 2, 512], F32)
        cs_r = P.ring("acs", 2, [64, 2, 512], F32)
        tk_r = P.ring("atk", 2, [64, 2, 512], F32)
        for (c0, W, j) in self.tok_tiles(True):
            u2, r_u2 = u2_r.next()
            P.ld("sp", u2[:, :, :W], self.Uap(OFF_KV, 256, c0, W).rearrange("(c p) t -> p c t", p=128), writes=[r_u2])
            st, r_st = self.rms_rstd([u2[:, c, :W] for c in range(2)], W, 256, sq_r, 7, st_r, [r_u2])
            for c in range(2):
                P.stt(ckv[:, c, c0:c0 + W], u2[:, c, :W], self.sp_col(l, SP_GKV + c), st[:, :W], ALU.mult, ALU.mult, [r_u2, r_st, self.r_const], [r_ckv])
            kr, r_krt = kr_r.next()
            P.ld("act", kr[:, 0, :W], self.Uap(OFF_KR, 64, c0, W), writes=[r_krt])
            if j == 0:
                P.ld("act", kr[:, 1, :W], self.Uap(OFF_KRR, 64, c0, W), writes=[r_krt])
                cs, r_cs = cs_r.next()
                P.ld("act", cs[:, :, :W], self.rope_cs[:, :, c0 - CTX:c0 - CTX + W], writes=[r_cs])
                tk, r_tk = tk_r.next()
                P.tt(P.pool, tk[:, :, :W], kr[:, :, :W], cs[:, :, :W], ALU.mult, [r_krt, r_cs], [r_tk])
                P.tt(P.pool, krope[:, c0:c0 + W], tk[:, 0, :W], tk[:, 1, :W], ALU.add, [r_tk], [r_kr])
            else:
                P.copy(P.pool, krope[:, c0:c0 + W], kr[:, 0, :W], [r_krt], [r_kr])
        qtiles = self.loc_tiles(need_ctx)
        for (c0, W, j) in qtiles:
            u4, r_u4 = u4_r.next()
            srcq = self.UC[OFF_Q:OFF_Q + 512, c0:c0 + W] if j == 1 else self.ULOC[OFF_Q:OFF_Q + 512, c0 - CTX:c0 - CTX + W]
            P.ld("sp", u4[:, :, :W], srcq.rearrange("(c p) t -> p c t", p=128), writes=[r_u4])
            st, r_st = self.rms_rstd([u4[:, c, :W] for c in range(4)], W, 512, sq_r, 6, st_r, [r_u4])
            for c in range(4):
                P.stt(cq[:, c, c0:c0 + W], u4[:, c, :W], self.sp_col(l, SP_GQ + c), st[:, :W], ALU.mult, ALU.mult, [r_u4, r_st, self.r_const], [r_cq])
        wkv_r = P.ring("wkv", 2, [128, 2, 256], BF16)
        wq_r = P.ring("wq", 2, [128, 4, 256], BF16)
        kn_r = P.ring("kn", 2, [128, NT], BF16)
        v_r = P.ring("vv", 2, [128, NT // 128, 128], BF16)
        qn_r = P.ring("qn", 2, [128, NTL], BF16)
        qr_r = P.ring("qr", 2, [64, NTL], BF16)
        pT_r = P.ring("pT", 4, [128, 512], BF16)
        ri_r = P.ring("ri", 2, [128, 512], F32)
        oo_r = P.ring("oo", 2, [128, 512], F32)
        ob_r = P.ring("aob", 2, [128, 512], BF16)
        WKV = self.WUKV[li].rearrange("(kc p) n -> p kc n", p=128)
        WQ = self.WUQ[li].rearrange("(kc p) n -> p kc n", p=128)
        kb = 0
        for h in range(8):
            wkv, r_wkv = wkv_r.next()
            wq, r_wq = wq_r.next()
            P.ld("sp", wkv[:], WKV[:, :, h * 256:(h + 1) * 256], writes=[r_wkv])
            P.ld("sp", wq[:], WQ[:, :, h * 256:(h + 1) * 256], writes=[r_wq])
            kn, r_kn = kn_r.next()
            vv, r_vv = v_r.next()
            qn, r_qn = qn_r.next()
            qr, r_qr = qr_r.next()
            for (c0, W, j) in self.tok_tiles(True):
                pt, r_pt = self.pb[kb % 4]
                kb += 1
                for kc in range(2):
                    P.mm(pt[:, :W], wkv[:, kc, 0:128], ckv[:, kc, c0:c0 + W], kc == 0, kc == 1, [r_wkv, r_ckv], [r_pt])
                P.copy(P.act, kn[:, c0:c0 + W], pt[:, :W], [r_pt], [r_kn])
            for (c0, W, j) in qtiles:
                pt, r_pt = self.pb[kb % 4]
                kb += 1
                for kc in range(4):
                    P.mm(pt[:, :W], wq[:, kc, 0:128], cq[:, kc, c0:c0 + W], kc == 0, kc == 3, [r_wq, r_cq], [r_pt])
                P.copy(P.dve, qn[:, c0:c0 + W], pt[:, :W], [r_pt], [r_qn])
                pt, r_pt = self.pb[kb % 4]
                kb += 1
                for kc in range(4):
                    P.mm(pt[0:64, :W], wq[:, kc, 128:192], cq[:, kc, c0:c0 + W], kc == 0, kc == 3, [r_wq, r_cq], [r_pt])
                if j == 0:
                    pt2, r_pt2 = self.pb[kb % 4]
                    kb += 1
                    for kc in range(4):
                        P.mm(pt2[0:64, :W], wq[:, kc, 192:256], cq[:, kc, c0:c0 + W], kc == 0, kc == 3, [r_wq, r_cq], [r_pt2])
                    cs, r_cs = cs_r.next()
                    P.ld("act", cs[:, :, :W], self.rope_q[:, :, c0 - CTX:c0 - CTX + W], writes=[r_cs])
                    tk, r_tk = tk_r.next()
                    P.tt(P.dve, tk[:, 0, :W], pt[0:64, :W], cs[:, 0, :W], ALU.mult, [r_pt, r_cs], [r_tk])
                    P.tt(P.dve, tk[:, 1, :W], pt2[0:64, :W], cs[:, 1, :W], ALU.mult, [r_pt2, r_cs], [r_tk])
                    P.tt(P.pool, qr[:, c0:c0 + W], tk[:, 0, :W], tk[:, 1, :W], ALU.add, [r_tk], [r_qr])
                else:
                    P.copy(P.dve, qr[:, c0:c0 + W], pt[0:64, :W], [r_pt], [r_qr])
            for kg in range(0, NT // 128, 4):
                nk4 = min(4, NT // 128 - kg)
                pt, r_pt = self.pb[kb % 4]
                kb += 1
                for q4 in range(nk4):
                    kc = kg + q4
                    for k2 in range(2):
                        P.mm(pt[:, q4 * 128:(q4 + 1) * 128], ckv[:, k2, kc * 128:(kc + 1) * 128], wkv[:, k2, 128:256], k2 == 0, k2 == 1, [r_ckv, r_wkv], [r_pt])
                P.copy(P.act, vv[:, kg:kg + nk4, :], pt[:, 0:nk4 * 128].rearrange("p (a b) -> p a b", b=128), [r_pt], [r_vv])
            for qi, (c0, W, j) in enumerate(qtiles):
                nk = 2 if j == 1 else NT // 128
                po, r_po = self.pb[4 + 2 * (qi % 2)]
                pl, r_pl = self.pb[5 + 2 * (qi % 2)]
                pend = None
                for kc in range(nk + 1):
                    if kc < nk:
                        pS, r_pS = self.pb[kc % 4]
                        P.mm(pS[:, :W], kn[:, kc * 128:(kc + 1) * 128], qn[:, c0:c0 + W], True, False, [r_kn, r_qn], [r_pS])
                        P.mm(pS[:, :W], krope[:, kc * 128:(kc + 1) * 128], qr[:, c0:c0 + W], False, True, [r_kr, r_qr], [r_pS])
                        pT, r_pT = pT_r.next()
                        P.actv(pT[:, :W], pS[:, :W], AF.Exp, [r_pS], [r_pT], scale=MLA_SCALE)
                    if pend is not None:
                        pkc, ppT, pr_pT = pend
                        P.mm(po[:, :W], vv[:, pkc, :], ppT[:, :W], pkc == 0, pkc == nk - 1, [r_vv, pr_pT], [r_po])
                        P.mm(pl[:, :W], self.ones_b[:], ppT[:, :W], pkc == 0, pkc == nk - 1, [self.r_const, pr_pT], [r_pl])
                    pend = (kc, pT, r_pT) if kc < nk else None
                ri, r_ri = ri_r.next()
                P.recip(ri[:, :W], pl[:, :W], [r_pl], [r_ri])
                oo, r_oo = oo_r.next()
                P.tt(P.dve, oo[:, :W], po[:, :W], ri[:, :W], ALU.mult, [r_po, r_ri], [r_oo])
                dsts = [lambda e, h=h, c0=c0, W=W: self.YM[h * 128:(h + 1) * 128, c0:c0 + W]]
                self.head_norm_store(oo[:, :W], W, l, self.sp_col(l, SP_HG + 8 + h), dsts, sq_r, st_r, ob_r, 4 + 2 * (qi % 2), r_oo)
        P.pop(mk)

    def phase_outproj(self, l, need_ctx):
        P = self.P
        li = self.lidx[l]
        mk = P.push()
        yc_r = P.ring("yc", 2, [128, 16, 512], BF16)
        w_r = P.ring("wo", 2, [128, 16, 512], BF16)
        mix_r = P.ring("mix", 1, [128, 16, 512], F32)
        xs_r = P.ring("oxs", 1, [128, 16, 512], F32)
        hb_r = P.ring("ohb", 1, [128, 16, 512], BF16)
        sq_r = P.ring("osq", 3, [128, 512], BF16)
        st_r = P.ring("ost", 2, [128, 512], F32)
        tmp_r = P.ring("otmp", 3, [128, 512], F32)
        YMv = self.YM.rearrange("(c p) t -> p c t", p=128)
        XTv = self.XT.rearrange("(c p) t -> p c t", p=128)
        H2v = self.H2.rearrange("(c p) t -> p c t", p=128)
        Wv = self.WOUT[li].rearrange("(kc p) n -> p kc n", p=128)
        kb = 0
        ltiles = self.loc_tiles(need_ctx)
        for (c0, W, j) in ltiles:
            yc, r_yc = yc_r.next()
            P.ld("sp", yc[:, 8:16, :W], YMv[:, :, c0:c0 + W], writes=[r_yc])
            for kind in range(2):
                def ldy(e, yc=yc, c0=c0, W=W, kind=kind):
                    win = self.YG[kind].rearrange("(x i) t -> i x t", i=128)
                    return e.dma_start(out=yc[:, kind * 4:(kind + 1) * 4, :W], in_=win[:, bass.ds(self.rv(e, "x"), NR, 2), c0:c0 + W])
                P.dma("pool", ldy, writes=[r_yc])
            xs, r_xs = xs_r.next()
            P.ld("sp", xs[:, :, :W], XTv[:, :, c0:c0 + W], writes=[r_xs])
            mix, r_mix = mix_r.next()
            for g in range(4):
                wt, r_wt = w_r.next()
                P.ld("act", wt[:], Wv[:, :, g * 512:(g + 1) * 512], writes=[r_wt])
                for m in range(4):
                    pt, r_pt = self.pb[kb % 6]
                    kb += 1
                    for kc in range(16):
                        P.mm(pt[:, :W], wt[:, kc, m * 128:(m + 1) * 128], yc[:, kc, :W], kc == 0, kc == 15, [r_wt, r_yc], [r_pt])
                    P.copy(P.act if kb % 2 == 0 else P.dve, mix[:, g * 4 + m, :W], pt[:, :W], [r_pt], [r_mix])
            st, r_st = self.rms_rstd([mix[:, c, :W] for c in range(16)], W, D, sq_r, 7, st_r, [r_mix])
            for c in range(16):
                tmp, r_tmp = tmp_r.next()
                P.tt(P.pool, tmp[:, :W], mix[:, c, :W], st[:, :W], ALU.mult, [r_mix, r_st], [r_tmp])
                P.stt(xs[:, c, :W], tmp[:, :W], self.mod(2, c, j), xs[:, c, :W], ALU.mult, ALU.add, [r_tmp, r_xs, self.r_mods], [r_xs])
            P.ld("sp", XTv[:, :, c0:c0 + W], xs[:, :, :W], reads=[r_xs])
            st, r_st = self.rms_rstd([xs[:, c, :W] for c in range(16)], W, D, sq_r, 6, st_r, [r_xs])
            hb, r_hb = hb_r.next()
            for c in range(16):
                tmp, r_tmp = tmp_r.next()
                P.tt(P.dve, tmp[:, :W], xs[:, c, :W], st[:, :W], ALU.mult, [r_xs, r_st], [r_tmp])
                P.actv(hb[:, c, :W], tmp[:, :W], AF.Identity, [r_tmp, self.r_mods], [r_hb], bias=self.mod(4, c, j), scale=self.mod(3, c, j))
            P.ld("sp", H2v[:, :, c0:c0 + W], hb[:, :, :W], reads=[r_hb])
            HBv = self.HB.rearrange("(c p) k -> p c k", p=128)
            if c0 == CTX:
                P.ld("act", HBv[:, :, 0:1], hb[:, :, 0:1], reads=[r_hb], slow=True)
            if c0 + W == NTL:
                P.ld("act", HBv[:, :, 1:2], hb[:, :, W - 1:W], reads=[r_hb], slow=True)
        P.pop(mk)
        self.allgather([(self.HB, self.HG)])
        P.ld("sp", self.HGP[D:(NR + 1) * D, :], self.HG[:, :])
        P.barrier()

    def phase_ffn(self, l, need_ctx):
        P = self.P
        li = self.lidx[l]
        mk = P.push()
        OT = 342
        tiles = []
        if need_ctx:
            tiles.append((0, CTX, 0, CTX, 1))
        for lo in range(0, SL, OT):
            tiles.append((CTX, SL, lo, min(SL, lo + OT), 0))
        groups = [tiles[i:i + 2] for i in range(0, len(tiles), 2)]
        WT = OT + 2
        h2_r = P.ring("fh2", 2, [128, 16, WT], BF16)
        acts = [(P.sbuf(f"fact{t}", [128, 44, OT], BF16), [Res() for _ in range(44)]) for t in range(2)]
        wu_r = P.ring("fwu", 2, [128, 2, 16, 256], BF16)
        wd_r = P.ring("fwd", 2, [128, 44, 128], BF16)
        f_r = P.ring("ff", 1, [128, 16, OT], F32)
        xs_r = P.ring("fxs", 1, [128, 16, OT], F32)
        gc_r = P.ring("fgc", 2, [128, OT], F32)
        vc_r = P.ring("fvc", 2, [128, OT], F32)
        sq_r = P.ring("fsq", 2, [128, 512], BF16)
        st_r = P.ring("fst_", 1, [128, 512], F32)
        tmp_r = P.ring("ftmp", 2, [128, OT], F32)
        XTv = self.XT.rearrange("(c p) t -> p c t", p=128)
        H2v = self.H2.rearrange("(c p) t -> p c t", p=128)
        HGPv = self.HGP.rearrange("(c p) k -> p c k", p=128)
        WU = self.WUP[li].rearrange("(kc p) n -> p kc n", p=128)
        WD = self.WDN[li].rearrange("(kc p) n -> p kc n", p=128)
        for grp in groups:
            h2s = []
            for (off, Ls, lo, hi, j) in grp:
                Wo = hi - lo
                Wt = Wo + 2
                h2, r_h2 = h2_r.next()
                a, b = lo - 1, hi + 1
                ca, cb = 0, Wt
                if a < 0:
                    if j == 1:
                        P.memset(P.pool, h2[:, :, 0:1], 0.0, [r_h2])
                    else:
                        P.dma("act", lambda e, h2=h2: e.dma_start(out=h2[:, :, 0:1], in_=HGPv[:, 0:64, 1:2][:, bass.ds(self.rv(e, "16"), 16), :], allow_slow_non_contiguous=True), writes=[r_h2])
                    a, ca = 0, 1
                if b > Ls:
                    if j == 1:
                        P.memset(P.pool, h2[:, :, Wt - 1:Wt], 0.0, [r_h2])
                    else:
                        P.dma("act", lambda e, h2=h2, Wt=Wt: e.dma_start(out=h2[:, :, Wt - 1:Wt], in_=HGPv[:, 32:96, 0:1][:, bass.ds(self.rv(e, "16"), 16), :], allow_slow_non_contiguous=True), writes=[r_h2])
                    b, cb = Ls, Wt - 1
                P.ld("sp", h2[:, :, ca:cb], H2v[:, :, off + a:off + b], writes=[r_h2])
                h2s.append((h2, r_h2, Wo, Wt))
            for i in range(44):
                if i % 2 == 0:
                    wu, r_wu = wu_r.next()
                    P.ld("act", wu[:, 0, :, :], WU[:, :, i * 128:i * 128 + 256], writes=[r_wu])
                    P.ld("sp", wu[:, 1, :, :], WU[:, :, DFF + i * 128:DFF + i * 128 + 256], writes=[r_wu])
                s0 = (i % 2) * 128
                for t, (h2, r_h2, Wo, Wt) in enumerate(h2s):
                    pg, r_pg = self.pb[(i % 2) * 4 + 2 * t]
                    pv, r_pv = self.pb[(i % 2) * 4 + 2 * t + 1]
                    for kc in range(16):
                        P.mm(pg[:, :Wt], wu[:, 0, kc, s0:s0 + 128], h2[:, kc, :Wt], kc == 0, kc == 15, [r_wu, r_h2], [r_pg])
                    for kc in range(16):
                        P.mm(pv[:, :Wt], wu[:, 1, kc, s0:s0 + 128], h2[:, kc, :Wt], kc == 0, kc == 15, [r_wu, r_h2], [r_pv])
                for t, (h2, r_h2, Wo, Wt) in enumerate(h2s):
                    pg, r_pg = self.pb[(i % 2) * 4 + 2 * t]
                    pv, r_pv = self.pb[(i % 2) * 4 + 2 * t + 1]
                    act, r_act = acts[t]
                    gc, r_gc = gc_r.next()
                    vc, r_vc = vc_r.next()
                    for (pp, r_pp, oc, r_oc, ci) in ((pg, r_pg, gc, r_gc, i), (pv, r_pv, vc, r_vc, 44 + i)):
                        P.ts(P.dve, oc[:, :Wo], pp[:, 0:Wo], self.sp_col(l, SP_FC + 0 * 88 + ci), None, ALU.mult, None, [r_pp, self.r_const], [r_oc])
                        P.stt(oc[:, :Wo], pp[:, 1:Wo + 1], self.sp_col(l, SP_FC + 1 * 88 + ci), oc[:, :Wo], ALU.mult, ALU.add, [r_pp, r_oc, self.r_const], [r_oc])
                        P.stt(oc[:, :Wo], pp[:, 2:Wo + 2], self.sp_col(l, SP_FC + 2 * 88 + ci), oc[:, :Wo], ALU.mult, ALU.add, [r_pp, r_oc, self.r_const], [r_oc])
                    P.actv(gc[:, :Wo], gc[:, :Wo], AF.Gelu_apprx_tanh, [r_gc], [r_gc])
                    P.tt(P.pool, act[:, i, :Wo], gc[:, :Wo], vc[:, :Wo], ALU.mult, [r_gc, r_vc], [r_act[i]])
            for t, (off, Ls, lo, hi, j) in enumerate(grp):
                Wo = hi - lo
                act, r_act = acts[t]
                f, r_f = f_r.next()
                for m in range(16):
                    wd, r_wd = wd_r.next()
                    P.ld("act" if m % 2 == 0 else "sp", wd[:], WD[:, :, m * 128:(m + 1) * 128], writes=[r_wd])
                    pd, r_pd = self.pb[m % 2]
                    for kc in range(44):
                        P.mm(pd[:, :Wo], wd[:, kc, :], act[:, kc, :Wo], kc == 0, kc == 43, [r_wd, r_act[kc]], [r_pd])
                    P.copy(P.act if m % 2 == 0 else P.dve, f[:, m, :Wo], pd[:, :Wo], [r_pd], [r_f])
                st, r_st = self.rms_rstd([f[:, c, :Wo] for c in range(16)], Wo, D, sq_r, 2, st_r, [r_f])
                xs, r_xs = xs_r.next()
                P.ld("sp", xs[:, :, :Wo], XTv[:, :, off + lo:off + hi], writes=[r_xs])
                for c in range(16):
                    tmp, r_tmp = tmp_r.next()
                    P.tt(P.pool, tmp[:, :Wo], f[:, c, :Wo], st[:, :Wo], ALU.mult, [r_f, r_st], [r_tmp])
                    P.stt(xs[:, c, :Wo], tmp[:, :Wo], self.mod(5, c, j), xs[:, c, :Wo], ALU.mult, ALU.add, [r_tmp, r_xs, self.r_mods], [r_xs])
                P.ld("sp", XTv[:, :, off + lo:off + hi], xs[:, :, :Wo], reads=[r_xs])
        P.pop(mk)


def make_inmaps(inp, n_cores=8, layers=None):
    c = _consts()
    layers = list(range(DEPTH)) if layers is None else list(layers)
    shared = {k: np.ascontiguousarray(np.asarray(inp[k], dtype=np.float32)[layers]) for k in (
        "w_in", "hy_w1", "hy_w2", "mla_wuq", "mla_wukv", "w_out", "ffn_up", "ffn_down")}
    adaw = np.asarray(inp["ada_w"], dtype=np.float32)
    shared["smallp"] = _pack_small({k: np.asarray(v, dtype=np.float32) for k, v in inp.items()}).reshape(128, DEPTH * NSP)
    shared.update({k: v for k, v in c.items() if not k.startswith("decay")})
    spf = shared["smallp"].reshape(128, DEPTH, NSP)
    own_map = []
    for p_ in range(3):
        for j in range(3):
            own_map.append(SP_HYC + j * 12 + p_ * 4)
    own_map += [SP_HYB + 0, SP_HYB + 4, SP_HG, SP_HG + 4]
    for d in range(2):
        for j in range(4):
            own_map.append(SP_LC + d * 16 + j * 4)
    own_map += [SP_BA, SP_BA + 4, SP_BX, SP_BX + 4, SP_LAM, SP_LAM + 4]
    w3 = np.asarray(inp["hy_w3"], dtype=np.float32)[layers].reshape(len(layers), 64, 4, 4, 128)
    lwa = np.asarray(inp["lru_wa"], dtype=np.float32)[layers]
    lwx = np.asarray(inp["lru_wx"], dtype=np.float32)[layers]
    maps = []
    for core in range(n_cores):
        b = core // NR
        r = core % NR
        m = dict(shared)
        m["x"] = np.ascontiguousarray(np.asarray(inp["x"][b, r * SL:(r + 1) * SL], dtype=np.float32))
        m["ownp"] = np.ascontiguousarray(spf[:, :, [cc + r for cc in own_map]].reshape(128, DEPTH * NOWN))
        m["hy_w3_own"] = np.ascontiguousarray(w3[:, :, :, r, :].reshape(len(layers), 64, 512))
        m["lru_wa_own"] = np.ascontiguousarray(lwa[:, :, r])
        m["lru_wx_own"] = np.ascontiguousarray(lwx[:, :, r])
        m["ada_w_own"] = np.ascontiguousarray(adaw[layers][:, :, r * (6 * D // NR):(r + 1) * (6 * D // NR)])
        m["rope_q"] = np.ascontiguousarray(c["rope_cs"][:, :, r * SL:(r + 1) * SL])
        m["decayL"] = np.ascontiguousarray(c["decayL"][:, r * 128:(r + 1) * 128])
        m["decayC"] = np.ascontiguousarray(c["decayC"][:, r * 128:(r + 1) * 128])
        m["ctx"] = np.ascontiguousarray(np.asarray(inp["ctx"][b], dtype=np.float32))
        cv = np.stack([np.asarray(inp["c"][b], dtype=np.float32), np.asarray(inp["c_ctx"], dtype=np.float32)], axis=-1)
        m["cvec"] = np.ascontiguousarray(cv.reshape(16, 128, 2).transpose(1, 0, 2).reshape(128, 32))
        maps.append(m)
    return maps


_NC_CACHE = {}


def kernel(**inputs):
    if "nc" not in _NC_CACHE:
        _NC_CACHE["nc"] = Builder(range(DEPTH)).build()
    nc = _NC_CACHE["nc"]
    maps = make_inmaps(inputs)
    res = run_bass_kernel_spmd(nc, maps, core_ids=list(range(8)))
    out = np.stack([np.concatenate([np.asarray(res.results[b * NR + r]["out"]) for r in range(NR)], axis=0) for b in range(2)], axis=0)
    return out.astype(np.float32)
```

```python
import math
import numpy as np
import ml_dtypes
import concourse.bass as bass
import concourse.mybir as mybir
from concourse.bass_utils import run_bass_kernel_spmd

F32 = mybir.dt.float32
BF16 = mybir.dt.bfloat16
AF = mybir.ActivationFunctionType
ALU = mybir.AluOpType

D = 2048
SEQ = 4096
CTX = 256
NT = SEQ + CTX
DEPTH = 4
NR = 4
SL = SEQ // NR
NTL = CTX + SL
GROUPS = [[0, 1, 2, 3], [4, 5, 6, 7]]
NOWN = 27
UCH = 14
DIN = 3392
DINX = 3456
OFF_HY, OFF_G, OFF_Q, OFF_X, OFF_KV, OFF_KR, OFF_KRR = 0, 1536, 2048, 2560, 3072, 3328, 3392
DFF = 5632
EPS = 1e-6
MLA_SCALE = 192 ** -0.5
MAGIC = 12582912.0

SP_ADAB, SP_NG, SP_HYC, SP_HYB, SP_LC, SP_BA, SP_BX, SP_LAM = 0, 96, 160, 196, 204, 236, 244, 252
SP_GQ, SP_GKV, SP_HG, SP_FC, SP_B1, SP_B2, NSP = 260, 264, 266, 282, 546, 547, 548


class Res:
    __slots__ = ("name", "w", "r")

    def __init__(self, name=""):
        self.name = name
        self.w = None
        self.r = {}


class EngQ:
    def __init__(self, name, eng, inorder=False):
        self.name = name
        self.eng = eng
        self.inorder = inorder
        self.sem = None
        self.step = 1
        self.count = 0
        self.waited = {}
        self.thunks = []


class DmaSlot:
    def __init__(self, sem, name):
        self.sem = sem
        self.name = name
        self.count = 0
        self.step = 16
        self.inorder = False


class Ring:
    def __init__(self, items):
        self.items = items
        self.i = 0

    def next(self):
        it = self.items[self.i % len(self.items)]
        self.i += 1
        return it


class Prog:
    def __init__(self, nc, n_dma_slots=20, same_engine_sync=True):
        self.nc = nc
        self.same_engine_sync = same_engine_sync
        self.ctx = []
        self.pe = EngQ("pe", nc.tensor, inorder=True)
        self.dve = EngQ("dve", nc.vector)
        self.act = EngQ("act", nc.scalar)
        self.pool = EngQ("pool", nc.gpsimd)
        self.sp = EngQ("sp", nc.sync)
        self.engs = [self.pe, self.dve, self.act, self.pool, self.sp]
        for e in self.engs:
            e.sem = self._sem("s_" + e.name)
        self.slots = {}
        for qn in ("sp", "act", "pool"):
            self.slots[qn] = [DmaSlot(self._sem(f"d_{qn}{i}"), f"d_{qn}{i}") for i in range(n_dma_slots)]
        self.slot_rr = {"sp": 0, "act": 0, "pool": 0}
        self.n_inst = 0
        self.uid = 0

    def _sem(self, name):
        cm = self.nc.semaphore(name)
        s = cm.__enter__()
        self.ctx.append(cm)
        return s

    def push(self):
        return len(self.ctx)

    def pop(self, mark):
        self.barrier()
        while len(self.ctx) > mark:
            self.ctx.pop().__exit__(None, None, None)

    def sbuf(self, name, shape, dtype):
        self.uid += 1
        cm = self.nc.sbuf_tensor(f"{name}_{self.uid}", list(shape), dtype)
        t = cm.__enter__()
        self.ctx.append(cm)
        return t

    def psum(self, name, shape, dtype=F32):
        cm = self.nc.psum_tensor(name, list(shape), dtype)
        t = cm.__enter__()
        self.ctx.append(cm)
        return t

    def ring(self, name, n, shape, dtype):
        return Ring([(self.sbuf(f"{name}{i}", shape, dtype), Res(f"{name}{i}")) for i in range(n)])

    def _deps(self, reads, writes):
        deps = []
        for r in reads:
            if r.w is not None:
                deps.append(r.w)
        for w in writes:
            if w.w is not None:
                deps.append(w.w)
            deps.extend(w.r.values())
        return deps

    def _emit_waits(self, q, deps):
        need = {}
        for (src, c) in deps:
            if src is q and (q.inorder or not self.same_engine_sync):
                continue
            if q.waited.get(id(src), 0) >= c:
                continue
            if need.get(id(src), (None, 0))[1] < c:
                need[id(src)] = (src, c)
        for src, c in need.values():
            q.waited[id(src)] = c
            sem, val = src.sem, c * src.step
            q.thunks.append(lambda e, sem=sem, val=val: e.wait_ge(sem, val))

    def _mark(self, src, c, reads, writes):
        for r in reads:
            old = r.r.get(id(src))
            if old is None or old[1] < c:
                r.r[id(src)] = (src, c)
        for w in writes:
            w.w = (src, c)
            w.r = {}

    def op(self, q, fn, reads=(), writes=()):
        self._emit_waits(q, self._deps(reads, writes))
        q.count += 1
        sem = q.sem
        q.thunks.append(lambda e, fn=fn, sem=sem: fn(e).then_inc(sem, 1))
        self._mark(q, q.count, reads, writes)
        self.n_inst += 1

    def dma(self, qname, fn, reads=(), writes=()):
        q = {"sp": self.sp, "act": self.act, "pool": self.pool}[qname]
        slots = self.slots[qname]
        i = self.slot_rr[qname]
        self.slot_rr[qname] = (i + 1) % len(slots)
        slot = slots[i]
        deps = self._deps(reads, writes)
        if slot.count > 0:
            deps.append((slot, slot.count))
        self._emit_waits(q, deps)
        slot.count += 1
        sem = slot.sem
        q.thunks.append(lambda e, fn=fn, sem=sem: fn(e).then_inc(sem, 16))
        self._mark(slot, slot.count, reads, writes)
        self.n_inst += 1

    def barrier(self):
        srcs = list(self.engs)
        for qn in self.slots:
            srcs.extend(self.slots[qn])
        deps = [(s, s.count) for s in srcs if s.count > 0]
        for q in self.engs:
            need = [(s, c) for (s, c) in deps]
            keep_sync = self.same_engine_sync
            self.same_engine_sync = True
            self._emit_waits(q, need)
            self.same_engine_sync = keep_sync

    def mm(self, out, lhsT, rhs, start, stop, reads, writes):
        self.op(self.pe, lambda e: e.matmul(out, lhsT=lhsT, rhs=rhs, start=start, stop=stop), reads, writes)

    def tr(self, out, in_, ident, reads, writes):
        self.op(self.pe, lambda e: e.transpose(out, in_, ident), reads, writes)

    def tt(self, q, out, in0, in1, op, reads, writes):
        self.op(q, lambda e: e.tensor_tensor(out=out, in0=in0, in1=in1, op=op), reads, writes)

    def ts(self, q, out, in0, s1, s2, op0, op1, reads, writes):
        if op1 is None:
            self.op(q, lambda e: e.tensor_scalar(out=out, in0=in0, scalar1=s1, scalar2=None, op0=op0), reads, writes)
        else:
            self.op(q, lambda e: e.tensor_scalar(out=out, in0=in0, scalar1=s1, scalar2=s2, op0=op0, op1=op1), reads, writes)

    def stt(self, out, in0, scalar, in1, op0, op1, reads, writes):
        self.op(self.dve, lambda e: e.scalar_tensor_tensor(out=out, in0=in0, scalar=scalar, in1=in1, op0=op0, op1=op1), reads, writes)

    def actv(self, out, in_, func, reads, writes, bias=None, scale=None):
        kw = {}
        if bias is not None:
            kw["bias"] = bias
        if scale is not None:
            kw["scale"] = scale
        self.op(self.act, lambda e: e.activation(out=out, in_=in_, func=func, **kw), reads, writes)

    def copy(self, q, out, in_, reads, writes):
        if q is self.act:
            self.op(q, lambda e: e.activation(out=out, in_=in_, func=AF.Copy), reads, writes)
        else:
            self.op(q, lambda e: e.tensor_copy(out=out, in_=in_), reads, writes)

    def recip(self, out, in_, reads, writes):
        self.op(self.dve, lambda e: e.reciprocal(out=out, in_=in_), reads, writes)

    def memset(self, q, ap, val, writes):
        self.op(q, lambda e: e.memset(ap, val), (), writes)

    def ld(self, qname, out, in_, reads=(), writes=(), slow=False):
        if slow:
            self.dma(qname, lambda e: e.dma_start(out=out, in_=in_, allow_slow_non_contiguous=True), reads, writes)
        else:
            self.dma(qname, lambda e: e.dma_start(out=out, in_=in_), reads, writes)

    def finish(self, final_res):
        deps = [r.w for r in final_res if r.w is not None]
        self._emit_waits(self.sp, deps)
        self.barrier()
        nc = self.nc
        with nc.Block() as block:
            @block.sync
            def _(e):
                for t in self.sp.thunks:
                    t(e)

            @block.tensor
            def _(e):
                for t in self.pe.thunks:
                    t(e)

            @block.vector
            def _(e):
                for t in self.dve.thunks:
                    t(e)

            @block.scalar
            def _(e):
                for t in self.act.thunks:
                    t(e)

            @block.gpsimd
            def _(e):
                for t in self.pool.thunks:
                    t(e)
        while self.ctx:
            self.ctx.pop().__exit__(None, None, None)


_CONST_CACHE = {}


def _bf(a):
    return np.ascontiguousarray(a.astype(np.float32)).astype(ml_dtypes.bfloat16)


def _dft_tables(L):
    N = 2 * L
    H = L // 2
    nfcA = (H + 1 + 127) // 128
    nfcB = H // 128
    nfc = nfcA + nfcB
    a = np.arange(nfcA * 128, dtype=np.int64)
    t = np.arange(L, dtype=np.int64)
    ang = 2.0 * np.pi * ((t[:, None] * a[None, :]) % N).astype(np.float64) / N
    validA = (a <= H).astype(np.float64)
    C = np.cos(ang) * validA
    S = np.sin(ang) * validA
    ntc = L // 128
    Fc = C.reshape(ntc, 128, nfcA, 128).transpose(2, 1, 0, 3)
    Fs = S.reshape(ntc, 128, nfcA, 128).transpose(2, 1, 0, 3)
    w = np.where(a == 0, 1.0, 2.0) * validA / N
    IcM = C.T * w[:, None]
    IsM = S.T * w[:, None]
    TW = min(512, L)
    ntt = L // TW
    Ic = IcM.reshape(nfcA, 128, ntt, TW).transpose(2, 1, 0, 3)
    Is = IsM.reshape(nfcA, 128, ntt, TW).transpose(2, 1, 0, 3)
    return _bf(Fc), _bf(Fs), _bf(Ic), _bf(Is)


def _hy_consts(L):
    t = (np.arange(L, dtype=np.float32) / np.float32(L)).astype(np.float32)
    ang = (2.0 * math.pi * t[:, None] * np.arange(1, 17, dtype=np.float32)).astype(np.float32)
    feats = np.concatenate([t[:, None], np.sin(ang), np.cos(ang)], axis=-1).astype(np.float32)
    deltas = np.abs(np.linspace(math.log(1e-2) / 1.5, math.log(1e-2) / 0.3, 512, dtype=np.float32))
    decay = np.exp(-t[:, None] * deltas[None, :]).astype(np.float32)
    return np.ascontiguousarray(feats.T), decay


def _rope_tables():
    row = np.repeat(np.arange(SEQ // 64, dtype=np.float32), 64)
    col = np.tile(np.arange(64, dtype=np.float32), SEQ // 64)
    inv = (10000.0 ** (-np.arange(16, dtype=np.float32) / 16)).astype(np.float32)
    ang = np.concatenate([row[:, None] * inv, col[:, None] * inv], axis=-1).astype(np.float32)
    cs = np.zeros((64, 2, SEQ), np.float32)
    cs[:32, 0] = np.cos(ang).T
    cs[32:, 0] = np.cos(ang).T
    cs[:32, 1] = np.sin(ang).T
    cs[32:, 1] = np.sin(ang).T
    return cs


def _consts():
    if _CONST_CACHE:
        return _CONST_CACHE
    c = {}
    c["ident_f"] = np.eye(128, dtype=np.float32)
    c["sgn"] = np.where(np.arange(128) % 2 == 0, 1.0, -1.0).astype(np.float32).reshape(128, 1)
    c["rope_cs"] = _rope_tables()
    for L, tag in ((SEQ, "L"), (CTX, "C")):
        Fc, Fs, Ic, Is = _dft_tables(L)
        c["Fc" + tag], c["Fs" + tag], c["Ic" + tag], c["Is" + tag] = Fc, Fs, Ic, Is
        f, d = _hy_consts(L)
        c["feats" + tag], c["decay" + tag] = f, d
    _CONST_CACHE.update(c)
    return _CONST_CACHE


def _pack_small(inp):
    sp = np.zeros((128, DEPTH, NSP), np.float32)

    def put(col, arr):
        n = arr.shape[1] // 128
        sp[:, :, col:col + n] = arr.reshape(DEPTH, n, 128).transpose(2, 0, 1)

    put(SP_ADAB, inp["ada_b"])
    put(SP_NG, inp["norm_g"].reshape(DEPTH, 4 * D))
    put(SP_HYC, inp["hy_conv"].reshape(DEPTH, 3 * 1536))
    put(SP_HYB, inp["hy_bias"].reshape(DEPTH, 2 * 512))
    put(SP_LC, inp["lru_conv"].reshape(DEPTH, 2 * 4 * 512))
    put(SP_BA, inp["lru_ba"].reshape(DEPTH, 1024))
    put(SP_BX, inp["lru_bx"].reshape(DEPTH, 1024))
    put(SP_LAM, inp["lru_lam"].reshape(DEPTH, 1024))
    put(SP_GQ, inp["mla_gq"])
    put(SP_GKV, inp["mla_gkv"])
    put(SP_HG, inp["head_g"])
    put(SP_FC, inp["ffn_conv"].reshape(DEPTH, 3 * 2 * DFF))
    sp[:64, :, SP_B1] = inp["hy_b1"].T
    sp[:64, :, SP_B2] = inp["hy_b2"].T
    return sp


class Builder:
    def __init__(self, layers, dbg=(), phases=None, same_engine_sync=True):
        self.layers = list(layers)
        self.lidx = {l: i for i, l in enumerate(self.layers)}
        NL = len(self.layers)
        self.dbg = set(dbg)
        self.phases = phases
        nc = self.nc = bass.Bass("TRN2", target_bir_lowering=False)
        self.I = {}
        din = self.din
        self.x_in = din("x", [SL, D])
        self.ctx_in = din("ctx", [CTX, D])
        self.cvec = din("cvec", [128, 32])
        self.smallp_in = din("smallp", [128, DEPTH * NSP])
        self.ownp_in = din("ownp", [128, DEPTH * NOWN])
        self.w3own = din("hy_w3_own", [NL, 64, 512])
        self.lwa_own = din("lru_wa_own", [NL, 2, 128, 128])
        self.lwx_own = din("lru_wx_own", [NL, 2, 128, 128])
        self.ada_w = din("ada_w_own", [NL, D, 6 * D // NR])
        self.w_in = din("w_in", [NL, D, DIN])
        self.hy_w1 = din("hy_w1", [NL, 33, 64])
        self.hy_w2 = din("hy_w2", [NL, 64, 64])
        self.mla_wuq = din("mla_wuq", [NL, 512, 1536])
        self.mla_wukv = din("mla_wukv", [NL, 256, 2048])
        self.w_out = din("w_out", [NL, D, D])
        self.ffn_up = din("ffn_up", [NL, D, 2 * DFF])
        self.ffn_down = din("ffn_down", [NL, DFF, D])
        self.ident_in = din("ident_f", [128, 128])
        self.sgn_in = din("sgn", [128, 1])
        self.rope_cs = din("rope_cs", [64, 2, SEQ])
        self.rope_q = din("rope_q", [64, 2, SL])
        self.tab = {}
        for L, tag in ((SEQ, "L"), (CTX, "C")):
            nfcA = (L // 2 + 1 + 127) // 128
            nfcB = (L // 2) // 128
            nfc = nfcA + nfcB
            ntc = L // 128
            TW = min(512, L)
            self.tab[tag] = dict(
                L=L, nfc=nfc, nfcA=nfcA, nfcB=nfcB, ntc=ntc, TW=TW, ntt=L // TW,
                Fc=din("Fc" + tag, [nfcA, 128, ntc, 128], BF16), Fs=din("Fs" + tag, [nfcA, 128, ntc, 128], BF16),
                Ic=din("Ic" + tag, [L // TW, 128, nfcA, TW], BF16), Is=din("Is" + tag, [L // TW, 128, nfcA, TW], BF16),
                feats=din("feats" + tag, [33, L]), decay=din("decay" + tag, [L, 128]),
                S=self.dscr("S" + tag, [NL, nfc * 128, 512], F32),
            )
        self.out = nc.dram_tensor("out", [SL, D], F32, kind="ExternalOutput").ap()
        ds = self.dscr
        self.XT = ds("XT", [D, NTL], F32)
        self.UC = ds("UC", [DINX, CTX], F32)
        self.ULOC = ds("ULOC", [UCH * 256, SL], F32)
        self.UG = ds("UG", [UCH * NR * 256, SL], F32)
        self.MIN = ds("MIN", [128, 48], F32)
        self.MG = ds("MG", [NR * 128, 48], F32)
        self.HB = ds("HB", [D, 2], BF16)
        self.HG = ds("HG", [NR * D, 2], BF16)
        self.HGP = ds("HGP", [(NR + 2) * D, 2], BF16)
        self.ZT = ds("ZT", [128, NT], F32)
        self.PT = ds("PT", [256, NT], F32)
        self.YIN = [ds(f"YIN{k}", [NR * 128, NTL], BF16) for k in range(2)]
        self.YG = [ds(f"YG{k}", [2 * NR * 256, NTL], BF16) for k in range(2)]
        self.YM = ds("YM", [1024, NTL], BF16)
        self.H2 = ds("H2", [D, NTL], BF16)
        self.WIN = ds("WIN", [NL, D, DINX], BF16)
        self.WUQ = ds("WUQ", [NL, 512, 2048], BF16)
        self.WUKV = ds("WUKV", [NL, 256, 2048], BF16)
        self.WOUT = ds("WOUT", [NL, D, D], BF16)
        self.WUP = ds("WUP", [NL, D, 2 * DFF], BF16)
        self.WDN = ds("WDN", [NL, DFF, D], BF16)
        self.P = Prog(nc, same_engine_sync=same_engine_sync)
        self.r_out = Res("out")
        self._rank = {}

    def din(self, name, shape, dt=F32):
        self.I[name] = (list(shape), dt)
        return self.nc.dram_tensor(name, list(shape), dt, kind="ExternalInput").ap()

    def dscr(self, name, shape, dt):
        kind = "ExternalOutput" if name in self.dbg else "Internal"
        return self.nc.dram_tensor(name, list(shape), dt, kind=kind).ap()

    def rank(self, e):
        key = id(e)
        if key not in self._rank:
            self._rank[key] = e.snap(e.partition_id() % NR)
        return self._rank[key]

    def rv(self, e, kind):
        key = (kind, id(e))
        if key not in self._rank:
            r = self.rank(e)
            expr = {"x": (r // 2) * 8 + (r % 2), "128": r * 128, "16": r * 16}[kind]
            self._rank[key] = e.snap(expr)
        return self._rank[key]

    def Uap(self, row0, nrows, c0, W):
        if c0 < CTX:
            return self.UC[row0:row0 + nrows, c0:c0 + W]
        t0 = c0 - CTX
        s_, tl = t0 // SL, t0 % SL
        assert tl + W <= SL
        k, rr = row0 // 256, row0 % 256
        assert rr + nrows <= 256
        base = (k * NR + s_) * 256 + rr
        return self.UG[base:base + nrows, tl:tl + W]

    def Ulat(self, row0):
        k, rr = row0 // 256, row0 % 256
        return self.UG.rearrange("(k s r) t -> k r s t", s=NR, r=256)[k][rr:rr + 128, :, :]

    def loc_tiles(self, need_ctx=True):
        tiles = [(0, CTX, 1)] if need_ctx else []
        tiles += [(CTX + i * 512, 512, 0) for i in range(SL // 512)]
        return tiles

    def allgather(self, pairs):
        P = self.P
        P.barrier()
        for (src, dst) in pairs:
            P.op(P.pool, lambda e, src=src, dst=dst: e.collective_compute("AllGather", ALU.bypass, replica_groups=GROUPS, ins=[src.opt()], outs=[dst.opt()]), (), ())
        P.barrier()

    def on(self, ph):
        return self.phases is None or ph in self.phases

    def build(self):
        P = self.P
        self.setup()
        for l in self.layers:
            last = (l == DEPTH - 1)
            if self.on("M"):
                self.phase_mod(l)
            if self.on("A"):
                self.phase_inproj(l)
            if self.on("H"):
                if not last:
                    self.phase_hyena(l, "C")
                self.phase_hyena(l, "L")
            if self.on("R"):
                self.phase_lru(l, not last)
                self.allgather([(self.YIN[k][q * 256:(q + 1) * 256, :], self.YG[k][q * 1024:(q + 1) * 1024, :]) for k in range(2) for q in range(2)])
            if self.on("T"):
                self.phase_attn(l, not last)
            if self.on("O"):
                self.phase_outproj(l, not last)
            if self.on("F"):
                self.phase_ffn(l, not last)
        if self.on("Z"):
            self.final()
        P.finish([self.r_out])
        return self.nc

    def setup(self):
        P = self.P
        self.pb = [(P.psum(f"pb{i}", [128, 512]), Res(f"pb{i}")) for i in range(8)]
        self.ident = P.sbuf("ident", [128, 128], F32)
        self.r_const = Res("const")
        P.ld("sp", self.ident[:], self.ident_in[:, :], writes=[self.r_const])
        self.sgn = P.sbuf("sgn", [128, 1], F32)
        P.ld("sp", self.sgn[:], self.sgn_in[:, :], writes=[self.r_const])
        self.sgnrow = P.sbuf("sgnrow", [128, 512], F32)
        P.memset(P.dve, self.sgnrow[:, 0:512:2], 1.0, [self.r_const])
        P.memset(P.dve, self.sgnrow[:, 1:512:2], -1.0, [self.r_const])
        self.ones_b = P.sbuf("ones_b", [128, 128], BF16)
        self.ones_f = P.sbuf("ones_f", [128, 128], F32)
        P.memset(P.dve, self.ones_b[:], 1.0, [self.r_const])
        P.memset(P.dve, self.ones_f[:], 1.0, [self.r_const])
        self.smallp = P.sbuf("smallp", [128, DEPTH * NSP], F32)
        P.ld("sp", self.smallp[:], self.smallp_in[:, :], writes=[self.r_const])
        self.scv = P.sbuf("scv", [128, 32], F32)
        cvt = P.sbuf("cvt", [128, 32], F32)
        P.ld("sp", cvt[:], self.cvec[:, :], writes=[self.r_const])
        P.actv(self.scv[:], cvt[:], AF.Silu, [self.r_const], [self.r_const])
        self.eps_t = P.sbuf("eps_t", [128, 1], F32)
        P.memset(P.dve, self.eps_t[:], EPS, [self.r_const])
        self.ownp = P.sbuf("ownp", [128, DEPTH * NOWN], F32)
        P.ld("sp", self.ownp[:], self.ownp_in[:, :], writes=[self.r_const])
        self.mods = P.sbuf("mods", [128, 6 * 32], F32)
        self.r_mods = Res("mods")
        self.lru_h0 = P.sbuf("lru_h0", [128, 8], F32)
        self.r_h0 = Res("h0")
        for l in self.layers:
            for (src, dst, rows, cols, dcols) in (
                (self.w_in[self.lidx[l]], self.WIN[self.lidx[l]], D, DIN, DINX),
                (self.w_out[self.lidx[l]], self.WOUT[self.lidx[l]], D, D, D),
                (self.ffn_up[self.lidx[l]], self.WUP[self.lidx[l]], D, 2 * DFF, 2 * DFF),
                (self.ffn_down[self.lidx[l]], self.WDN[self.lidx[l]], DFF, D, D),
                (self.mla_wukv[self.lidx[l]], self.WUKV[self.lidx[l]], 256, 2048, 2048),
            ):
                cw = 1408 if cols % 1408 == 0 else (1696 if cols % 1696 == 0 else 1024)
                assert cols % cw == 0 and cw <= 2048
                for r0 in range(0, rows, 1024):
                    r1 = min(rows, r0 + 1024)
                    for c0 in range(0, cols, cw):
                        P.ld("pool", dst[r0:r1, c0:c0 + cw], src[r0:r1, c0:c0 + cw])
            for h in range(8):
                P.ld("pool", self.WUQ[self.lidx[l]][:, h * 256:h * 256 + 192], self.mla_wuq[self.lidx[l]][:, h * 192:(h + 1) * 192])
            mk = P.push()
            wk = P.sbuf("wk", [128, 16, 64], F32)
            wr = P.sbuf("wr", [128, 16, 64], BF16)
            r_wk, r_wr = Res(), Res()
            P.ld("sp", wk[:], self.w_in[self.lidx[l]].rearrange("(kc p) n -> p kc n", p=128)[:, :, OFF_KR:OFF_KR + 64], writes=[r_wk])
            P.ts(P.dve, wr[:, :, 0:32], wk[:, :, 32:64], -1.0, None, ALU.mult, None, [r_wk], [r_wr])
            P.copy(P.dve, wr[:, :, 32:64], wk[:, :, 0:32], [r_wk], [r_wr])
            P.ld("sp", self.WIN[self.lidx[l]].rearrange("(kc p) n -> p kc n", p=128)[:, :, OFF_KRR:OFF_KRR + 64], wr[:], reads=[r_wr])
            qk = P.sbuf("qk", [128, 4, 8, 64], F32)
            qr = P.sbuf("qr", [128, 4, 8, 64], BF16)
            r_qk, r_qr = Res(), Res()
            srcv = self.mla_wuq[self.lidx[l]].rearrange("(kc p) (h n) -> p kc h n", p=128, n=192)
            dstv = self.WUQ[self.lidx[l]].rearrange("(kc p) (h n) -> p kc h n", p=128, n=256)
            for kc in range(4):
                P.ld("sp", qk[:, kc, :, :], srcv[:, kc, :, 128:192], writes=[r_qk])
            P.ts(P.dve, qr[:, :, :, 0:32], qk[:, :, :, 32:64], -1.0, None, ALU.mult, None, [r_qk], [r_qr])
            P.copy(P.dve, qr[:, :, :, 32:64], qk[:, :, :, 0:32], [r_qk], [r_qr])
            for kc in range(4):
                P.ld("sp", dstv[:, kc, :, 192:256], qr[:, kc, :, :], reads=[r_qr])
            P.pop(mk)
        mk = P.push()
        xin = P.ring("xin", 2, [128, D], F32)
        xst = P.ring("xst", 2, [128, 16, 128], F32)
        XTv = self.XT.rearrange("(c p) t -> p c t", p=128)
        k = 0
        for ti in range(NTL // 128):
            src = self.ctx_in[ti * 128:(ti + 1) * 128, :] if ti < 2 else self.x_in[(ti - 2) * 128:(ti - 1) * 128, :]
            xt, r_xt = xin.next()
            P.ld("sp" if ti % 2 == 0 else "act", xt[:], src, writes=[r_xt])
            st, r_st = xst.next()
            for g in range(4):
                pt, r_pt = self.pb[k % 8]
                k += 1
                for j in range(4):
                    c = g * 4 + j
                    P.tr(pt[:, j * 128:(j + 1) * 128], xt[:, c * 128:(c + 1) * 128], self.ident[:], [r_xt, self.r_const], [r_pt])
                P.copy(P.act if g % 2 == 0 else P.dve, st[:, g * 4:(g + 1) * 4, :], pt[:].rearrange("p (j t) -> p j t", j=4), [r_pt], [r_st])
            P.ld("sp" if ti % 2 == 1 else "act", XTv[:, :, ti * 128:(ti + 1) * 128], st[:], reads=[r_st])
        P.pop(mk)
        if self.on("H"):
            ls = self.layers
            for i in range(0, len(ls), 2):
                self.hy_filter_all(self.tab["L"], ls[i:i + 2])
            lc_ = [l for l in ls if l < DEPTH - 1]
            for i in range(0, len(lc_), 2):
                self.hy_filter_all(self.tab["C"], lc_[i:i + 2])
        mk = P.push()
        zt = P.sbuf("zpad", [128, 16, 2], BF16)
        r_z = Res()
        P.memset(P.dve, zt[:], 0.0, [r_z])
        for blk in (0, NR + 1):
            P.ld("sp", self.HGP[blk * D:(blk + 1) * D, :].rearrange("(c p) k -> p c k", p=128), zt[:], reads=[r_z])
        P.pop(mk)

    def final(self):
        P = self.P
        mk = P.push()
        xin = P.ring("fin", 2, [128, 16, 128], F32)
        xst = P.ring("fst", 2, [128, D], F32)
        XTv = self.XT.rearrange("(c p) t -> p c t", p=128)
        k = 0
        for ti in range(SL // 128):
            xt, r_xt = xin.next()
            P.ld("sp" if ti % 2 == 0 else "act", xt[:], XTv[:, :, CTX + ti * 128:CTX + (ti + 1) * 128], writes=[r_xt])
            st, r_st = xst.next()
            for g in range(4):
                pt, r_pt = self.pb[k % 8]
                k += 1
                for j in range(4):
                    c = g * 4 + j
                    P.tr(pt[:, j * 128:(j + 1) * 128], xt[:, c, :], self.ident[:], [r_xt, self.r_const], [r_pt])
                P.copy(P.act if g % 2 == 0 else P.dve, st[:, g * 512:(g + 1) * 512], pt[:], [r_pt], [r_st])
            P.ld("sp" if ti % 2 == 1 else "act", self.out[ti * 128:(ti + 1) * 128, :], st[:], reads=[r_st], writes=[self.r_out])
        P.pop(mk)

    def sp_col(self, l, col, n=1, parts=128):
        base = l * NSP + col
        return self.smallp[0:parts, base:base + n]

    def rms_rstd(self, chunks, W, n_feat, sq_ring, pbi, st_ring, reads):
        P = self.P
        pt, r_pt = self.pb[pbi]
        for i, ap in enumerate(chunks):
            sq, r_sq = sq_ring.next()
            P.actv(sq[:, :W], ap, AF.Square, reads, [r_sq])
            P.mm(pt[:, :W], self.ones_b[:], sq[:, :W], i == 0, i == len(chunks) - 1, [r_sq, self.r_const], [r_pt])
        st, r_st = st_ring.next()
        P.actv(st[:, :W], pt[:, :W], AF.Sqrt, [r_pt], [r_st], bias=self.eps_t[:], scale=1.0 / n_feat)
        P.recip(st[:, :W], st[:, :W], [r_st], [r_st])
        return st, r_st

    def phase_mod(self, l):
        P = self.P
        mk = P.push()
        wr = P.ring("adaw", 2, [128, 16, 512], F32)
        modT = P.sbuf("modT", [128, 96, 2], F32)
        r_modT = Res()
        pm, r_pm = self.pb[0]
        src = self.ada_w[self.lidx[l]].rearrange("(kc p) n -> p kc n", p=128)
        scv = self.scv[:].rearrange("p (kc j) -> p kc j", j=2)
        for g in range(6):
            wt, r_wt = wr.next()
            P.ld("sp" if g % 2 == 0 else "act", wt[:], src[:, :, g * 512:(g + 1) * 512], writes=[r_wt])
            for m in range(4):
                col = (g * 4 + m) * 2
                for kc in range(16):
                    P.mm(pm[:, col:col + 2], wt[:, kc, m * 128:(m + 1) * 128], scv[:, kc, :], kc == 0, kc == 15, [r_wt, self.r_const], [r_pm])
        mloc = P.sbuf("mloc", [128, 48], F32)
        r_ml = Res()
        P.copy(P.dve, mloc[:], pm[:, 0:48], [r_pm], [r_ml])
        P.ld("sp", self.MIN[:, :], mloc[:], reads=[r_ml])
        self.allgather([(self.MIN, self.MG)])
        raw = P.sbuf("mraw", [128, 96, 2], F32)
        r_raw = Res()
        P.ld("sp", raw[:].rearrange("p (s m) j -> p s (m j)", s=NR), self.MG.rearrange("(s p) c -> p s c", p=128), writes=[r_raw])
        for j in range(2):
            P.tt(P.dve, modT[:, :, j], raw[:, :, j], self.sp_col(l, SP_ADAB, 96), ALU.add, [r_raw, self.r_const], [r_modT])
        mods = self.mods[:].rearrange("p (k c j) -> p k c j", k=6, j=2)
        for j in range(2):
            for half, (sh, sc, g, nga, ngb) in enumerate(((0, 1, 2, 0, 1), (3, 4, 5, 2, 3))):
                ng_a = self.sp_col(l, SP_NG + nga * 16, 16)
                ng_b = self.sp_col(l, SP_NG + ngb * 16, 16)
                A, B, G = mods[:, half * 3 + 0, :, j], mods[:, half * 3 + 1, :, j], mods[:, half * 3 + 2, :, j]
                P.stt(A, modT[:, sc * 16:(sc + 1) * 16, j], 1.0, ng_a, ALU.add, ALU.mult, [r_modT, self.r_const], [self.r_mods])
                P.copy(P.dve, B, modT[:, sh * 16:(sh + 1) * 16, j], [r_modT], [self.r_mods])
                P.tt(P.dve, G, modT[:, g * 16:(g + 1) * 16, j], ng_b, ALU.mult, [r_modT, self.r_const], [self.r_mods])
        P.pop(mk)

    def mod(self, k, c, j):
        i = (k * 16 + c) * 2 + j
        return self.mods[:, i:i + 1]

    def tok_tiles(self, need_ctx=True):
        tiles = [(0, CTX, 1)] if need_ctx else []
        tiles += [(CTX + i * 512, 512, 0) for i in range(SEQ // 512)]
        return tiles

    def phase_inproj(self, l):
        P = self.P
        mk = P.push()
        xs_r = P.ring("xs", 2, [128, 16, 512], F32)
        hb_r = P.ring("hb", 2, [128, 16, 512], BF16)
        sq_r = P.ring("sq", 3, [128, 512], BF16)
        tmp_r = P.ring("tmp", 3, [128, 512], F32)
        st_r = P.ring("st", 2, [128, 512], F32)
        w_r = P.ring("wg", 2, [128, 16, 512], BF16)
        ev_r = P.ring("ev", 4, [128, 512], F32)
        XTv = self.XT.rearrange("(c p) t -> p c t", p=128)
        Wv = self.WIN[self.lidx[l]].rearrange("(kc p) n -> p kc n", p=128)
        kb = 0
        for (c0, W, j) in self.loc_tiles():
            xs, r_xs = xs_r.next()
            P.ld("sp", xs[:, :, :W], XTv[:, :, c0:c0 + W], writes=[r_xs])
            st, r_st = self.rms_rstd([xs[:, c, :W] for c in range(16)], W, D, sq_r, 7, st_r, [r_xs])
            hb, r_hb = hb_r.next()
            for c in range(16):
                tmp, r_tmp = tmp_r.next()
                P.tt(P.dve, tmp[:, :W], xs[:, c, :W], st[:, :W], ALU.mult, [r_xs, r_st], [r_tmp])
                P.actv(hb[:, c, :W], tmp[:, :W], AF.Identity, [r_tmp, self.r_mods], [r_hb], bias=self.mod(1, c, j), scale=self.mod(0, c, j))
            for g in range(7):
                ncol = 512 if g < 6 else DINX - 6 * 512
                wt, r_wt = w_r.next()
                P.ld("act", wt[:, :, :ncol], Wv[:, :, g * 512:g * 512 + ncol], writes=[r_wt])
                for m in range(ncol // 128):
                    pt, r_pt = self.pb[kb % 6]
                    kb += 1
                    for kc in range(16):
                        P.mm(pt[:, :W], wt[:, kc, m * 128:(m + 1) * 128], hb[:, kc, :W], kc == 0, kc == 15, [r_wt, r_hb], [r_pt])
                    ev, r_ev = ev_r.next()
                    P.copy(P.act if kb % 2 == 0 else P.dve, ev[:, :W], pt[:, :W], [r_pt], [r_ev])
                    row = g * 512 + m * 128
                    dstU = self.UC[row:row + 128, c0:c0 + W] if j == 1 else self.ULOC[row:row + 128, c0 - CTX:c0 - CTX + W]
                    P.ld("sp", dstU, ev[:, :W], reads=[r_ev])
        P.pop(mk)
        self.allgather([(self.ULOC[k * 256:(k + 1) * 256, :], self.UG[k * NR * 256:(k + 1) * NR * 256, :]) for k in range(UCH) if k not in (8, 9)])


    def sin_layer(self, ps_ap, bias_ap, out_ap, W, rings, reads, r_ps, r_out):
        P = self.P
        v, r_v = rings[0].next()
        t1, r_t1 = rings[1].next()
        P.ts(P.dve, v[:, :W], ps_ap, bias_ap, None, ALU.add, None, [r_ps] + reads, [r_v])
        P.ts(P.dve, t1[:, :W], v[:, :W], 1.0 / (2 * math.pi), MAGIC, ALU.mult, ALU.add, [r_v], [r_t1])
        P.ts(P.dve, t1[:, :W], t1[:, :W], -MAGIC, None, ALU.add, None, [r_t1], [r_t1])
        P.stt(v[:, :W], t1[:, :W], -2 * math.pi, v[:, :W], ALU.mult, ALU.add, [r_t1, r_v], [r_v])
        P.ts(P.dve, v[:, :W], v[:, :W], -3.1415925, 3.1415925, ALU.max, ALU.min, [r_v], [r_v])
        P.actv(out_ap, v[:, :W], AF.Sin, [r_v], [r_out])

    def hy_filter_all(self, T, layers):
        P = self.P
        L, nfc, ntc, TW, ntt = T["L"], T["nfc"], T["ntc"], T["TW"], T["ntt"]
        G = len(layers)
        mk = P.push()
        nfcA, nfcB = T["nfcA"], T["nfcB"]
        ksum = P.sbuf("ksum", [128, ntc, G * 256], BF16)
        kdiff = P.sbuf("kdiff", [128, ntc, G * 256], BF16)
        ksa = P.sbuf("ksa", [128, ntc, G * 256], BF16)
        kda = P.sbuf("kda", [128, ntc, G * 256], BF16)
        r_k = [Res() for _ in range(ntc)]
        rn = P.sbuf("rn", [128, G * 256], F32)
        r_rn = Res()
        dec = P.sbuf("dec", [128, ntc, 128], F32)
        r_dec = Res()
        P.ld("act", dec[:], T["decay"].rearrange("(tc p) c -> p tc c", p=128), writes=[r_dec])
        mk1 = P.push()
        w1 = P.sbuf("hw1", [33, 64], F32)
        w2 = P.sbuf("hw2", [64, 64], F32)
        w3 = P.sbuf("hw3", [64, 4, 128], F32)
        r_w = Res()
        hid2 = P.sbuf("hid2", [64, L], F32)
        r_hid2 = Res()
        f_r = P.ring("ft", 2, [33, 512], F32)
        h1_r = P.ring("h1", 2, [64, 512], F32)
        v_r = P.ring("sv", 2, [64, 512], F32)
        t_r = P.ring("st1", 2, [64, 512], F32)
        hf_r = P.ring("hf", 2, [128, 4, 128], F32)
        ab_r = P.ring("ab", 2, [128, 512], F32)
        pns = P.sbuf("pns", [128, 512], F32)
        r_pns = Res()
        for gi, l in enumerate(layers):
            li = self.lidx[l]
            P.ld("sp", w1[:], self.hy_w1[li], writes=[r_w])
            P.ld("sp", w2[:], self.hy_w2[li], writes=[r_w])
            P.ld("sp", w3[:].rearrange("p g c -> p (g c)"), self.w3own[li], writes=[r_w])
            b1 = self.sp_col(l, SP_B1, 1, 64)
            b2 = self.sp_col(l, SP_B2, 1, 64)
            for tt in range(ntt):
                ft, r_ft = f_r.next()
                P.ld("sp", ft[:, :TW], T["feats"][:, tt * TW:(tt + 1) * TW], writes=[r_ft])
                p0, r_p0 = self.pb[4]
                P.mm(p0[0:64, :TW], w1[:, :], ft[:, :TW], True, True, [r_w, r_ft], [r_p0])
                h1, r_h1 = h1_r.next()
                self.sin_layer(p0[0:64, :TW], b1, h1[:, :TW], TW, (v_r, t_r), [self.r_const], r_p0, r_h1)
                p1, r_p1 = self.pb[5]
                P.mm(p1[0:64, :TW], w2[:, :], h1[:, :TW], True, True, [r_w, r_h1], [r_p1])
                self.sin_layer(p1[0:64, :TW], b2, hid2[:, tt * TW:(tt + 1) * TW], TW, (v_r, t_r), [self.r_const], r_p1, r_hid2)
            pn, r_pn = self.pb[6]
            w3f = w3[:].rearrange("p g c -> p (g c)")
            for lc in range(ntc):
                hf, r_hf = hf_r.next()
                ph, r_ph = self.pb[lc % 4]
                P.mm(ph[:, :], hid2[:, lc * 128:(lc + 1) * 128], w3f, True, True, [r_hid2, r_w], [r_ph])
                for g in range(4):
                    P.tt(P.dve, hf[:, g, :], ph[:, g * 128:(g + 1) * 128], dec[:, lc, :], ALU.mult, [r_ph, r_dec], [r_hf])
                if lc == 0:
                    P.memset(P.dve, hf[0:1, 1, :], 0.0, [r_hf])
                    P.memset(P.dve, hf[0:1, 3, :], 0.0, [r_hf])
                ab, r_ab = ab_r.next()
                P.actv(ab[:], hf[:].rearrange("p g c -> p (g c)"), AF.Abs, [r_hf], [r_ab])
                P.mm(pn[:, :], self.ones_f[:], ab[:], lc == 0, lc == ntc - 1, [r_ab, self.r_const], [r_pn])
                for o_ in range(2):
                    cs = slice(gi * 256 + o_ * 128, gi * 256 + (o_ + 1) * 128)
                    P.tt(P.pool, ksum[:, lc, cs], hf[:, 2 * o_, :], hf[:, 2 * o_ + 1, :], ALU.add, [r_hf], [r_k[lc]])
                    P.tt(P.pool, kdiff[:, lc, cs], hf[:, 2 * o_, :], hf[:, 2 * o_ + 1, :], ALU.subtract, [r_hf], [r_k[lc]])
                gsl = slice(gi * 256, (gi + 1) * 256)
                P.ts(P.pool, ksa[:, lc, gsl], ksum[:, lc, gsl], self.sgn[:, 0:1], None, ALU.mult, None, [r_k[lc], self.r_const], [r_k[lc]])
                P.ts(P.pool, kda[:, lc, gsl], kdiff[:, lc, gsl], self.sgn[:, 0:1], None, ALU.mult, None, [r_k[lc], self.r_const], [r_k[lc]])
            P.copy(P.dve, pns[:], pn[:, :], [r_pn], [r_pns])
            pnv = pns[:].rearrange("p (o d c) -> p o d c", o=2, d=2)
            rnv = rn[:, gi * 256:(gi + 1) * 256].rearrange("p (o c) -> p o c", o=2)
            P.tt(P.dve, rnv, pnv[:, :, 0, :], pnv[:, :, 1, :], ALU.add, [r_pns], [r_rn])
            P.recip(rn[:, gi * 256:(gi + 1) * 256], rn[:, gi * 256:(gi + 1) * 256], [r_rn], [r_rn])
        P.pop(mk1)
        AW = ntc * 128
        tbs = [(P.sbuf(f"ftb{i}", [128, 2 * AW], BF16), Res(), Res()) for i in range(2)]
        sst_r = P.ring("sst", 3, [128, 512], F32)
        for fc in range(nfcA):
            tb, r_tc, r_ts = tbs[fc % 2]
            P.ld("sp", tb[:, 0:AW], T["Fc"][fc].rearrange("p tc f -> p (tc f)"), writes=[r_tc])
            P.ld("act", tb[:, AW:2 * AW], T["Fs"][fc].rearrange("p tc f -> p (tc f)"), writes=[r_ts])
            Fc_t = tb[:, 0:AW].rearrange("p (tc f) -> p tc f", f=128)
            Fs_t = tb[:, AW:2 * AW].rearrange("p (tc f) -> p tc f", f=128)
            for gi, l in enumerate(layers):
                li = self.lidx[l]
                gs = slice(gi * 256, (gi + 1) * 256)
                halves = [(ksum, kdiff, fc)] + ([(ksa, kda, nfcA + fc)] if fc < nfcB else [])
                for hb, (km, kd, chunk) in enumerate(halves):
                    pr, r_pr = self.pb[gi * 4 + hb * 2]
                    ps_, r_ps = self.pb[gi * 4 + hb * 2 + 1]
                    for tc in range(ntc):
                        P.mm(pr[:, 0:256], Fc_t[:, tc, :], km[:, tc, gs], tc == 0, tc == ntc - 1, [r_tc, r_k[tc]], [r_pr])
                    for tc in range(ntc):
                        P.mm(ps_[:, 0:256], Fs_t[:, tc, :], kd[:, tc, gs], tc == 0, tc == ntc - 1, [r_ts, r_k[tc]], [r_ps])
                    sst, r_sst = sst_r.next()
                    P.tt(P.dve, sst[:, 0:256], pr[:, 0:256], rn[:, gs], ALU.mult, [r_pr, r_rn], [r_sst])
                    P.tt(P.dve, sst[:, 256:512], ps_[:, 0:256], rn[:, gs], ALU.mult, [r_ps, r_rn], [r_sst])
                    P.ld("sp", T["S"][li][chunk * 128:(chunk + 1) * 128, :], sst[:], reads=[r_sst])
        P.pop(mk)

    def own(self, l, col):
        i = l * NOWN + col
        return self.ownp[:, i:i + 1]

    def head_norm_store(self, y_ap, W, l, hg_ap, dsts, sq_r, st_r, ob_r, pbi, r_y):
        P = self.P
        st, r_st = self.rms_rstd([y_ap], W, 128, sq_r, pbi, st_r, [r_y])
        ob, r_ob = ob_r.next()
        P.stt(ob[:, :W], y_ap, hg_ap, st[:, :W], ALU.mult, ALU.mult, [r_y, r_st, self.r_const], [r_ob])
        for k, d in enumerate(dsts):
            P.dma("sp" if k % 2 == 0 else "act", lambda e, d=d, ob=ob: e.dma_start(out=d(e), in_=ob[:, :W]), reads=[r_ob])

    def yin_dsts(self, kind, c0, W):
        Y = self.YIN[kind]
        if c0 < CTX:
            return [(lambda e, d=d: Y[d * 128:(d + 1) * 128, c0:c0 + W]) for d in range(NR)]
        t0 = c0 - CTX
        d, lc = t0 // SL, CTX + t0 % SL
        return [lambda e: Y[d * 128:(d + 1) * 128, lc:lc + W]]


    def ld_u_own(self, dst2d, A, isctx, L, col0, writes, qn="sp"):
        P = self.P
        if isctx:
            P.dma(qn, lambda e: e.dma_start(out=dst2d[:, col0:col0 + L], in_=self.UC[A:A + 512, 0:L][bass.ds(self.rv(e, "128"), 128), :]), writes=writes)
        else:
            win = self.UG[A * NR:A * NR + 2048, :].rearrange("(x i) t -> i x t", i=128)
            P.dma(qn, lambda e: e.dma_start(out=dst2d[:, col0:col0 + SEQ].rearrange("p (s t) -> p s t", s=NR), in_=win[:, bass.ds(self.rv(e, "x"), NR, 2), :]), writes=writes)

    def phase_hyena(self, l, tag):
        P = self.P
        T = self.tab[tag]
        L, nfc, ntc, TW, ntt = T["L"], T["nfc"], T["ntc"], T["TW"], T["ntt"]
        off = 0 if tag == "C" else CTX
        mk = P.push()
        u_r = P.ring("hu", 2, [128, L + 2], F32)
        o_r = P.ring("ho", 2, [128, L], F32)
        for p_ in range(3):
            u, r_u = u_r.next()
            P.memset(P.pool, u[:, 0:1], 0.0, [r_u])
            P.memset(P.pool, u[:, L + 1:L + 2], 0.0, [r_u])
            self.ld_u_own(u, p_ * 512, tag == "C", L, 1, [r_u])
            o, r_o = o_r.next()
            P.ts(P.dve, o[:, :], u[:, 0:L], self.own(l, p_ * 3 + 0), None, ALU.mult, None, [r_u, self.r_const], [r_o])
            P.stt(o[:, :], u[:, 1:L + 1], self.own(l, p_ * 3 + 1), o[:, :], ALU.mult, ALU.add, [r_u, r_o, self.r_const], [r_o])
            P.stt(o[:, :], u[:, 2:L + 2], self.own(l, p_ * 3 + 2), o[:, :], ALU.mult, ALU.add, [r_u, r_o, self.r_const], [r_o])
            dst = self.ZT[:, off:off + L] if p_ == 0 else self.PT[(p_ - 1) * 128:p_ * 128, off:off + L]
            P.ld("act", dst, o[:, :], reads=[r_o])
        P.pop(mk)
        nfcA_, nfcB_ = T["nfcA"], T["nfcB"]
        FG = min(9, nfcA_)
        nfg = (nfcA_ + FG - 1) // FG
        AW = max(ntc * 128, FG * TW)
        for o_ in range(2):
            mk = P.push()
            nfcA, nfcB = T["nfcA"], T["nfcB"]
            zT = P.sbuf("zT", [128, ntc, 256], BF16)
            r_zT = [Res() for _ in range(ntc)]
            Yr = P.sbuf("Yr", [128, nfc, 128], BF16)
            Ys = P.sbuf("Ys", [128, nfc, 128], BF16)
            r_Y = [Res() for _ in range(nfc)]
            tbs = [(P.sbuf(f"tb{i}", [128, 2 * AW], BF16), Res(), Res()) for i in range(3)]
            zl_r = P.ring("zl", 2, [128, TW], F32)
            for tt in range(ntt):
                zl, r_zl = zl_r.next()
                P.ld("sp", zl[:, :], self.ZT[:, off + tt * TW:off + (tt + 1) * TW], writes=[r_zl])
                nsub = TW // 128
                pt, r_pt = self.pb[tt % 4]
                for s_ in range(nsub):
                    P.tr(pt[:, s_ * 128:(s_ + 1) * 128], zl[:, s_ * 128:(s_ + 1) * 128], self.ident[:], [r_zl, self.r_const], [r_pt])
                P.copy(P.act if tt % 2 == 0 else P.dve, zT[:, tt * nsub:(tt + 1) * nsub, 0:128], pt[:, 0:TW].rearrange("p (a b) -> p a b", b=128), [r_pt], [r_zT[tt * nsub]])
                P.ts(P.pool, zT[:, tt * nsub:(tt + 1) * nsub, 128:256], zT[:, tt * nsub:(tt + 1) * nsub, 0:128], self.sgn[:, 0:1], None, ALU.mult, None,
                     [r_zT[tt * nsub], self.r_const], [r_zT[tt * nsub]])
                for s_ in range(1, nsub):
                    r_zT[tt * nsub + s_] = r_zT[tt * nsub]
            S_r = P.ring("Sld", 3, [128, 512], F32)
            t_rs = [P.ring(f"ty{i}", 2, [128, 128], F32) for i in range(4)]
            FW = ntc * 128
            for fc in range(nfcA):
                tb, r_tc, r_ts = tbs[fc % 3]
                P.ld("sp", tb[:, 0:FW], T["Fc"][fc].rearrange("p tc f -> p (tc f)"), writes=[r_tc])
                P.ld("act", tb[:, AW:AW + FW], T["Fs"][fc].rearrange("p tc f -> p (tc f)"), writes=[r_ts])
                Fc_t = tb[:, 0:FW].rearrange("p (tc f) -> p tc f", f=128)
                Fs_t = tb[:, AW:AW + FW].rearrange("p (tc f) -> p tc f", f=128)
                NW = 256 if fc < nfcB else 128
                pr, r_pr = self.pb[4 + 2 * (fc % 2)]
                ps_, r_ps = self.pb[5 + 2 * (fc % 2)]
                for tc in range(ntc):
                    P.mm(pr[:, 0:NW], Fc_t[:, tc, :], zT[:, tc, 0:NW], tc == 0, tc == ntc - 1, [r_tc, r_zT[tc]], [r_pr])
                for tc in range(ntc):
                    P.mm(ps_[:, 0:NW], Fs_t[:, tc, :], zT[:, tc, 0:NW], tc == 0, tc == ntc - 1, [r_ts, r_zT[tc]], [r_ps])
                for hb, chunk in enumerate([fc] + ([nfcA + fc] if fc < nfcB else [])):
                    St, r_S = S_r.next()
                    P.ld("sp", St[:], T["S"][self.lidx[l]][chunk * 128:(chunk + 1) * 128, :], writes=[r_S])
                    Sr = St[:, o_ * 128:(o_ + 1) * 128]
                    Ss = St[:, 256 + o_ * 128:256 + (o_ + 1) * 128]
                    cs = slice(hb * 128, (hb + 1) * 128)
                    (t1, r1), (t2, r2), (t3, r3), (t4, r4) = [r.next() for r in t_rs]
                    P.tt(P.dve, t1[:], pr[:, cs], Sr, ALU.mult, [r_pr, r_S], [r1])
                    P.tt(P.dve, t2[:], ps_[:, cs], Ss, ALU.mult, [r_ps, r_S], [r2])
                    P.tt(P.dve, t3[:], pr[:, cs], Ss, ALU.mult, [r_pr, r_S], [r3])
                    P.tt(P.dve, t4[:], ps_[:, cs], Sr, ALU.mult, [r_ps, r_S], [r4])
                    P.tt(P.pool, Yr[:, chunk, :], t1[:], t2[:], ALU.subtract, [r1, r2], [r_Y[chunk]])
                    P.tt(P.pool, Ys[:, chunk, :], t3[:], t4[:], ALU.add, [r3, r4], [r_Y[chunk]])
            zt_r = P.ring("zt", 2, [128, TW], F32)
            pp_r = P.ring("pp", 2, [128, TW], F32)
            tm_r = P.ring("tm", 2, [128, TW], F32)
            sq_r = P.ring("hsq", 2, [128, 512], BF16)
            st_r = P.ring("hst", 2, [128, 512], F32)
            ob_r = P.ring("hob", 2, [128, 512], BF16)
            k = 0
            tm2_r = P.ring("tm2", 2, [128, TW], F32)
            for tt in range(ntt):
                c0 = off + tt * TW
                pa, r_pa = self.pb[(2 * tt) % 4]
                pb2, r_pb2 = self.pb[(2 * tt + 1) % 4]
                for fg in range(nfg):
                    f_lo, f_hi = fg * FG, min(nfcA_, (fg + 1) * FG)
                    nf = f_hi - f_lo
                    tb, r_tc, r_ts = tbs[k % 3]
                    k += 1
                    P.ld("sp", tb[:, 0:nf * TW], T["Ic"][tt][:, f_lo:f_hi, :].rearrange("p f t -> p (f t)"), writes=[r_tc])
                    P.ld("act", tb[:, AW:AW + nf * TW], T["Is"][tt][:, f_lo:f_hi, :].rearrange("p f t -> p (f t)"), writes=[r_ts])
                    Ic_t = tb[:, 0:nf * TW].rearrange("p (f t) -> p f t", t=TW)
                    Is_t = tb[:, AW:AW + nf * TW].rearrange("p (f t) -> p f t", t=TW)
                    for f_ in range(nf):
                        fc = f_lo + f_
                        P.mm(pa[:, :TW], Yr[:, fc, :], Ic_t[:, f_, :], fc == 0, False, [r_Y[fc], r_tc], [r_pa])
                        P.mm(pa[:, :TW], Ys[:, fc, :], Is_t[:, f_, :], False, fc == nfcA_ - 1, [r_Y[fc], r_ts], [r_pa])
                        if fc < nfcB_:
                            cb = nfcA_ + fc
                            P.mm(pb2[:, :TW], Yr[:, cb, :], Ic_t[:, f_, :], fc == 0, False, [r_Y[cb], r_tc], [r_pb2])
                            P.mm(pb2[:, :TW], Ys[:, cb, :], Is_t[:, f_, :], False, fc == nfcB_ - 1, [r_Y[cb], r_ts], [r_pb2])
                zt, r_zt = zt_r.next()
                pp, r_pp = pp_r.next()
                tm, r_tm = tm_r.next()
                tm2, r_tm2 = tm2_r.next()
                P.ld("sp", zt[:], self.ZT[:, c0:c0 + TW], writes=[r_zt])
                P.ld("act", pp[:], self.PT[o_ * 128:(o_ + 1) * 128, c0:c0 + TW], writes=[r_pp])
                P.stt(tm[:], zt[:], self.own(l, 9 + o_), pa[:, :TW], ALU.mult, ALU.add, [r_zt, r_pa, self.r_const], [r_tm])
                P.tt(P.dve, tm2[:], pb2[:, :TW], self.sgnrow[:, 0:TW], ALU.mult, [r_pb2, self.r_const], [r_tm2])
                P.tt(P.pool, tm[:], tm[:], tm2[:], ALU.add, [r_tm, r_tm2], [r_tm])
                P.tt(P.pool, zt[:], tm[:], pp[:], ALU.mult, [r_tm, r_pp], [r_zt])
                if o_ == 0:
                    P.ld("sp", self.ZT[:, c0:c0 + TW], zt[:], reads=[r_zt])
                else:
                    self.head_norm_store(zt[:], TW, l, self.own(l, 11), self.yin_dsts(0, c0, TW), sq_r, st_r, ob_r, 4 + tt % 4, r_zt)
            P.pop(mk)

    def phase_lru(self, l, need_ctx):
        P = self.P
        li = self.lidx[l]
        mk = P.push()
        c8 = P.sbuf("c8", [128, 2], F32)
        r_c8 = Res()
        P.actv(c8[:], self.ownp[:, l * NOWN + 25:l * NOWN + 27], AF.Exp, [self.r_const], [r_c8], scale=-1.0)
        P.actv(c8[:], c8[:], AF.Ln, [r_c8, self.r_const], [r_c8], bias=self.ones_f[:, 0:1], scale=1.0)
        P.ts(P.dve, c8[:], c8[:], -8.0, None, ALU.mult, None, [r_c8], [r_c8])
        wt = P.sbuf("wab", [128, 4, 128], BF16)
        r_wt = Res()
        for d in range(2):
            for k, Wsrc in enumerate((self.lwa_own, self.lwx_own)):
                P.ld("pool", wt[:, d * 2 + k, :], Wsrc[li, d], writes=[r_wt])
        seqs = [(0, CTX, True), (CTX, SEQ, False)]
        sq_r = P.ring("lsq", 2, [128, 512], BF16)
        st_r = P.ring("lst", 2, [128, 512], F32)
        ob_r = P.ring("lob", 2, [128, 512], BF16)
        for (off, Ls, isctx) in seqs:
            mk2 = P.push()
            t = "c" if isctx else "l"
            xr_r = P.ring("xr" + t, 2, [128, Ls + 6], F32)
            xc, r_xc = P.sbuf("xc" + t, [128, Ls], F32), Res()
            xcb, r_xcb = P.sbuf("xcb" + t, [128, Ls], BF16), Res()
            ra, r_ra = P.sbuf("ra" + t, [128, Ls], F32), Res()
            ib, r_ib = P.sbuf("ib" + t, [128, Ls], F32), Res()
            ta, r_ta = P.sbuf("ta" + t, [128, Ls], F32), Res()
            hs = [(P.sbuf(f"h{d}" + t, [128, Ls], F32), Res()) for d in range(2)]
            W = min(512, Ls)
            for d in range(2):
                xr, r_xr = xr_r.next()
                P.memset(P.pool, xr[:, 0:3], 0.0, [r_xr])
                P.memset(P.pool, xr[:, Ls + 3:Ls + 6], 0.0, [r_xr])
                self.ld_u_own(xr, OFF_X, isctx, Ls, 3, [r_xr], qn="act")
                left = 3 if d == 0 else 0
                for jj in range(4):
                    s0 = 3 + jj - left
                    wj = self.own(l, 13 + d * 4 + jj)
                    if jj == 0:
                        P.ts(P.dve, xc[:, :], xr[:, s0:s0 + Ls], wj, None, ALU.mult, None, [r_xr, self.r_const], [r_xc])
                    else:
                        P.stt(xc[:, :], xr[:, s0:s0 + Ls], wj, xc[:, :], ALU.mult, ALU.add, [r_xr, r_xc, self.r_const], [r_xc])
                P.copy(P.pool, xcb[:, :], xc[:, :], [r_xc], [r_xcb])
                for ti in range(Ls // W):
                    cs = slice(ti * W, (ti + 1) * W)
                    pa, r_pa = self.pb[(2 * ti) % 4]
                    px, r_px = self.pb[(2 * ti + 1) % 4]
                    P.mm(pa[:, :W], wt[:, d * 2 + 0, :], xcb[:, cs], True, True, [r_wt, r_xcb], [r_pa])
                    P.mm(px[:, :W], wt[:, d * 2 + 1, :], xcb[:, cs], True, True, [r_wt, r_xcb], [r_px])
                    P.actv(ra[:, cs], pa[:, :W], AF.Sigmoid, [r_pa, self.r_const], [r_ra], bias=self.own(l, 21 + d))
                    P.actv(ib[:, cs], px[:, :W], AF.Sigmoid, [r_px, self.r_const], [r_ib], bias=self.own(l, 23 + d))
                P.actv(ra[:, :], ra[:, :], AF.Exp, [r_ra, r_c8], [r_ra], scale=c8[:, d:d + 1])
                P.tt(P.pool, ta[:, :], ra[:, :], ra[:, :], ALU.mult, [r_ra], [r_ta])
                P.ts(P.pool, ta[:, :], ta[:, :], -1.0, 1.0, ALU.mult, ALU.add, [r_ta], [r_ta])
                P.actv(ta[:, :], ta[:, :], AF.Sqrt, [r_ta], [r_ta])
                P.tt(P.dve, ib[:, :], ib[:, :], xc[:, :], ALU.mult, [r_ib, r_xc], [r_ib])
                P.tt(P.dve, ib[:, :], ib[:, :], ta[:, :], ALU.mult, [r_ib, r_ta], [r_ib])
                if not isctx:
                    f0 = Ls - 1 if d == 1 else 0
                    P.stt(ib[:, f0:f0 + 1], ra[:, f0:f0 + 1], self.lru_h0[:, d:d + 1], ib[:, f0:f0 + 1],
                          ALU.mult, ALU.add, [r_ra, r_ib, self.r_h0], [r_ib])
                h, r_h = hs[d]
                if d == 0:
                    P.op(P.dve, lambda e, h=h, ra=ra, ib=ib: e.tensor_tensor_scan(out=h[:, :], data0=ra[:, :], data1=ib[:, :], initial=0.0, op0=ALU.mult, op1=ALU.add), [r_ra, r_ib], [r_h])
                else:
                    P.op(P.dve, lambda e, h=h, ra=ra, ib=ib: e.tensor_tensor_scan(out=h[:, ::-1], data0=ra[:, ::-1], data1=ib[:, ::-1], initial=0.0, op0=ALU.mult, op1=ALU.add), [r_ra, r_ib], [r_h])
                if isctx:
                    f1 = Ls - 1 if d == 0 else 0
                    P.copy(P.dve, self.lru_h0[:, d:d + 1], h[:, f1:f1 + 1], [r_h], [self.r_h0])
            if not (isctx and not need_ctx):
                xr, r_xr = xr_r.next()
                self.ld_u_own(xr, OFF_G, isctx, Ls, 0, [r_xr], qn="act")
                P.actv(xr[:, 0:Ls], xr[:, 0:Ls], AF.Gelu_apprx_tanh, [r_xr], [r_xr])
                (h0_, r_h0_), (h1_, r_h1_) = hs
                P.tt(P.pool, h0_[:, :], h0_[:, :], h1_[:, :], ALU.add, [r_h0_, r_h1_], [r_h0_])
                P.tt(P.dve, h0_[:, :], h0_[:, :], xr[:, 0:Ls], ALU.mult, [r_h0_, r_xr], [r_h0_])
                for ti in range(Ls // W):
                    self.head_norm_store(h0_[:, ti * W:(ti + 1) * W], W, l, self.own(l, 12), self.yin_dsts(1, off + ti * W, W), sq_r, st_r, ob_r, 4 + ti % 4, r_h0_)
            P.pop(mk2)
        P.pop(mk)

    def phase_attn(self, l, need_ctx):
        P = self.P
        li = self.lidx[l]
        mk = P.push()
        ckv = P.sbuf("ckv", [128, 2, NT], BF16)
        r_ckv = Res()
        krope = P.sbuf("krope", [64, NT], BF16)
        r_kr = Res()
        cq = P.sbuf("cq", [128, 4, NTL], BF16)
        r_cq = Res()
        sq_r = P.ring("asq", 3, [128, 512], BF16)
        st_r = P.ring("ast", 2, [128, 512], F32)
        u2_r = P.ring("au2", 2, [128, 2, 512], F32)
        u4_r = P.ring("au4", 2, [128, 4, 512], F32)
        kr_r = P.ring("akr", 2, [64, 2, 512], F32)
        cs_r = P.ring("acs", 2, [64, 2, 512], F32)
        tk_r = P.ring("atk", 2, [64, 2, 512], F32)
        for (c0, W, j) in self.tok_tiles(True):
            u2, r_u2 = u2_r.next()
            P.ld("sp", u2[:, :, :W], self.Uap(OFF_KV, 256, c0, W).rearrange("(c p) t -> p c t", p=128), writes=[r_u2])
            st, r_st = self.rms_rstd([u2[:, c, :W] for c in range(2)], W, 256, sq_r, 7, st_r, [r_u2])
            for c in range(2):
                P.stt(ckv[:, c, c0:c0 + W], u2[:, c, :W], self.sp_col(l, SP_GKV + c), st[:, :W], ALU.mult, ALU.mult, [r_u2, r_st, self.r_const], [r_ckv])
            kr, r_krt = kr_r.next()
            P.ld("act", kr[:, 0, :W], self.Uap(OFF_KR, 64, c0, W), writes=[r_krt])
            if j == 0:
                P.ld("act", kr[:, 1, :W], self.Uap(OFF_KRR, 64, c0, W), writes=[r_krt])
                cs, r_cs = cs_r.next()
                P.ld("act", cs[:, :, :W], self.rope_cs[:, :, c0 - CTX:c0 - CTX + W], writes=[r_cs])
                tk, r_tk = tk_r.next()
                P.tt(P.pool, tk[:, :, :W], kr[:, :, :W], cs[:, :, :W], ALU.mult, [r_krt, r_cs], [r_tk])
                P.tt(P.pool, krope[:, c0:c0 + W], tk[:, 0, :W], tk[:, 1, :W], ALU.add, [r_tk], [r_kr])
            else:
                P.copy(P.pool, krope[:, c0:c0 + W], kr[:, 0, :W], [r_krt], [r_kr])
        qtiles = self.loc_tiles(need_ctx)
        for (c0, W, j) in qtiles:
            u4, r_u4 = u4_r.next()
            srcq = self.UC[OFF_Q:OFF_Q + 512, c0:c0 + W] if j == 1 else self.ULOC[OFF_Q:OFF_Q + 512, c0 - CTX:c0 - CTX + W]
            P.ld("sp", u4[:, :, :W], srcq.rearrange("(c p) t -> p c t", p=128), writes=[r_u4])
            st, r_st = self.rms_rstd([u4[:, c, :W] for c in range(4)], W, 512, sq_r, 6, st_r, [r_u4])
            for c in range(4):
                P.stt(cq[:, c, c0:c0 + W], u4[:, c, :W], self.sp_col(l, SP_GQ + c), st[:, :W], ALU.mult, ALU.mult, [r_u4, r_st, self.r_const], [r_cq])
        wkv_r = P.ring("wkv", 2, [128, 2, 256], BF16)
        wq_r = P.ring("wq", 2, [128, 4, 256], BF16)
        kn_r = P.ring("kn", 2, [128, NT], BF16)
        v_r = P.ring("vv", 2, [128, NT // 128, 128], BF16)
        qn_r = P.ring("qn", 2, [128, NTL], BF16)
        qr_r = P.ring("qr", 2, [64, NTL], BF16)
        pT_r = P.ring("pT", 4, [128, 512], BF16)
        ri_r = P.ring("ri", 2, [128, 512], F32)
        oo_r = P.ring("oo", 2, [128, 512], F32)
        ob_r = P.ring("aob", 2, [128, 512], BF16)
        WKV = self.WUKV[li].rearrange("(kc p) n -> p kc n", p=128)
        WQ = self.WUQ[li].rearrange("(kc p) n -> p kc n", p=128)
        kb = 0
        for h in range(8):
            wkv, r_wkv = wkv_r.next()
            wq, r_wq = wq_r.next()
            P.ld("sp", wkv[:], WKV[:, :, h * 256:(h + 1) * 256], writes=[r_wkv])
            P.ld("sp", wq[:], WQ[:, :, h * 256:(h + 1) * 256], writes=[r_wq])
            kn, r_kn = kn_r.next()
            vv, r_vv = v_r.next()
            qn, r_qn = qn_r.next()
            qr, r_qr = qr_r.next()
            for (c0, W, j) in self.tok_tiles(True):
                pt, r_pt = self.pb[kb % 4]
                kb += 1
                for kc in range(2):
                    P.mm(pt[:, :W], wkv[:, kc, 0:128], ckv[:, kc, c0:c0 + W], kc == 0, kc == 1, [r_wkv, r_ckv], [r_pt])
                P.copy(P.act, kn[:, c0:c0 + W], pt[:, :W], [r_pt], [r_kn])
            for (c0, W, j) in qtiles:
                pt, r_pt = self.pb[kb % 4]
                kb += 1
                for kc in range(4):
                    P.mm(pt[:, :W], wq[:, kc, 0:128], cq[:, kc, c0:c0 + W], kc == 0, kc == 3, [r_wq, r_cq], [r_pt])
                P.copy(P.dve, qn[:, c0:c0 + W], pt[:, :W], [r_pt], [r_qn])
                pt, r_pt = self.pb[kb % 4]
                kb += 1
                for kc in range(4):
                    P.mm(pt[0:64, :W], wq[:, kc, 128:192], cq[:, kc, c0:c0 + W], kc == 0, kc == 3, [r_wq, r_cq], [r_pt])
                if j == 0:
                    pt2, r_pt2 = self.pb[kb % 4]
                    kb += 1
                    for kc in range(4):
                        P.mm(pt2[0:64, :W], wq[:, kc, 192:256], cq[:, kc, c0:c0 + W], kc == 0, kc == 3, [r_wq, r_cq], [r_pt2])
                    cs, r_cs = cs_r.next()
                    P.ld("act", cs[:, :, :W], self.rope_q[:, :, c0 - CTX:c0 - CTX + W], writes=[r_cs])
                    tk, r_tk = tk_r.next()
                    P.tt(P.dve, tk[:, 0, :W], pt[0:64, :W], cs[:, 0, :W], ALU.mult, [r_pt, r_cs], [r_tk])
                    P.tt(P.dve, tk[:, 1, :W], pt2[0:64, :W], cs[:, 1, :W], ALU.mult, [r_pt2, r_cs], [r_tk])
                    P.tt(P.pool, qr[:, c0:c0 + W], tk[:, 0, :W], tk[:, 1, :W], ALU.add, [r_tk], [r_qr])
                else:
                    P.copy(P.dve, qr[:, c0:c0 + W], pt[0:64, :W], [r_pt], [r_qr])
            for kg in range(0, NT // 128, 4):
                nk4 = min(4, NT // 128 - kg)
                pt, r_pt = self.pb[kb % 4]
                kb += 1
                for q4 in range(nk4):
                    kc = kg + q4
                    for k2 in range(2):
                        P.mm(pt[:, q4 * 128:(q4 + 1) * 128], ckv[:, k2, kc * 128:(kc + 1) * 128], wkv[:, k2, 128:256], k2 == 0, k2 == 1, [r_ckv, r_wkv], [r_pt])
                P.copy(P.act, vv[:, kg:kg + nk4, :], pt[:, 0:nk4 * 128].rearrange("p (a b) -> p a b", b=128), [r_pt], [r_vv])
            for qi, (c0, W, j) in enumerate(qtiles):
                nk = 2 if j == 1 else NT // 128
                po, r_po = self.pb[4 + 2 * (qi % 2)]
                pl, r_pl = self.pb[5 + 2 * (qi % 2)]
                pend = None
                for kc in range(nk + 1):
                    if kc < nk:
                        pS, r_pS = self.pb[kc % 4]
                        P.mm(pS[:, :W], kn[:, kc * 128:(kc + 1) * 128], qn[:, c0:c0 + W], True, False, [r_kn, r_qn], [r_pS])
                        P.mm(pS[:, :W], krope[:, kc * 128:(kc + 1) * 128], qr[:, c0:c0 + W], False, True, [r_kr, r_qr], [r_pS])
                        pT, r_pT = pT_r.next()
                        P.actv(pT[:, :W], pS[:, :W], AF.Exp, [r_pS], [r_pT], scale=MLA_SCALE)
                    if pend is not None:
                        pkc, ppT, pr_pT = pend
                        P.mm(po[:, :W], vv[:, pkc, :], ppT[:, :W], pkc == 0, pkc == nk - 1, [r_vv, pr_pT], [r_po])
                        P.mm(pl[:, :W], self.ones_b[:], ppT[:, :W], pkc == 0, pkc == nk - 1, [self.r_const, pr_pT], [r_pl])
                    pend = (kc, pT, r_pT) if kc < nk else None
                ri, r_ri = ri_r.next()
                P.recip(ri[:, :W], pl[:, :W], [r_pl], [r_ri])
                oo, r_oo = oo_r.next()
                P.tt(P.dve, oo[:, :W], po[:, :W], ri[:, :W], ALU.mult, [r_po, r_ri], [r_oo])
                dsts = [lambda e, h=h, c0=c0, W=W: self.YM[h * 128:(h + 1) * 128, c0:c0 + W]]
                self.head_norm_store(oo[:, :W], W, l, self.sp_col(l, SP_HG + 8 + h), dsts, sq_r, st_r, ob_r, 4 + 2 * (qi % 2), r_oo)
        P.pop(mk)

    def phase_outproj(self, l, need_ctx):
        P = self.P
        li = self.lidx[l]
        mk = P.push()
        yc_r = P.ring("yc", 2, [128, 16, 512], BF16)
        w_r = P.ring("wo", 2, [128, 16, 512], BF16)
        mix_r = P.ring("mix", 1, [128, 16, 512], F32)
        xs_r = P.ring("oxs", 1, [128, 16, 512], F32)
        hb_r = P.ring("ohb", 1, [128, 16, 512], BF16)
        sq_r = P.ring("osq", 3, [128, 512], BF16)
        st_r = P.ring("ost", 2, [128, 512], F32)
        tmp_r = P.ring("otmp", 3, [128, 512], F32)
        YMv = self.YM.rearrange("(c p) t -> p c t", p=128)
        XTv = self.XT.rearrange("(c p) t -> p c t", p=128)
        H2v = self.H2.rearrange("(c p) t -> p c t", p=128)
        Wv = self.WOUT[li].rearrange("(kc p) n -> p kc n", p=128)
        kb = 0
        ltiles = self.loc_tiles(need_ctx)
        for (c0, W, j) in ltiles:
            yc, r_yc = yc_r.next()
            P.ld("sp", yc[:, 8:16, :W], YMv[:, :, c0:c0 + W], writes=[r_yc])
            for kind in range(2):
                def ldy(e, yc=yc, c0=c0, W=W, kind=kind):
                    win = self.YG[kind].rearrange("(x i) t -> i x t", i=128)
                    return e.dma_start(out=yc[:, kind * 4:(kind + 1) * 4, :W], in_=win[:, bass.ds(self.rv(e, "x"), NR, 2), c0:c0 + W])
                P.dma("pool", ldy, writes=[r_yc])
            xs, r_xs = xs_r.next()
            P.ld("sp", xs[:, :, :W], XTv[:, :, c0:c0 + W], writes=[r_xs])
            mix, r_mix = mix_r.next()
            for g in range(4):
                wt, r_wt = w_r.next()
                P.ld("act", wt[:], Wv[:, :, g * 512:(g + 1) * 512], writes=[r_wt])
                for m in range(4):
                    pt, r_pt = self.pb[kb % 6]
                    kb += 1
                    for kc in range(16):
                        P.mm(pt[:, :W], wt[:, kc, m * 128:(m + 1) * 128], yc[:, kc, :W], kc == 0, kc == 15, [r_wt, r_yc], [r_pt])
                    P.copy(P.act if kb % 2 == 0 else P.dve, mix[:, g * 4 + m, :W], pt[:, :W], [r_pt], [r_mix])
            st, r_st = self.rms_rstd([mix[:, c, :W] for c in range(16)], W, D, sq_r, 7, st_r, [r_mix])
            for c in range(16):
                tmp, r_tmp = tmp_r.next()
                P.tt(P.pool, tmp[:, :W], mix[:, c, :W], st[:, :W], ALU.mult, [r_mix, r_st], [r_tmp])
                P.stt(xs[:, c, :W], tmp[:, :W], self.mod(2, c, j), xs[:, c, :W], ALU.mult, ALU.add, [r_tmp, r_xs, self.r_mods], [r_xs])
            P.ld("sp", XTv[:, :, c0:c0 + W], xs[:, :, :W], reads=[r_xs])
            st, r_st = self.rms_rstd([xs[:, c, :W] for c in range(16)], W, D, sq_r, 6, st_r, [r_xs])
            hb, r_hb = hb_r.next()
            for c in range(16):
                tmp, r_tmp = tmp_r.next()
                P.tt(P.dve, tmp[:, :W], xs[:, c, :W], st[:, :W], ALU.mult, [r_xs, r_st], [r_tmp])
                P.actv(hb[:, c, :W], tmp[:, :W], AF.Identity, [r_tmp, self.r_mods], [r_hb], bias=self.mod(4, c, j), scale=self.mod(3, c, j))
            P.ld("sp", H2v[:, :, c0:c0 + W], hb[:, :, :W], reads=[r_hb])
            HBv = self.HB.rearrange("(c p) k -> p c k", p=128)
            if c0 == CTX:
                P.ld("act", HBv[:, :, 0:1], hb[:, :, 0:1], reads=[r_hb], slow=True)
            if c0 + W == NTL:
                P.ld("act", HBv[:, :, 1:2], hb[:, :, W - 1:W], reads=[r_hb], slow=True)
        P.pop(mk)
        self.allgather([(self.HB, self.HG)])
        P.ld("sp", self.HGP[D:(NR + 1) * D, :], self.HG[:, :])
        P.barrier()

    def phase_ffn(self, l, need_ctx):
        P = self.P
        li = self.lidx[l]
        mk = P.push()
        OT = 342
        tiles = []
        if need_ctx:
            tiles.append((0, CTX, 0, CTX, 1))
        for lo in range(0, SL, OT):
            tiles.append((CTX, SL, lo, min(SL, lo + OT), 0))
        groups = [tiles[i:i + 2] for i in range(0, len(tiles), 2)]
        WT = OT + 2
        h2_r = P.ring("fh2", 2, [128, 16, WT], BF16)
        acts = [(P.sbuf(f"fact{t}", [128, 44, OT], BF16), [Res() for _ in range(44)]) for t in range(2)]
        wu_r = P.ring("fwu", 2, [128, 2, 16, 256], BF16)
        wd_r = P.ring("fwd", 2, [128, 44, 128], BF16)
        f_r = P.ring("ff", 1, [128, 16, OT], F32)
        xs_r = P.ring("fxs", 1, [128, 16, OT], F32)
        gc_r = P.ring("fgc", 2, [128, OT], F32)
        vc_r = P.ring("fvc", 2, [128, OT], F32)
        sq_r = P.ring("fsq", 2, [128, 512], BF16)
        st_r = P.ring("fst_", 1, [128, 512], F32)
        tmp_r = P.ring("ftmp", 2, [128, OT], F32)
        XTv = self.XT.rearrange("(c p) t -> p c t", p=128)
        H2v = self.H2.rearrange("(c p) t -> p c t", p=128)
        HGPv = self.HGP.rearrange("(c p) k -> p c k", p=128)
        WU = self.WUP[li].rearrange("(kc p) n -> p kc n", p=128)
        WD = self.WDN[li].rearrange("(kc p) n -> p kc n", p=128)
        for grp in groups:
            h2s = []
            for (off, Ls, lo, hi, j) in grp:
                Wo = hi - lo
                Wt = Wo + 2
                h2, r_h2 = h2_r.next()
                a, b = lo - 1, hi + 1
                ca, cb = 0, Wt
                if a < 0:
                    if j == 1:
                        P.memset(P.pool, h2[:, :, 0:1], 0.0, [r_h2])
                    else:
                        P.dma("act", lambda e, h2=h2: e.dma_start(out=h2[:, :, 0:1], in_=HGPv[:, 0:64, 1:2][:, bass.ds(self.rv(e, "16"), 16), :], allow_slow_non_contiguous=True), writes=[r_h2])
                    a, ca = 0, 1
                if b > Ls:
                    if j == 1:
                        P.memset(P.pool, h2[:, :, Wt - 1:Wt], 0.0, [r_h2])
                    else:
                        P.dma("act", lambda e, h2=h2, Wt=Wt: e.dma_start(out=h2[:, :, Wt - 1:Wt], in_=HGPv[:, 32:96, 0:1][:, bass.ds(self.rv(e, "16"), 16), :], allow_slow_non_contiguous=True), writes=[r_h2])
                    b, cb = Ls, Wt - 1
                P.ld("sp", h2[:, :, ca:cb], H2v[:, :, off + a:off + b], writes=[r_h2])
                h2s.append((h2, r_h2, Wo, Wt))
            for i in range(44):
                if i % 2 == 0:
                    wu, r_wu = wu_r.next()
                    P.ld("act", wu[:, 0, :, :], WU[:, :, i * 128:i * 128 + 256], writes=[r_wu])
                    P.ld("sp", wu[:, 1, :, :], WU[:, :, DFF + i * 128:DFF + i * 128 + 256], writes=[r_wu])
                s0 = (i % 2) * 128
                for t, (h2, r_h2, Wo, Wt) in enumerate(h2s):
                    pg, r_pg = self.pb[(i % 2) * 4 + 2 * t]
                    pv, r_pv = self.pb[(i % 2) * 4 + 2 * t + 1]
                    for kc in range(16):
                        P.mm(pg[:, :Wt], wu[:, 0, kc, s0:s0 + 128], h2[:, kc, :Wt], kc == 0, kc == 15, [r_wu, r_h2], [r_pg])
                    for kc in range(16):
                        P.mm(pv[:, :Wt], wu[:, 1, kc, s0:s0 + 128], h2[:, kc, :Wt], kc == 0, kc == 15, [r_wu, r_h2], [r_pv])
                for t, (h2, r_h2, Wo, Wt) in enumerate(h2s):
                    pg, r_pg = self.pb[(i % 2) * 4 + 2 * t]
                    pv, r_pv = self.pb[(i % 2) * 4 + 2 * t + 1]
                    act, r_act = acts[t]
                    gc, r_gc = gc_r.next()
                    vc, r_vc = vc_r.next()
                    for (pp, r_pp, oc, r_oc, ci) in ((pg, r_pg, gc, r_gc, i), (pv, r_pv, vc, r_vc, 44 + i)):
                        P.ts(P.dve, oc[:, :Wo], pp[:, 0:Wo], self.sp_col(l, SP_FC + 0 * 88 + ci), None, ALU.mult, None, [r_pp, self.r_const], [r_oc])
                        P.stt(oc[:, :Wo], pp[:, 1:Wo + 1], self.sp_col(l, SP_FC + 1 * 88 + ci), oc[:, :Wo], ALU.mult, ALU.add, [r_pp, r_oc, self.r_const], [r_oc])
                        P.stt(oc[:, :Wo], pp[:, 2:Wo + 2], self.sp_col(l, SP_FC + 2 * 88 + ci), oc[:, :Wo], ALU.mult, ALU.add, [r_pp, r_oc, self.r_const], [r_oc])
                    P.actv(gc[:, :Wo], gc[:, :Wo], AF.Gelu_apprx_tanh, [r_gc], [r_gc])
                    P.tt(P.pool, act[:, i, :Wo], gc[:, :Wo], vc[:, :Wo], ALU.mult, [r_gc, r_vc], [r_act[i]])
            for t, (off, Ls, lo, hi, j) in enumerate(grp):
                Wo = hi - lo
                act, r_act = acts[t]
                f, r_f = f_r.next()
                for m in range(16):
                    wd, r_wd = wd_r.next()
                    P.ld("act" if m % 2 == 0 else "sp", wd[:], WD[:, :, m * 128:(m + 1) * 128], writes=[r_wd])
                    pd, r_pd = self.pb[m % 2]
                    for kc in range(44):
                        P.mm(pd[:, :Wo], wd[:, kc, :], act[:, kc, :Wo], kc == 0, kc == 43, [r_wd, r_act[kc]], [r_pd])
                    P.copy(P.act if m % 2 == 0 else P.dve, f[:, m, :Wo], pd[:, :Wo], [r_pd], [r_f])
                st, r_st = self.rms_rstd([f[:, c, :Wo] for c in range(16)], Wo, D, sq_r, 2, st_r, [r_f])
                xs, r_xs = xs_r.next()
                P.ld("sp", xs[:, :, :Wo], XTv[:, :, off + lo:off + hi], writes=[r_xs])
                for c in range(16):
                    tmp, r_tmp = tmp_r.next()
                    P.tt(P.pool, tmp[:, :Wo], f[:, c, :Wo], st[:, :Wo], ALU.mult, [r_f, r_st], [r_tmp])
                    P.stt(xs[:, c, :Wo], tmp[:, :Wo], self.mod(5, c, j), xs[:, c, :Wo], ALU.mult, ALU.add, [r_tmp, r_xs, self.r_mods], [r_xs])
                P.ld("sp", XTv[:, :, off + lo:off + hi], xs[:, :, :Wo], reads=[r_xs])
        P.pop(mk)


def make_inmaps(inp, n_cores=8, layers=None):
    c = _consts()
    layers = list(range(DEPTH)) if layers is None else list(layers)
    shared = {k: np.ascontiguousarray(np.asarray(inp[k], dtype=np.float32)[layers]) for k in (
        "w_in", "hy_w1", "hy_w2", "mla_wuq", "mla_wukv", "w_out", "ffn_up", "ffn_down")}
    adaw = np.asarray(inp["ada_w"], dtype=np.float32)
    shared["smallp"] = _pack_small({k: np.asarray(v, dtype=np.float32) for k, v in inp.items()}).reshape(128, DEPTH * NSP)
    shared.update({k: v for k, v in c.items() if not k.startswith("decay")})
    spf = shared["smallp"].reshape(128, DEPTH, NSP)
    own_map = []
    for p_ in range(3):
        for j in range(3):
            own_map.append(SP_HYC + j * 12 + p_ * 4)
    own_map += [SP_HYB + 0, SP_HYB + 4, SP_HG, SP_HG + 4]
    for d in range(2):
        for j in range(4):
            own_map.append(SP_LC + d * 16 + j * 4)
    own_map += [SP_BA, SP_BA + 4, SP_BX, SP_BX + 4, SP_LAM, SP_LAM + 4]
    w3 = np.asarray(inp["hy_w3"], dtype=np.float32)[layers].reshape(len(layers), 64, 4, 4, 128)
    lwa = np.asarray(inp["lru_wa"], dtype=np.float32)[layers]
    lwx = np.asarray(inp["lru_wx"], dtype=np.float32)[layers]
    maps = []
    for core in range(n_cores):
        b = core // NR
        r = core % NR
        m = dict(shared)
        m["x"] = np.ascontiguousarray(np.asarray(inp["x"][b, r * SL:(r + 1) * SL], dtype=np.float32))
        m["ownp"] = np.ascontiguousarray(spf[:, :, [cc + r for cc in own_map]].reshape(128, DEPTH * NOWN))
        m["hy_w3_own"] = np.ascontiguousarray(w3[:, :, :, r, :].reshape(len(layers), 64, 512))
        m["lru_wa_own"] = np.ascontiguousarray(lwa[:, :, r])
        m["lru_wx_own"] = np.ascontiguousarray(lwx[:, :, r])
        m["ada_w_own"] = np.ascontiguousarray(adaw[layers][:, :, r * (6 * D // NR):(r + 1) * (6 * D // NR)])
        m["rope_q"] = np.ascontiguousarray(c["rope_cs"][:, :, r * SL:(r + 1) * SL])
        m["decayL"] = np.ascontiguousarray(c["decayL"][:, r * 128:(r + 1) * 128])
        m["decayC"] = np.ascontiguousarray(c["decayC"][:, r * 128:(r + 1) * 128])
        m["ctx"] = np.ascontiguousarray(np.asarray(inp["ctx"][b], dtype=np.float32))
        cv = np.stack([np.asarray(inp["c"][b], dtype=np.float32), np.asarray(inp["c_ctx"], dtype=np.float32)], axis=-1)
        m["cvec"] = np.ascontiguousarray(cv.reshape(16, 128, 2).transpose(1, 0, 2).reshape(128, 32))
        maps.append(m)
    return maps


_NC_CACHE = {}


def kernel(**inputs):
    if "nc" not in _NC_CACHE:
        _NC_CACHE["nc"] = Builder(range(DEPTH)).build()
    nc = _NC_CACHE["nc"]
    maps = make_inmaps(inputs)
    res = run_bass_kernel_spmd(nc, maps, core_ids=list(range(8)))
    out = np.stack([np.concatenate([np.asarray(res.results[b * NR + r]["out"]) for r in range(NR)], axis=0) for b in range(2)], axis=0)
    return out.astype(np.float32)
```
